# Optimizing a Trainium2 kernel written in Bass

```python
import jax
import jax.numpy as jnp
from jax import lax
import numpy as np

D_MODEL = 1024
BATCH = 4
SEQ = 4096
DEPTH = 2
DEC_BATCH = 32
DEC_SEQ = 1
PAST_LEN = 16384
PAGE_SIZE = 128

W_A = D_MODEL // 2
A_GROUP_DIM = 128
G_A = W_A // A_GROUP_DIM
CHUNK = 128
W_B = D_MODEL - W_A
HD_B = 64
H_B = W_B // HD_B
DIL_PATTERNS = ((128, 1), (512, 4), (2048, 16))
WIN_MAX = max(w for w, _ in DIL_PATTERNS)
ATT_BLOCK = 128
N_IN_EVEN = 2 * W_A + 3 * W_B
D_RNN = D_MODEL
RG_BLOCK = 128
RG_HEADS = D_RNN // RG_BLOCK
RG_CONV = 4
RG_C = 8.0
D_FF = 2816
FFN_CONV = 3
N_EVEN = (DEPTH + 1) // 2
N_ODD = DEPTH // 2
EPS = 1e-6
NEG_INF = -1e30

kernel_name = 'hybrid_sgu_dilattn_rglru_convffn_step'


def _rmsnorm(x):
    xf = x.astype(jnp.float32)
    return (xf * lax.rsqrt(jnp.mean(xf * xf, -1, keepdims=True) + EPS)).astype(x.dtype)


def _layernorm(x, g, b):
    xf = x.astype(jnp.float32)
    mu = jnp.mean(xf, -1, keepdims=True)
    var = jnp.mean(jnp.square(xf - mu), -1, keepdims=True)
    return ((xf - mu) * lax.rsqrt(var + EPS)).astype(x.dtype) * g + b


def _causal_dwconv(x, buf, w, b):
    K = w.shape[0]
    T = x.shape[1]
    xp = jnp.concatenate([buf.astype(x.dtype), x], axis=1)
    y = b + w[0] * xp[:, :T]
    for k in range(1, K):
        y = y + w[k] * xp[:, k:k + T]
    return y, xp[:, xp.shape[1] - (K - 1):]


def _chunk_sgu(u, v, w_s, b_s):
    N, T = u.shape[:2]
    nc = -(-T // CHUNK)
    pad = nc * CHUNK - T
    vp = jnp.pad(v, ((0, 0), (0, pad), (0, 0), (0, 0))).reshape(N, nc, CHUNK, G_A, A_GROUP_DIM)
    w_causal = jnp.tril(w_s)
    mix = jnp.einsum('gts,ncsgd->nctgd', w_causal, vp) + b_s.T[None, None, :, :, None]
    mix = mix.reshape(N, nc * CHUNK, G_A, A_GROUP_DIM)[:, :T]
    return u * mix


def _band_attn(q, k, v, n_back):
    N, M, H, Dh = q.shape
    nb = -(-M // ATT_BLOCK)
    pad = nb * ATT_BLOCK - M

    def blocks(t):
        t = jnp.pad(t, ((0, 0), (0, pad), (0, 0), (0, 0)))
        return t.reshape(N, nb, ATT_BLOCK, H, Dh)

    def with_prev(t):
        prev = jnp.concatenate([jnp.zeros_like(t[:, :1]), t[:, :-1]], axis=1)
        return jnp.concatenate([prev, t], axis=2)

    qb = blocks(q)
    kc = with_prev(blocks(k))
    vc = with_prev(blocks(v))
    s = jnp.einsum('nbqhd,nbkhd->nbhqk', qb, kc).astype(jnp.float32) * (Dh ** -0.5)
    qi = jnp.arange(ATT_BLOCK)[:, None]
    kj = jnp.arange(2 * ATT_BLOCK)[None, :]
    dist = qi + ATT_BLOCK - kj
    key_pos = (jnp.arange(nb) * ATT_BLOCK)[:, None, None] + kj - ATT_BLOCK
    valid = (dist >= 0) & (dist <= n_back) & (key_pos >= 0)
    s = jnp.where(valid[None, :, None], s, NEG_INF)
    m = jnp.max(s, -1, keepdims=True)
    p = jnp.exp(s - m)
    den = jnp.sum(p, -1)
    o = jnp.einsum('nbhqk,nbkhd->nbqhd', p.astype(v.dtype), vc)
    o = o / jnp.swapaxes(den, 2, 3)[..., None].astype(o.dtype)
    lse = jnp.swapaxes(m[..., 0] + jnp.log(den), 2, 3)
    return (o.reshape(N, nb * ATT_BLOCK, H, Dh)[:, :M],
            lse.reshape(N, nb * ATT_BLOCK, H)[:, :M])


def _mix_dilations(outs, lses):
    wts = jax.nn.softmax(jnp.stack(lses, 0), axis=0)
    return jnp.einsum('pnth,pnthd->nthd', wts.astype(outs[0].dtype), jnp.stack(outs, 0))


def _to_strided(t, d):
    N, T, H, Dh = t.shape
    return t.reshape(N, T // d, d, H, Dh).transpose(0, 2, 1, 3, 4).reshape(N * d, T // d, H, Dh)


def _dilated_prompt(q, k, v):
    N, T, H, Dh = q.shape
    outs, lses = [], []
    for w, d in DIL_PATTERNS:
        o, lse = _band_attn(_to_strided(q, d), _to_strided(k, d), _to_strided(v, d), w // d)
        outs.append(o.reshape(N, d, T // d, H, Dh).transpose(0, 2, 1, 3, 4).reshape(N, T, H, Dh))
        lses.append(lse.reshape(N, d, T // d, H).transpose(0, 2, 1, 3).reshape(N, T, H))
    return _mix_dilations(outs, lses)


def _dilated_sample(q, kc, vc):
    N, T, H, Dh = q.shape
    L = kc.shape[1] - T
    t_idx = jnp.arange(T)[:, None]
    outs, lses = [], []
    for w, d in DIL_PATTERNS:
        j = jnp.arange(w // d + 1)[None, :]
        idx = L + t_idx - j * d
        valid = idx >= 0
        idx = jnp.maximum(idx, 0)
        kg = kc[:, idx]
        vg = vc[:, idx]
        s = jnp.einsum('nthd,ntjhd->nthj', q, kg).astype(jnp.float32) * (Dh ** -0.5)
        s = jnp.where(valid[None, :, None, :], s, NEG_INF)
        m = jnp.max(s, -1, keepdims=True)
        p = jnp.exp(s - m)
        den = jnp.sum(p, -1)
        o = jnp.einsum('nthj,ntjhd->nthd', p.astype(vc.dtype), vg) / den[..., None].astype(vc.dtype)
        outs.append(o)
        lses.append(m[..., 0] + jnp.log(den))
    return _mix_dilations(outs, lses)


def _even_mixer(h, w_in, ln_g, ln_b, w_s, b_s, w_out, k_buf, v_buf):
    N, T, _ = h.shape
    z = h @ w_in
    u_a = jax.nn.gelu(z[..., :W_A])
    v_a = _layernorm(jax.nn.gelu(z[..., W_A:2 * W_A]), ln_g, ln_b)
    q = z[..., 2 * W_A:2 * W_A + W_B].reshape(N, T, H_B, HD_B)
    k = z[..., 2 * W_A + W_B:2 * W_A + 2 * W_B].reshape(N, T, H_B, HD_B)
    v = z[..., 2 * W_A + 2 * W_B:].reshape(N, T, H_B, HD_B)
    a_out = _chunk_sgu(u_a.reshape(N, T, G_A, A_GROUP_DIM), v_a.reshape(N, T, G_A, A_GROUP_DIM),
                       w_s, b_s).reshape(N, T, W_A)
    if k_buf is None:
        b_out = _dilated_prompt(q, k, v)
        keep = min(WIN_MAX, T)
        k_new, v_new = k[:, T - keep:], v[:, T - keep:]
    else:
        kc = jnp.concatenate([k_buf.astype(k.dtype), k], axis=1)
        vc = jnp.concatenate([v_buf.astype(v.dtype), v], axis=1)
        b_out = _dilated_sample(q, kc, vc)
        k_new, v_new = kc[:, T:], vc[:, T:]
    out = jnp.concatenate([a_out, b_out.reshape(N, T, W_B)], axis=-1) @ w_out
    return out, v_a, k_new, v_new


def _rglru_mixer(h, w_in, conv_w, conv_b, w_a, b_a, w_x, b_x, lam, w_out, conv_buf, h0):
    N, T, _ = h.shape
    z = h @ w_in
    gate = z[..., :D_RNN]
    xr = z[..., D_RNN:]
    xc, new_buf = _causal_dwconv(xr, conv_buf, conv_w, conv_b)
    xb = xc.reshape(N, T, RG_HEADS, RG_BLOCK)
    r = jax.nn.sigmoid(jnp.einsum('ntgi,gio->ntgo', xb, w_a).reshape(N, T, D_RNN) + b_a)
    i = jax.nn.sigmoid(jnp.einsum('ntgi,gio->ntgo', xb, w_x).reshape(N, T, D_RNN) + b_x)
    log_a = -RG_C * r.astype(jnp.float32) * jax.nn.softplus(-lam.astype(jnp.float32))
    a = jnp.exp(log_a)
    bx = jnp.sqrt(-jnp.expm1(2.0 * log_a)) * (i * xc).astype(jnp.float32)
    bx = bx.at[:, 0].add(a[:, 0] * h0.astype(jnp.float32))

    def comb(lhs, rhs):
        return (lhs[0] * rhs[0], rhs[0] * lhs[1] + rhs[1])

    _, hs = lax.associative_scan(comb, (a, bx), axis=1)
    y = jax.nn.gelu(gate) * hs.astype(h.dtype)
    return y @ w_out, new_buf, hs[:, -1].astype(h0.dtype)


def _conv_ffn(h, w_up, conv_w, conv_b, w_down, buf):
    up = h @ w_up
    uc, new_buf = _causal_dwconv(up, buf, conv_w, conv_b)
    a, b = jnp.split(uc, 2, axis=-1)
    return (jax.nn.silu(a) * b) @ w_down, new_buf


def _trunk(x, c, win_k, win_v, rg_conv, rg_h, ffn_buf, P):
    n = x.shape[0]
    win_k_new, win_v_new, chunk_v, rg_conv_new, rg_h_new, ffn_new = [], [], [], [], [], []
    for l in range(DEPTH):
        mod = (jax.nn.silu(c) @ P['w_ada'][l] + P['b_ada'][l])[:, None, :]
        sh1, sc1, g1, sh2, sc2, g2 = jnp.split(mod, 6, axis=-1)
        h = _rmsnorm(x) * (1 + sc1) + sh1
        if l % 2 == 0:
            e = l // 2
            kb = None if win_k is None else win_k[e]
            vb = None if win_v is None else win_v[e]
            mo, v_a, kn, vn = _even_mixer(h, P['w_in_even'][e], P['ln_v_g'][e], P['ln_v_b'][e],
                                          P['w_sgu'][e], P['b_sgu'][e], P['w_out_even'][e], kb, vb)
            win_k_new.append(kn)
            win_v_new.append(vn)
            chunk_v.append(v_a)
        else:
            o = l // 2
            cb = jnp.zeros((n, RG_CONV - 1, D_RNN), x.dtype) if rg_conv is None else rg_conv[o]
            h0 = jnp.zeros((n, D_RNN), jnp.float32) if rg_h is None else rg_h[o]
            mo, cn, hn = _rglru_mixer(h, P['w_in_odd'][o], P['rg_conv_w'][o], P['rg_conv_b'][o],
                                      P['rg_w_a'][o], P['rg_b_a'][o], P['rg_w_x'][o], P['rg_b_x'][o],
                                      P['rg_lambda'][o], P['w_out_odd'][o], cb, h0)
            rg_conv_new.append(cn)
            rg_h_new.append(hn)
        x = x + g1 * mo
        h = _rmsnorm(x) * (1 + sc2) + sh2
        fb = jnp.zeros((n, FFN_CONV - 1, 2 * D_FF), x.dtype) if ffn_buf is None else ffn_buf[l]
        fo, fn = _conv_ffn(h, P['ffn_w_up'][l], P['ffn_conv_w'][l], P['ffn_conv_b'][l],
                           P['ffn_w_down'][l], fb)
        ffn_new.append(fn)
        x = x + g2 * fo
    y = _rmsnorm(x) * P['final_g']
    return (y, jnp.stack(win_k_new), jnp.stack(win_v_new), jnp.stack(chunk_v),
            jnp.stack(rg_conv_new), jnp.stack(rg_h_new), jnp.stack(ffn_new))


def setup_inputs(seed: int = 0) -> dict:
    key = jax.random.key(seed)
    ks = iter(jax.random.split(key, 48))

    def nrm(shape, s):
        return jax.random.normal(next(ks), shape, jnp.float32) * s

    L = min(WIN_MAX, PAST_LEN)
    F2 = 2 * D_FF
    a_init = jax.random.uniform(next(ks), (N_ODD, D_RNN), jnp.float32, 0.9, 0.999)
    return {
        'x_prompt': nrm((BATCH, SEQ, D_MODEL), 1.0),
        'x_sample': nrm((DEC_BATCH, DEC_SEQ, D_MODEL), 1.0),
        'cache_win_k': nrm((N_EVEN, DEC_BATCH, L, H_B, HD_B), 1.0),
        'cache_win_v': nrm((N_EVEN, DEC_BATCH, L, H_B, HD_B), 1.0),
        'state_rglru_conv': nrm((N_ODD, DEC_BATCH, RG_CONV - 1, D_RNN), 0.5),
        'state_rglru_h': nrm((N_ODD, DEC_BATCH, D_RNN), 0.5),
        'state_ffn_conv': nrm((DEPTH, DEC_BATCH, FFN_CONV - 1, F2), 0.5),
        'c_prompt': nrm((BATCH, D_MODEL), 1.0),
        'c_sample': nrm((DEC_BATCH, D_MODEL), 1.0),
        'w_ada': nrm((DEPTH, D_MODEL, 6 * D_MODEL), 0.5 * D_MODEL ** -0.5),
        'b_ada': nrm((DEPTH, 6 * D_MODEL), 0.02),
        'w_in_even': nrm((N_EVEN, D_MODEL, N_IN_EVEN), D_MODEL ** -0.5),
        'ln_v_g': 1.0 + nrm((N_EVEN, W_A), 0.02),
        'ln_v_b': nrm((N_EVEN, W_A), 0.02),
        'w_sgu': nrm((N_EVEN, G_A, CHUNK, CHUNK), CHUNK ** -0.5),
        'b_sgu': 1.0 + nrm((N_EVEN, G_A, CHUNK), 0.02),
        'w_out_even': nrm((N_EVEN, W_A + W_B, D_MODEL), (W_A + W_B) ** -0.5),
        'w_in_odd': nrm((N_ODD, D_MODEL, 2 * D_RNN), D_MODEL ** -0.5),
        'rg_conv_w': nrm((N_ODD, RG_CONV, D_RNN), RG_CONV ** -0.5),
        'rg_conv_b': nrm((N_ODD, D_RNN), 0.02),
        'rg_w_a': nrm((N_ODD, RG_HEADS, RG_BLOCK, RG_BLOCK), RG_BLOCK ** -0.5),
        'rg_b_a': nrm((N_ODD, D_RNN), 0.02),
        'rg_w_x': nrm((N_ODD, RG_HEADS, RG_BLOCK, RG_BLOCK), RG_BLOCK ** -0.5),
        'rg_b_x': nrm((N_ODD, D_RNN), 0.02),
        'rg_lambda': jnp.log(a_init) - jnp.log1p(-a_init),
        'w_out_odd': nrm((N_ODD, D_RNN, D_MODEL), D_RNN ** -0.5),
        'ffn_w_up': nrm((DEPTH, D_MODEL, F2), D_MODEL ** -0.5),
        'ffn_conv_w': nrm((DEPTH, FFN_CONV, F2), FFN_CONV ** -0.5),
        'ffn_conv_b': nrm((DEPTH, F2), 0.02),
        'ffn_w_down': nrm((DEPTH, D_FF, D_MODEL), D_FF ** -0.5),
        'final_g': 1.0 + nrm((D_MODEL,), 0.02),
    }


def reference(x_prompt, x_sample, cache_win_k, cache_win_v, state_rglru_conv, state_rglru_h,
              state_ffn_conv, c_prompt, c_sample, w_ada, b_ada, w_in_even, ln_v_g, ln_v_b,
              w_sgu, b_sgu, w_out_even, w_in_odd, rg_conv_w, rg_conv_b, rg_w_a, rg_b_a,
              rg_w_x, rg_b_x, rg_lambda, w_out_odd, ffn_w_up, ffn_conv_w, ffn_conv_b,
              ffn_w_down, final_g):
    P = dict(w_ada=w_ada, b_ada=b_ada, w_in_even=w_in_even, ln_v_g=ln_v_g, ln_v_b=ln_v_b,
             w_sgu=w_sgu, b_sgu=b_sgu, w_out_even=w_out_even, w_in_odd=w_in_odd,
             rg_conv_w=rg_conv_w, rg_conv_b=rg_conv_b, rg_w_a=rg_w_a, rg_b_a=rg_b_a,
             rg_w_x=rg_w_x, rg_b_x=rg_b_x, rg_lambda=rg_lambda, w_out_odd=w_out_odd,
             ffn_w_up=ffn_w_up, ffn_conv_w=ffn_conv_w, ffn_conv_b=ffn_conv_b,
             ffn_w_down=ffn_w_down, final_g=final_g)
    (y_prompt, win_k_prompt, win_v_prompt, _unused_prompt_chunk, rglru_conv_prompt,
     rglru_h_prompt, ffn_conv_prompt) = _trunk(x_prompt, c_prompt, None, None, None, None, None, P)
    (y_sample, win_k_sample, win_v_sample, chunk_v_sample, rglru_conv_sample,
     rglru_h_sample, ffn_conv_sample) = _trunk(x_sample, c_sample, cache_win_k, cache_win_v,
                                               state_rglru_conv, state_rglru_h, state_ffn_conv, P)
    return (y_prompt, y_sample, win_k_prompt, win_v_prompt, rglru_conv_prompt, rglru_h_prompt,
            ffn_conv_prompt, chunk_v_sample, win_k_sample, win_v_sample, rglru_conv_sample,
            rglru_h_sample, ffn_conv_sample)
```

```python
import contextlib
import numpy as np
import concourse.bass as bass
import concourse.mybir as mybir
from concourse.bass_utils import run_bass_kernel_spmd

F32 = mybir.dt.float32
BF16 = mybir.dt.bfloat16
AF = mybir.ActivationFunctionType
ALU = mybir.AluOpType
AX = mybir.AxisListType

D = 1024
NCH = 8
WA = 512
NIN_E = 2560
DFF = 2816
F2 = 5632
NJ = 22
NS = 4
LCACHE = 2048
EPS = 1e-6
PATTERNS = ((128, 1), (512, 4), (2048, 16))
GELU = AF.Gelu_apprx_tanh

PAR_SPEC = [("b_ada", 96), ("ln_g", 4), ("ln_b", 4), ("rg_cw", 32), ("rg_cb", 8), ("rg_ba", 8),
            ("rg_bx", 8), ("rg_lam", 8), ("f_cw", 2 * 3 * 44), ("f_cb", 2 * 44), ("fin_g", 8)]
PAR_OFF = {}
_o = 0
for _n, _w in PAR_SPEC:
    PAR_OFF[_n] = _o
    _o += _w
NPAR = _o


class Prog:
    NDS = 8

    def __init__(self, nc, es):
        self.nc = nc
        self.eng = {"pe": nc.tensor, "act": nc.scalar, "dve": nc.vector, "pool": nc.gpsimd, "sp": nc.sync}
        self.sem = {e: es.enter_context(nc.semaphore("s_" + e)) for e in self.eng}
        self.cnt = {e: 0 for e in self.eng}
        self.dsem = {q: [es.enter_context(nc.semaphore("d_%s%d" % (q, i))) for i in range(self.NDS)]
                     for q in ("sp", "pool", "actq")}
        self.qeng = {"sp": "sp", "pool": "pool", "actq": "act"}
        self.dcnt = {q: 0 for q in self.dsem}
        self.seen = {e: {} for e in self.eng}
        self.ops = []

    def op(self, eng, fn, r=(), w=()):
        self.ops.append([eng, fn, tuple(r), tuple(w), False])

    def dma(self, q, fn, r=(), w=()):
        self.ops.append([q, fn, tuple(r), tuple(w), True])

    def _wait(self, e, sem, val):
        key = sem.name if hasattr(sem, "name") else id(sem)
        if self.seen[e].get(key, 0) >= val:
            return
        self.seen[e][key] = val
        self.eng[e].wait_ge(sem, val)

    def flush(self):
        import os
        self.nflush = getattr(self, "nflush", 0) + 1
        stop = int(os.environ.get("K_STOP", "99"))
        only = os.environ.get("K_ONLY")
        if self.nflush > stop or (only and str(self.nflush) not in only.split(",")):
            self.ops = []
            return
        kops = os.environ.get("K_OPS")
        if kops and self.nflush == stop:
            self.ops = self.ops[:int(kops)]
            print("last kept op:", self.ops[-1][0], self.ops[-1][2], self.ops[-1][3], "of", len(self.ops))
        ops = self.ops
        n = len(ops)
        lastw, readers = {}, {}
        deps = [None] * n
        needed = [False] * n
        for i, (e, fn, r, w, isd) in enumerate(ops):
            d = set()
            for k in r:
                if k in lastw:
                    d.add(lastw[k])
            for k in w:
                if k in lastw:
                    d.add(lastw[k])
                for j in readers.get(k, ()):
                    d.add(j)
            d.discard(i)
            dd = []
            for j in d:
                if ops[j][0] == "pe" and e == "pe" and not ops[j][4] and not isd:
                    continue
                dd.append(j)
                needed[j] = True
            deps[i] = sorted(dd)
            for k in r:
                readers.setdefault(k, []).append(i)
            for k in w:
                lastw[k] = i
                readers[k] = []
        lastop = {}
        for i, (e, fn, r, w, isd) in enumerate(ops):
            if not isd:
                lastop[e] = i
        for i in lastop.values():
            needed[i] = True
        sig = [None] * n
        for i, (e, fn, r, w, isd) in enumerate(ops):
            q = e
            if isd:
                e = self.qeng[q]
            for j in deps[i]:
                s, v = sig[j]
                self._wait(e, s, v)
            if isd:
                m = self.dcnt[q]
                self.dcnt[q] += 1
                s = self.dsem[q][m % self.NDS]
                rnd = m // self.NDS
                if rnd > 0:
                    self._wait(e, s, 16 * rnd)
                ins = fn()
                ins.then_inc(s, 16)
                sig[i] = (s, 16 * (rnd + 1))
            else:
                ins = fn()
                if needed[i]:
                    self.cnt[e] += 1
                    ins.then_inc(self.sem[e], 1)
                    sig[i] = (self.sem[e], self.cnt[e])
        self.ops = []
        self.barrier()

    def barrier(self, final=False):
        for e in self.eng:
            for e2 in self.eng:
                if e2 != e and self.cnt[e2] > 0:
                    self._wait(e, self.sem[e2], self.cnt[e2])
            for q in self.dsem:
                if q == "actq" and not final:
                    continue
                m = self.dcnt[q]
                for k in range(min(m, self.NDS)):
                    cntk = (m - k + self.NDS - 1) // self.NDS
                    self._wait(e, self.dsem[q][k], 16 * cntk)


def build_program(T):
    nc = bass.Bass("TRN2", target_bir_lowering=False)
    TK = min(LCACHE, T)
    NT5 = T // 512
    NT2 = T // 256

    def din(name, shape, dt=F32):
        return nc.dram_tensor(name, list(shape), dt, kind="ExternalInput").ap()

    def dout(name, shape):
        return nc.dram_tensor(name, list(shape), F32, kind="ExternalOutput").ap()

    def dscr(name, shape, dt=F32):
        import os
        kind = "ExternalOutput" if (os.environ.get("K_DBG") and name in ("x1T", "x2T", "x3T", "O_s", "aoT_s")) else "Internal"
        return nc.dram_tensor(name, list(shape), dt, kind=kind).ap()

    I = dict(
        xT=din("xT", [128, 8, T]), xsT=din("xsT", [128, 8, NS]), cT=din("cT", [128, 8, 1 + NS]),
        wada=din("wada", [2, 128, 8, 6144]), params=din("params", [128, NPAR]),
        w_in_e=din("w_in_e", [128, 8, NIN_E]), w_out_e=din("w_out_e", [128, 8, D]),
        w_out_eh=din("w_out_eh", [64, 8, D]), w_sguT=din("w_sguT", [128, 4, 128]),
        b_sgu=din("b_sgu", [1, 512]), sg0=din("sg0", [1, 8]),
        w_in_o=din("w_in_o", [128, 8, 2048]), rg_wa=din("rg_wa", [128, 8, 128]),
        rg_wx=din("rg_wx", [128, 8, 128]), w_out_o=din("w_out_o", [128, 8, D]),
        w_up=din("w_up", [2, 128, 8, F2]), w_dn=din("w_dn", [2, 128, NJ, D]),
        ck=din("ck", [NS, LCACHE, 512]), cv=din("cv", [NS, LCACHE, 512]),
        st_rgc=din("st_rgc", [128, 8, 3, NS]), st_rgh=din("st_rgh", [128, 8, NS]),
        st_ffn=din("st_ffn", [2, 128, 44, 2, NS]),
        rowmask=din("rowmask", [128, NS]), dmask=din("dmask", [128, 512]),
    )
    O = dict(
        yT=dout("yT", [128, 8, T]), ysT=dout("ysT", [128, 8, NS]),
        kT_o=dout("kT_o", [128, 4, TK]), vT_o=dout("vT_o", [128, 4, TK]),
        rgc_o=dout("rgc_o", [128, 8, 3]), rgh_o=dout("rgh_o", [128, 8]),
        ffn_o=dout("ffn_o", [2, 128, 44, 2]), cv_o=dout("cv_o", [128, 4, NS]),
        wk_s=dout("wk_s", [NS, LCACHE, 512]), wv_s=dout("wv_s", [NS, LCACHE, 512]),
        rgc_s=dout("rgc_s", [128, 8, 3, NS]), rgh_s=dout("rgh_s", [128, 8, NS]),
        ffn_s=dout("ffn_s", [2, 128, 44, 2, NS]),
    )
    X1 = dscr("x1T", [128, 8, T]); X2 = dscr("x2T", [128, 8, T]); X3 = dscr("x3T", [128, 8, T])
    QS = dscr("qT_s", [128, 4, T], BF16); KS = dscr("kT_s", [128, 4, T], BF16); VS = dscr("vT_s", [128, 4, T], BF16)
    AOS = dscr("aoT_s", [128, 4, T], BF16)
    OS = dscr("O_s", [3, T, 8 * 66])
    QKVS = dscr("qkv_s", [NS, 1536])
    XS1 = None

    top = contextlib.ExitStack()
    with top:
        P = Prog(nc, top)

        def SB(es, name, shape, dt=F32):
            return es.enter_context(nc.sbuf_tensor("sb_" + name, list(shape), dt))

        def PS(es, name, shape, dt=F32):
            return es.enter_context(nc.psum_tensor("ps_" + name, list(shape), dt))

        V, A, G, PE = nc.vector, nc.scalar, nc.gpsimd, nc.tensor

        par = SB(top, "par", [128, NPAR])
        ones_b = SB(top, "ones_b", [128, 128], BF16)
        ones_f = SB(top, "ones_f", [128, 128])
        ident_f = SB(top, "ident_f", [128, 128])
        ident_b = SB(top, "ident_b", [128, 128], BF16)
        modP = SB(top, "modP", [128, 2, 48])
        modS = SB(top, "modS", [128, 2, 48, NS])
        cst = SB(top, "cst", [128, 8])
        cst2 = SB(top, "cst2", [128, 8])
        xs_cur = SB(top, "xs_cur", [128, 8, NS])
        epsb = SB(top, "epsb", [128, 1])
        oneb = SB(top, "oneb", [128, 1])

        def pcol(name, i=0, n=1):
            o = PAR_OFF[name] + i
            return par[:, o:o + n]

        P.dma("sp", lambda: nc.sync.dma_start(out=par[:], in_=I["params"][:, :]), w=["par"])
        P.op("pool", lambda: G.memset(ones_f[:], 1.0), w=["ones_f"])
        P.op("pool", lambda: G.memset(ones_b[:], 1.0), w=["ones_b"])
        P.op("pool", lambda: G.memset(epsb[:], EPS), w=["epsb"])
        P.op("pool", lambda: G.memset(oneb[:], 1.0), w=["oneb"])
        P.op("pool", lambda: G.memset(ident_f[:], 1.0), w=["ident_f"])
        P.op("pool", lambda: G.affine_select(out=ident_f[:], in_=ident_f[:], pattern=[[-1, 128]],
                                             compare_op=ALU.is_equal, fill=0.0, base=0, channel_multiplier=1),
             r=["ident_f"], w=["ident_f"])
        P.op("dve", lambda: V.tensor_copy(out=ident_b[:], in_=ident_f[:]), r=["ident_f"], w=["ident_b"])
        P.dma("sp", lambda: nc.sync.dma_start(out=xs_cur[:], in_=I["xsT"][:, :, :]), w=["xs"])

        with contextlib.ExitStack() as ph:
            cT = SB(ph, "cT_sb", [128, 8, 1 + NS])
            cb = SB(ph, "cb_sb", [128, 8, 1 + NS], BF16)
            wada = SB(ph, "wada_sb", [128, 8, 6144], BF16)
            mps = PS(ph, "mod_ps", [128, 48, 1 + NS])
            P.dma("sp", lambda: nc.sync.dma_start(out=cT[:], in_=I["cT"][:, :, :]), w=["cT"])
            P.op("act", lambda: A.activation(out=cb[:], in_=cT[:], func=AF.Silu), r=["cT"], w=["cb"])
            for l in range(2):
                for kc in range(8):
                    P.dma("pool", (lambda l=l, kc=kc: G.dma_start(out=wada[:, kc, :], in_=I["wada"][l, :, kc, :])),
                          w=[("wada", kc)])
                for n in range(48):
                    for kc in range(8):
                        P.op("pe", (lambda n=n, kc=kc: PE.matmul(mps[:, n, :], lhsT=wada[:, kc, n * 128:(n + 1) * 128],
                                                                  rhs=cb[:, kc, :], start=(kc == 0), stop=(kc == 7))),
                             r=[("wada", kc), "cb"], w=["mps"])
                bada = par[:, PAR_OFF["b_ada"] + l * 48: PAR_OFF["b_ada"] + (l + 1) * 48]
                P.op("dve", (lambda l=l, bada=bada: V.tensor_tensor(out=modP[:, l, :], in0=mps[:, :, 0], in1=bada, op=ALU.add)),
                     r=["mps", "par"], w=["modP"])
                P.op("dve", (lambda l=l, bada=bada: V.tensor_tensor(
                    out=modS[:, l, :, :], in0=mps[:, :, 1:1 + NS],
                    in1=bada.unsqueeze(2).to_broadcast([128, 48, NS]), op=ALU.add)),
                     r=["mps", "par"], w=["modS"])
            for l in range(2):
                for c0 in (8, 32):
                    P.op("dve", (lambda l=l, c0=c0: V.tensor_scalar_add(out=modP[:, l, c0:c0 + 8], in0=modP[:, l, c0:c0 + 8], scalar1=1.0)),
                         r=["modP"], w=["modP"])
                    P.op("dve", (lambda l=l, c0=c0: V.tensor_scalar_add(out=modS[:, l, c0:c0 + 8, :], in0=modS[:, l, c0:c0 + 8, :], scalar1=1.0)),
                         r=["modS"], w=["modS"])
            lam = pcol("rg_lam", 0, 8)
            P.op("act", lambda: A.activation(out=cst[:], in_=lam, func=AF.Exp, scale=-1.0), r=["par"], w=["cst"])
            P.op("act", lambda: A.activation(out=cst[:], in_=cst[:], func=AF.Ln, bias=oneb[:, 0:1], scale=1.0), r=["cst", "oneb"], w=["cst"])
            P.op("dve", lambda: V.tensor_scalar_mul(out=cst2[:], in0=cst[:], scalar1=-16.0), r=["cst"], w=["cst2"])
            P.op("dve", lambda: V.tensor_scalar_mul(out=cst[:], in0=cst[:], scalar1=-8.0), r=["cst", "cst2"], w=["cst"])
            P.flush()

        SH1, SC1, G1, SH2, SC2, G2 = 0, 8, 16, 24, 32, 40

        def norm_prompt(xkey, tag, xt, TT, l, sh0, sc0, sqb, nps, rt, hT, xn, hkey=None):
            hkey = hkey or (tag, "hT")
            P.op("act", lambda: A.activation(out=sqb[:, :, :TT], in_=xt[:, :, :TT], func=AF.Square), r=[xkey], w=[(tag, "sqb")])
            for c in range(8):
                P.op("pe", (lambda c=c: PE.matmul(nps[:, :TT], lhsT=ones_b[:], rhs=sqb[:, c, :TT], start=(c == 0), stop=(c == 7))),
                     r=[(tag, "sqb"), "ones_b"], w=[(tag, "nps")])
            P.op("act", lambda: A.activation(out=rt[:, :TT], in_=nps[:, :TT], func=AF.Sqrt, scale=1.0 / D, bias=epsb[:, 0:1]),
                 r=[(tag, "nps"), "epsb"], w=[(tag, "rt")])
            P.op("dve", lambda: V.reciprocal(out=rt[:, :TT], in_=rt[:, :TT]), r=[(tag, "rt")], w=[(tag, "rt")])
            P.op("dve", lambda: V.tensor_tensor(out=xn[:, :, :TT], in0=xt[:, :, :TT],
                                                in1=rt[:, :TT].unsqueeze(1).to_broadcast([128, 8, TT]), op=ALU.mult),
                 r=[xkey, (tag, "rt")], w=[(tag, "xn")])
            for c in range(8):
                P.op("act", (lambda c=c: A.activation(out=hT[:, c, :TT], in_=xn[:, c, :TT], func=AF.Identity,
                                                      scale=modP[:, l, sc0 + c:sc0 + c + 1], bias=modP[:, l, sh0 + c:sh0 + c + 1])),
                     r=[(tag, "xn"), "modP"], w=[hkey])

        def norm_sample(tag, l, sh0, sc0, sq, nps, rt, hT, xn):
            P.op("dve", lambda: V.tensor_tensor(out=sq[:], in0=xs_cur[:], in1=xs_cur[:], op=ALU.mult), r=["xs"], w=[(tag, "sq")])
            for c in range(8):
                P.op("pe", (lambda c=c: PE.matmul(nps[:, :NS], lhsT=ones_f[:], rhs=sq[:, c, :], start=(c == 0), stop=(c == 7))),
                     r=[(tag, "sq"), "ones_f"], w=[(tag, "nps")])
            P.op("act", lambda: A.activation(out=rt[:, :NS], in_=nps[:, :NS], func=AF.Sqrt, scale=1.0 / D, bias=epsb[:, 0:1]),
                 r=[(tag, "nps"), "epsb"], w=[(tag, "rt")])
            P.op("dve", lambda: V.reciprocal(out=rt[:, :NS], in_=rt[:, :NS]), r=[(tag, "rt")], w=[(tag, "rt")])
            P.op("dve", lambda: V.tensor_tensor(out=xn[:], in0=xs_cur[:], in1=rt[:, :NS].unsqueeze(1).to_broadcast([128, 8, NS]), op=ALU.mult),
                 r=["xs", (tag, "rt")], w=[(tag, "xn")])
            P.op("dve", lambda: V.tensor_tensor(out=xn[:], in0=xn[:], in1=modS[:, l, sc0:sc0 + 8, :], op=ALU.mult),
                 r=[(tag, "xn"), "modS"], w=[(tag, "xn")])
            P.op("dve", lambda: V.tensor_tensor(out=hT[:], in0=xn[:], in1=modS[:, l, sh0:sh0 + 8, :], op=ALU.add),
                 r=[(tag, "xn"), "modS"], w=[(tag, "hT")])

        def load_w(q_tensor, dst, src_ap_fn, nk, key):
            for kc in range(nk):
                P.dma("pool", (lambda kc=kc: G.dma_start(out=dst[:, kc, :], in_=src_ap_fn(kc))), w=[(key, kc)])

        with contextlib.ExitStack() as ph:
            w_in = SB(ph, "w_in", [128, 8, NIN_E], BF16)
            load_w(None, w_in, lambda kc: I["w_in_e"][:, kc, :], 8, "w_in")
            for s in range(NS):
                P.dma("actq", (lambda s=s: nc.scalar.dma_start(out=O["wk_s"][s, 0:LCACHE - 1, :], in_=I["ck"][s, 1:LCACHE, :])), w=[("wk", s)])
                P.dma("actq", (lambda s=s: nc.scalar.dma_start(out=O["wv_s"][s, 0:LCACHE - 1, :], in_=I["cv"][s, 1:LCACHE, :])), w=[("wv", s)])
            wsT = SB(ph, "wsT", [128, 4, 128])
            wsTb = SB(ph, "wsTb", [128, 4, 128], BF16)
            bsb = SB(ph, "bsb", [128, 512])
            lng_bc = SB(ph, "lng_bc", [128, 512]); lnb_bc = SB(ph, "lnb_bc", [128, 512])
            sg0 = SB(ph, "sg0", [128, 8])
            P.dma("sp", lambda: nc.sync.dma_start(out=wsT[:], in_=I["w_sguT"][:, :, :]), w=["wsT"])
            P.dma("sp", lambda: nc.sync.dma_start(out=bsb[:], in_=I["b_sgu"][0:1, :].partition_broadcast(128)), w=["bsb"])
            P.dma("sp", lambda: nc.sync.dma_start(out=sg0[:], in_=I["sg0"][0:1, :].partition_broadcast(128)), w=["sg0"])
            P.op("pool", lambda: G.affine_select(out=wsT[:], in_=wsT[:], pattern=[[0, 4], [1, 128]], compare_op=ALU.is_ge,
                                                 fill=0.0, base=0, channel_multiplier=-1), r=["wsT"], w=["wsT"])
            P.op("dve", lambda: V.tensor_copy(out=wsTb[:], in_=wsT[:]), r=["wsT"], w=["wsTb"])
            dg = SB(ph, "dg", [128, 8, 128])
            for i2, nm in enumerate(("ln_g", "ln_b")):
                for c in range(4):
                    P.op("dve", (lambda i2=i2, c=c, nm=nm: V.tensor_scalar_mul(out=dg[:, i2 * 4 + c, :], in0=ident_f[:], scalar1=pcol(nm, c))),
                         r=["ident_f", "par"], w=[("dg", i2 * 4 + c)])

            xt_ = [SB(ph, "xt%d" % i, [128, 8, 512]) for i in range(2)]
            sqb = SB(ph, "sqb", [128, 8, 512], BF16)
            xn = SB(ph, "xn", [128, 8, 512])
            rt = SB(ph, "rt", [128, 512])
            hT = SB(ph, "hT", [128, 8, 512], BF16)
            uT = SB(ph, "uT", [128, 4, 512])
            gv_ = [SB(ph, "gv%d" % i, [128, 512]) for i in range(2)]
            st6_ = [SB(ph, "st6_%d" % i, [128, 6]) for i in range(2)]; mv_ = [SB(ph, "mv%d" % i, [128, 2]) for i in range(2)]; rs_ = [SB(ph, "rs%d" % i, [128, 1]) for i in range(2)]
            va_ = [SB(ph, "va%d" % i, [128, 512]) for i in range(2)]; vab_ = [SB(ph, "vab%d" % i, [128, 512], BF16) for i in range(4)]
            mixs_ = [SB(ph, "mixs%d" % i, [128, 512]) for i in range(2)]
            aoT = [SB(ph, "aoT%d" % i, [128, 4, 512], BF16) for i in range(1)]
            qTb = [SB(ph, "qTb%d" % i, [128, 4, 512], BF16) for i in range(1)]
            kTf = [SB(ph, "kTf%d" % i, [128, 4, 512]) for i in range(1)]
            vTf = [SB(ph, "vTf%d" % i, [128, 4, 512]) for i in range(1)]
            kTb = [SB(ph, "kTb%d" % i, [128, 4, 512], BF16) for i in range(1)]
            vTb = [SB(ph, "vTb%d" % i, [128, 4, 512], BF16) for i in range(1)]
            nps = PS(ph, "nps1", [128, 512])
            zps = [PS(ph, "zps%d" % i, [128, 512]) for i in range(3)]
            vps = [PS(ph, "vps%d" % i, [128, 512]) for i in range(2)]
            mps2 = [PS(ph, "mixps%d" % i, [128, 512]) for i in range(2)]
            zcount = [0]
            for i2 in range(2):
                for c in range(4):
                    P.op("pe", (lambda i2=i2, c=c: PE.matmul(zps[i2][:, c * 128:(c + 1) * 128], lhsT=ones_f[:],
                                                              rhs=dg[:, i2 * 4 + c, :], start=True, stop=True)),
                         r=[("dg", i2 * 4 + c), "ones_f"], w=[("zps", i2)])
            P.op("dve", lambda: V.tensor_copy(out=lng_bc[:], in_=zps[0][:]), r=[("zps", 0)], w=["lng_bc"])
            P.op("dve", lambda: V.tensor_copy(out=lnb_bc[:], in_=zps[1][:]), r=[("zps", 1)], w=["lnb_bc"])

            def loadx(i):
                b = i % 2
                P.dma("sp", (lambda: nc.sync.dma_start(out=xt_[b][:], in_=I["xT"][:, :, i * 512:(i + 1) * 512])), w=[("xt1", b)])

            loadx(0)
            for i in range(NT5):
                b = i % 2
                if i + 1 < NT5:
                    loadx(i + 1)
                tag = "p1"
                norm_prompt(("xt1", b), tag, xt_[b], 512, 0, SH1, SC1, sqb, nps, rt, hT, xn)
                t0 = i * 512
                xb_ = b
                b = 0

                def zchunk(n):
                    z = zcount[0] % 3
                    zcount[0] += 1
                    for kc in range(8):
                        P.op("pe", (lambda kc=kc, n=n, z=z: PE.matmul(zps[z][:], lhsT=w_in[:, kc, n * 128:(n + 1) * 128], rhs=hT[:, kc, :],
                                                                       start=(kc == 0), stop=(kc == 7))),
                             r=[("w_in", kc), (tag, "hT")], w=[("zps", z)])
                    return z
                for n in range(4):
                    z = zchunk(n)
                    P.op("act", (lambda n=n, z=z: A.activation(out=uT[:, n, :], in_=zps[z][:], func=GELU)), r=[("zps", z)], w=[("uT", n)])
                for blk in range(4):
                    vb = blk % 2
                    gv, st6, mv, rs, va, vab = gv_[vb], st6_[vb], mv_[vb], rs_[vb], va_[vb], vab_[blk]
                    for kc in range(8):
                        P.op("pe", (lambda kc=kc, blk=blk, vb=vb: PE.matmul(vps[vb][:], lhsT=hT[:, kc, blk * 128:(blk + 1) * 128], rhs=w_in[:, kc, 512:1024],
                                                                             start=(kc == 0), stop=(kc == 7))),
                             r=[("w_in", kc), (tag, "hT")], w=[("vps", vb)])
                    P.op("act", (lambda gv=gv, vb=vb: A.activation(out=gv[:], in_=vps[vb][:], func=GELU)), r=[("vps", vb)], w=[("gv", vb)])
                    P.op("dve", (lambda gv=gv, st6=st6: V.bn_stats(out=st6[:], in_=gv[:])), r=[("gv", vb)], w=[("st6", vb)])
                    P.op("dve", (lambda mv=mv, st6=st6: V.bn_aggr(out=mv[:], in_=st6[:])), r=[("st6", vb)], w=[("mv", vb)])
                    P.op("act", (lambda mv=mv, rs=rs: A.activation(out=rs[:], in_=mv[:, 1:2], func=AF.Sqrt, scale=1.0, bias=epsb[:, 0:1])), r=[("mv", vb), "epsb"], w=[("rs", vb)])
                    P.op("dve", (lambda rs=rs: V.reciprocal(out=rs[:], in_=rs[:])), r=[("rs", vb)], w=[("rs", vb)])
                    P.op("dve", (lambda va=va, gv=gv, mv=mv, rs=rs: V.tensor_scalar(out=va[:], in0=gv[:], scalar1=mv[:, 0:1], scalar2=rs[:, 0:1], op0=ALU.subtract, op1=ALU.mult)),
                         r=[("gv", vb), ("mv", vb), ("rs", vb)], w=[("va", vb)])
                    P.op("pool", (lambda va=va: G.tensor_tensor(out=va[:], in0=va[:], in1=lng_bc[:], op=ALU.mult)), r=[("va", vb), "lng_bc"], w=[("va", vb)])
                    P.op("pool", (lambda va=va, vab=vab: G.tensor_tensor(out=vab[:], in0=va[:], in1=lnb_bc[:], op=ALU.add)), r=[("va", vb), "lnb_bc"], w=[("vab", blk)])

                def sgu_mix():
                    for blk in range(4):
                        vb = blk % 2
                        vab, mixs = vab_[blk], mixs_[vb]
                        for g in range(4):
                            P.op("pe", (lambda g=g, vab=vab, vb=vb: PE.matmul(mps2[vb][:, g * 128:(g + 1) * 128], lhsT=vab[:, g * 128:(g + 1) * 128], rhs=wsTb[:, g, :],
                                                                               start=True, stop=True)), r=[("vab", blk), "wsTb"], w=[("mixps", vb)])
                        P.op("dve", (lambda mixs=mixs, vb=vb: V.tensor_tensor(out=mixs[:], in0=mps2[vb][:], in1=bsb[:], op=ALU.add)), r=[("mixps", vb), "bsb"], w=[("mixs", vb)])
                        P.op("pool", (lambda blk=blk, b=b, mixs=mixs: G.tensor_tensor(out=aoT[b][:, :, blk * 128:(blk + 1) * 128],
                                                                                      in0=mixs[:].rearrange("p (g t) -> p g t", g=4),
                                                                                      in1=uT[:, :, blk * 128:(blk + 1) * 128], op=ALU.mult)),
                             r=[("mixs", vb)] + [("uT", n) for n in range(4)], w=[("aoT", b, blk)])
                    P.dma("pool", (lambda b=b, t0=t0: G.dma_start(out=AOS[:, :, t0:t0 + 512], in_=aoT[b][:])),
                          r=[("aoT", b, k) for k in range(4)], w=[("AOS", i)])

                for n in range(4):
                    z = zchunk(8 + n)
                    P.op("act", (lambda n=n, z=z, b=b: A.activation(out=qTb[b][:, n, :], in_=zps[z][:], func=AF.Copy, scale=0.125)),
                         r=[("zps", z)], w=[("qTb", b, n)])
                P.dma("pool", (lambda b=b, t0=t0: G.dma_start(out=QS[:, :, t0:t0 + 512], in_=qTb[b][:])),
                      r=[("qTb", b, k) for k in range(4)], w=[("QS", i)])
                for (off, tf, tb, SCR, OUT, nm) in ((12, kTf, kTb, KS, O["kT_o"], "k"), (16, vTf, vTb, VS, O["vT_o"], "v")):
                    for n in range(4):
                        z = zchunk(off + n)
                        P.op("act", (lambda n=n, z=z, b=b, tf=tf: A.activation(out=tf[b][:, n, :], in_=zps[z][:], func=AF.Copy)),
                             r=[("zps", z)], w=[(nm + "f", b, n)])
                        P.op("dve", (lambda n=n, b=b, tf=tf, tb=tb: V.tensor_copy(out=tb[b][:, n, :], in_=tf[b][:, n, :])),
                             r=[(nm + "f", b, n)], w=[(nm + "b", b, n)])
                    P.dma("pool", (lambda b=b, t0=t0, tb=tb, SCR=SCR: G.dma_start(out=SCR[:, :, t0:t0 + 512], in_=tb[b][:])),
                          r=[(nm + "b", b, k) for k in range(4)], w=[(nm + "S", i)])
                    if t0 >= T - TK:
                        o0 = t0 - (T - TK)
                        P.dma("pool", (lambda b=b, o0=o0, tf=tf, OUT=OUT: G.dma_start(out=OUT[:, :, o0:o0 + 512], in_=tf[b][:])),
                              r=[(nm + "f", b, k) for k in range(4)], w=[(nm + "O", i)])
                sgu_mix()

            sq_s = SB(ph, "sq_s", [128, 8, NS]); xn_s = SB(ph, "xn_s", [128, 8, NS]); hTs = SB(ph, "hTs", [128, 8, NS], BF16)
            guv = SB(ph, "guv", [128, 8, NS]); gsq = SB(ph, "gsq", [128, 4, NS])
            mean_s = SB(ph, "mean_s", [128, NS]); var_s = SB(ph, "var_s", [128, NS]); msq_s = SB(ph, "msq_s", [128, NS])
            vas = SB(ph, "vas", [128, 4, NS]); aos = SB(ph, "aos", [128, 4, NS]); aosb = SB(ph, "aosb", [128, 4, NS], BF16)
            qkv = SB(ph, "qkv", [NS, 1536])
            norm_sample("s1", 0, SH1, SC1, sq_s, nps, rt, hTs, xn_s)
            for n in range(8):
                for kc in range(8):
                    P.op("pe", (lambda n=n, kc=kc: PE.matmul(zps[0][:, n * NS:(n + 1) * NS], lhsT=w_in[:, kc, n * 128:(n + 1) * 128], rhs=hTs[:, kc, :],
                                                              start=(kc == 0), stop=(kc == 7))), r=[("w_in", kc), ("s1", "hT")], w=[("zps", 0)])
            P.op("act", lambda: A.activation(out=guv[:], in_=zps[0][:, 0:8 * NS].rearrange("p (c s) -> p c s", c=8), func=GELU),
                 r=[("zps", 0)], w=["guv"])
            P.op("dve", lambda: V.tensor_tensor(out=gsq[:], in0=guv[:, 4:8, :], in1=guv[:, 4:8, :], op=ALU.mult), r=["guv"], w=["gsq"])
            for c in range(4):
                P.op("pe", (lambda c=c: PE.matmul(zps[1][:, 0:NS], lhsT=ones_f[:], rhs=guv[:, 4 + c, :], start=(c == 0), stop=(c == 3))),
                     r=["guv", "ones_f"], w=[("zps", 1)])
            for c in range(4):
                P.op("pe", (lambda c=c: PE.matmul(zps[1][:, 8:8 + NS], lhsT=ones_f[:], rhs=gsq[:, c, :], start=(c == 0), stop=(c == 3))),
                     r=["gsq", "ones_f"], w=[("zps", 1)])
            P.op("dve", lambda: V.tensor_scalar_mul(out=mean_s[:], in0=zps[1][:, 0:NS], scalar1=1.0 / WA), r=[("zps", 1)], w=["mean_s"])
            P.op("dve", lambda: V.tensor_scalar_mul(out=var_s[:], in0=zps[1][:, 8:8 + NS], scalar1=1.0 / WA), r=[("zps", 1)], w=["var_s"])
            P.op("dve", lambda: V.tensor_tensor(out=msq_s[:], in0=mean_s[:], in1=mean_s[:], op=ALU.mult), r=["mean_s"], w=["msq_s"])
            P.op("dve", lambda: V.tensor_tensor(out=var_s[:], in0=var_s[:], in1=msq_s[:], op=ALU.subtract), r=["var_s", "msq_s"], w=["var_s"])
            P.op("act", lambda: A.activation(out=var_s[:], in_=var_s[:], func=AF.Sqrt, scale=1.0, bias=epsb[:, 0:1]), r=["var_s", "epsb"], w=["var_s"])
            P.op("dve", lambda: V.reciprocal(out=var_s[:], in_=var_s[:]), r=["var_s"], w=["var_s"])
            P.op("dve", lambda: V.tensor_tensor(out=vas[:], in0=guv[:, 4:8, :], in1=mean_s[:].unsqueeze(1).to_broadcast([128, 4, NS]), op=ALU.subtract),
                 r=["guv", "mean_s"], w=["vas"])
            P.op("dve", lambda: V.tensor_tensor(out=vas[:], in0=vas[:], in1=var_s[:].unsqueeze(1).to_broadcast([128, 4, NS]), op=ALU.mult),
                 r=["vas", "var_s"], w=["vas"])
            P.op("dve", lambda: V.tensor_tensor(out=vas[:], in0=vas[:], in1=pcol("ln_g", 0, 4).unsqueeze(2).to_broadcast([128, 4, NS]), op=ALU.mult),
                 r=["vas", "par"], w=["vas"])
            P.op("dve", lambda: V.tensor_tensor(out=vas[:], in0=vas[:], in1=pcol("ln_b", 0, 4).unsqueeze(2).to_broadcast([128, 4, NS]), op=ALU.add),
                 r=["vas", "par"], w=["vas"])
            P.dma("sp", lambda: nc.sync.dma_start(out=O["cv_o"][:, :, :], in_=vas[:]), r=["vas"], w=["cv_o"])
            P.op("dve", lambda: V.tensor_tensor(out=aos[:], in0=vas[:], in1=sg0[:, 0:4].unsqueeze(2).to_broadcast([128, 4, NS]), op=ALU.mult),
                 r=["vas", "sg0"], w=["aos"])
            P.op("dve", lambda: V.tensor_tensor(out=aos[:], in0=aos[:], in1=sg0[:, 4:8].unsqueeze(2).to_broadcast([128, 4, NS]), op=ALU.add),
                 r=["aos", "sg0"], w=["aos"])
            P.op("dve", lambda: V.tensor_tensor(out=aosb[:], in0=aos[:], in1=guv[:, 0:4, :], op=ALU.mult), r=["aos", "guv"], w=["aosb"])
            AOSS = dscr("aoT_ss", [128, 4, NS], BF16)
            P.dma("sp", lambda: nc.sync.dma_start(out=AOSS[:, :, :], in_=aosb[:]), r=["aosb"], w=["AOSS"])
            for blk3 in range(3):
                for kc in range(8):
                    P.op("pe", (lambda blk3=blk3, kc=kc: PE.matmul(zps[2][0:NS, :], lhsT=hTs[:, kc, :], rhs=w_in[:, kc, 1024 + blk3 * 512: 1024 + (blk3 + 1) * 512],
                                                                    start=(kc == 0), stop=(kc == 7))), r=[("w_in", kc), ("s1", "hT")], w=[("zps", 2)])
                P.op("act", (lambda blk3=blk3: A.activation(out=qkv[:, blk3 * 512:(blk3 + 1) * 512], in_=zps[2][0:NS, :], func=AF.Copy,
                                                            scale=(0.125 if blk3 == 0 else 1.0))), r=[("zps", 2)], w=["qkv"])
            P.dma("sp", lambda: nc.sync.dma_start(out=QKVS[:, :], in_=qkv[:]), r=["qkv"], w=["QKVS"])
            P.flush()

        with contextlib.ExitStack() as ph:
            qT = SB(ph, "qT", [128, 4, T], BF16); kT = SB(ph, "kT", [128, 4, T], BF16); vT = SB(ph, "vT", [128, 4, T], BF16)
            NBT = T // 128
            Vp = SB(ph, "Vp", [128, NBT, 8 * 65], BF16)
            mask2 = SB(ph, "mask2", [128, 2, 128], BF16)
            E_ = [SB(ph, "E%d" % i, [128, 4, 256], BF16) for i in range(2)]
            PTs = [SB(ph, "PTs%d" % i, [128, 4, 2, 128], BF16) for i in range(2)]
            Osb = [SB(ph, "Osb%d" % i, [128, 8, 66]) for i in range(4)]
            negm = [SB(ph, "negm%d" % i, [128, 4]) for i in range(2)]
            Sps = [PS(ph, "Sps%d" % i, [128, 1024]) for i in range(2)]
            PTp = [PS(ph, "PTp%d" % i, [128, 1024], BF16) for i in range(2)]
            Ops = [PS(ph, "Ops%d" % i, [128, 512]) for i in range(2)]
            P.dma("sp", lambda: nc.sync.dma_start(out=qT[:], in_=QS[:, :, :]), w=["qT"])
            P.dma("sp", lambda: nc.sync.dma_start(out=kT[:], in_=KS[:, :, :]), w=["kT"])
            P.dma("sp", lambda: nc.sync.dma_start(out=vT[:], in_=VS[:, :, :]), w=["vT"])
            P.op("pool", lambda: G.memset(Vp[:], 1.0), w=[("Vp", k) for k in range(NBT)])
            P.op("pool", lambda: G.memset(mask2[:], 1.0), w=["mask2"])
            P.op("pool", lambda: G.affine_select(out=mask2[:, 0, :], in_=mask2[:, 0, :], pattern=[[-1, 128]], compare_op=ALU.is_ge, fill=0.0,
                                                 base=0, channel_multiplier=1), r=["mask2"], w=["mask2"])
            P.op("pool", lambda: G.affine_select(out=mask2[:, 1, :], in_=mask2[:, 1, :], pattern=[[1, 128]], compare_op=ALU.is_ge, fill=0.0,
                                                 base=0, channel_multiplier=-1), r=["mask2"], w=["mask2"])
            u = 0
            ucnt = [0]
            for pi, (wdw, d) in enumerate(PATTERNS):
                M = T // d
                nb = M // 128
                for r_ in range(d):
                    for b_ in range(nb):
                        bi = r_ * nb + b_
                        tok = slice(r_ + d * 128 * b_, r_ + d * 128 * b_ + d * 127 + 1, d)
                        pb = u % 2
                        for c in range(4):
                            P.op("pe", (lambda c=c, tok=tok, pb=pb: PE.transpose(PTp[pb][:, c * 128:(c + 1) * 128], vT[:, c, tok], ident_b[:])),
                                 r=["vT", "ident_b"], w=[("PTp", pb)])
                        P.op("act", (lambda bi=bi, pb=pb: A.activation(out=Vp[:, bi, :].rearrange("p (h e) -> p h e", h=8)[:, :, 0:64],
                                                                      in_=PTp[pb][:, 0:512].rearrange("p (h e) -> p h e", h=8), func=AF.Copy)),
                             r=[("PTp", pb)], w=[("Vp", bi)])
                        u += 1
                units = []
                for r_ in range(d):
                    for b_ in range(nb):
                        bi = r_ * nb + b_
                        nkb = 2 if b_ > 0 else 1
                        nk = 128 * nkb
                        qtok = slice(r_ + d * 128 * b_, r_ + d * 128 * b_ + d * 127 + 1, d)
                        k0 = r_ + d * 128 * (b_ - (nkb - 1))
                        ktok = slice(k0, k0 + d * (nk - 1) + 1, d)
                        for hh in range(2):
                            units.append(dict(bi=bi, nkb=nkb, nk=nk, qtok=qtok, ktok=ktok, hh=hh, ob=bi % 4))
                for ui, U in enumerate(units):
                    U["pb"] = (ucnt[0] + ui) % 2
                ucnt[0] += len(units)

                def colS(h4):
                    return (h4 % 2) * 512 + (h4 // 2) * 256

                def stageA(U):
                    pb, nk, hh, ob, qtok, ktok = U["pb"], U["nk"], U["hh"], U["ob"], U["qtok"], U["ktok"]
                    for h4 in range(4):
                        h = hh * 4 + h4
                        c, po = h // 2, (h % 2) * 64
                        P.op("pe", (lambda h4=h4, c=c, po=po: PE.matmul(Sps[pb][:, colS(h4): colS(h4) + nk], lhsT=qT[po:po + 64, c, qtok], rhs=kT[po:po + 64, c, ktok],
                                                                         start=True, stop=True)), r=["qT", "kT"], w=[("Sps", pb)])
                    for bk in range(2):
                        Sv = Sps[pb][:, bk * 512:(bk + 1) * 512].rearrange("p (h k) -> p h k", h=2)[:, :, 0:nk]
                        P.op("dve", (lambda Sv=Sv, bk=bk: V.tensor_reduce(out=Osb[ob][:, hh * 4 + bk:hh * 4 + bk + 3:2, 65], in_=Sv, axis=AX.X, op=ALU.max)),
                             r=[("Sps", pb)], w=[("Osb", ob, hh, "m")])
                    P.op("dve", (lambda: V.tensor_scalar_mul(out=negm[pb][:], in0=Osb[ob][:, hh * 4:(hh + 1) * 4, 65], scalar1=-1.0)),
                         r=[("Osb", ob, hh, "m")], w=[("negm", pb)])
                    for h4 in range(4):
                        P.op("act", (lambda h4=h4: A.activation(out=E_[pb][:, h4, 0:nk], in_=Sps[pb][:, colS(h4):colS(h4) + nk], func=AF.Exp,
                                                                 bias=negm[pb][:, h4:h4 + 1], scale=1.0)),
                             r=[("Sps", pb), ("negm", pb)], w=[("E", pb)])

                def stageB(U):
                    pb, nkb = U["pb"], U["nkb"]
                    for h4 in range(4):
                        for kb in range(nkb):
                            P.op("pe", (lambda h4=h4, kb=kb: PE.transpose(PTp[pb][:, (h4 * 2 + kb) * 128:(h4 * 2 + kb + 1) * 128],
                                                                          E_[pb][:, h4, kb * 128:(kb + 1) * 128], ident_b[:])),
                                 r=[("E", pb), "ident_b"], w=[("PTp", pb)])
                    PTv = PTp[pb][:].rearrange("p (h k q) -> p h k q", h=4, k=2)
                    if nkb == 2:
                        P.op("dve", (lambda: V.tensor_tensor(out=PTs[pb][:], in0=PTv, in1=mask2[:].unsqueeze(1).to_broadcast([128, 4, 2, 128]), op=ALU.mult)),
                             r=[("PTp", pb), "mask2"], w=[("PTs", pb)])
                    else:
                        P.op("dve", (lambda: V.tensor_tensor(out=PTs[pb][:, :, 0, :], in0=PTv[:, :, 0, :],
                                                             in1=mask2[:, 1, :].unsqueeze(1).to_broadcast([128, 4, 128]), op=ALU.mult)),
                             r=[("PTp", pb), "mask2"], w=[("PTs", pb)])

                def stageC(U, pi=pi):
                    pb, nkb, hh, ob, bi, qtok = U["pb"], U["nkb"], U["hh"], U["ob"], U["bi"], U["qtok"]
                    for h4 in range(4):
                        h = hh * 4 + h4
                        for kb in range(nkb):
                            kbi = bi - (nkb - 1) + kb
                            P.op("pe", (lambda h4=h4, h=h, kb=kb, kbi=kbi: PE.matmul(Ops[pb][:, h4 * 128:h4 * 128 + 65], lhsT=PTs[pb][:, h4, kb, :],
                                                                                     rhs=Vp[:, kbi, h * 65:(h + 1) * 65], start=(kb == 0), stop=(kb == nkb - 1))),
                                 r=[("PTs", pb), ("Vp", kbi)], w=[("Ops", pb)])
                    P.op("act", (lambda: A.activation(out=Osb[ob][:, hh * 4:(hh + 1) * 4, 0:65],
                                                      in_=Ops[pb][:].rearrange("p (h e) -> p h e", h=4)[:, :, 0:65], func=AF.Copy)),
                         r=[("Ops", pb)], w=[("Osb", ob, hh, "o")])
                    if hh == 1:
                        orow = OS[pi, qtok, :]
                        P.dma("pool", (lambda: G.dma_start(out=orow, in_=Osb[ob][:].rearrange("p h e -> p (h e)"))),
                              r=[("Osb", ob, 0, "o"), ("Osb", ob, 1, "o"), ("Osb", ob, 0, "m"), ("Osb", ob, 1, "m")], w=[("OS", pi, bi)])

                nu = len(units)
                for st in range(nu + 2):
                    if st < nu:
                        stageA(units[st])
                    if 0 <= st - 1 < nu:
                        stageB(units[st - 1])
                    if 0 <= st - 2 < nu:
                        stageC(units[st - 2])
            P.flush()

        with contextlib.ExitStack() as ph:
            Kt = [SB(ph, "Kt%d" % i, [128, 512]) for i in range(2)]
            Vt = [SB(ph, "Vt%d" % i, [128, 512]) for i in range(16)]
            qbc = [SB(ph, "qbc%d" % i, [128, 512]) for i in range(NS)]
            prod = SB(ph, "prod", [128, 512])
            ST = SB(ph, "ST", [128, 4, 128])
            Es = SB(ph, "Es", [128, 4, 128]); PTs2 = SB(ph, "PTs2", [128, 4, 128])
            mx = SB(ph, "mx_s", [128, 1]); den = SB(ph, "den_s", [128, 1])
            rowm = SB(ph, "rowm", [128, NS]); dmk = SB(ph, "dmk", [128, 512])
            acc = SB(ph, "acc_s", [128, 512]); bo = SB(ph, "bo_s", [128, 64])
            boT = SB(ph, "boT_s", [64, 128], BF16)
            w_oe = SB(ph, "w_oe_s", [128, 4, D], BF16); w_oh = SB(ph, "w_oh_s", [64, 8, D], BF16)
            aosb2 = SB(ph, "aosb2", [128, 4, NS], BF16)
            mo_s = SB(ph, "mo_s", [128, 8, NS])
            Sp = PS(ph, "Sp_s", [128, 512]); PTp2 = PS(ph, "PTp_s", [128, 512])
            Opp = [PS(ph, "Opp%d" % i, [128, 512]) for i in range(NS)]
            bTp = PS(ph, "bTp", [64, 128]); mop = PS(ph, "mop_s", [128, 8 * NS])
            P.dma("sp", lambda: nc.sync.dma_start(out=rowm[:], in_=I["rowmask"][:, :]), w=["rowm"])
            P.dma("sp", lambda: nc.sync.dma_start(out=dmk[:], in_=I["dmask"][:, :]), w=["dmk"])
            for kc in range(4):
                P.dma("pool", (lambda kc=kc: G.dma_start(out=w_oe[:, kc, :], in_=I["w_out_e"][:, kc, :])), w=[("w_oe", kc)])
            for h in range(8):
                P.dma("pool", (lambda h=h: G.dma_start(out=w_oh[:, h, :], in_=I["w_out_eh"][:, h, :])), w=[("w_oh", h)])
            P.dma("sp", lambda: nc.sync.dma_start(out=aosb2[:], in_=AOSS[:, :, :]), w=["aosb2"])
            P.op("pool", lambda: G.memset(ST[:], 0.0), w=["ST"])
            for s in range(NS):
                P.dma("sp", (lambda s=s: nc.sync.dma_start(out=O["wk_s"][s, LCACHE - 1:LCACHE, :], in_=QKVS[s:s + 1, 512:1024])), w=[("wk2", s)])
                P.dma("sp", (lambda s=s: nc.sync.dma_start(out=O["wv_s"][s, LCACHE - 1:LCACHE, :], in_=QKVS[s:s + 1, 1024:1536])), w=[("wv2", s)])
            for s in range(NS):
                P.dma("sp", (lambda s=s: nc.sync.dma_start(out=qbc[s][:], in_=QKVS[s:s + 1, 0:512].partition_broadcast(128))), w=[("qbc", s)])
            it = 0
            for s in range(NS):
                for p in range(4):
                    kb_ = it % 2
                    vi = s * 4 + p
                    if p < 3:
                        dd = PATTERNS[p][1]
                        r0 = LCACHE - 128 * dd
                        ksrc = I["ck"][s, r0:r0 + 127 * dd + 1:dd, :]
                        vsrc = I["cv"][s, r0:r0 + 127 * dd + 1:dd, :]
                    else:
                        ksrc = QKVS[s:s + 1, 512:1024].partition_broadcast(128)
                        vsrc = QKVS[s:s + 1, 1024:1536].partition_broadcast(128)
                    P.dma("sp", (lambda kb_=kb_, ksrc=ksrc: nc.sync.dma_start(out=Kt[kb_][:], in_=ksrc)), w=[("Kt", kb_)])
                    P.dma("sp", (lambda vi=vi, vsrc=vsrc: nc.sync.dma_start(out=Vt[vi][:], in_=vsrc)), w=[("Vt", vi)])
                    P.op("dve", (lambda kb_=kb_, s=s: V.tensor_tensor(out=prod[:], in0=Kt[kb_][:], in1=qbc[s][:], op=ALU.mult)),
                         r=[("Kt", kb_), ("qbc", s)], w=["prod"])
                    P.op("dve", (lambda s=s, p=p: V.tensor_reduce(out=ST[:, p, 32 * s:32 * s + 8], in_=prod[:].rearrange("p (h e) -> p h e", h=8),
                                                                  axis=AX.X, op=ALU.add)), r=["prod"], w=["ST"])
                    it += 1
            for p in range(4):
                P.op("pe", (lambda p=p: PE.transpose(Sp[:, p * 128:(p + 1) * 128], ST[:, p, :], ident_f[:])), r=["ST", "ident_f"], w=["Sp"])
            P.op("dve", lambda: V.tensor_reduce(out=mx[:], in_=Sp[:], axis=AX.X, op=ALU.max, negate=True), r=["Sp"], w=["mx"])
            P.op("act", lambda: A.activation(out=Es[:].rearrange("p a b -> p (a b)"), in_=Sp[:], func=AF.Exp, bias=mx[:, 0:1], scale=1.0), r=["Sp", "mx"], w=["Es"])
            P.op("dve", lambda: V.tensor_scalar_mul(out=Es[:, 3, :], in0=Es[:, 3, :], scalar1=3.0 / 128.0), r=["Es"], w=["Es"])
            P.op("dve", lambda: V.tensor_reduce(out=den[:], in_=Es[:].rearrange("p a b -> p (a b)"), axis=AX.X, op=ALU.add), r=["Es"], w=["den"])
            P.op("dve", lambda: V.reciprocal(out=den[:], in_=den[:]), r=["den"], w=["den"])
            for p in range(4):
                P.op("pe", (lambda p=p: PE.transpose(PTp2[:, p * 128:(p + 1) * 128], Es[:, p, :], ident_f[:])), r=["Es", "ident_f"], w=["PTp2"])
            P.op("dve", lambda: V.tensor_copy(out=PTs2[:].rearrange("p a b -> p (a b)"), in_=PTp2[:]), r=["PTp2"], w=["PTs2"])
            for s in range(NS):
                for p in range(4):
                    P.op("pe", (lambda s=s, p=p: PE.matmul(Opp[s][:], lhsT=PTs2[:, p, :], rhs=Vt[s * 4 + p][:], start=(p == 0), stop=(p == 3))),
                         r=["PTs2", ("Vt", s * 4 + p)], w=[("Opp", s)])
            P.op("dve", lambda: V.tensor_scalar_mul(out=acc[:], in0=Opp[0][:], scalar1=rowm[:, 0:1]), r=[("Opp", 0), "rowm"], w=["acc"])
            for s in range(1, NS):
                P.op("dve", (lambda s=s: V.scalar_tensor_tensor(out=acc[:], in0=Opp[s][:], scalar=rowm[:, s:s + 1], in1=acc[:], op0=ALU.mult, op1=ALU.add)),
                     r=[("Opp", s), "rowm", "acc"], w=["acc"])
            P.op("dve", lambda: V.tensor_tensor(out=acc[:], in0=acc[:], in1=dmk[:], op=ALU.mult), r=["acc", "dmk"], w=["acc"])
            P.op("dve", lambda: V.tensor_reduce(out=bo[:], in_=acc[:].rearrange("p (h e) -> p e h", h=8), axis=AX.X, op=ALU.add), r=["acc"], w=["bo"])
            P.op("dve", lambda: V.tensor_scalar_mul(out=bo[:], in0=bo[:], scalar1=den[:, 0:1]), r=["bo", "den"], w=["bo"])
            P.op("pe", lambda: PE.transpose(bTp[:], bo[:], ident_f[:]), r=["bo", "ident_f"], w=["bTp"])
            P.op("dve", lambda: V.tensor_copy(out=boT[:], in_=bTp[:]), r=["bTp"], w=["boT"])
            for n in range(8):
                for kc in range(4):
                    P.op("pe", (lambda n=n, kc=kc: PE.matmul(mop[:, n * NS:(n + 1) * NS], lhsT=w_oe[:, kc, n * 128:(n + 1) * 128], rhs=aosb2[:, kc, :],
                                                              start=(kc == 0), stop=False)), r=[("w_oe", kc), "aosb2"], w=["mop"])
                for h in range(8):
                    P.op("pe", (lambda n=n, h=h: PE.matmul(mop[:, n * NS:(n + 1) * NS], lhsT=w_oh[:, h, n * 128:(n + 1) * 128], rhs=boT[:, h:128:32],
                                                            start=False, stop=(h == 7))), r=[("w_oh", h), "boT"], w=["mop"])
            P.op("dve", lambda: V.tensor_tensor(out=mo_s[:], in0=mop[:].rearrange("p (c s) -> p c s", c=8), in1=modS[:, 0, G1:G1 + 8, :], op=ALU.mult),
                 r=["mop", "modS"], w=["mo_s"])
            P.op("dve", lambda: V.tensor_tensor(out=xs_cur[:], in0=xs_cur[:], in1=mo_s[:], op=ALU.add), r=["mo_s", "xs"], w=["xs"])
            P.flush()

        with contextlib.ExitStack() as ph:
            w_o = SB(ph, "w_o", [128, 8, D], BF16)
            load_w(None, w_o, lambda kc: I["w_out_e"][:, kc, :], 8, "w_o")
            xt_ = [SB(ph, "xt3_%d" % i, [128, 8, 512]) for i in range(2)]
            ao_ = [SB(ph, "ao3_%d" % i, [128, 4, 512], BF16) for i in range(2)]
            Om = [SB(ph, "Om%d" % i, [128, 4, 3, 8 * 66]) for i in range(2)]
            Mx = SB(ph, "Mx", [128, 8]); dm = SB(ph, "dm", [128, 3, 8]); ee = SB(ph, "ee", [128, 3, 8])
            wt = SB(ph, "wt", [128, 3, 8, 65]); ac = SB(ph, "ac", [128, 8, 65]); rd = SB(ph, "rd", [128, 8])
            bob = SB(ph, "bob", [128, 8, 64], BF16)
            boT3 = SB(ph, "boT3", [128, 4, 512], BF16)
            x1t = [SB(ph, "x1t%d" % i, [128, 8, 512]) for i in range(2)]
            tp = [PS(ph, "tp3_%d" % i, [128, 1024], BF16) for i in range(2)]
            ops_ = [PS(ph, "op3_%d" % i, [128, 512]) for i in range(3)]

            def load3(i):
                b = i % 2
                t0 = i * 512
                P.dma("sp", (lambda: nc.sync.dma_start(out=xt_[b][:], in_=I["xT"][:, :, t0:t0 + 512])), w=[("xt3", b)])
                P.dma("sp", (lambda: nc.sync.dma_start(out=ao_[b][:], in_=AOS[:, :, t0:t0 + 512])), w=[("ao3", b)])
                for pi in range(3):
                    P.dma("sp", (lambda pi=pi: nc.sync.dma_start(out=Om[b][:, :, pi, :], in_=OS[pi, t0:t0 + 512, :].rearrange("(k i) f -> i k f", i=128))),
                          w=[("Om", b, pi)])
            load3(0)
            oc = 0
            for i in range(NT5):
                b = i % 2
                t0 = i * 512
                if i + 1 < NT5:
                    load3(i + 1)
                for blk in range(4):
                    Ov = Om[b][:, blk, :, :].rearrange("p a (h e) -> p a h e", h=8)
                    omk = [("Om", b, pi) for pi in range(3)]
                    P.op("dve", (lambda Ov=Ov: V.tensor_tensor(out=Mx[:], in0=Ov[:, 0, :, 65], in1=Ov[:, 1, :, 65], op=ALU.max)), r=omk, w=["Mx"])
                    P.op("dve", (lambda Ov=Ov: V.tensor_tensor(out=Mx[:], in0=Mx[:], in1=Ov[:, 2, :, 65], op=ALU.max)), r=omk + ["Mx"], w=["Mx"])
                    P.op("dve", (lambda Ov=Ov: V.tensor_tensor(out=dm[:], in0=Ov[:, :, :, 65], in1=Mx[:].unsqueeze(1).to_broadcast([128, 3, 8]), op=ALU.subtract)),
                         r=omk + ["Mx"], w=["dm"])
                    P.op("act", lambda: A.activation(out=ee[:], in_=dm[:], func=AF.Exp), r=["dm"], w=["ee"])
                    P.op("pool", (lambda Ov=Ov: G.tensor_tensor(out=wt[:], in0=Ov[:, :, :, 0:65], in1=ee[:].unsqueeze(3).to_broadcast([128, 3, 8, 65]), op=ALU.mult)),
                         r=omk + ["ee"], w=["wt"])
                    P.op("dve", lambda: V.tensor_tensor(out=ac[:], in0=wt[:, 0, :, :], in1=wt[:, 1, :, :], op=ALU.add), r=["wt"], w=["ac"])
                    P.op("dve", lambda: V.tensor_tensor(out=ac[:], in0=ac[:], in1=wt[:, 2, :, :], op=ALU.add), r=["wt", "ac"], w=["ac"])
                    P.op("dve", lambda: V.reciprocal(out=rd[:], in_=ac[:, :, 64]), r=["ac"], w=["rd"])
                    P.op("dve", lambda: V.tensor_tensor(out=bob[:], in0=ac[:, :, 0:64], in1=rd[:].unsqueeze(2).to_broadcast([128, 8, 64]), op=ALU.mult),
                         r=["ac", "rd"], w=["bob"])
                    tb_ = (i * 4 + blk) % 2
                    for c in range(4):
                        P.op("pe", (lambda c=c, tb_=tb_: PE.transpose(tp[tb_][:, c * 128:(c + 1) * 128], bob[:].rearrange("p h e -> p (h e)")[:, c * 128:(c + 1) * 128], ident_b[:])),
                             r=["bob", "ident_b"], w=[("tp3", tb_)])
                    P.op("act", (lambda blk=blk, tb_=tb_: A.activation(out=boT3[:, :, blk * 128:(blk + 1) * 128],
                                                                      in_=tp[tb_][:, 0:512].rearrange("p (c t) -> p c t", c=4), func=AF.Copy)),
                         r=[("tp3", tb_)], w=[("boT3", blk)])
                for n in range(8):
                    z = oc % 3
                    oc += 1
                    for kc in range(8):
                        rhs = ao_[b][:, kc, :] if kc < 4 else boT3[:, kc - 4, :]
                        rk = [("ao3", b)] if kc < 4 else [("boT3", k) for k in range(4)]
                        P.op("pe", (lambda n=n, kc=kc, z=z, rhs=rhs: PE.matmul(ops_[z][:], lhsT=w_o[:, kc, n * 128:(n + 1) * 128], rhs=rhs, start=(kc == 0), stop=(kc == 7))),
                             r=[("w_o", kc)] + rk, w=[("op3", z)])
                    P.op("dve", (lambda n=n, z=z, b=b: V.scalar_tensor_tensor(out=x1t[b][:, n, :], in0=ops_[z][:], scalar=modP[:, 0, G1 + n:G1 + n + 1],
                                                                               in1=xt_[b][:, n, :], op0=ALU.mult, op1=ALU.add)),
                         r=[("op3", z), ("xt3", b), "modP"], w=[("x1t", b, n)])
                P.dma("pool", (lambda b=b, t0=t0: G.dma_start(out=X1[:, :, t0:t0 + 512], in_=x1t[b][:])), r=[("x1t", b, n) for n in range(8)], w=[("X1", i)])
            P.flush()

        def ffn_phase(l, XIN, XOUT, final):
            with contextlib.ExitStack() as ph:
                TT = 256
                w_up = SB(ph, "w_up%d" % l, [128, 8, F2], BF16)
                w_dn = SB(ph, "w_dn%d" % l, [128, NJ, D], BF16)
                load_w(None, w_up, lambda kc: I["w_up"][l, :, kc, :], 8, "w_up")
                for j in range(NJ):
                    P.dma("pool", (lambda j=j: G.dma_start(out=w_dn[:, j, :], in_=I["w_dn"][l, :, j, :])), w=[("w_dn", j)])
                xt_ = [SB(ph, "xtf%d_%d" % (l, i), [128, 8, TT]) for i in range(2)]
                sqb = SB(ph, "sqbf%d" % l, [128, 8, TT], BF16); xn = SB(ph, "xnf%d" % l, [128, 8, TT]); rt = SB(ph, "rtf%d" % l, [128, TT])
                hT_ = [SB(ph, "hTf%d_%d" % (l, i), [128, 8, TT], BF16) for i in range(2)]
                up = [SB(ph, "up%d_%d" % (l, i), [128, 2, 2 + TT]) for i in range(2)]
                uc = [SB(ph, "uc%d_%d" % (l, i), [128, 2, TT]) for i in range(2)]
                sa = [SB(ph, "sa%d_%d" % (l, i), [128, TT]) for i in range(2)]
                actj = [SB(ph, "actj%d_%d" % (l, i), [128, TT], BF16) for i in range(2)]
                hist = SB(ph, "hist%d" % l, [128, 44, 2])
                _xo1 = SB(ph, "xo%d_0" % l, [128, 8, TT])
                xo = [_xo1, _xo1]
                nps = PS(ph, "npsf%d" % l, [128, 512])
                ups = [PS(ph, "ups%d_%d" % (l, i), [128, 512]) for i in range(3)]
                accp = PS(ph, "accp%d" % l, [128, 8 * TT])
                yo = None
                if final:
                    rty = SB(ph, "rty", [128, TT])
                fcw = lambda k, j: par[:, PAR_OFF["f_cw"] + (l * 3 + k) * 44 + j: PAR_OFF["f_cw"] + (l * 3 + k) * 44 + j + 1]
                fcb = lambda j: par[:, PAR_OFF["f_cb"] + l * 44 + j: PAR_OFF["f_cb"] + l * 44 + j + 1]
                P.op("pool", lambda: G.memset(hist[:], 0.0), w=[("hist", j) for j in range(44)])

                def loadf(i):
                    b = i % 2
                    P.dma("sp", (lambda: nc.sync.dma_start(out=xt_[b][:], in_=XIN[:, :, i * TT:(i + 1) * TT])), w=[("xtf", b)])
                loadf(0)
                jc_ = [0]
                tag = "ff"
                norm_prompt(("xtf", 0), tag, xt_[0], TT, l, SH2, SC2, sqb, nps, rt, hT_[0], xn, hkey=("hTf", 0))
                for i in range(NT2):
                    b = i % 2
                    if i + 1 < NT2:
                        loadf(i + 1)
                    hT = hT_[b]
                    hk = ("hTf", b)
                    zub = {}

                    def emit_up(j):
                        z = jc_[0] % 3
                        ub = jc_[0] % 2
                        jc_[0] += 1
                        zub[j] = (z, ub)
                        for half, jj in ((0, j), (1, j + NJ)):
                            for kc in range(8):
                                P.op("pe", (lambda half=half, jj=jj, kc=kc, z=z, hT=hT: PE.matmul(ups[z][:, half * TT:(half + 1) * TT], lhsT=w_up[:, kc, jj * 128:(jj + 1) * 128],
                                                                                            rhs=hT[:, kc, :], start=(kc == 0), stop=(kc == 7))),
                                     r=[("w_up", kc), hk], w=[("ups", z)])
                    def elemA(j):
                        z, ub = zub[j]
                        P.op("act", (lambda z=z, ub=ub: A.activation(out=up[ub][:, :, 2:2 + TT], in_=ups[z][:, 0:2 * TT].rearrange("p (h t) -> p h t", h=2), func=AF.Copy)),
                             r=[("ups", z)], w=[("up", ub, 0), ("up", ub, 1)])
                        for half, jj in ((0, j), (1, j + NJ)):
                            P.op("pool", (lambda half=half, jj=jj, ub=ub: G.tensor_copy(out=up[ub][:, half, 0:2], in_=hist[:, jj, :])),
                                 r=[("hist", jj)], w=[("up", ub, half, "h")])
                            P.op("act", (lambda half=half, jj=jj, z=z, ub=ub: A.activation(out=uc[ub][:, half, :], in_=ups[z][:, half * TT:(half + 1) * TT], func=AF.Identity,
                                                                                            scale=fcw(2, jj), bias=fcb(jj))),
                                 r=[("ups", z), "par"], w=[("uc", ub, half)])
                            P.op("dve", (lambda half=half, jj=jj, ub=ub: V.scalar_tensor_tensor(out=uc[ub][:, half, :], in0=up[ub][:, half, 1:1 + TT], scalar=fcw(1, jj),
                                                                                                 in1=uc[ub][:, half, :], op0=ALU.mult, op1=ALU.add)),
                                 r=[("up", ub, half), ("up", ub, half, "h"), ("uc", ub, half), "par"], w=[("uc", ub, half)])
                            P.op("dve", (lambda half=half, jj=jj, ub=ub: V.scalar_tensor_tensor(out=uc[ub][:, half, :], in0=up[ub][:, half, 0:TT], scalar=fcw(0, jj),
                                                                                                 in1=uc[ub][:, half, :], op0=ALU.mult, op1=ALU.add)),
                                 r=[("up", ub, half), ("up", ub, half, "h"), ("uc", ub, half), "par"], w=[("uc", ub, half)])
                            P.op("pool", (lambda half=half, jj=jj, ub=ub: G.tensor_copy(out=hist[:, jj, :], in_=up[ub][:, half, TT:TT + 2])),
                                 r=[("up", ub, half)], w=[("hist", jj)])

                    def elemB(j):
                        z, ub = zub[j]
                        P.op("act", (lambda ub=ub: A.activation(out=sa[ub][:], in_=uc[ub][:, 0, :], func=AF.Silu)), r=[("uc", ub, 0)], w=[("sa", ub)])
                        P.op("pool", (lambda ub=ub: G.tensor_tensor(out=actj[ub][:], in0=sa[ub][:], in1=uc[ub][:, 1, :], op=ALU.mult)),
                             r=[("sa", ub), ("uc", ub, 1)], w=[("actj", ub)])
                        for n in range(8):
                            P.op("pe", (lambda n=n, j=j, ub=ub: PE.matmul(accp[:, n * TT:(n + 1) * TT], lhsT=w_dn[:, j, n * 128:(n + 1) * 128], rhs=actj[ub][:],
                                                                          start=(j == 0 and n % 2 == 0), stop=(j == NJ - 1), skip_group_check=True)),
                                 r=[("w_dn", j), ("actj", ub)], w=["accp"])

                    emit_up(0)
                    emit_up(1)
                    for j in range(NJ):
                        if j + 2 < NJ:
                            emit_up(j + 2)
                        elemA(j)
                        if j >= 1:
                            elemB(j - 1)
                        if j == 8 and i + 1 < NT2:
                            nb_ = (i + 1) % 2
                            norm_prompt(("xtf", nb_), tag, xt_[nb_], TT, l, SH2, SC2, sqb, nps, rt, hT_[nb_], xn, hkey=("hTf", nb_))
                    elemB(NJ - 1)
                    for n in range(8):
                        P.op("dve", (lambda n=n, b=b: V.scalar_tensor_tensor(out=xo[b][:, n, :], in0=accp[:, n * TT:(n + 1) * TT], scalar=modP[:, l, G2 + n:G2 + n + 1],
                                                                              in1=xt_[b][:, n, :], op0=ALU.mult, op1=ALU.add)),
                             r=["accp", ("xtf", b), "modP"], w=[("xo", 0, n)])
                    if not final:
                        P.dma("pool", (lambda b=b, i=i: G.dma_start(out=XOUT[:, :, i * TT:(i + 1) * TT], in_=xo[b][:])), r=[("xo", 0, n) for n in range(8)], w=[("XOUT", i)])
                    else:
                        xok = [("xo", 0, n) for n in range(8)]
                        P.op("act", (lambda b=b: A.activation(out=sqb[:], in_=xo[b][:], func=AF.Square)), r=xok, w=[(tag, "sqb")])
                        for c in range(8):
                            P.op("pe", (lambda c=c: PE.matmul(nps[:, :TT], lhsT=ones_b[:], rhs=sqb[:, c, :], start=(c == 0), stop=(c == 7))), r=[(tag, "sqb"), "ones_b"], w=[(tag, "nps")])
                        P.op("act", lambda: A.activation(out=rty[:], in_=nps[:, :TT], func=AF.Sqrt, scale=1.0 / D, bias=epsb[:, 0:1]), r=[(tag, "nps"), "epsb"], w=["rty"])
                        P.op("dve", lambda: V.reciprocal(out=rty[:], in_=rty[:]), r=["rty"], w=["rty"])
                        for c in range(8):
                            P.op("dve", (lambda c=c, b=b: V.scalar_tensor_tensor(out=xo[b][:, c, :], in0=xo[b][:, c, :], scalar=pcol("fin_g", c), in1=rty[:],
                                                                                  op0=ALU.mult, op1=ALU.mult)), r=[("xo", 0, c), "rty", "par"], w=[("xo", 0, c)])
                        P.dma("pool", (lambda b=b, i=i: G.dma_start(out=O["yT"][:, :, i * TT:(i + 1) * TT], in_=xo[b][:])), r=[("xo", 0, c) for c in range(8)], w=[("YO", i)])
                P.dma("sp", lambda: nc.sync.dma_start(out=O["ffn_o"][l, :, :, :], in_=hist[:]), r=[("hist", j) for j in range(44)], w=[("ffn_o", l)])

                sq_s = SB(ph, "sqs%d" % l, [128, 8, NS]); xn_s = SB(ph, "xns%d" % l, [128, 8, NS]); hTs = SB(ph, "hTs%d" % l, [128, 8, NS], BF16)
                stf = SB(ph, "stf%d" % l, [128, 44, 2, NS])
                ups_s = SB(ph, "ups_s%d" % l, [128, 44, NS]); ucs = SB(ph, "ucs%d" % l, [128, 44, NS]); tmp = SB(ph, "tmps%d" % l, [128, 44, NS])
                fo = SB(ph, "fo%d" % l, [128, 44, 2, NS])
                acts = SB(ph, "acts%d" % l, [128, NJ, NS], BF16); sas = SB(ph, "sas%d" % l, [128, NJ, NS])
                mo_s = SB(ph, "mofs%d" % l, [128, 8, NS])
                P.dma("sp", lambda: nc.sync.dma_start(out=stf[:], in_=I["st_ffn"][l, :, :, :, :]), w=["stf"])
                norm_sample(("fs", l), l, SH2, SC2, sq_s, nps, rt, hTs, xn_s)
                for jj in range(44):
                    for kc in range(8):
                        P.op("pe", (lambda jj=jj, kc=kc: PE.matmul(ups[0][:, jj * NS:(jj + 1) * NS], lhsT=w_up[:, kc, jj * 128:(jj + 1) * 128], rhs=hTs[:, kc, :],
                                                                    start=(kc == 0), stop=(kc == 7))), r=[("w_up", kc), (("fs", l), "hT")], w=[("ups", 0)])
                upv = ups[0][:, 0:44 * NS].rearrange("p (j s) -> p j s", j=44)
                P.op("act", lambda: A.activation(out=ups_s[:], in_=upv, func=AF.Copy), r=[("ups", 0)], w=["ups_s"])
                bc = lambda k: par[:, PAR_OFF["f_cw"] + (l * 3 + k) * 44: PAR_OFF["f_cw"] + (l * 3 + k + 1) * 44].unsqueeze(2).to_broadcast([128, 44, NS])
                bcb = par[:, PAR_OFF["f_cb"] + l * 44: PAR_OFF["f_cb"] + (l + 1) * 44].unsqueeze(2).to_broadcast([128, 44, NS])
                P.op("dve", lambda: V.tensor_tensor(out=ucs[:], in0=ups_s[:], in1=bc(2), op=ALU.mult), r=["ups_s", "par"], w=["ucs"])
                P.op("dve", lambda: V.tensor_tensor(out=ucs[:], in0=ucs[:], in1=bcb, op=ALU.add), r=["ucs", "par"], w=["ucs"])
                for k in range(2):
                    P.op("dve", (lambda k=k: V.tensor_tensor(out=tmp[:], in0=stf[:, :, k, :], in1=bc(k), op=ALU.mult)), r=["stf", "par", "ucs"], w=["tmp"])
                    P.op("dve", lambda: V.tensor_tensor(out=ucs[:], in0=ucs[:], in1=tmp[:], op=ALU.add), r=["ucs", "tmp"], w=["ucs"])
                P.op("act", lambda: A.activation(out=sas[:], in_=ucs[:, 0:NJ, :], func=AF.Silu), r=["ucs"], w=["sas"])
                P.op("dve", lambda: V.tensor_tensor(out=acts[:], in0=sas[:], in1=ucs[:, NJ:44, :], op=ALU.mult), r=["sas", "ucs"], w=["acts"])
                for n in range(8):
                    for j in range(NJ):
                        P.op("pe", (lambda n=n, j=j: PE.matmul(ups[1][:, n * NS:(n + 1) * NS], lhsT=w_dn[:, j, n * 128:(n + 1) * 128], rhs=acts[:, j, :],
                                                                start=(j == 0), stop=(j == NJ - 1))), r=[("w_dn", j), "acts"], w=[("ups", 1)])
                P.op("dve", lambda: V.tensor_tensor(out=mo_s[:], in0=ups[1][:, 0:8 * NS].rearrange("p (c s) -> p c s", c=8), in1=modS[:, l, G2:G2 + 8, :], op=ALU.mult),
                     r=[("ups", 1), "modS"], w=["mofs"])
                P.op("dve", lambda: V.tensor_tensor(out=xs_cur[:], in0=xs_cur[:], in1=mo_s[:], op=ALU.add), r=["mofs", "xs"], w=["xs"])
                P.op("pool", lambda: G.tensor_copy(out=fo[:, :, 0, :], in_=stf[:, :, 1, :]), r=["stf"], w=["fo0"])
                P.op("pool", lambda: G.tensor_copy(out=fo[:, :, 1, :], in_=ups_s[:]), r=["ups_s"], w=["fo1"])
                P.dma("sp", lambda: nc.sync.dma_start(out=O["ffn_s"][l, :, :, :, :], in_=fo[:]), r=["fo0", "fo1"], w=[("ffn_s", l)])
                if final:
                    ys = SB(ph, "ys", [128, 8, NS])
                    P.op("dve", lambda: V.tensor_tensor(out=sq_s[:], in0=xs_cur[:], in1=xs_cur[:], op=ALU.mult), r=["xs"], w=["sqs_f"])
                    for c in range(8):
                        P.op("pe", (lambda c=c: PE.matmul(nps[:, :NS], lhsT=ones_f[:], rhs=sq_s[:, c, :], start=(c == 0), stop=(c == 7))), r=["sqs_f", "ones_f"], w=["nps_f"])
                    P.op("act", lambda: A.activation(out=rt[:, :NS], in_=nps[:, :NS], func=AF.Sqrt, scale=1.0 / D, bias=epsb[:, 0:1]), r=["nps_f", "epsb"], w=["rt_f"])
                    P.op("dve", lambda: V.reciprocal(out=rt[:, :NS], in_=rt[:, :NS]), r=["rt_f"], w=["rt_f"])
                    P.op("dve", lambda: V.tensor_tensor(out=ys[:], in0=xs_cur[:], in1=rt[:, :NS].unsqueeze(1).to_broadcast([128, 8, NS]), op=ALU.mult), r=["xs", "rt_f"], w=["ys"])
                    P.op("dve", lambda: V.tensor_tensor(out=ys[:], in0=ys[:], in1=pcol("fin_g", 0, 8).unsqueeze(2).to_broadcast([128, 8, NS]), op=ALU.mult), r=["ys", "par"], w=["ys"])
                    P.dma("sp", lambda: nc.sync.dma_start(out=O["ysT"][:, :, :], in_=ys[:]), r=["ys"], w=["ysT"])
                P.flush()

        ffn_phase(0, X1, X2, False)

        with contextlib.ExitStack() as ph:
            TT = 256
            w_in = SB(ph, "w_ino", [128, 8, 2048], BF16); w_o = SB(ph, "w_oo", [128, 8, D], BF16)
            wa = SB(ph, "wa_sb", [128, 8, 128], BF16); wx = SB(ph, "wx_sb", [128, 8, 128], BF16)
            load_w(None, w_in, lambda kc: I["w_in_o"][:, kc, :], 8, "w_ino")
            load_w(None, w_o, lambda kc: I["w_out_o"][:, kc, :], 8, "w_oo")
            P.dma("pool", lambda: G.dma_start(out=wa[:], in_=I["rg_wa"][:, :, :]), w=["wa"])
            P.dma("pool", lambda: G.dma_start(out=wx[:], in_=I["rg_wx"][:, :, :]), w=["wx"])
            xt_ = [SB(ph, "xtr%d" % i, [128, 8, TT]) for i in range(2)]
            sqb = SB(ph, "sqbr", [128, 8, TT], BF16); xn = SB(ph, "xnr", [128, 8, TT]); rt = SB(ph, "rtr", [128, TT]); hT = SB(ph, "hTr", [128, 8, TT], BF16)
            xr = SB(ph, "xr", [128, 8, 3 + TT])
            gg = [SB(ph, "gg%d" % i, [128, 8, TT]) for i in range(2)]
            xc = [SB(ph, "xc%d" % i, [128, 8, TT]) for i in range(2)]
            xcb = [SB(ph, "xcb%d" % i, [128, 8, TT], BF16) for i in range(2)]
            rr = SB(ph, "rr", [128, 8, TT]); ii = SB(ph, "ii", [128, 8, TT]); aa = SB(ph, "aa", [128, 8, TT]); ss = SB(ph, "ss", [128, 8, TT])
            hs = SB(ph, "hs", [128, 8, TT])
            yb = SB(ph, "yb", [128, 8, TT], BF16)
            hst = SB(ph, "hst", [128, 8])
            xo = [SB(ph, "xor%d" % i, [128, 8, TT]) for i in range(2)]
            nps = PS(ph, "npsr", [128, 512])
            zps = [PS(ph, "zpr%d" % i, [128, 512]) for i in range(3)]
            gps = [PS(ph, "gpr%d" % i, [128, 1024]) for i in range(2)]
            rcw = lambda k, c: par[:, PAR_OFF["rg_cw"] + k * 8 + c: PAR_OFF["rg_cw"] + k * 8 + c + 1]
            P.op("pool", lambda: G.memset(hst[:], 0.0), w=[("hst", c) for c in range(8)])
            P.op("pool", lambda: G.memset(xr[:, :, 0:3], 0.0), w=[("xrh", c) for c in range(8)])

            def loadr(i):
                b = i % 2
                P.dma("sp", (lambda: nc.sync.dma_start(out=xt_[b][:], in_=X2[:, :, i * TT:(i + 1) * TT])), w=[("xtr", b)])
            zc_ = [0]
            tag = "rg"

            def stage1a(i):
                b = i % 2
                norm_prompt(("xtr", b), tag, xt_[b], TT, 1, SH1, SC1, sqb, nps, rt, hT, xn)

            def stage1b(i):
                b = i % 2
                for n in list(range(8, 16)) + list(range(8)):
                    z = zc_[0] % 3
                    zc_[0] += 1
                    for kc in range(8):
                        P.op("pe", (lambda n=n, kc=kc, z=z: PE.matmul(zps[z][:, 0:TT], lhsT=w_in[:, kc, n * 128:(n + 1) * 128], rhs=hT[:, kc, :], start=(kc == 0), stop=(kc == 7))),
                             r=[("w_ino", kc), (tag, "hT")], w=[("zpr", z)])
                    if n < 8:
                        P.op("act", (lambda n=n, z=z, b=b: A.activation(out=gg[b][:, n, :], in_=zps[z][:, 0:TT], func=GELU)), r=[("zpr", z)], w=[("gg", b, n)])
                    else:
                        c = n - 8
                        P.op("act", (lambda c=c, z=z: A.activation(out=xr[:, c, 3:3 + TT], in_=zps[z][:, 0:TT], func=AF.Copy)), r=[("zpr", z)], w=[("xr", c)])
                        P.op("act", (lambda c=c, z=z, b=b: A.activation(out=xc[b][:, c, :], in_=zps[z][:, 0:TT], func=AF.Identity, scale=rcw(3, c), bias=pcol("rg_cb", c))),
                             r=[("zpr", z), "par"], w=[("xc", b, c)])
                        for k in range(3):
                            P.op("dve", (lambda c=c, k=k, b=b: V.scalar_tensor_tensor(out=xc[b][:, c, :], in0=xr[:, c, k:k + TT], scalar=rcw(k, c), in1=xc[b][:, c, :],
                                                                                        op0=ALU.mult, op1=ALU.add)),
                                 r=[("xr", c), ("xrh", c), ("xc", b, c), "par"], w=[("xc", b, c)])
                        P.op("pool", (lambda c=c: G.tensor_copy(out=xr[:, c, 0:3], in_=xr[:, c, TT:TT + 3])), r=[("xr", c)], w=[("xrh", c)])
                        P.op("pool", (lambda c=c, b=b: G.tensor_copy(out=xcb[b][:, c, :], in_=xc[b][:, c, :])), r=[("xc", b, c)], w=[("xcb", b, c)])

            def stage2a(i):
                b = i % 2
                for c in range(8):
                    q2 = c % 2
                    P.op("pe", (lambda c=c, q2=q2, b=b: PE.matmul(gps[q2][:, 0:TT], lhsT=wa[:, c, :], rhs=xcb[b][:, c, :], start=True, stop=True)), r=["wa", ("xcb", b, c)], w=[("gpr", q2)])
                    P.op("pe", (lambda c=c, q2=q2, b=b: PE.matmul(gps[q2][:, 512:512 + TT], lhsT=wx[:, c, :], rhs=xcb[b][:, c, :], start=True, stop=True)), r=["wx", ("xcb", b, c)], w=[("gpr", q2)])
                    P.op("act", (lambda c=c, q2=q2: A.activation(out=rr[:, c, :], in_=gps[q2][:, 0:TT], func=AF.Sigmoid, bias=pcol("rg_ba", c), scale=1.0)),
                         r=[("gpr", q2), "par"], w=[("rr", c)])
                    P.op("act", (lambda c=c, q2=q2: A.activation(out=ii[:, c, :], in_=gps[q2][:, 512:512 + TT], func=AF.Sigmoid, bias=pcol("rg_bx", c), scale=1.0)),
                         r=[("gpr", q2), "par"], w=[("ii", c)])
                for c in range(8):
                    P.op("act", (lambda c=c: A.activation(out=aa[:, c, :], in_=rr[:, c, :], func=AF.Exp, scale=cst[:, c:c + 1])), r=[("rr", c), "cst"], w=[("aa", c)])
                    P.op("act", (lambda c=c: A.activation(out=ss[:, c, :], in_=rr[:, c, :], func=AF.Exp, scale=cst2[:, c:c + 1])), r=[("rr", c), "cst2"], w=[("ss", c)])
                ssk = [("ss", c) for c in range(8)]
                iik = [("ii", c) for c in range(8)]
                P.op("act", lambda: A.activation(out=ss[:], in_=ss[:], func=AF.Sqrt, scale=-1.0, bias=oneb[:, 0:1]), r=ssk + ["oneb"], w=ssk)
                P.op("pool", (lambda b=b: G.tensor_tensor(out=ii[:], in0=ii[:], in1=xc[b][:], op=ALU.mult)), r=iik + [("xc", b, c) for c in range(8)], w=iik)
                P.op("pool", lambda: G.tensor_tensor(out=ii[:], in0=ii[:], in1=ss[:], op=ALU.mult), r=iik + ssk, w=iik)
                for c in range(8):
                    P.op("dve", (lambda c=c: V.tensor_tensor_scan(out=hs[:, c, :], data0=aa[:, c, :], data1=ii[:, c, :], initial=hst[:, c:c + 1], op0=ALU.mult, op1=ALU.add)),
                         r=[("aa", c), ("ii", c), ("hst", c)], w=[("hs", c)])
                    P.op("dve", (lambda c=c: V.tensor_copy(out=hst[:, c:c + 1], in_=hs[:, c, TT - 1:TT])), r=[("hs", c)], w=[("hst", c)])
                P.op("pool", (lambda b=b: G.tensor_tensor(out=yb[:], in0=hs[:], in1=gg[b][:], op=ALU.mult)),
                     r=[("hs", c) for c in range(8)] + [("gg", b, c) for c in range(8)], w=[("yb", c) for c in range(8)])

            def stage2b(i):
                b = i % 2
                for n in range(8):
                    z = zc_[0] % 3
                    zc_[0] += 1
                    for kc in range(8):
                        P.op("pe", (lambda n=n, kc=kc, z=z: PE.matmul(zps[z][:, 0:TT], lhsT=w_o[:, kc, n * 128:(n + 1) * 128], rhs=yb[:, kc, :], start=(kc == 0), stop=(kc == 7))),
                             r=[("w_oo", kc), ("yb", kc)], w=[("zpr", z)])
                    P.op("dve", (lambda n=n, z=z, b=b: V.scalar_tensor_tensor(out=xo[b][:, n, :], in0=zps[z][:, 0:TT], scalar=modP[:, 1, G1 + n:G1 + n + 1], in1=xt_[b][:, n, :],
                                                                               op0=ALU.mult, op1=ALU.add)), r=[("zpr", z), ("xtr", b), "modP"], w=[("xor", b, n)])
                P.dma("pool", (lambda b=b, i=i: G.dma_start(out=X3[:, :, i * TT:(i + 1) * TT], in_=xo[b][:])), r=[("xor", b, n) for n in range(8)], w=[("X3", i)])

            loadr(0)
            stage1a(0)
            stage1b(0)
            for i in range(NT2):
                if i + 1 < NT2:
                    loadr(i + 1)
                    stage1a(i + 1)
                stage2a(i)
                if i + 1 < NT2:
                    stage1b(i + 1)
                stage2b(i)
            P.dma("sp", lambda: nc.sync.dma_start(out=O["rgc_o"][:, :, :], in_=xr[:, :, 0:3]), r=[("xrh", c) for c in range(8)], w=["rgc_o"])
            P.dma("sp", lambda: nc.sync.dma_start(out=O["rgh_o"][:, :], in_=hst[:]), r=[("hst", c) for c in range(8)], w=["rgh_o"])

            sq_s = SB(ph, "sqsr", [128, 8, NS]); xn_s = SB(ph, "xnsr", [128, 8, NS]); hTs = SB(ph, "hTsr", [128, 8, NS], BF16)
            stc = SB(ph, "stc", [128, 8, 3, NS]); sth = SB(ph, "sth", [128, 8, NS])
            zs = SB(ph, "zs", [128, 16, NS]); ggs = SB(ph, "ggs", [128, 8, NS])
            xcs = SB(ph, "xcs", [128, 8, NS]); tmp = SB(ph, "tmpr", [128, 8, NS]); xcsb = SB(ph, "xcsb", [128, 8, NS], BF16)
            rs_ = SB(ph, "rs_", [128, 8, NS]); is_ = SB(ph, "is_", [128, 8, NS]); as_ = SB(ph, "as_", [128, 8, NS]); s2 = SB(ph, "s2_", [128, 8, NS])
            hn = SB(ph, "hn", [128, 8, NS]); ybs = SB(ph, "ybs", [128, 8, NS], BF16); mo_s = SB(ph, "mors", [128, 8, NS])
            co = SB(ph, "co", [128, 8, 3, NS])
            P.dma("sp", lambda: nc.sync.dma_start(out=stc[:], in_=I["st_rgc"][:, :, :, :]), w=["stc"])
            P.dma("sp", lambda: nc.sync.dma_start(out=sth[:], in_=I["st_rgh"][:, :, :]), w=["sth"])
            norm_sample("rs", 1, SH1, SC1, sq_s, nps, rt, hTs, xn_s)
            for n in range(16):
                for kc in range(8):
                    P.op("pe", (lambda n=n, kc=kc: PE.matmul(zps[0][:, n * NS:(n + 1) * NS], lhsT=w_in[:, kc, n * 128:(n + 1) * 128], rhs=hTs[:, kc, :], start=(kc == 0), stop=(kc == 7))),
                         r=[("w_ino", kc), ("rs", "hT")], w=[("zpr", 0)])
            P.op("act", lambda: A.activation(out=zs[:], in_=zps[0][:, 0:16 * NS].rearrange("p (c s) -> p c s", c=16), func=AF.Copy), r=[("zpr", 0)], w=["zs"])
            P.op("act", lambda: A.activation(out=ggs[:], in_=zs[:, 0:8, :], func=GELU), r=["zs"], w=["ggs"])
            bw = lambda k: par[:, PAR_OFF["rg_cw"] + k * 8: PAR_OFF["rg_cw"] + (k + 1) * 8].unsqueeze(2).to_broadcast([128, 8, NS])
            b8 = lambda nm: pcol(nm, 0, 8).unsqueeze(2).to_broadcast([128, 8, NS])
            P.op("dve", lambda: V.tensor_tensor(out=xcs[:], in0=zs[:, 8:16, :], in1=bw(3), op=ALU.mult), r=["zs", "par"], w=["xcs"])
            P.op("dve", lambda: V.tensor_tensor(out=xcs[:], in0=xcs[:], in1=b8("rg_cb"), op=ALU.add), r=["xcs", "par"], w=["xcs"])
            for k in range(3):
                P.op("dve", (lambda k=k: V.tensor_tensor(out=tmp[:], in0=stc[:, :, k, :], in1=bw(k), op=ALU.mult)), r=["stc", "par", "xcs"], w=["tmpr"])
                P.op("dve", lambda: V.tensor_tensor(out=xcs[:], in0=xcs[:], in1=tmp[:], op=ALU.add), r=["xcs", "tmpr"], w=["xcs"])
            P.op("dve", lambda: V.tensor_copy(out=xcsb[:], in_=xcs[:]), r=["xcs"], w=["xcsb"])
            for c in range(8):
                P.op("pe", (lambda c=c: PE.matmul(zps[1][:, c * NS:(c + 1) * NS], lhsT=wa[:, c, :], rhs=xcsb[:, c, :], start=True, stop=True)), r=["wa", "xcsb"], w=[("zpr", 1)])
                P.op("pe", (lambda c=c: PE.matmul(zps[1][:, 64 + c * NS: 64 + (c + 1) * NS], lhsT=wx[:, c, :], rhs=xcsb[:, c, :], start=True, stop=True)), r=["wx", "xcsb"], w=[("zpr", 1)])
            P.op("dve", lambda: V.tensor_tensor(out=rs_[:], in0=zps[1][:, 0:8 * NS].rearrange("p (c s) -> p c s", c=8), in1=b8("rg_ba"), op=ALU.add), r=[("zpr", 1), "par"], w=["rs_"])
            P.op("dve", lambda: V.tensor_tensor(out=is_[:], in0=zps[1][:, 64:64 + 8 * NS].rearrange("p (c s) -> p c s", c=8), in1=b8("rg_bx"), op=ALU.add), r=[("zpr", 1), "par"], w=["is_"])
            P.op("act", lambda: A.activation(out=rs_[:], in_=rs_[:], func=AF.Sigmoid), r=["rs_"], w=["rs_"])
            P.op("act", lambda: A.activation(out=is_[:], in_=is_[:], func=AF.Sigmoid), r=["is_"], w=["is_"])
            P.op("dve", lambda: V.tensor_tensor(out=as_[:], in0=rs_[:], in1=cst[:].unsqueeze(2).to_broadcast([128, 8, NS]), op=ALU.mult), r=["rs_", "cst"], w=["as_"])
            P.op("act", lambda: A.activation(out=s2[:], in_=as_[:], func=AF.Exp, scale=2.0), r=["as_"], w=["s2"])
            P.op("act", lambda: A.activation(out=as_[:], in_=as_[:], func=AF.Exp), r=["as_", "s2"], w=["as_"])
            P.op("act", lambda: A.activation(out=s2[:], in_=s2[:], func=AF.Sqrt, scale=-1.0, bias=oneb[:, 0:1]), r=["s2", "oneb"], w=["s2"])
            P.op("dve", lambda: V.tensor_tensor(out=is_[:], in0=is_[:], in1=xcs[:], op=ALU.mult), r=["is_", "xcs"], w=["is_"])
            P.op("dve", lambda: V.tensor_tensor(out=is_[:], in0=is_[:], in1=s2[:], op=ALU.mult), r=["is_", "s2"], w=["is_"])
            P.op("dve", lambda: V.tensor_tensor(out=hn[:], in0=as_[:], in1=sth[:], op=ALU.mult), r=["as_", "sth"], w=["hn"])
            P.op("dve", lambda: V.tensor_tensor(out=hn[:], in0=hn[:], in1=is_[:], op=ALU.add), r=["hn", "is_"], w=["hn"])
            P.dma("sp", lambda: nc.sync.dma_start(out=O["rgh_s"][:, :, :], in_=hn[:]), r=["hn"], w=["rgh_s"])
            P.op("pool", lambda: G.tensor_copy(out=co[:, :, 0:2, :], in_=stc[:, :, 1:3, :]), r=["stc"], w=["co0"])
            P.op("pool", lambda: G.tensor_copy(out=co[:, :, 2, :], in_=zs[:, 8:16, :]), r=["zs"], w=["co1"])
            P.dma("sp", lambda: nc.sync.dma_start(out=O["rgc_s"][:, :, :, :], in_=co[:]), r=["co0", "co1"], w=["rgc_s"])
            P.op("dve", lambda: V.tensor_tensor(out=ybs[:], in0=hn[:], in1=ggs[:], op=ALU.mult), r=["hn", "ggs"], w=["ybs"])
            for n in range(8):
                for kc in range(8):
                    P.op("pe", (lambda n=n, kc=kc: PE.matmul(zps[2][:, n * NS:(n + 1) * NS], lhsT=w_o[:, kc, n * 128:(n + 1) * 128], rhs=ybs[:, kc, :], start=(kc == 0), stop=(kc == 7))),
                         r=[("w_oo", kc), "ybs"], w=[("zpr", 2)])
            P.op("dve", lambda: V.tensor_tensor(out=mo_s[:], in0=zps[2][:, 0:8 * NS].rearrange("p (c s) -> p c s", c=8), in1=modS[:, 1, G1:G1 + 8, :], op=ALU.mult),
                 r=[("zpr", 2), "modS"], w=["mors"])
            P.op("dve", lambda: V.tensor_tensor(out=xs_cur[:], in0=xs_cur[:], in1=mo_s[:], op=ALU.add), r=["mors", "xs"], w=["xs"])
            P.flush()

        ffn_phase(1, X3, None, True)
        P.barrier(final=True)
    return nc


_CACHE = {}


def _fm(v, n):
    return np.ascontiguousarray(np.asarray(v, np.float32).reshape(n, 128).T)


def _wk(w):
    K, N = w.shape
    return np.ascontiguousarray(np.asarray(w, np.float32).reshape(K // 128, 128, N).transpose(1, 0, 2))


def kernel(**inp):
    inp = {k: np.asarray(v) for k, v in inp.items()}
    x_prompt = inp["x_prompt"]
    B, T, _ = x_prompt.shape
    TK = min(LCACHE, T)
    if T not in _CACHE:
        _CACHE[T] = build_program(T)
    nc = _CACHE[T]
    NCORES = 8
    par = np.zeros((128, NPAR), np.float32)

    def put(name, arr):
        par[:, PAR_OFF[name]:PAR_OFF[name] + arr.shape[1]] = arr
    put("b_ada", np.concatenate([_fm(inp["b_ada"][0], 48), _fm(inp["b_ada"][1], 48)], 1))
    put("ln_g", _fm(inp["ln_v_g"][0], 4)); put("ln_b", _fm(inp["ln_v_b"][0], 4))
    put("rg_cw", np.concatenate([_fm(inp["rg_conv_w"][0, k], 8) for k in range(4)], 1))
    put("rg_cb", _fm(inp["rg_conv_b"][0], 8)); put("rg_ba", _fm(inp["rg_b_a"][0], 8)); put("rg_bx", _fm(inp["rg_b_x"][0], 8))
    put("rg_lam", _fm(inp["rg_lambda"][0], 8))
    put("f_cw", np.concatenate([_fm(inp["ffn_conv_w"][l, k], 44) for l in range(2) for k in range(3)], 1))
    put("f_cb", np.concatenate([_fm(inp["ffn_conv_b"][l], 44) for l in range(2)], 1))
    put("fin_g", _fm(inp["final_g"], 8))
    w_oe = inp["w_out_even"][0]
    shared = dict(
        wada=np.stack([_wk(inp["w_ada"][l]) for l in range(2)]), params=par,
        w_in_e=_wk(inp["w_in_even"][0]), w_out_e=_wk(w_oe),
        w_out_eh=np.ascontiguousarray(w_oe[512:].reshape(8, 64, D).transpose(1, 0, 2)),
        w_sguT=np.ascontiguousarray(inp["w_sgu"][0].transpose(2, 0, 1)),
        b_sgu=np.ascontiguousarray(inp["b_sgu"][0].reshape(1, 512)),
        sg0=np.ascontiguousarray(np.concatenate([inp["w_sgu"][0][:, 0, 0], inp["b_sgu"][0][:, 0]]).reshape(1, 8)),
        w_in_o=_wk(inp["w_in_odd"][0]),
        rg_wa=np.ascontiguousarray(inp["rg_w_a"][0].transpose(1, 0, 2)), rg_wx=np.ascontiguousarray(inp["rg_w_x"][0].transpose(1, 0, 2)),
        w_out_o=_wk(inp["w_out_odd"][0]),
        w_up=np.stack([_wk(inp["ffn_w_up"][l]) for l in range(2)]), w_dn=np.stack([_wk(inp["ffn_w_down"][l]) for l in range(2)]),
    )
    rowmask = np.zeros((128, NS), np.float32)
    dmask = np.zeros((128, 8, 64), np.float32)
    for s in range(NS):
        rowmask[32 * s:32 * s + 8, s] = 1.0
        for h in range(8):
            dmask[32 * s + h, h, :] = 1.0
    shared["rowmask"] = rowmask
    shared["dmask"] = dmask.reshape(128, 512)
    in_maps = []
    SEQ_CORE = [0, 1, 4, 5][:B] if B <= 4 else list(range(B))
    zero_x = np.zeros((128, 8, T), np.float32)
    for c in range(NCORES):
        ss = slice(NS * c, NS * (c + 1))
        m = dict(shared)
        if c in SEQ_CORE:
            sq = SEQ_CORE.index(c)
            m["xT"] = np.ascontiguousarray(x_prompt[sq].T.reshape(8, 128, T).transpose(1, 0, 2))
            cp = inp["c_prompt"][sq:sq + 1]
        else:
            m["xT"] = zero_x
            cp = np.zeros((1, D), np.float32)
        m["xsT"] = np.ascontiguousarray(inp["x_sample"][ss, 0, :].T.reshape(8, 128, NS).transpose(1, 0, 2))
        cc = np.concatenate([cp, inp["c_sample"][ss]], 0)
        m["cT"] = np.ascontiguousarray(cc.T.reshape(8, 128, 1 + NS).transpose(1, 0, 2))
        m["ck"] = np.ascontiguousarray(inp["cache_win_k"][0, ss].reshape(NS, LCACHE, 512))
        m["cv"] = np.ascontiguousarray(inp["cache_win_v"][0, ss].reshape(NS, LCACHE, 512))
        m["st_rgc"] = np.ascontiguousarray(inp["state_rglru_conv"][0, ss].transpose(2, 1, 0).reshape(8, 128, 3, NS).transpose(1, 0, 2, 3))
        m["st_rgh"] = np.ascontiguousarray(inp["state_rglru_h"][0, ss].T.reshape(8, 128, NS).transpose(1, 0, 2))
        m["st_ffn"] = np.ascontiguousarray(inp["state_ffn_conv"][:, ss].transpose(0, 3, 2, 1).reshape(2, 44, 128, 2, NS).transpose(0, 2, 1, 3, 4))
        in_maps.append(m)
    res = run_bass_kernel_spmd(nc, in_maps, core_ids=list(range(NCORES)))
    R = res.results
    global _LAST
    _LAST = R

    def unfm(a):
        a = np.asarray(a)
        n = a.shape[1]
        rest = a.shape[2:]
        return np.moveaxis(a.transpose(1, 0, *range(2, a.ndim)).reshape(n * 128, *rest), 0, -1)

    y_prompt = np.stack([unfm(R[SEQ_CORE[b]]["yT"]) for b in range(B)]).astype(np.float32)
    y_sample = np.concatenate([unfm(R[c]["ysT"]) for c in range(NCORES)], 0)[:, None, :].astype(np.float32)
    win_k_p = np.stack([unfm(R[SEQ_CORE[b]]["kT_o"]).reshape(TK, 8, 64) for b in range(B)])[None].astype(np.float32)
    win_v_p = np.stack([unfm(R[SEQ_CORE[b]]["vT_o"]).reshape(TK, 8, 64) for b in range(B)])[None].astype(np.float32)
    rgc_p = np.stack([unfm(R[SEQ_CORE[b]]["rgc_o"]) for b in range(B)])[None].astype(np.float32)
    rgh_p = np.stack([unfm(R[SEQ_CORE[b]]["rgh_o"][:, :, None])[0] for b in range(B)])[None].astype(np.float32)
    ffn_p = np.stack([np.stack([unfm(R[SEQ_CORE[b]]["ffn_o"][l]) for b in range(B)]) for l in range(2)]).astype(np.float32)
    cv_s = np.concatenate([unfm(R[c]["cv_o"]) for c in range(NCORES)], 0)[None, :, None, :].astype(np.float32)
    wk_s = np.concatenate([R[c]["wk_s"].reshape(NS, LCACHE, 8, 64) for c in range(NCORES)], 0)[None].astype(np.float32)
    wv_s = np.concatenate([R[c]["wv_s"].reshape(NS, LCACHE, 8, 64) for c in range(NCORES)], 0)[None].astype(np.float32)
    rgc_s = np.concatenate([unfm(R[c]["rgc_s"]).transpose(1, 0, 2) for c in range(NCORES)], 0)[None].astype(np.float32)
    rgh_s = np.concatenate([unfm(R[c]["rgh_s"]) for c in range(NCORES)], 0)[None].astype(np.float32)
    ffn_s = np.stack([np.concatenate([unfm(R[c]["ffn_s"][l]).transpose(1, 0, 2) for c in range(NCORES)], 0) for l in range(2)]).astype(np.float32)
    return (y_prompt, y_sample, win_k_p, win_v_p, rgc_p, rgh_p, ffn_p, cv_s, wk_s, wv_s, rgc_s, rgh_s, ffn_s)
```

```python
import contextlib
import numpy as np
import concourse.bass as bass
import concourse.mybir as mybir
from concourse.bass_utils import run_bass_kernel_spmd

F32 = mybir.dt.float32
BF16 = mybir.dt.bfloat16
AF = mybir.ActivationFunctionType
ALU = mybir.AluOpType
AX = mybir.AxisListType

D = 1024
NCH = 8
WA = 512
NIN_E = 2560
DFF = 2816
F2 = 5632
NJ = 22
NS = 4
LCACHE = 2048
EPS = 1e-6
PATTERNS = ((128, 1), (512, 4), (2048, 16))
GELU = AF.Gelu_apprx_tanh

PAR_SPEC = [("b_ada", 96), ("ln_g", 4), ("ln_b", 4), ("rg_cw", 32), ("rg_cb", 8), ("rg_ba", 8),
            ("rg_bx", 8), ("rg_lam", 8), ("f_cw", 2 * 3 * 44), ("f_cb", 2 * 44), ("fin_g", 8)]
PAR_OFF = {}
_o = 0
for _n, _w in PAR_SPEC:
    PAR_OFF[_n] = _o
    _o += _w
NPAR = _o


class Prog:
    NDS = 8

    def __init__(self, nc, es):
        self.nc = nc
        self.eng = {"pe": nc.tensor, "act": nc.scalar, "dve": nc.vector, "pool": nc.gpsimd, "sp": nc.sync}
        self.sem = {e: es.enter_context(nc.semaphore("s_" + e)) for e in self.eng}
        self.cnt = {e: 0 for e in self.eng}
        self.dsem = {q: [es.enter_context(nc.semaphore("d_%s%d" % (q, i))) for i in range(self.NDS)]
                     for q in ("sp", "pool", "actq")}
        self.qeng = {"sp": "sp", "pool": "pool", "actq": "act"}
        self.dcnt = {q: 0 for q in self.dsem}
        self.seen = {e: {} for e in self.eng}
        self.ops = []

    def op(self, eng, fn, r=(), w=()):
        self.ops.append([eng, fn, tuple(r), tuple(w), False])

    def dma(self, q, fn, r=(), w=()):
        self.ops.append([q, fn, tuple(r), tuple(w), True])

    def _wait(self, e, sem, val):
        key = sem.name if hasattr(sem, "name") else id(sem)
        if self.seen[e].get(key, 0) >= val:
            return
        self.seen[e][key] = val
        self.eng[e].wait_ge(sem, val)

    def flush(self):
        import os
        self.nflush = getattr(self, "nflush", 0) + 1
        stop = int(os.environ.get("K_STOP", "99"))
        only = os.environ.get("K_ONLY")
        if self.nflush > stop or (only and str(self.nflush) not in only.split(",")):
            self.ops = []
            return
        kops = os.environ.get("K_OPS")
        if kops and self.nflush == stop:
            self.ops = self.ops[:int(kops)]
            print("last kept op:", self.ops[-1][0], self.ops[-1][2], self.ops[-1][3], "of", len(self.ops))
        ops = self.ops
        n = len(ops)
        lastw, readers = {}, {}
        deps = [None] * n
        needed = [False] * n
        for i, (e, fn, r, w, isd) in enumerate(ops):
            d = set()
            for k in r:
                if k in lastw:
                    d.add(lastw[k])
            for k in w:
                if k in lastw:
                    d.add(lastw[k])
                for j in readers.get(k, ()):
                    d.add(j)
            d.discard(i)
            dd = []
            for j in d:
                if ops[j][0] == "pe" and e == "pe" and not ops[j][4] and not isd:
                    continue
                dd.append(j)
                needed[j] = True
            deps[i] = sorted(dd)
            for k in r:
                readers.setdefault(k, []).append(i)
            for k in w:
                lastw[k] = i
                readers[k] = []
        lastop = {}
        for i, (e, fn, r, w, isd) in enumerate(ops):
            if not isd:
                lastop[e] = i
        for i in lastop.values():
            needed[i] = True
        sig = [None] * n
        for i, (e, fn, r, w, isd) in enumerate(ops):
            q = e
            if isd:
                e = self.qeng[q]
            for j in deps[i]:
                s, v = sig[j]
                self._wait(e, s, v)
            if isd:
                m = self.dcnt[q]
                self.dcnt[q] += 1
                s = self.dsem[q][m % self.NDS]
                rnd = m // self.NDS
                if rnd > 0:
                    self._wait(e, s, 16 * rnd)
                ins = fn()
                ins.then_inc(s, 16)
                sig[i] = (s, 16 * (rnd + 1))
            else:
                ins = fn()
                if needed[i]:
                    self.cnt[e] += 1
                    ins.then_inc(self.sem[e], 1)
                    sig[i] = (self.sem[e], self.cnt[e])
        self.ops = []
        self.barrier()

    def barrier(self, final=False):
        for e in self.eng:
            for e2 in self.eng:
                if e2 != e and self.cnt[e2] > 0:
                    self._wait(e, self.sem[e2], self.cnt[e2])
            for q in self.dsem:
                if q == "actq" and not final:
                    continue
                m = self.dcnt[q]
                for k in range(min(m, self.NDS)):
                    cntk = (m - k + self.NDS - 1) // self.NDS
                    self._wait(e, self.dsem[q][k], 16 * cntk)


def build_program(T):
    nc = bass.Bass("TRN2", target_bir_lowering=False)
    TK = min(LCACHE, T)
    NT5 = T // 512
    NT2 = T // 256

    def din(name, shape, dt=F32):
        return nc.dram_tensor(name, list(shape), dt, kind="ExternalInput").ap()

    def dout(name, shape):
        return nc.dram_tensor(name, list(shape), F32, kind="ExternalOutput").ap()

    def dscr(name, shape, dt=F32):
        import os
        kind = "ExternalOutput" if (os.environ.get("K_DBG") and name in ("x1T", "x2T", "x3T", "O_s", "aoT_s")) else "Internal"
        return nc.dram_tensor(name, list(shape), dt, kind=kind).ap()

    I = dict(
        xT=din("xT", [128, 8, T]), xsT=din("xsT", [128, 8, NS]), cT=din("cT", [128, 8, 1 + NS]),
        wada=din("wada", [2, 128, 8, 6144]), params=din("params", [128, NPAR]),
        w_in_e=din("w_in_e", [128, 8, NIN_E]), w_out_e=din("w_out_e", [128, 8, D]),
        w_out_eh=din("w_out_eh", [64, 8, D]), w_sguT=din("w_sguT", [128, 4, 128]),
        b_sgu=din("b_sgu", [1, 512]), sg0=din("sg0", [1, 8]),
        w_in_o=din("w_in_o", [128, 8, 2048]), rg_wa=din("rg_wa", [128, 8, 128]),
        rg_wx=din("rg_wx", [128, 8, 128]), w_out_o=din("w_out_o", [128, 8, D]),
        w_up=din("w_up", [2, 128, 8, F2]), w_dn=din("w_dn", [2, 128, NJ, D]),
        ck=din("ck", [NS, LCACHE, 512]), cv=din("cv", [NS, LCACHE, 512]),
        st_rgc=din("st_rgc", [128, 8, 3, NS]), st_rgh=din("st_rgh", [128, 8, NS]),
        st_ffn=din("st_ffn", [2, 128, 44, 2, NS]),
        rowmask=din("rowmask", [128, NS]), dmask=din("dmask", [128, 512]),
    )
    O = dict(
        yT=dout("yT", [128, 8, T]), ysT=dout("ysT", [128, 8, NS]),
        kT_o=dout("kT_o", [128, 4, TK]), vT_o=dout("vT_o", [128, 4, TK]),
        rgc_o=dout("rgc_o", [128, 8, 3]), rgh_o=dout("rgh_o", [128, 8]),
        ffn_o=dout("ffn_o", [2, 128, 44, 2]), cv_o=dout("cv_o", [128, 4, NS]),
        wk_s=dout("wk_s", [NS, LCACHE, 512]), wv_s=dout("wv_s", [NS, LCACHE, 512]),
        rgc_s=dout("rgc_s", [128, 8, 3, NS]), rgh_s=dout("rgh_s", [128, 8, NS]),
        ffn_s=dout("ffn_s", [2, 128, 44, 2, NS]),
    )
    X1 = dscr("x1T", [128, 8, T]); X2 = dscr("x2T", [128, 8, T]); X3 = dscr("x3T", [128, 8, T])
    QS = dscr("qT_s", [128, 4, T], BF16); KS = dscr("kT_s", [128, 4, T], BF16); VS = dscr("vT_s", [128, 4, T], BF16)
    AOS = dscr("aoT_s", [128, 4, T], BF16)
    OS = dscr("O_s", [3, T, 8 * 66])
    QKVS = dscr("qkv_s", [NS, 1536])
    XS1 = None

    top = contextlib.ExitStack()
    with top:
        P = Prog(nc, top)

        def SB(es, name, shape, dt=F32):
            return es.enter_context(nc.sbuf_tensor("sb_" + name, list(shape), dt))

        def PS(es, name, shape, dt=F32):
            return es.enter_context(nc.psum_tensor("ps_" + name, list(shape), dt))

        V, A, G, PE = nc.vector, nc.scalar, nc.gpsimd, nc.tensor

        par = SB(top, "par", [128, NPAR])
        ones_b = SB(top, "ones_b", [128, 128], BF16)
        ones_f = SB(top, "ones_f", [128, 128])
        ident_f = SB(top, "ident_f", [128, 128])
        ident_b = SB(top, "ident_b", [128, 128], BF16)
        modP = SB(top, "modP", [128, 2, 48])
        modS = SB(top, "modS", [128, 2, 48, NS])
        cst = SB(top, "cst", [128, 8])
        cst2 = SB(top, "cst2", [128, 8])
        xs_cur = SB(top, "xs_cur", [128, 8, NS])
        epsb = SB(top, "epsb", [128, 1])
        oneb = SB(top, "oneb", [128, 1])

        def pcol(name, i=0, n=1):
            o = PAR_OFF[name] + i
            return par[:, o:o + n]

        P.dma("sp", lambda: nc.sync.dma_start(out=par[:], in_=I["params"][:, :]), w=["par"])
        P.op("pool", lambda: G.memset(ones_f[:], 1.0), w=["ones_f"])
        P.op("pool", lambda: G.memset(ones_b[:], 1.0), w=["ones_b"])
        P.op("pool", lambda: G.memset(epsb[:], EPS), w=["epsb"])
        P.op("pool", lambda: G.memset(oneb[:], 1.0), w=["oneb"])
        P.op("pool", lambda: G.memset(ident_f[:], 1.0), w=["ident_f"])
        P.op("pool", lambda: G.affine_select(out=ident_f[:], in_=ident_f[:], pattern=[[-1, 128]],
                                             compare_op=ALU.is_equal, fill=0.0, base=0, channel_multiplier=1),
             r=["ident_f"], w=["ident_f"])
        P.op("dve", lambda: V.tensor_copy(out=ident_b[:], in_=ident_f[:]), r=["ident_f"], w=["ident_b"])
        P.dma("sp", lambda: nc.sync.dma_start(out=xs_cur[:], in_=I["xsT"][:, :, :]), w=["xs"])

        with contextlib.ExitStack() as ph:
            cT = SB(ph, "cT_sb", [128, 8, 1 + NS])
            cb = SB(ph, "cb_sb", [128, 8, 1 + NS], BF16)
            wada = SB(ph, "wada_sb", [128, 8, 6144], BF16)
            mps = PS(ph, "mod_ps", [128, 48, 1 + NS])
            P.dma("sp", lambda: nc.sync.dma_start(out=cT[:], in_=I["cT"][:, :, :]), w=["cT"])
            P.op("act", lambda: A.activation(out=cb[:], in_=cT[:], func=AF.Silu), r=["cT"], w=["cb"])
            for l in range(2):
                for kc in range(8):
                    P.dma("pool", (lambda l=l, kc=kc: G.dma_start(out=wada[:, kc, :], in_=I["wada"][l, :, kc, :])),
                          w=[("wada", kc)])
                for n in range(48):
                    for kc in range(8):
                        P.op("pe", (lambda n=n, kc=kc: PE.matmul(mps[:, n, :], lhsT=wada[:, kc, n * 128:(n + 1) * 128],
                                                                  rhs=cb[:, kc, :], start=(kc == 0), stop=(kc == 7))),
                             r=[("wada", kc), "cb"], w=["mps"])
                bada = par[:, PAR_OFF["b_ada"] + l * 48: PAR_OFF["b_ada"] + (l + 1) * 48]
                P.op("dve", (lambda l=l, bada=bada: V.tensor_tensor(out=modP[:, l, :], in0=mps[:, :, 0], in1=bada, op=ALU.add)),
                     r=["mps", "par"], w=["modP"])
                P.op("dve", (lambda l=l, bada=bada: V.tensor_tensor(
                    out=modS[:, l, :, :], in0=mps[:, :, 1:1 + NS],
                    in1=bada.unsqueeze(2).to_broadcast([128, 48, NS]), op=ALU.add)),
                     r=["mps", "par"], w=["modS"])
            for l in range(2):
                for c0 in (8, 32):
                    P.op("dve", (lambda l=l, c0=c0: V.tensor_scalar_add(out=modP[:, l, c0:c0 + 8], in0=modP[:, l, c0:c0 + 8], scalar1=1.0)),
                         r=["modP"], w=["modP"])
                    P.op("dve", (lambda l=l, c0=c0: V.tensor_scalar_add(out=modS[:, l, c0:c0 + 8, :], in0=modS[:, l, c0:c0 + 8, :], scalar1=1.0)),
                         r=["modS"], w=["modS"])
            lam = pcol("rg_lam", 0, 8)
            P.op("act", lambda: A.activation(out=cst[:], in_=lam, func=AF.Exp, scale=-1.0), r=["par"], w=["cst"])
            P.op("act", lambda: A.activation(out=cst[:], in_=cst[:], func=AF.Ln, bias=oneb[:, 0:1], scale=1.0), r=["cst", "oneb"], w=["cst"])
            P.op("dve", lambda: V.tensor_scalar_mul(out=cst2[:], in0=cst[:], scalar1=-16.0), r=["cst"], w=["cst2"])
            P.op("dve", lambda: V.tensor_scalar_mul(out=cst[:], in0=cst[:], scalar1=-8.0), r=["cst", "cst2"], w=["cst"])
            P.flush()

        SH1, SC1, G1, SH2, SC2, G2 = 0, 8, 16, 24, 32, 40

        def norm_prompt(xkey, tag, xt, TT, l, sh0, sc0, sqb, nps, rt, hT, xn, hkey=None):
            hkey = hkey or (tag, "hT")
            P.op("act", lambda: A.activation(out=sqb[:, :, :TT], in_=xt[:, :, :TT], func=AF.Square), r=[xkey], w=[(tag, "sqb")])
            for c in range(8):
                P.op("pe", (lambda c=c: PE.matmul(nps[:, :TT], lhsT=ones_b[:], rhs=sqb[:, c, :TT], start=(c == 0), stop=(c == 7))),
                     r=[(tag, "sqb"), "ones_b"], w=[(tag, "nps")])
            P.op("act", lambda: A.activation(out=rt[:, :TT], in_=nps[:, :TT], func=AF.Sqrt, scale=1.0 / D, bias=epsb[:, 0:1]),
                 r=[(tag, "nps"), "epsb"], w=[(tag, "rt")])
            P.op("dve", lambda: V.reciprocal(out=rt[:, :TT], in_=rt[:, :TT]), r=[(tag, "rt")], w=[(tag, "rt")])
            P.op("dve", lambda: V.tensor_tensor(out=xn[:, :, :TT], in0=xt[:, :, :TT],
                                                in1=rt[:, :TT].unsqueeze(1).to_broadcast([128, 8, TT]), op=ALU.mult),
                 r=[xkey, (tag, "rt")], w=[(tag, "xn")])
            for c in range(8):
                P.op("act", (lambda c=c: A.activation(out=hT[:, c, :TT], in_=xn[:, c, :TT], func=AF.Identity,
                                                      scale=modP[:, l, sc0 + c:sc0 + c + 1], bias=modP[:, l, sh0 + c:sh0 + c + 1])),
                     r=[(tag, "xn"), "modP"], w=[hkey])

        def norm_sample(tag, l, sh0, sc0, sq, nps, rt, hT, xn):
            P.op("dve", lambda: V.tensor_tensor(out=sq[:], in0=xs_cur[:], in1=xs_cur[:], op=ALU.mult), r=["xs"], w=[(tag, "sq")])
            for c in range(8):
                P.op("pe", (lambda c=c: PE.matmul(nps[:, :NS], lhsT=ones_f[:], rhs=sq[:, c, :], start=(c == 0), stop=(c == 7))),
                     r=[(tag, "sq"), "ones_f"], w=[(tag, "nps")])
            P.op("act", lambda: A.activation(out=rt[:, :NS], in_=nps[:, :NS], func=AF.Sqrt, scale=1.0 / D, bias=epsb[:, 0:1]),
                 r=[(tag, "nps"), "epsb"], w=[(tag, "rt")])
            P.op("dve", lambda: V.reciprocal(out=rt[:, :NS], in_=rt[:, :NS]), r=[(tag, "rt")], w=[(tag, "rt")])
            P.op("dve", lambda: V.tensor_tensor(out=xn[:], in0=xs_cur[:], in1=rt[:, :NS].unsqueeze(1).to_broadcast([128, 8, NS]), op=ALU.mult),
                 r=["xs", (tag, "rt")], w=[(tag, "xn")])
            P.op("dve", lambda: V.tensor_tensor(out=xn[:], in0=xn[:], in1=modS[:, l, sc0:sc0 + 8, :], op=ALU.mult),
                 r=[(tag, "xn"), "modS"], w=[(tag, "xn")])
            P.op("dve", lambda: V.tensor_tensor(out=hT[:], in0=xn[:], in1=modS[:, l, sh0:sh0 + 8, :], op=ALU.add),
                 r=[(tag, "xn"), "modS"], w=[(tag, "hT")])

        def load_w(q_tensor, dst, src_ap_fn, nk, key):
            for kc in range(nk):
                P.dma("pool", (lambda kc=kc: G.dma_start(out=dst[:, kc, :], in_=src_ap_fn(kc))), w=[(key, kc)])

        with contextlib.ExitStack() as ph:
            w_in = SB(ph, "w_in", [128, 8, NIN_E], BF16)
            load_w(None, w_in, lambda kc: I["w_in_e"][:, kc, :], 8, "w_in")
            for s in range(NS):
                P.dma("actq", (lambda s=s: nc.scalar.dma_start(out=O["wk_s"][s, 0:LCACHE - 1, :], in_=I["ck"][s, 1:LCACHE, :])), w=[("wk", s)])
                P.dma("actq", (lambda s=s: nc.scalar.dma_start(out=O["wv_s"][s, 0:LCACHE - 1, :], in_=I["cv"][s, 1:LCACHE, :])), w=[("wv", s)])
            wsT = SB(ph, "wsT", [128, 4, 128])
            wsTb = SB(ph, "wsTb", [128, 4, 128], BF16)
            bsb = SB(ph, "bsb", [128, 512])
            lng_bc = SB(ph, "lng_bc", [128, 512]); lnb_bc = SB(ph, "lnb_bc", [128, 512])
            sg0 = SB(ph, "sg0", [128, 8])
            P.dma("sp", lambda: nc.sync.dma_start(out=wsT[:], in_=I["w_sguT"][:, :, :]), w=["wsT"])
            P.dma("sp", lambda: nc.sync.dma_start(out=bsb[:], in_=I["b_sgu"][0:1, :].partition_broadcast(128)), w=["bsb"])
            P.dma("sp", lambda: nc.sync.dma_start(out=sg0[:], in_=I["sg0"][0:1, :].partition_broadcast(128)), w=["sg0"])
            P.op("pool", lambda: G.affine_select(out=wsT[:], in_=wsT[:], pattern=[[0, 4], [1, 128]], compare_op=ALU.is_ge,
                                                 fill=0.0, base=0, channel_multiplier=-1), r=["wsT"], w=["wsT"])
            P.op("dve", lambda: V.tensor_copy(out=wsTb[:], in_=wsT[:]), r=["wsT"], w=["wsTb"])
            dg = SB(ph, "dg", [128, 8, 128])
            for i2, nm in enumerate(("ln_g", "ln_b")):
                for c in range(4):
                    P.op("dve", (lambda i2=i2, c=c, nm=nm: V.tensor_scalar_mul(out=dg[:, i2 * 4 + c, :], in0=ident_f[:], scalar1=pcol(nm, c))),
                         r=["ident_f", "par"], w=[("dg", i2 * 4 + c)])

            xt_ = [SB(ph, "xt%d" % i, [128, 8, 512]) for i in range(2)]
            sqb = SB(ph, "sqb", [128, 8, 512], BF16)
            xn = SB(ph, "xn", [128, 8, 512])
            rt = SB(ph, "rt", [128, 512])
            hT = SB(ph, "hT", [128, 8, 512], BF16)
            uT = SB(ph, "uT", [128, 4, 512])
            gv_ = [SB(ph, "gv%d" % i, [128, 512]) for i in range(2)]
            st6_ = [SB(ph, "st6_%d" % i, [128, 6]) for i in range(2)]; mv_ = [SB(ph, "mv%d" % i, [128, 2]) for i in range(2)]; rs_ = [SB(ph, "rs%d" % i, [128, 1]) for i in range(2)]
            va_ = [SB(ph, "va%d" % i, [128, 512]) for i in range(2)]; vab_ = [SB(ph, "vab%d" % i, [128, 512], BF16) for i in range(4)]
            mixs_ = [SB(ph, "mixs%d" % i, [128, 512]) for i in range(2)]
            aoT = [SB(ph, "aoT%d" % i, [128, 4, 512], BF16) for i in range(1)]
            qTb = [SB(ph, "qTb%d" % i, [128, 4, 512], BF16) for i in range(1)]
            kTf = [SB(ph, "kTf%d" % i, [128, 4, 512]) for i in range(1)]
            vTf = [SB(ph, "vTf%d" % i, [128, 4, 512]) for i in range(1)]
            kTb = [SB(ph, "kTb%d" % i, [128, 4, 512], BF16) for i in range(1)]
            vTb = [SB(ph, "vTb%d" % i, [128, 4, 512], BF16) for i in range(1)]
            nps = PS(ph, "nps1", [128, 512])
            zps = [PS(ph, "zps%d" % i, [128, 512]) for i in range(3)]
            vps = [PS(ph, "vps%d" % i, [128, 512]) for i in range(2)]
            mps2 = [PS(ph, "mixps%d" % i, [128, 512]) for i in range(2)]
            zcount = [0]
            for i2 in range(2):
                for c in range(4):
                    P.op("pe", (lambda i2=i2, c=c: PE.matmul(zps[i2][:, c * 128:(c + 1) * 128], lhsT=ones_f[:],
                                                              rhs=dg[:, i2 * 4 + c, :], start=True, stop=True)),
                         r=[("dg", i2 * 4 + c), "ones_f"], w=[("zps", i2)])
            P.op("dve", lambda: V.tensor_copy(out=lng_bc[:], in_=zps[0][:]), r=[("zps", 0)], w=["lng_bc"])
            P.op("dve", lambda: V.tensor_copy(out=lnb_bc[:], in_=zps[1][:]), r=[("zps", 1)], w=["lnb_bc"])

            def loadx(i):
                b = i % 2
                P.dma("sp", (lambda: nc.sync.dma_start(out=xt_[b][:], in_=I["xT"][:, :, i * 512:(i + 1) * 512])), w=[("xt1", b)])

            loadx(0)
            for i in range(NT5):
                b = i % 2
                if i + 1 < NT5:
                    loadx(i + 1)
                tag = "p1"
                norm_prompt(("xt1", b), tag, xt_[b], 512, 0, SH1, SC1, sqb, nps, rt, hT, xn)
                t0 = i * 512
                xb_ = b
                b = 0

                def zchunk(n):
                    z = zcount[0] % 3
                    zcount[0] += 1
                    for kc in range(8):
                        P.op("pe", (lambda kc=kc, n=n, z=z: PE.matmul(zps[z][:], lhsT=w_in[:, kc, n * 128:(n + 1) * 128], rhs=hT[:, kc, :],
                                                                       start=(kc == 0), stop=(kc == 7))),
                             r=[("w_in", kc), (tag, "hT")], w=[("zps", z)])
                    return z
                for n in range(4):
                    z = zchunk(n)
                    P.op("act", (lambda n=n, z=z: A.activation(out=uT[:, n, :], in_=zps[z][:], func=GELU)), r=[("zps", z)], w=[("uT", n)])
                for blk in range(4):
                    vb = blk % 2
                    gv, st6, mv, rs, va, vab = gv_[vb], st6_[vb], mv_[vb], rs_[vb], va_[vb], vab_[blk]
                    for kc in range(8):
                        P.op("pe", (lambda kc=kc, blk=blk, vb=vb: PE.matmul(vps[vb][:], lhsT=hT[:, kc, blk * 128:(blk + 1) * 128], rhs=w_in[:, kc, 512:1024],
                                                                             start=(kc == 0), stop=(kc == 7))),
                             r=[("w_in", kc), (tag, "hT")], w=[("vps", vb)])
                    P.op("act", (lambda gv=gv, vb=vb: A.activation(out=gv[:], in_=vps[vb][:], func=GELU)), r=[("vps", vb)], w=[("gv", vb)])
                    P.op("dve", (lambda gv=gv, st6=st6: V.bn_stats(out=st6[:], in_=gv[:])), r=[("gv", vb)], w=[("st6", vb)])
                    P.op("dve", (lambda mv=mv, st6=st6: V.bn_aggr(out=mv[:], in_=st6[:])), r=[("st6", vb)], w=[("mv", vb)])
                    P.op("act", (lambda mv=mv, rs=rs: A.activation(out=rs[:], in_=mv[:, 1:2], func=AF.Sqrt, scale=1.0, bias=epsb[:, 0:1])), r=[("mv", vb), "epsb"], w=[("rs", vb)])
                    P.op("dve", (lambda rs=rs: V.reciprocal(out=rs[:], in_=rs[:])), r=[("rs", vb)], w=[("rs", vb)])
                    P.op("dve", (lambda va=va, gv=gv, mv=mv, rs=rs: V.tensor_scalar(out=va[:], in0=gv[:], scalar1=mv[:, 0:1], scalar2=rs[:, 0:1], op0=ALU.subtract, op1=ALU.mult)),
                         r=[("gv", vb), ("mv", vb), ("rs", vb)], w=[("va", vb)])
                    P.op("pool", (lambda va=va: G.tensor_tensor(out=va[:], in0=va[:], in1=lng_bc[:], op=ALU.mult)), r=[("va", vb), "lng_bc"], w=[("va", vb)])
                    P.op("pool", (lambda va=va, vab=vab: G.tensor_tensor(out=vab[:], in0=va[:], in1=lnb_bc[:], op=ALU.add)), r=[("va", vb), "lnb_bc"], w=[("vab", blk)])

                def sgu_mix():
                    for blk in range(4):
                        vb = blk % 2
                        vab, mixs = vab_[blk], mixs_[vb]
                        for g in range(4):
                            P.op("pe", (lambda g=g, vab=vab, vb=vb: PE.matmul(mps2[vb][:, g * 128:(g + 1) * 128], lhsT=vab[:, g * 128:(g + 1) * 128], rhs=wsTb[:, g, :],
                                                                               start=True, stop=True)), r=[("vab", blk), "wsTb"], w=[("mixps", vb)])
                        P.op("dve", (lambda mixs=mixs, vb=vb: V.tensor_tensor(out=mixs[:], in0=mps2[vb][:], in1=bsb[:], op=ALU.add)), r=[("mixps", vb), "bsb"], w=[("mixs", vb)])
                        P.op("pool", (lambda blk=blk, b=b, mixs=mixs: G.tensor_tensor(out=aoT[b][:, :, blk * 128:(blk + 1) * 128],
                                                                                      in0=mixs[:].rearrange("p (g t) -> p g t", g=4),
                                                                                      in1=uT[:, :, blk * 128:(blk + 1) * 128], op=ALU.mult)),
                             r=[("mixs", vb)] + [("uT", n) for n in range(4)], w=[("aoT", b, blk)])
                    P.dma("pool", (lambda b=b, t0=t0: G.dma_start(out=AOS[:, :, t0:t0 + 512], in_=aoT[b][:])),
                          r=[("aoT", b, k) for k in range(4)], w=[("AOS", i)])

                for n in range(4):
                    z = zchunk(8 + n)
                    P.op("act", (lambda n=n, z=z, b=b: A.activation(out=qTb[b][:, n, :], in_=zps[z][:], func=AF.Copy, scale=0.125)),
                         r=[("zps", z)], w=[("qTb", b, n)])
                P.dma("pool", (lambda b=b, t0=t0: G.dma_start(out=QS[:, :, t0:t0 + 512], in_=qTb[b][:])),
                      r=[("qTb", b, k) for k in range(4)], w=[("QS", i)])
                for (off, tf, tb, SCR, OUT, nm) in ((12, kTf, kTb, KS, O["kT_o"], "k"), (16, vTf, vTb, VS, O["vT_o"], "v")):
                    for n in range(4):
                        z = zchunk(off + n)
                        P.op("act", (lambda n=n, z=z, b=b, tf=tf: A.activation(out=tf[b][:, n, :], in_=zps[z][:], func=AF.Copy)),
                             r=[("zps", z)], w=[(nm + "f", b, n)])
                        P.op("dve", (lambda n=n, b=b, tf=tf, tb=tb: V.tensor_copy(out=tb[b][:, n, :], in_=tf[b][:, n, :])),
                             r=[(nm + "f", b, n)], w=[(nm + "b", b, n)])
                    P.dma("pool", (lambda b=b, t0=t0, tb=tb, SCR=SCR: G.dma_start(out=SCR[:, :, t0:t0 + 512], in_=tb[b][:])),
                          r=[(nm + "b", b, k) for k in range(4)], w=[(nm + "S", i)])
                    if t0 >= T - TK:
                        o0 = t0 - (T - TK)
                        P.dma("pool", (lambda b=b, o0=o0, tf=tf, OUT=OUT: G.dma_start(out=OUT[:, :, o0:o0 + 512], in_=tf[b][:])),
                              r=[(nm + "f", b, k) for k in range(4)], w=[(nm + "O", i)])
                sgu_mix()

            sq_s = SB(ph, "sq_s", [128, 8, NS]); xn_s = SB(ph, "xn_s", [128, 8, NS]); hTs = SB(ph, "hTs", [128, 8, NS], BF16)
            guv = SB(ph, "guv", [128, 8, NS]); gsq = SB(ph, "gsq", [128, 4, NS])
            mean_s = SB(ph, "mean_s", [128, NS]); var_s = SB(ph, "var_s", [128, NS]); msq_s = SB(ph, "msq_s", [128, NS])
            vas = SB(ph, "vas", [128, 4, NS]); aos = SB(ph, "aos", [128, 4, NS]); aosb = SB(ph, "aosb", [128, 4, NS], BF16)
            qkv = SB(ph, "qkv", [NS, 1536])
            norm_sample("s1", 0, SH1, SC1, sq_s, nps, rt, hTs, xn_s)
            for n in range(8):
                for kc in range(8):
                    P.op("pe", (lambda n=n, kc=kc: PE.matmul(zps[0][:, n * NS:(n + 1) * NS], lhsT=w_in[:, kc, n * 128:(n + 1) * 128], rhs=hTs[:, kc, :],
                                                              start=(kc == 0), stop=(kc == 7))), r=[("w_in", kc), ("s1", "hT")], w=[("zps", 0)])
            P.op("act", lambda: A.activation(out=guv[:], in_=zps[0][:, 0:8 * NS].rearrange("p (c s) -> p c s", c=8), func=GELU),
                 r=[("zps", 0)], w=["guv"])
            P.op("dve", lambda: V.tensor_tensor(out=gsq[:], in0=guv[:, 4:8, :], in1=guv[:, 4:8, :], op=ALU.mult), r=["guv"], w=["gsq"])
            for c in range(4):
                P.op("pe", (lambda c=c: PE.matmul(zps[1][:, 0:NS], lhsT=ones_f[:], rhs=guv[:, 4 + c, :], start=(c == 0), stop=(c == 3))),
                     r=["guv", "ones_f"], w=[("zps", 1)])
            for c in range(4):
                P.op("pe", (lambda c=c: PE.matmul(zps[1][:, 8:8 + NS], lhsT=ones_f[:], rhs=gsq[:, c, :], start=(c == 0), stop=(c == 3))),
                     r=["gsq", "ones_f"], w=[("zps", 1)])
            P.op("dve", lambda: V.tensor_scalar_mul(out=mean_s[:], in0=zps[1][:, 0:NS], scalar1=1.0 / WA), r=[("zps", 1)], w=["mean_s"])
            P.op("dve", lambda: V.tensor_scalar_mul(out=var_s[:], in0=zps[1][:, 8:8 + NS], scalar1=1.0 / WA), r=[("zps", 1)], w=["var_s"])
            P.op("dve", lambda: V.tensor_tensor(out=msq_s[:], in0=mean_s[:], in1=mean_s[:], op=ALU.mult), r=["mean_s"], w=["msq_s"])
            P.op("dve", lambda: V.tensor_tensor(out=var_s[:], in0=var_s[:], in1=msq_s[:], op=ALU.subtract), r=["var_s", "msq_s"], w=["var_s"])
            P.op("act", lambda: A.activation(out=var_s[:], in_=var_s[:], func=AF.Sqrt, scale=1.0, bias=epsb[:, 0:1]), r=["var_s", "epsb"], w=["var_s"])
            P.op("dve", lambda: V.reciprocal(out=var_s[:], in_=var_s[:]), r=["var_s"], w=["var_s"])
            P.op("dve", lambda: V.tensor_tensor(out=vas[:], in0=guv[:, 4:8, :], in1=mean_s[:].unsqueeze(1).to_broadcast([128, 4, NS]), op=ALU.subtract),
                 r=["guv", "mean_s"], w=["vas"])
            P.op("dve", lambda: V.tensor_tensor(out=vas[:], in0=vas[:], in1=var_s[:].unsqueeze(1).to_broadcast([128, 4, NS]), op=ALU.mult),
                 r=["vas", "var_s"], w=["vas"])
            P.op("dve", lambda: V.tensor_tensor(out=vas[:], in0=vas[:], in1=pcol("ln_g", 0, 4).unsqueeze(2).to_broadcast([128, 4, NS]), op=ALU.mult),
                 r=["vas", "par"], w=["vas"])
            P.op("dve", lambda: V.tensor_tensor(out=vas[:], in0=vas[:], in1=pcol("ln_b", 0, 4).unsqueeze(2).to_broadcast([128, 4, NS]), op=ALU.add),
                 r=["vas", "par"], w=["vas"])
            P.dma("sp", lambda: nc.sync.dma_start(out=O["cv_o"][:, :, :], in_=vas[:]), r=["vas"], w=["cv_o"])
            P.op("dve", lambda: V.tensor_tensor(out=aos[:], in0=vas[:], in1=sg0[:, 0:4].unsqueeze(2).to_broadcast([128, 4, NS]), op=ALU.mult),
                 r=["vas", "sg0"], w=["aos"])
            P.op("dve", lambda: V.tensor_tensor(out=aos[:], in0=aos[:], in1=sg0[:, 4:8].unsqueeze(2).to_broadcast([128, 4, NS]), op=ALU.add),
                 r=["aos", "sg0"], w=["aos"])
            P.op("dve", lambda: V.tensor_tensor(out=aosb[:], in0=aos[:], in1=guv[:, 0:4, :], op=ALU.mult), r=["aos", "guv"], w=["aosb"])
            AOSS = dscr("aoT_ss", [128, 4, NS], BF16)
            P.dma("sp", lambda: nc.sync.dma_start(out=AOSS[:, :, :], in_=aosb[:]), r=["aosb"], w=["AOSS"])
            for blk3 in range(3):
                for kc in range(8):
                    P.op("pe", (lambda blk3=blk3, kc=kc: PE.matmul(zps[2][0:NS, :], lhsT=hTs[:, kc, :], rhs=w_in[:, kc, 1024 + blk3 * 512: 1024 + (blk3 + 1) * 512],
                                                                    start=(kc == 0), stop=(kc == 7))), r=[("w_in", kc), ("s1", "hT")], w=[("zps", 2)])
                P.op("act", (lambda blk3=blk3: A.activation(out=qkv[:, blk3 * 512:(blk3 + 1) * 512], in_=zps[2][0:NS, :], func=AF.Copy,
                                                            scale=(0.125 if blk3 == 0 else 1.0))), r=[("zps", 2)], w=["qkv"])
            P.dma("sp", lambda: nc.sync.dma_start(out=QKVS[:, :], in_=qkv[:]), r=["qkv"], w=["QKVS"])
            P.flush()

        with contextlib.ExitStack() as ph:
            qT = SB(ph, "qT", [128, 4, T], BF16); kT = SB(ph, "kT", [128, 4, T], BF16); vT = SB(ph, "vT", [128, 4, T], BF16)
            NBT = T // 128
            Vp = SB(ph, "Vp", [128, NBT, 8 * 65], BF16)
            mask2 = SB(ph, "mask2", [128, 2, 128], BF16)
            E_ = [SB(ph, "E%d" % i, [128, 4, 256], BF16) for i in range(2)]
            PTs = [SB(ph, "PTs%d" % i, [128, 4, 2, 128], BF16) for i in range(2)]
            Osb = [SB(ph, "Osb%d" % i, [128, 8, 66]) for i in range(4)]
            negm = [SB(ph, "negm%d" % i, [128, 4]) for i in range(2)]
            Sps = [PS(ph, "Sps%d" % i, [128, 1024]) for i in range(2)]
            PTp = [PS(ph, "PTp%d" % i, [128, 1024], BF16) for i in range(2)]
            Ops = [PS(ph, "Ops%d" % i, [128, 512]) for i in range(2)]
            P.dma("sp", lambda: nc.sync.dma_start(out=qT[:], in_=QS[:, :, :]), w=["qT"])
            P.dma("sp", lambda: nc.sync.dma_start(out=kT[:], in_=KS[:, :, :]), w=["kT"])
            P.dma("sp", lambda: nc.sync.dma_start(out=vT[:], in_=VS[:, :, :]), w=["vT"])
            P.op("pool", lambda: G.memset(Vp[:], 1.0), w=[("Vp", k) for k in range(NBT)])
            P.op("pool", lambda: G.memset(mask2[:], 1.0), w=["mask2"])
            P.op("pool", lambda: G.affine_select(out=mask2[:, 0, :], in_=mask2[:, 0, :], pattern=[[-1, 128]], compare_op=ALU.is_ge, fill=0.0,
                                                 base=0, channel_multiplier=1), r=["mask2"], w=["mask2"])
            P.op("pool", lambda: G.affine_select(out=mask2[:, 1, :], in_=mask2[:, 1, :], pattern=[[1, 128]], compare_op=ALU.is_ge, fill=0.0,
                                                 base=0, channel_multiplier=-1), r=["mask2"], w=["mask2"])
            u = 0
            ucnt = [0]
            for pi, (wdw, d) in enumerate(PATTERNS):
                M = T // d
                nb = M // 128
                for r_ in range(d):
                    for b_ in range(nb):
                        bi = r_ * nb + b_
                        tok = slice(r_ + d * 128 * b_, r_ + d * 128 * b_ + d * 127 + 1, d)
                        pb = u % 2
                        for c in range(4):
                            P.op("pe", (lambda c=c, tok=tok, pb=pb: PE.transpose(PTp[pb][:, c * 128:(c + 1) * 128], vT[:, c, tok], ident_b[:])),
                                 r=["vT", "ident_b"], w=[("PTp", pb)])
                        P.op("act", (lambda bi=bi, pb=pb: A.activation(out=Vp[:, bi, :].rearrange("p (h e) -> p h e", h=8)[:, :, 0:64],
                                                                      in_=PTp[pb][:, 0:512].rearrange("p (h e) -> p h e", h=8), func=AF.Copy)),
                             r=[("PTp", pb)], w=[("Vp", bi)])
                        u += 1
                units = []
                for r_ in range(d):
                    for b_ in range(nb):
                        bi = r_ * nb + b_
                        nkb = 2 if b_ > 0 else 1
                        nk = 128 * nkb
                        qtok = slice(r_ + d * 128 * b_, r_ + d * 128 * b_ + d * 127 + 1, d)
                        k0 = r_ + d * 128 * (b_ - (nkb - 1))
                        ktok = slice(k0, k0 + d * (nk - 1) + 1, d)
                        for hh in range(2):
                            units.append(dict(bi=bi, nkb=nkb, nk=nk, qtok=qtok, ktok=ktok, hh=hh, ob=bi % 4))
                for ui, U in enumerate(units):
                    U["pb"] = (ucnt[0] + ui) % 2
                ucnt[0] += len(units)

                def colS(h4):
                    return (h4 % 2) * 512 + (h4 // 2) * 256

                def stageA(U):
                    pb, nk, hh, ob, qtok, ktok = U["pb"], U["nk"], U["hh"], U["ob"], U["qtok"], U["ktok"]
                    for h4 in range(4):
                        h = hh * 4 + h4
                        c, po = h // 2, (h % 2) * 64
                        P.op("pe", (lambda h4=h4, c=c, po=po: PE.matmul(Sps[pb][:, colS(h4): colS(h4) + nk], lhsT=qT[po:po + 64, c, qtok], rhs=kT[po:po + 64, c, ktok],
                                                                         start=True, stop=True)), r=["qT", "kT"], w=[("Sps", pb)])
                    for bk in range(2):
                        Sv = Sps[pb][:, bk * 512:(bk + 1) * 512].rearrange("p (h k) -> p h k", h=2)[:, :, 0:nk]
                        P.op("dve", (lambda Sv=Sv, bk=bk: V.tensor_reduce(out=Osb[ob][:, hh * 4 + bk:hh * 4 + bk + 3:2, 65], in_=Sv, axis=AX.X, op=ALU.max)),
                             r=[("Sps", pb)], w=[("Osb", ob, hh, "m")])
                    P.op("dve", (lambda: V.tensor_scalar_mul(out=negm[pb][:], in0=Osb[ob][:, hh * 4:(hh + 1) * 4, 65], scalar1=-1.0)),
                         r=[("Osb", ob, hh, "m")], w=[("negm", pb)])
                    for h4 in range(4):
                        P.op("act", (lambda h4=h4: A.activation(out=E_[pb][:, h4, 0:nk], in_=Sps[pb][:, colS(h4):colS(h4) + nk], func=AF.Exp,
                                                                 bias=negm[pb][:, h4:h4 + 1], scale=1.0)),
                             r=[("Sps", pb), ("negm", pb)], w=[("E", pb)])

                def stageB(U):
                    pb, nkb = U["pb"], U["nkb"]
                    for h4 in range(4):
                        for kb in range(nkb):
                            P.op("pe", (lambda h4=h4, kb=kb: PE.transpose(PTp[pb][:, (h4 * 2 + kb) * 128:(h4 * 2 + kb + 1) * 128],
                                                                          E_[pb][:, h4, kb * 128:(kb + 1) * 128], ident_b[:])),
                                 r=[("E", pb), "ident_b"], w=[("PTp", pb)])
                    PTv = PTp[pb][:].rearrange("p (h k q) -> p h k q", h=4, k=2)
                    if nkb == 2:
                        P.op("dve", (lambda: V.tensor_tensor(out=PTs[pb][:], in0=PTv, in1=mask2[:].unsqueeze(1).to_broadcast([128, 4, 2, 128]), op=ALU.mult)),
                             r=[("PTp", pb), "mask2"], w=[("PTs", pb)])
                    else:
                        P.op("dve", (lambda: V.tensor_tensor(out=PTs[pb][:, :, 0, :], in0=PTv[:, :, 0, :],
                                                             in1=mask2[:, 1, :].unsqueeze(1).to_broadcast([128, 4, 128]), op=ALU.mult)),
                             r=[("PTp", pb), "mask2"], w=[("PTs", pb)])

                def stageC(U, pi=pi):
                    pb, nkb, hh, ob, bi, qtok = U["pb"], U["nkb"], U["hh"], U["ob"], U["bi"], U["qtok"]
                    for h4 in range(4):
                        h = hh * 4 + h4
                        for kb in range(nkb):
                            kbi = bi - (nkb - 1) + kb
                            P.op("pe", (lambda h4=h4, h=h, kb=kb, kbi=kbi: PE.matmul(Ops[pb][:, h4 * 128:h4 * 128 + 65], lhsT=PTs[pb][:, h4, kb, :],
                                                                                     rhs=Vp[:, kbi, h * 65:(h + 1) * 65], start=(kb == 0), stop=(kb == nkb - 1))),
                                 r=[("PTs", pb), ("Vp", kbi)], w=[("Ops", pb)])
                    P.op("act", (lambda: A.activation(out=Osb[ob][:, hh * 4:(hh + 1) * 4, 0:65],
                                                      in_=Ops[pb][:].rearrange("p (h e) -> p h e", h=4)[:, :, 0:65], func=AF.Copy)),
                         r=[("Ops", pb)], w=[("Osb", ob, hh, "o")])
                    if hh == 1:
                        orow = OS[pi, qtok, :]
                        P.dma("pool", (lambda: G.dma_start(out=orow, in_=Osb[ob][:].rearrange("p h e -> p (h e)"))),
                              r=[("Osb", ob, 0, "o"), ("Osb", ob, 1, "o"), ("Osb", ob, 0, "m"), ("Osb", ob, 1, "m")], w=[("OS", pi, bi)])

                nu = len(units)
                for st in range(nu + 2):
                    if st < nu:
                        stageA(units[st])
                    if 0 <= st - 1 < nu:
                        stageB(units[st - 1])
                    if 0 <= st - 2 < nu:
                        stageC(units[st - 2])
            P.flush()

        with contextlib.ExitStack() as ph:
            Kt = [SB(ph, "Kt%d" % i, [128, 512]) for i in range(2)]
            Vt = [SB(ph, "Vt%d" % i, [128, 512]) for i in range(16)]
            qbc = [SB(ph, "qbc%d" % i, [128, 512]) for i in range(NS)]
            prod = SB(ph, "prod", [128, 512])
            ST = SB(ph, "ST", [128, 4, 128])
            Es = SB(ph, "Es", [128, 4, 128]); PTs2 = SB(ph, "PTs2", [128, 4, 128])
            mx = SB(ph, "mx_s", [128, 1]); den = SB(ph, "den_s", [128, 1])
            rowm = SB(ph, "rowm", [128, NS]); dmk = SB(ph, "dmk", [128, 512])
            acc = SB(ph, "acc_s", [128, 512]); bo = SB(ph, "bo_s", [128, 64])
            boT = SB(ph, "boT_s", [64, 128], BF16)
            w_oe = SB(ph, "w_oe_s", [128, 4, D], BF16); w_oh = SB(ph, "w_oh_s", [64, 8, D], BF16)
            aosb2 = SB(ph, "aosb2", [128, 4, NS], BF16)
            mo_s = SB(ph, "mo_s", [128, 8, NS])
            Sp = PS(ph, "Sp_s", [128, 512]); PTp2 = PS(ph, "PTp_s", [128, 512])
            Opp = [PS(ph, "Opp%d" % i, [128, 512]) for i in range(NS)]
            bTp = PS(ph, "bTp", [64, 128]); mop = PS(ph, "mop_s", [128, 8 * NS])
            P.dma("sp", lambda: nc.sync.dma_start(out=rowm[:], in_=I["rowmask"][:, :]), w=["rowm"])
            P.dma("sp", lambda: nc.sync.dma_start(out=dmk[:], in_=I["dmask"][:, :]), w=["dmk"])
            for kc in range(4):
                P.dma("pool", (lambda kc=kc: G.dma_start(out=w_oe[:, kc, :], in_=I["w_out_e"][:, kc, :])), w=[("w_oe", kc)])
            for h in range(8):
                P.dma("pool", (lambda h=h: G.dma_start(out=w_oh[:, h, :], in_=I["w_out_eh"][:, h, :])), w=[("w_oh", h)])
            P.dma("sp", lambda: nc.sync.dma_start(out=aosb2[:], in_=AOSS[:, :, :]), w=["aosb2"])
            P.op("pool", lambda: G.memset(ST[:], 0.0), w=["ST"])
            for s in range(NS):
                P.dma("sp", (lambda s=s: nc.sync.dma_start(out=O["wk_s"][s, LCACHE - 1:LCACHE, :], in_=QKVS[s:s + 1, 512:1024])), w=[("wk2", s)])
                P.dma("sp", (lambda s=s: nc.sync.dma_start(out=O["wv_s"][s, LCACHE - 1:LCACHE, :], in_=QKVS[s:s + 1, 1024:1536])), w=[("wv2", s)])
            for s in range(NS):
                P.dma("sp", (lambda s=s: nc.sync.dma_start(out=qbc[s][:], in_=QKVS[s:s + 1, 0:512].partition_broadcast(128))), w=[("qbc", s)])
            it = 0
            for s in range(NS):
                for p in range(4):
                    kb_ = it % 2
                    vi = s * 4 + p
                    if p < 3:
                        dd = PATTERNS[p][1]
                        r0 = LCACHE - 128 * dd
                        ksrc = I["ck"][s, r0:r0 + 127 * dd + 1:dd, :]
                        vsrc = I["cv"][s, r0:r0 + 127 * dd + 1:dd, :]
                    else:
                        ksrc = QKVS[s:s + 1, 512:1024].partition_broadcast(128)
                        vsrc = QKVS[s:s + 1, 1024:1536].partition_broadcast(128)
                    P.dma("sp", (lambda kb_=kb_, ksrc=ksrc: nc.sync.dma_start(out=Kt[kb_][:], in_=ksrc)), w=[("Kt", kb_)])
                    P.dma("sp", (lambda vi=vi, vsrc=vsrc: nc.sync.dma_start(out=Vt[vi][:], in_=vsrc)), w=[("Vt", vi)])
                    P.op("dve", (lambda kb_=kb_, s=s: V.tensor_tensor(out=prod[:], in0=Kt[kb_][:], in1=qbc[s][:], op=ALU.mult)),
                         r=[("Kt", kb_), ("qbc", s)], w=["prod"])
                    P.op("dve", (lambda s=s, p=p: V.tensor_reduce(out=ST[:, p, 32 * s:32 * s + 8], in_=prod[:].rearrange("p (h e) -> p h e", h=8),
                                                                  axis=AX.X, op=ALU.add)), r=["prod"], w=["ST"])
                    it += 1
            for p in range(4):
                P.op("pe", (lambda p=p: PE.transpose(Sp[:, p * 128:(p + 1) * 128], ST[:, p, :], ident_f[:])), r=["ST", "ident_f"], w=["Sp"])
            P.op("dve", lambda: V.tensor_reduce(out=mx[:], in_=Sp[:], axis=AX.X, op=ALU.max, negate=True), r=["Sp"], w=["mx"])
            P.op("act", lambda: A.activation(out=Es[:].rearrange("p a b -> p (a b)"), in_=Sp[:], func=AF.Exp, bias=mx[:, 0:1], scale=1.0), r=["Sp", "mx"], w=["Es"])
            P.op("dve", lambda: V.tensor_scalar_mul(out=Es[:, 3, :], in0=Es[:, 3, :], scalar1=3.0 / 128.0), r=["Es"], w=["Es"])
            P.op("dve", lambda: V.tensor_reduce(out=den[:], in_=Es[:].rearrange("p a b -> p (a b)"), axis=AX.X, op=ALU.add), r=["Es"], w=["den"])
            P.op("dve", lambda: V.reciprocal(out=den[:], in_=den[:]), r=["den"], w=["den"])
            for p in range(4):
                P.op("pe", (lambda p=p: PE.transpose(PTp2[:, p * 128:(p + 1) * 128], Es[:, p, :], ident_f[:])), r=["Es", "ident_f"], w=["PTp2"])
            P.op("dve", lambda: V.tensor_copy(out=PTs2[:].rearrange("p a b -> p (a b)"), in_=PTp2[:]), r=["PTp2"], w=["PTs2"])
            for s in range(NS):
                for p in range(4):
                    P.op("pe", (lambda s=s, p=p: PE.matmul(Opp[s][:], lhsT=PTs2[:, p, :], rhs=Vt[s * 4 + p][:], start=(p == 0), stop=(p == 3))),
                         r=["PTs2", ("Vt", s * 4 + p)], w=[("Opp", s)])
            P.op("dve", lambda: V.tensor_scalar_mul(out=acc[:], in0=Opp[0][:], scalar1=rowm[:, 0:1]), r=[("Opp", 0), "rowm"], w=["acc"])
            for s in range(1, NS):
                P.op("dve", (lambda s=s: V.scalar_tensor_tensor(out=acc[:], in0=Opp[s][:], scalar=rowm[:, s:s + 1], in1=acc[:], op0=ALU.mult, op1=ALU.add)),
                     r=[("Opp", s), "rowm", "acc"], w=["acc"])
            P.op("dve", lambda: V.tensor_tensor(out=acc[:], in0=acc[:], in1=dmk[:], op=ALU.mult), r=["acc", "dmk"], w=["acc"])
            P.op("dve", lambda: V.tensor_reduce(out=bo[:], in_=acc[:].rearrange("p (h e) -> p e h", h=8), axis=AX.X, op=ALU.add), r=["acc"], w=["bo"])
            P.op("dve", lambda: V.tensor_scalar_mul(out=bo[:], in0=bo[:], scalar1=den[:, 0:1]), r=["bo", "den"], w=["bo"])
            P.op("pe", lambda: PE.transpose(bTp[:], bo[:], ident_f[:]), r=["bo", "ident_f"], w=["bTp"])
            P.op("dve", lambda: V.tensor_copy(out=boT[:], in_=bTp[:]), r=["bTp"], w=["boT"])
            for n in range(8):
                for kc in range(4):
                    P.op("pe", (lambda n=n, kc=kc: PE.matmul(mop[:, n * NS:(n + 1) * NS], lhsT=w_oe[:, kc, n * 128:(n + 1) * 128], rhs=aosb2[:, kc, :],
                                                              start=(kc == 0), stop=False)), r=[("w_oe", kc), "aosb2"], w=["mop"])
                for h in range(8):
                    P.op("pe", (lambda n=n, h=h: PE.matmul(mop[:, n * NS:(n + 1) * NS], lhsT=w_oh[:, h, n * 128:(n + 1) * 128], rhs=boT[:, h:128:32],
                                                            start=False, stop=(h == 7))), r=[("w_oh", h), "boT"], w=["mop"])
            P.op("dve", lambda: V.tensor_tensor(out=mo_s[:], in0=mop[:].rearrange("p (c s) -> p c s", c=8), in1=modS[:, 0, G1:G1 + 8, :], op=ALU.mult),
                 r=["mop", "modS"], w=["mo_s"])
            P.op("dve", lambda: V.tensor_tensor(out=xs_cur[:], in0=xs_cur[:], in1=mo_s[:], op=ALU.add), r=["mo_s", "xs"], w=["xs"])
            P.flush()

        with contextlib.ExitStack() as ph:
            w_o = SB(ph, "w_o", [128, 8, D], BF16)
            load_w(None, w_o, lambda kc: I["w_out_e"][:, kc, :], 8, "w_o")
            xt_ = [SB(ph, "xt3_%d" % i, [128, 8, 512]) for i in range(2)]
            ao_ = [SB(ph, "ao3_%d" % i, [128, 4, 512], BF16) for i in range(2)]
            Om = [SB(ph, "Om%d" % i, [128, 4, 3, 8 * 66]) for i in range(2)]
            Mx = SB(ph, "Mx", [128, 8]); dm = SB(ph, "dm", [128, 3, 8]); ee = SB(ph, "ee", [128, 3, 8])
            wt = SB(ph, "wt", [128, 3, 8, 65]); ac = SB(ph, "ac", [128, 8, 65]); rd = SB(ph, "rd", [128, 8])
            bob = SB(ph, "bob", [128, 8, 64], BF16)
            boT3 = SB(ph, "boT3", [128, 4, 512], BF16)
            x1t = [SB(ph, "x1t%d" % i, [128, 8, 512]) for i in range(2)]
            tp = [PS(ph, "tp3_%d" % i, [128, 1024], BF16) for i in range(2)]
            ops_ = [PS(ph, "op3_%d" % i, [128, 512]) for i in range(3)]

            def load3(i):
                b = i % 2
                t0 = i * 512
                P.dma("sp", (lambda: nc.sync.dma_start(out=xt_[b][:], in_=I["xT"][:, :, t0:t0 + 512])), w=[("xt3", b)])
                P.dma("sp", (lambda: nc.sync.dma_start(out=ao_[b][:], in_=AOS[:, :, t0:t0 + 512])), w=[("ao3", b)])
                for pi in range(3):
                    P.dma("sp", (lambda pi=pi: nc.sync.dma_start(out=Om[b][:, :, pi, :], in_=OS[pi, t0:t0 + 512, :].rearrange("(k i) f -> i k f", i=128))),
                          w=[("Om", b, pi)])
            load3(0)
            oc = 0
            for i in range(NT5):
                b = i % 2
                t0 = i * 512
                if i + 1 < NT5:
                    load3(i + 1)
                for blk in range(4):
                    Ov = Om[b][:, blk, :, :].rearrange("p a (h e) -> p a h e", h=8)
                    omk = [("Om", b, pi) for pi in range(3)]
                    P.op("dve", (lambda Ov=Ov: V.tensor_tensor(out=Mx[:], in0=Ov[:, 0, :, 65], in1=Ov[:, 1, :, 65], op=ALU.max)), r=omk, w=["Mx"])
                    P.op("dve", (lambda Ov=Ov: V.tensor_tensor(out=Mx[:], in0=Mx[:], in1=Ov[:, 2, :, 65], op=ALU.max)), r=omk + ["Mx"], w=["Mx"])
                    P.op("dve", (lambda Ov=Ov: V.tensor_tensor(out=dm[:], in0=Ov[:, :, :, 65], in1=Mx[:].unsqueeze(1).to_broadcast([128, 3, 8]), op=ALU.subtract)),
                         r=omk + ["Mx"], w=["dm"])
                    P.op("act", lambda: A.activation(out=ee[:], in_=dm[:], func=AF.Exp), r=["dm"], w=["ee"])
                    P.op("pool", (lambda Ov=Ov: G.tensor_tensor(out=wt[:], in0=Ov[:, :, :, 0:65], in1=ee[:].unsqueeze(3).to_broadcast([128, 3, 8, 65]), op=ALU.mult)),
                         r=omk + ["ee"], w=["wt"])
                    P.op("dve", lambda: V.tensor_tensor(out=ac[:], in0=wt[:, 0, :, :], in1=wt[:, 1, :, :], op=ALU.add), r=["wt"], w=["ac"])
                    P.op("dve", lambda: V.tensor_tensor(out=ac[:], in0=ac[:], in1=wt[:, 2, :, :], op=ALU.add), r=["wt", "ac"], w=["ac"])
                    P.op("dve", lambda: V.reciprocal(out=rd[:], in_=ac[:, :, 64]), r=["ac"], w=["rd"])
                    P.op("dve", lambda: V.tensor_tensor(out=bob[:], in0=ac[:, :, 0:64], in1=rd[:].unsqueeze(2).to_broadcast([128, 8, 64]), op=ALU.mult),
                         r=["ac", "rd"], w=["bob"])
                    tb_ = (i * 4 + blk) % 2
                    for c in range(4):
                        P.op("pe", (lambda c=c, tb_=tb_: PE.transpose(tp[tb_][:, c * 128:(c + 1) * 128], bob[:].rearrange("p h e -> p (h e)")[:, c * 128:(c + 1) * 128], ident_b[:])),
                             r=["bob", "ident_b"], w=[("tp3", tb_)])
                    P.op("act", (lambda blk=blk, tb_=tb_: A.activation(out=boT3[:, :, blk * 128:(blk + 1) * 128],
                                                                      in_=tp[tb_][:, 0:512].rearrange("p (c t) -> p c t", c=4), func=AF.Copy)),
                         r=[("tp3", tb_)], w=[("boT3", blk)])
                for n in range(8):
                    z = oc % 3
                    oc += 1
                    for kc in range(8):
                        rhs = ao_[b][:, kc, :] if kc < 4 else boT3[:, kc - 4, :]
                        rk = [("ao3", b)] if kc < 4 else [("boT3", k) for k in range(4)]
                        P.op("pe", (lambda n=n, kc=kc, z=z, rhs=rhs: PE.matmul(ops_[z][:], lhsT=w_o[:, kc, n * 128:(n + 1) * 128], rhs=rhs, start=(kc == 0), stop=(kc == 7))),
                             r=[("w_o", kc)] + rk, w=[("op3", z)])
                    P.op("dve", (lambda n=n, z=z, b=b: V.scalar_tensor_tensor(out=x1t[b][:, n, :], in0=ops_[z][:], scalar=modP[:, 0, G1 + n:G1 + n + 1],
                                                                               in1=xt_[b][:, n, :], op0=ALU.mult, op1=ALU.add)),
                         r=[("op3", z), ("xt3", b), "modP"], w=[("x1t", b, n)])
                P.dma("pool", (lambda b=b, t0=t0: G.dma_start(out=X1[:, :, t0:t0 + 512], in_=x1t[b][:])), r=[("x1t", b, n) for n in range(8)], w=[("X1", i)])
            P.flush()

        def ffn_phase(l, XIN, XOUT, final):
            with contextlib.ExitStack() as ph:
                TT = 256
                w_up = SB(ph, "w_up%d" % l, [128, 8, F2], BF16)
                w_dn = SB(ph, "w_dn%d" % l, [128, NJ, D], BF16)
                load_w(None, w_up, lambda kc: I["w_up"][l, :, kc, :], 8, "w_up")
                for j in range(NJ):
                    P.dma("pool", (lambda j=j: G.dma_start(out=w_dn[:, j, :], in_=I["w_dn"][l, :, j, :])), w=[("w_dn", j)])
                xt_ = [SB(ph, "xtf%d_%d" % (l, i), [128, 8, TT]) for i in range(2)]
                sqb = SB(ph, "sqbf%d" % l, [128, 8, TT], BF16); xn = SB(ph, "xnf%d" % l, [128, 8, TT]); rt = SB(ph, "rtf%d" % l, [128, TT])
                hT_ = [SB(ph, "hTf%d_%d" % (l, i), [128, 8, TT], BF16) for i in range(2)]
                up = [SB(ph, "up%d_%d" % (l, i), [128, 2, 2 + TT]) for i in range(2)]
                uc = [SB(ph, "uc%d_%d" % (l, i), [128, 2, TT]) for i in range(2)]
                sa = [SB(ph, "sa%d_%d" % (l, i), [128, TT]) for i in range(2)]
                actj = [SB(ph, "actj%d_%d" % (l, i), [128, TT], BF16) for i in range(2)]
                hist = SB(ph, "hist%d" % l, [128, 44, 2])
                _xo1 = SB(ph, "xo%d_0" % l, [128, 8, TT])
                xo = [_xo1, _xo1]
                nps = PS(ph, "npsf%d" % l, [128, 512])
                ups = [PS(ph, "ups%d_%d" % (l, i), [128, 512]) for i in range(3)]
                accp = PS(ph, "accp%d" % l, [128, 8 * TT])
                yo = None
                if final:
                    rty = SB(ph, "rty", [128, TT])
                fcw = lambda k, j: par[:, PAR_OFF["f_cw"] + (l * 3 + k) * 44 + j: PAR_OFF["f_cw"] + (l * 3 + k) * 44 + j + 1]
                fcb = lambda j: par[:, PAR_OFF["f_cb"] + l * 44 + j: PAR_OFF["f_cb"] + l * 44 + j + 1]
                P.op("pool", lambda: G.memset(hist[:], 0.0), w=[("hist", j) for j in range(44)])

                def loadf(i):
                    b = i % 2
                    P.dma("sp", (lambda: nc.sync.dma_start(out=xt_[b][:], in_=XIN[:, :, i * TT:(i + 1) * TT])), w=[("xtf", b)])
                loadf(0)
                jc_ = [0]
                tag = "ff"
                norm_prompt(("xtf", 0), tag, xt_[0], TT, l, SH2, SC2, sqb, nps, rt, hT_[0], xn, hkey=("hTf", 0))
                for i in range(NT2):
                    b = i % 2
                    if i + 1 < NT2:
                        loadf(i + 1)
                    hT = hT_[b]
                    hk = ("hTf", b)
                    zub = {}

                    def emit_up(j):
                        z = jc_[0] % 3
                        ub = jc_[0] % 2
                        jc_[0] += 1
                        zub[j] = (z, ub)
                        for half, jj in ((0, j), (1, j + NJ)):
                            for kc in range(8):
                                P.op("pe", (lambda half=half, jj=jj, kc=kc, z=z, hT=hT: PE.matmul(ups[z][:, half * TT:(half + 1) * TT], lhsT=w_up[:, kc, jj * 128:(jj + 1) * 128],
                                                                                            rhs=hT[:, kc, :], start=(kc == 0), stop=(kc == 7))),
                                     r=[("w_up", kc), hk], w=[("ups", z)])
                    def elemA(j):
                        z, ub = zub[j]
                        P.op("act", (lambda z=z, ub=ub: A.activation(out=up[ub][:, :, 2:2 + TT], in_=ups[z][:, 0:2 * TT].rearrange("p (h t) -> p h t", h=2), func=AF.Copy)),
                             r=[("ups", z)], w=[("up", ub, 0), ("up", ub, 1)])
                        for half, jj in ((0, j), (1, j + NJ)):
                            P.op("pool", (lambda half=half, jj=jj, ub=ub: G.tensor_copy(out=up[ub][:, half, 0:2], in_=hist[:, jj, :])),
                                 r=[("hist", jj)], w=[("up", ub, half, "h")])
                            P.op("act", (lambda half=half, jj=jj, z=z, ub=ub: A.activation(out=uc[ub][:, half, :], in_=ups[z][:, half * TT:(half + 1) * TT], func=AF.Identity,
                                                                                            scale=fcw(2, jj), bias=fcb(jj))),
                                 r=[("ups", z), "par"], w=[("uc", ub, half)])
                            P.op("dve", (lambda half=half, jj=jj, ub=ub: V.scalar_tensor_tensor(out=uc[ub][:, half, :], in0=up[ub][:, half, 1:1 + TT], scalar=fcw(1, jj),
                                                                                                 in1=uc[ub][:, half, :], op0=ALU.mult, op1=ALU.add)),
                                 r=[("up", ub, half), ("up", ub, half, "h"), ("uc", ub, half), "par"], w=[("uc", ub, half)])
                            P.op("dve", (lambda half=half, jj=jj, ub=ub: V.scalar_tensor_tensor(out=uc[ub][:, half, :], in0=up[ub][:, half, 0:TT], scalar=fcw(0, jj),
                                                                                                 in1=uc[ub][:, half, :], op0=ALU.mult, op1=ALU.add)),
                                 r=[("up", ub, half), ("up", ub, half, "h"), ("uc", ub, half), "par"], w=[("uc", ub, half)])
                            P.op("pool", (lambda half=half, jj=jj, ub=ub: G.tensor_copy(out=hist[:, jj, :], in_=up[ub][:, half, TT:TT + 2])),
                                 r=[("up", ub, half)], w=[("hist", jj)])

                    def elemB(j):
                        z, ub = zub[j]
                        P.op("act", (lambda ub=ub: A.activation(out=sa[ub][:], in_=uc[ub][:, 0, :], func=AF.Silu)), r=[("uc", ub, 0)], w=[("sa", ub)])
                        P.op("pool", (lambda ub=ub: G.tensor_tensor(out=actj[ub][:], in0=sa[ub][:], in1=uc[ub][:, 1, :], op=ALU.mult)),
                             r=[("sa", ub), ("uc", ub, 1)], w=[("actj", ub)])
                        for n in range(8):
                            P.op("pe", (lambda n=n, j=j, ub=ub: PE.matmul(accp[:, n * TT:(n + 1) * TT], lhsT=w_dn[:, j, n * 128:(n + 1) * 128], rhs=actj[ub][:],
                                                                          start=(j == 0 and n % 2 == 0), stop=(j == NJ - 1), skip_group_check=True)),
                                 r=[("w_dn", j), ("actj", ub)], w=["accp"])

                    emit_up(0)
                    emit_up(1)
                    for j in range(NJ):
                        if j + 2 < NJ:
                            emit_up(j + 2)
                        elemA(j)
                        if j >= 1:
                            elemB(j - 1)
                        if j == 8 and i + 1 < NT2:
                            nb_ = (i + 1) % 2
                            norm_prompt(("xtf", nb_), tag, xt_[nb_], TT, l, SH2, SC2, sqb, nps, rt, hT_[nb_], xn, hkey=("hTf", nb_))
                    elemB(NJ - 1)
                    for n in range(8):
                        P.op("dve", (lambda n=n, b=b: V.scalar_tensor_tensor(out=xo[b][:, n, :], in0=accp[:, n * TT:(n + 1) * TT], scalar=modP[:, l, G2 + n:G2 + n + 1],
                                                                              in1=xt_[b][:, n, :], op0=ALU.mult, op1=ALU.add)),
                             r=["accp", ("xtf", b), "modP"], w=[("xo", 0, n)])
                    if not final:
                        P.dma("pool", (lambda b=b, i=i: G.dma_start(out=XOUT[:, :, i * TT:(i + 1) * TT], in_=xo[b][:])), r=[("xo", 0, n) for n in range(8)], w=[("XOUT", i)])
                    else:
                        xok = [("xo", 0, n) for n in range(8)]
                        P.op("act", (lambda b=b: A.activation(out=sqb[:], in_=xo[b][:], func=AF.Square)), r=xok, w=[(tag, "sqb")])
                        for c in range(8):
                            P.op("pe", (lambda c=c: PE.matmul(nps[:, :TT], lhsT=ones_b[:], rhs=sqb[:, c, :], start=(c == 0), stop=(c == 7))), r=[(tag, "sqb"), "ones_b"], w=[(tag, "nps")])
                        P.op("act", lambda: A.activation(out=rty[:], in_=nps[:, :TT], func=AF.Sqrt, scale=1.0 / D, bias=epsb[:, 0:1]), r=[(tag, "nps"), "epsb"], w=["rty"])
                        P.op("dve", lambda: V.reciprocal(out=rty[:], in_=rty[:]), r=["rty"], w=["rty"])
                        for c in range(8):
                            P.op("dve", (lambda c=c, b=b: V.scalar_tensor_tensor(out=xo[b][:, c, :], in0=xo[b][:, c, :], scalar=pcol("fin_g", c), in1=rty[:],
                                                                                  op0=ALU.mult, op1=ALU.mult)), r=[("xo", 0, c), "rty", "par"], w=[("xo", 0, c)])
                        P.dma("pool", (lambda b=b, i=i: G.dma_start(out=O["yT"][:, :, i * TT:(i + 1) * TT], in_=xo[b][:])), r=[("xo", 0, c) for c in range(8)], w=[("YO", i)])
                P.dma("sp", lambda: nc.sync.dma_start(out=O["ffn_o"][l, :, :, :], in_=hist[:]), r=[("hist", j) for j in range(44)], w=[("ffn_o", l)])

                sq_s = SB(ph, "sqs%d" % l, [128, 8, NS]); xn_s = SB(ph, "xns%d" % l, [128, 8, NS]); hTs = SB(ph, "hTs%d" % l, [128, 8, NS], BF16)
                stf = SB(ph, "stf%d" % l, [128, 44, 2, NS])
                ups_s = SB(ph, "ups_s%d" % l, [128, 44, NS]); ucs = SB(ph, "ucs%d" % l, [128, 44, NS]); tmp = SB(ph, "tmps%d" % l, [128, 44, NS])
                fo = SB(ph, "fo%d" % l, [128, 44, 2, NS])
                acts = SB(ph, "acts%d" % l, [128, NJ, NS], BF16); sas = SB(ph, "sas%d" % l, [128, NJ, NS])
                mo_s = SB(ph, "mofs%d" % l, [128, 8, NS])
                P.dma("sp", lambda: nc.sync.dma_start(out=stf[:], in_=I["st_ffn"][l, :, :, :, :]), w=["stf"])
                norm_sample(("fs", l), l, SH2, SC2, sq_s, nps, rt, hTs, xn_s)
                for jj in range(44):
                    for kc in range(8):
                        P.op("pe", (lambda jj=jj, kc=kc: PE.matmul(ups[0][:, jj * NS:(jj + 1) * NS], lhsT=w_up[:, kc, jj * 128:(jj + 1) * 128], rhs=hTs[:, kc, :],
                                                                    start=(kc == 0), stop=(kc == 7))), r=[("w_up", kc), (("fs", l), "hT")], w=[("ups", 0)])
                upv = ups[0][:, 0:44 * NS].rearrange("p (j s) -> p j s", j=44)
                P.op("act", lambda: A.activation(out=ups_s[:], in_=upv, func=AF.Copy), r=[("ups", 0)], w=["ups_s"])
                bc = lambda k: par[:, PAR_OFF["f_cw"] + (l * 3 + k) * 44: PAR_OFF["f_cw"] + (l * 3 + k + 1) * 44].unsqueeze(2).to_broadcast([128, 44, NS])
                bcb = par[:, PAR_OFF["f_cb"] + l * 44: PAR_OFF["f_cb"] + (l + 1) * 44].unsqueeze(2).to_broadcast([128, 44, NS])
                P.op("dve", lambda: V.tensor_tensor(out=ucs[:], in0=ups_s[:], in1=bc(2), op=ALU.mult), r=["ups_s", "par"], w=["ucs"])
                P.op("dve", lambda: V.tensor_tensor(out=ucs[:], in0=ucs[:], in1=bcb, op=ALU.add), r=["ucs", "par"], w=["ucs"])
                for k in range(2):
                    P.op("dve", (lambda k=k: V.tensor_tensor(out=tmp[:], in0=stf[:, :, k, :], in1=bc(k), op=ALU.mult)), r=["stf", "par", "ucs"], w=["tmp"])
                    P.op("dve", lambda: V.tensor_tensor(out=ucs[:], in0=ucs[:], in1=tmp[:], op=ALU.add), r=["ucs", "tmp"], w=["ucs"])
                P.op("act", lambda: A.activation(out=sas[:], in_=ucs[:, 0:NJ, :], func=AF.Silu), r=["ucs"], w=["sas"])
                P.op("dve", lambda: V.tensor_tensor(out=acts[:], in0=sas[:], in1=ucs[:, NJ:44, :], op=ALU.mult), r=["sas", "ucs"], w=["acts"])
                for n in range(8):
                    for j in range(NJ):
                        P.op("pe", (lambda n=n, j=j: PE.matmul(ups[1][:, n * NS:(n + 1) * NS], lhsT=w_dn[:, j, n * 128:(n + 1) * 128], rhs=acts[:, j, :],
                                                                start=(j == 0), stop=(j == NJ - 1))), r=[("w_dn", j), "acts"], w=[("ups", 1)])
                P.op("dve", lambda: V.tensor_tensor(out=mo_s[:], in0=ups[1][:, 0:8 * NS].rearrange("p (c s) -> p c s", c=8), in1=modS[:, l, G2:G2 + 8, :], op=ALU.mult),
                     r=[("ups", 1), "modS"], w=["mofs"])
                P.op("dve", lambda: V.tensor_tensor(out=xs_cur[:], in0=xs_cur[:], in1=mo_s[:], op=ALU.add), r=["mofs", "xs"], w=["xs"])
                P.op("pool", lambda: G.tensor_copy(out=fo[:, :, 0, :], in_=stf[:, :, 1, :]), r=["stf"], w=["fo0"])
                P.op("pool", lambda: G.tensor_copy(out=fo[:, :, 1, :], in_=ups_s[:]), r=["ups_s"], w=["fo1"])
                P.dma("sp", lambda: nc.sync.dma_start(out=O["ffn_s"][l, :, :, :, :], in_=fo[:]), r=["fo0", "fo1"], w=[("ffn_s", l)])
                if final:
                    ys = SB(ph, "ys", [128, 8, NS])
                    P.op("dve", lambda: V.tensor_tensor(out=sq_s[:], in0=xs_cur[:], in1=xs_cur[:], op=ALU.mult), r=["xs"], w=["sqs_f"])
                    for c in range(8):
                        P.op("pe", (lambda c=c: PE.matmul(nps[:, :NS], lhsT=ones_f[:], rhs=sq_s[:, c, :], start=(c == 0), stop=(c == 7))), r=["sqs_f", "ones_f"], w=["nps_f"])
                    P.op("act", lambda: A.activation(out=rt[:, :NS], in_=nps[:, :NS], func=AF.Sqrt, scale=1.0 / D, bias=epsb[:, 0:1]), r=["nps_f", "epsb"], w=["rt_f"])
                    P.op("dve", lambda: V.reciprocal(out=rt[:, :NS], in_=rt[:, :NS]), r=["rt_f"], w=["rt_f"])
                    P.op("dve", lambda: V.tensor_tensor(out=ys[:], in0=xs_cur[:], in1=rt[:, :NS].unsqueeze(1).to_broadcast([128, 8, NS]), op=ALU.mult), r=["xs", "rt_f"], w=["ys"])
                    P.op("dve", lambda: V.tensor_tensor(out=ys[:], in0=ys[:], in1=pcol("fin_g", 0, 8).unsqueeze(2).to_broadcast([128, 8, NS]), op=ALU.mult), r=["ys", "par"], w=["ys"])
                    P.dma("sp", lambda: nc.sync.dma_start(out=O["ysT"][:, :, :], in_=ys[:]), r=["ys"], w=["ysT"])
                P.flush()

        ffn_phase(0, X1, X2, False)

        with contextlib.ExitStack() as ph:
            TT = 256
            w_in = SB(ph, "w_ino", [128, 8, 2048], BF16); w_o = SB(ph, "w_oo", [128, 8, D], BF16)
            wa = SB(ph, "wa_sb", [128, 8, 128], BF16); wx = SB(ph, "wx_sb", [128, 8, 128], BF16)
            load_w(None, w_in, lambda kc: I["w_in_o"][:, kc, :], 8, "w_ino")
            load_w(None, w_o, lambda kc: I["w_out_o"][:, kc, :], 8, "w_oo")
            P.dma("pool", lambda: G.dma_start(out=wa[:], in_=I["rg_wa"][:, :, :]), w=["wa"])
            P.dma("pool", lambda: G.dma_start(out=wx[:], in_=I["rg_wx"][:, :, :]), w=["wx"])
            xt_ = [SB(ph, "xtr%d" % i, [128, 8, TT]) for i in range(2)]
            sqb = SB(ph, "sqbr", [128, 8, TT], BF16); xn = SB(ph, "xnr", [128, 8, TT]); rt = SB(ph, "rtr", [128, TT]); hT = SB(ph, "hTr", [128, 8, TT], BF16)
            xr = SB(ph, "xr", [128, 8, 3 + TT])
            gg = [SB(ph, "gg%d" % i, [128, 8, TT]) for i in range(2)]
            xc = [SB(ph, "xc%d" % i, [128, 8, TT]) for i in range(2)]
            xcb = [SB(ph, "xcb%d" % i, [128, 8, TT], BF16) for i in range(2)]
            rr = SB(ph, "rr", [128, 8, TT]); ii = SB(ph, "ii", [128, 8, TT]); aa = SB(ph, "aa", [128, 8, TT]); ss = SB(ph, "ss", [128, 8, TT])
            hs = SB(ph, "hs", [128, 8, TT])
            yb = SB(ph, "yb", [128, 8, TT], BF16)
            hst = SB(ph, "hst", [128, 8])
            xo = [SB(ph, "xor%d" % i, [128, 8, TT]) for i in range(2)]
            nps = PS(ph, "npsr", [128, 512])
            zps = [PS(ph, "zpr%d" % i, [128, 512]) for i in range(3)]
            gps = [PS(ph, "gpr%d" % i, [128, 1024]) for i in range(2)]
            rcw = lambda k, c: par[:, PAR_OFF["rg_cw"] + k * 8 + c: PAR_OFF["rg_cw"] + k * 8 + c + 1]
            P.op("pool", lambda: G.memset(hst[:], 0.0), w=[("hst", c) for c in range(8)])
            P.op("pool", lambda: G.memset(xr[:, :, 0:3], 0.0), w=[("xrh", c) for c in range(8)])

            def loadr(i):
                b = i % 2
                P.dma("sp", (lambda: nc.sync.dma_start(out=xt_[b][:], in_=X2[:, :, i * TT:(i + 1) * TT])), w=[("xtr", b)])
            zc_ = [0]
            tag = "rg"

            def stage1a(i):
                b = i % 2
                norm_prompt(("xtr", b), tag, xt_[b], TT, 1, SH1, SC1, sqb, nps, rt, hT, xn)

            def stage1b(i):
                b = i % 2
                for n in list(range(8, 16)) + list(range(8)):
                    z = zc_[0] % 3
                    zc_[0] += 1
                    for kc in range(8):
                        P.op("pe", (lambda n=n, kc=kc, z=z: PE.matmul(zps[z][:, 0:TT], lhsT=w_in[:, kc, n * 128:(n + 1) * 128], rhs=hT[:, kc, :], start=(kc == 0), stop=(kc == 7))),
                             r=[("w_ino", kc), (tag, "hT")], w=[("zpr", z)])
                    if n < 8:
                        P.op("act", (lambda n=n, z=z, b=b: A.activation(out=gg[b][:, n, :], in_=zps[z][:, 0:TT], func=GELU)), r=[("zpr", z)], w=[("gg", b, n)])
                    else:
                        c = n - 8
                        P.op("act", (lambda c=c, z=z: A.activation(out=xr[:, c, 3:3 + TT], in_=zps[z][:, 0:TT], func=AF.Copy)), r=[("zpr", z)], w=[("xr", c)])
                        P.op("act", (lambda c=c, z=z, b=b: A.activation(out=xc[b][:, c, :], in_=zps[z][:, 0:TT], func=AF.Identity, scale=rcw(3, c), bias=pcol("rg_cb", c))),
                             r=[("zpr", z), "par"], w=[("xc", b, c)])
                        for k in range(3):
                            P.op("dve", (lambda c=c, k=k, b=b: V.scalar_tensor_tensor(out=xc[b][:, c, :], in0=xr[:, c, k:k + TT], scalar=rcw(k, c), in1=xc[b][:, c, :],
                                                                                        op0=ALU.mult, op1=ALU.add)),
                                 r=[("xr", c), ("xrh", c), ("xc", b, c), "par"], w=[("xc", b, c)])
                        P.op("pool", (lambda c=c: G.tensor_copy(out=xr[:, c, 0:3], in_=xr[:, c, TT:TT + 3])), r=[("xr", c)], w=[("xrh", c)])
                        P.op("pool", (lambda c=c, b=b: G.tensor_copy(out=xcb[b][:, c, :], in_=xc[b][:, c, :])), r=[("xc", b, c)], w=[("xcb", b, c)])

            def stage2a(i):
                b = i % 2
                for c in range(8):
                    q2 = c % 2
                    P.op("pe", (lambda c=c, q2=q2, b=b: PE.matmul(gps[q2][:, 0:TT], lhsT=wa[:, c, :], rhs=xcb[b][:, c, :], start=True, stop=True)), r=["wa", ("xcb", b, c)], w=[("gpr", q2)])
                    P.op("pe", (lambda c=c, q2=q2, b=b: PE.matmul(gps[q2][:, 512:512 + TT], lhsT=wx[:, c, :], rhs=xcb[b][:, c, :], start=True, stop=True)), r=["wx", ("xcb", b, c)], w=[("gpr", q2)])
                    P.op("act", (lambda c=c, q2=q2: A.activation(out=rr[:, c, :], in_=gps[q2][:, 0:TT], func=AF.Sigmoid, bias=pcol("rg_ba", c), scale=1.0)),
                         r=[("gpr", q2), "par"], w=[("rr", c)])
                    P.op("act", (lambda c=c, q2=q2: A.activation(out=ii[:, c, :], in_=gps[q2][:, 512:512 + TT], func=AF.Sigmoid, bias=pcol("rg_bx", c), scale=1.0)),
                         r=[("gpr", q2), "par"], w=[("ii", c)])
                    P.op("pool", (lambda c=c, b=b: G.tensor_tensor(out=ii[:, c, :], in0=ii[:, c, :], in1=xc[b][:, c, :], op=ALU.mult)), r=[("ii", c), ("xc", b, c)], w=[("ii", c)])
                for c in range(8):
                    P.op("act", (lambda c=c: A.activation(out=aa[:, c, :], in_=rr[:, c, :], func=AF.Exp, scale=cst[:, c:c + 1])), r=[("rr", c), "cst"], w=[("aa", c)])
                    P.op("act", (lambda c=c: A.activation(out=ss[:, c, :], in_=rr[:, c, :], func=AF.Exp, scale=cst2[:, c:c + 1])), r=[("rr", c), "cst2"], w=[("ss", c)])
                for c in range(8):
                    P.op("act", (lambda c=c: A.activation(out=ss[:, c, :], in_=ss[:, c, :], func=AF.Sqrt, scale=-1.0, bias=oneb[:, 0:1])), r=[("ss", c), "oneb"], w=[("ss", c)])
                    P.op("dve", (lambda c=c: V.tensor_tensor(out=ii[:, c, :], in0=ii[:, c, :], in1=ss[:, c, :], op=ALU.mult)), r=[("ii", c), ("ss", c)], w=[("ii", c)])
                    P.op("dve", (lambda c=c: V.tensor_tensor_scan(out=hs[:, c, :], data0=aa[:, c, :], data1=ii[:, c, :], initial=hst[:, c:c + 1], op0=ALU.mult, op1=ALU.add)),
                         r=[("aa", c), ("ii", c), ("hst", c)], w=[("hs", c)])
                    P.op("dve", (lambda c=c: V.tensor_copy(out=hst[:, c:c + 1], in_=hs[:, c, TT - 1:TT])), r=[("hs", c)], w=[("hst", c)])
                    P.op("pool", (lambda c=c, b=b: G.tensor_tensor(out=yb[:, c, :], in0=hs[:, c, :], in1=gg[b][:, c, :], op=ALU.mult)),
                         r=[("hs", c), ("gg", b, c)], w=[("yb", c)])

            def stage2b(i):
                b = i % 2
                for n in range(8):
                    z = zc_[0] % 3
                    zc_[0] += 1
                    for kc in range(8):
                        P.op("pe", (lambda n=n, kc=kc, z=z: PE.matmul(zps[z][:, 0:TT], lhsT=w_o[:, kc, n * 128:(n + 1) * 128], rhs=yb[:, kc, :], start=(kc == 0), stop=(kc == 7))),
                             r=[("w_oo", kc), ("yb", kc)], w=[("zpr", z)])
                    P.op("dve", (lambda n=n, z=z, b=b: V.scalar_tensor_tensor(out=xo[b][:, n, :], in0=zps[z][:, 0:TT], scalar=modP[:, 1, G1 + n:G1 + n + 1], in1=xt_[b][:, n, :],
                                                                               op0=ALU.mult, op1=ALU.add)), r=[("zpr", z), ("xtr", b), "modP"], w=[("xor", b, n)])
                P.dma("pool", (lambda b=b, i=i: G.dma_start(out=X3[:, :, i * TT:(i + 1) * TT], in_=xo[b][:])), r=[("xor", b, n) for n in range(8)], w=[("X3", i)])

            loadr(0)
            stage1a(0)
            stage1b(0)
            for i in range(NT2):
                if i + 1 < NT2:
                    loadr(i + 1)
                stage2a(i)
                if i + 1 < NT2:
                    stage1a(i + 1)
                    stage1b(i + 1)
                stage2b(i)
            P.dma("sp", lambda: nc.sync.dma_start(out=O["rgc_o"][:, :, :], in_=xr[:, :, 0:3]), r=[("xrh", c) for c in range(8)], w=["rgc_o"])
            P.dma("sp", lambda: nc.sync.dma_start(out=O["rgh_o"][:, :], in_=hst[:]), r=[("hst", c) for c in range(8)], w=["rgh_o"])

            sq_s = SB(ph, "sqsr", [128, 8, NS]); xn_s = SB(ph, "xnsr", [128, 8, NS]); hTs = SB(ph, "hTsr", [128, 8, NS], BF16)
            stc = SB(ph, "stc", [128, 8, 3, NS]); sth = SB(ph, "sth", [128, 8, NS])
            zs = SB(ph, "zs", [128, 16, NS]); ggs = SB(ph, "ggs", [128, 8, NS])
            xcs = SB(ph, "xcs", [128, 8, NS]); tmp = SB(ph, "tmpr", [128, 8, NS]); xcsb = SB(ph, "xcsb", [128, 8, NS], BF16)
            rs_ = SB(ph, "rs_", [128, 8, NS]); is_ = SB(ph, "is_", [128, 8, NS]); as_ = SB(ph, "as_", [128, 8, NS]); s2 = SB(ph, "s2_", [128, 8, NS])
            hn = SB(ph, "hn", [128, 8, NS]); ybs = SB(ph, "ybs", [128, 8, NS], BF16); mo_s = SB(ph, "mors", [128, 8, NS])
            co = SB(ph, "co", [128, 8, 3, NS])
            P.dma("sp", lambda: nc.sync.dma_start(out=stc[:], in_=I["st_rgc"][:, :, :, :]), w=["stc"])
            P.dma("sp", lambda: nc.sync.dma_start(out=sth[:], in_=I["st_rgh"][:, :, :]), w=["sth"])
            norm_sample("rs", 1, SH1, SC1, sq_s, nps, rt, hTs, xn_s)
            for n in range(16):
                for kc in range(8):
                    P.op("pe", (lambda n=n, kc=kc: PE.matmul(zps[0][:, n * NS:(n + 1) * NS], lhsT=w_in[:, kc, n * 128:(n + 1) * 128], rhs=hTs[:, kc, :], start=(kc == 0), stop=(kc == 7))),
                         r=[("w_ino", kc), ("rs", "hT")], w=[("zpr", 0)])
            P.op("act", lambda: A.activation(out=zs[:], in_=zps[0][:, 0:16 * NS].rearrange("p (c s) -> p c s", c=16), func=AF.Copy), r=[("zpr", 0)], w=["zs"])
            P.op("act", lambda: A.activation(out=ggs[:], in_=zs[:, 0:8, :], func=GELU), r=["zs"], w=["ggs"])
            bw = lambda k: par[:, PAR_OFF["rg_cw"] + k * 8: PAR_OFF["rg_cw"] + (k + 1) * 8].unsqueeze(2).to_broadcast([128, 8, NS])
            b8 = lambda nm: pcol(nm, 0, 8).unsqueeze(2).to_broadcast([128, 8, NS])
            P.op("dve", lambda: V.tensor_tensor(out=xcs[:], in0=zs[:, 8:16, :], in1=bw(3), op=ALU.mult), r=["zs", "par"], w=["xcs"])
            P.op("dve", lambda: V.tensor_tensor(out=xcs[:], in0=xcs[:], in1=b8("rg_cb"), op=ALU.add), r=["xcs", "par"], w=["xcs"])
            for k in range(3):
                P.op("dve", (lambda k=k: V.tensor_tensor(out=tmp[:], in0=stc[:, :, k, :], in1=bw(k), op=ALU.mult)), r=["stc", "par", "xcs"], w=["tmpr"])
                P.op("dve", lambda: V.tensor_tensor(out=xcs[:], in0=xcs[:], in1=tmp[:], op=ALU.add), r=["xcs", "tmpr"], w=["xcs"])
            P.op("dve", lambda: V.tensor_copy(out=xcsb[:], in_=xcs[:]), r=["xcs"], w=["xcsb"])
            for c in range(8):
                P.op("pe", (lambda c=c: PE.matmul(zps[1][:, c * NS:(c + 1) * NS], lhsT=wa[:, c, :], rhs=xcsb[:, c, :], start=True, stop=True)), r=["wa", "xcsb"], w=[("zpr", 1)])
                P.op("pe", (lambda c=c: PE.matmul(zps[1][:, 64 + c * NS: 64 + (c + 1) * NS], lhsT=wx[:, c, :], rhs=xcsb[:, c, :], start=True, stop=True)), r=["wx", "xcsb"], w=[("zpr", 1)])
            P.op("dve", lambda: V.tensor_tensor(out=rs_[:], in0=zps[1][:, 0:8 * NS].rearrange("p (c s) -> p c s", c=8), in1=b8("rg_ba"), op=ALU.add), r=[("zpr", 1), "par"], w=["rs_"])
            P.op("dve", lambda: V.tensor_tensor(out=is_[:], in0=zps[1][:, 64:64 + 8 * NS].rearrange("p (c s) -> p c s", c=8), in1=b8("rg_bx"), op=ALU.add), r=[("zpr", 1), "par"], w=["is_"])
            P.op("act", lambda: A.activation(out=rs_[:], in_=rs_[:], func=AF.Sigmoid), r=["rs_"], w=["rs_"])
            P.op("act", lambda: A.activation(out=is_[:], in_=is_[:], func=AF.Sigmoid), r=["is_"], w=["is_"])
            P.op("dve", lambda: V.tensor_tensor(out=as_[:], in0=rs_[:], in1=cst[:].unsqueeze(2).to_broadcast([128, 8, NS]), op=ALU.mult), r=["rs_", "cst"], w=["as_"])
            P.op("act", lambda: A.activation(out=s2[:], in_=as_[:], func=AF.Exp, scale=2.0), r=["as_"], w=["s2"])
            P.op("act", lambda: A.activation(out=as_[:], in_=as_[:], func=AF.Exp), r=["as_", "s2"], w=["as_"])
            P.op("act", lambda: A.activation(out=s2[:], in_=s2[:], func=AF.Sqrt, scale=-1.0, bias=oneb[:, 0:1]), r=["s2", "oneb"], w=["s2"])
            P.op("dve", lambda: V.tensor_tensor(out=is_[:], in0=is_[:], in1=xcs[:], op=ALU.mult), r=["is_", "xcs"], w=["is_"])
            P.op("dve", lambda: V.tensor_tensor(out=is_[:], in0=is_[:], in1=s2[:], op=ALU.mult), r=["is_", "s2"], w=["is_"])
            P.op("dve", lambda: V.tensor_tensor(out=hn[:], in0=as_[:], in1=sth[:], op=ALU.mult), r=["as_", "sth"], w=["hn"])
            P.op("dve", lambda: V.tensor_tensor(out=hn[:], in0=hn[:], in1=is_[:], op=ALU.add), r=["hn", "is_"], w=["hn"])
            P.dma("sp", lambda: nc.sync.dma_start(out=O["rgh_s"][:, :, :], in_=hn[:]), r=["hn"], w=["rgh_s"])
            P.op("pool", lambda: G.tensor_copy(out=co[:, :, 0:2, :], in_=stc[:, :, 1:3, :]), r=["stc"], w=["co0"])
            P.op("pool", lambda: G.tensor_copy(out=co[:, :, 2, :], in_=zs[:, 8:16, :]), r=["zs"], w=["co1"])
            P.dma("sp", lambda: nc.sync.dma_start(out=O["rgc_s"][:, :, :, :], in_=co[:]), r=["co0", "co1"], w=["rgc_s"])
            P.op("dve", lambda: V.tensor_tensor(out=ybs[:], in0=hn[:], in1=ggs[:], op=ALU.mult), r=["hn", "ggs"], w=["ybs"])
            for n in range(8):
                for kc in range(8):
                    P.op("pe", (lambda n=n, kc=kc: PE.matmul(zps[2][:, n * NS:(n + 1) * NS], lhsT=w_o[:, kc, n * 128:(n + 1) * 128], rhs=ybs[:, kc, :], start=(kc == 0), stop=(kc == 7))),
                         r=[("w_oo", kc), "ybs"], w=[("zpr", 2)])
            P.op("dve", lambda: V.tensor_tensor(out=mo_s[:], in0=zps[2][:, 0:8 * NS].rearrange("p (c s) -> p c s", c=8), in1=modS[:, 1, G1:G1 + 8, :], op=ALU.mult),
                 r=[("zpr", 2), "modS"], w=["mors"])
            P.op("dve", lambda: V.tensor_tensor(out=xs_cur[:], in0=xs_cur[:], in1=mo_s[:], op=ALU.add), r=["mors", "xs"], w=["xs"])
            P.flush()

        ffn_phase(1, X3, None, True)
        P.barrier(final=True)
    return nc


_CACHE = {}


def _fm(v, n):
    return np.ascontiguousarray(np.asarray(v, np.float32).reshape(n, 128).T)


def _wk(w):
    K, N = w.shape
    return np.ascontiguousarray(np.asarray(w, np.float32).reshape(K // 128, 128, N).transpose(1, 0, 2))


def kernel(**inp):
    inp = {k: np.asarray(v) for k, v in inp.items()}
    x_prompt = inp["x_prompt"]
    B, T, _ = x_prompt.shape
    TK = min(LCACHE, T)
    if T not in _CACHE:
        _CACHE[T] = build_program(T)
    nc = _CACHE[T]
    NCORES = 8
    par = np.zeros((128, NPAR), np.float32)

    def put(name, arr):
        par[:, PAR_OFF[name]:PAR_OFF[name] + arr.shape[1]] = arr
    put("b_ada", np.concatenate([_fm(inp["b_ada"][0], 48), _fm(inp["b_ada"][1], 48)], 1))
    put("ln_g", _fm(inp["ln_v_g"][0], 4)); put("ln_b", _fm(inp["ln_v_b"][0], 4))
    put("rg_cw", np.concatenate([_fm(inp["rg_conv_w"][0, k], 8) for k in range(4)], 1))
    put("rg_cb", _fm(inp["rg_conv_b"][0], 8)); put("rg_ba", _fm(inp["rg_b_a"][0], 8)); put("rg_bx", _fm(inp["rg_b_x"][0], 8))
    put("rg_lam", _fm(inp["rg_lambda"][0], 8))
    put("f_cw", np.concatenate([_fm(inp["ffn_conv_w"][l, k], 44) for l in range(2) for k in range(3)], 1))
    put("f_cb", np.concatenate([_fm(inp["ffn_conv_b"][l], 44) for l in range(2)], 1))
    put("fin_g", _fm(inp["final_g"], 8))
    w_oe = inp["w_out_even"][0]
    shared = dict(
        wada=np.stack([_wk(inp["w_ada"][l]) for l in range(2)]), params=par,
        w_in_e=_wk(inp["w_in_even"][0]), w_out_e=_wk(w_oe),
        w_out_eh=np.ascontiguousarray(w_oe[512:].reshape(8, 64, D).transpose(1, 0, 2)),
        w_sguT=np.ascontiguousarray(inp["w_sgu"][0].transpose(2, 0, 1)),
        b_sgu=np.ascontiguousarray(inp["b_sgu"][0].reshape(1, 512)),
        sg0=np.ascontiguousarray(np.concatenate([inp["w_sgu"][0][:, 0, 0], inp["b_sgu"][0][:, 0]]).reshape(1, 8)),
        w_in_o=_wk(inp["w_in_odd"][0]),
        rg_wa=np.ascontiguousarray(inp["rg_w_a"][0].transpose(1, 0, 2)), rg_wx=np.ascontiguousarray(inp["rg_w_x"][0].transpose(1, 0, 2)),
        w_out_o=_wk(inp["w_out_odd"][0]),
        w_up=np.stack([_wk(inp["ffn_w_up"][l]) for l in range(2)]), w_dn=np.stack([_wk(inp["ffn_w_down"][l]) for l in range(2)]),
    )
    rowmask = np.zeros((128, NS), np.float32)
    dmask = np.zeros((128, 8, 64), np.float32)
    for s in range(NS):
        rowmask[32 * s:32 * s + 8, s] = 1.0
        for h in range(8):
            dmask[32 * s + h, h, :] = 1.0
    shared["rowmask"] = rowmask
    shared["dmask"] = dmask.reshape(128, 512)
    in_maps = []
    SEQ_CORE = [0, 1, 4, 5][:B] if B <= 4 else list(range(B))
    zero_x = np.zeros((128, 8, T), np.float32)
    for c in range(NCORES):
        ss = slice(NS * c, NS * (c + 1))
        m = dict(shared)
        if c in SEQ_CORE:
            sq = SEQ_CORE.index(c)
            m["xT"] = np.ascontiguousarray(x_prompt[sq].T.reshape(8, 128, T).transpose(1, 0, 2))
            cp = inp["c_prompt"][sq:sq + 1]
        else:
            m["xT"] = zero_x
            cp = np.zeros((1, D), np.float32)
        m["xsT"] = np.ascontiguousarray(inp["x_sample"][ss, 0, :].T.reshape(8, 128, NS).transpose(1, 0, 2))
        cc = np.concatenate([cp, inp["c_sample"][ss]], 0)
        m["cT"] = np.ascontiguousarray(cc.T.reshape(8, 128, 1 + NS).transpose(1, 0, 2))
        m["ck"] = np.ascontiguousarray(inp["cache_win_k"][0, ss].reshape(NS, LCACHE, 512))
        m["cv"] = np.ascontiguousarray(inp["cache_win_v"][0, ss].reshape(NS, LCACHE, 512))
        m["st_rgc"] = np.ascontiguousarray(inp["state_rglru_conv"][0, ss].transpose(2, 1, 0).reshape(8, 128, 3, NS).transpose(1, 0, 2, 3))
        m["st_rgh"] = np.ascontiguousarray(inp["state_rglru_h"][0, ss].T.reshape(8, 128, NS).transpose(1, 0, 2))
        m["st_ffn"] = np.ascontiguousarray(inp["state_ffn_conv"][:, ss].transpose(0, 3, 2, 1).reshape(2, 44, 128, 2, NS).transpose(0, 2, 1, 3, 4))
        in_maps.append(m)
    res = run_bass_kernel_spmd(nc, in_maps, core_ids=list(range(NCORES)))
    R = res.results
    global _LAST
    _LAST = R

    def unfm(a):
        a = np.asarray(a)
        n = a.shape[1]
        rest = a.shape[2:]
        return np.moveaxis(a.transpose(1, 0, *range(2, a.ndim)).reshape(n * 128, *rest), 0, -1)

    y_prompt = np.stack([unfm(R[SEQ_CORE[b]]["yT"]) for b in range(B)]).astype(np.float32)
    y_sample = np.concatenate([unfm(R[c]["ysT"]) for c in range(NCORES)], 0)[:, None, :].astype(np.float32)
    win_k_p = np.stack([unfm(R[SEQ_CORE[b]]["kT_o"]).reshape(TK, 8, 64) for b in range(B)])[None].astype(np.float32)
    win_v_p = np.stack([unfm(R[SEQ_CORE[b]]["vT_o"]).reshape(TK, 8, 64) for b in range(B)])[None].astype(np.float32)
    rgc_p = np.stack([unfm(R[SEQ_CORE[b]]["rgc_o"]) for b in range(B)])[None].astype(np.float32)
    rgh_p = np.stack([unfm(R[SEQ_CORE[b]]["rgh_o"][:, :, None])[0] for b in range(B)])[None].astype(np.float32)
    ffn_p = np.stack([np.stack([unfm(R[SEQ_CORE[b]]["ffn_o"][l]) for b in range(B)]) for l in range(2)]).astype(np.float32)
    cv_s = np.concatenate([unfm(R[c]["cv_o"]) for c in range(NCORES)], 0)[None, :, None, :].astype(np.float32)
    wk_s = np.concatenate([R[c]["wk_s"].reshape(NS, LCACHE, 8, 64) for c in range(NCORES)], 0)[None].astype(np.float32)
    wv_s = np.concatenate([R[c]["wv_s"].reshape(NS, LCACHE, 8, 64) for c in range(NCORES)], 0)[None].astype(np.float32)
    rgc_s = np.concatenate([unfm(R[c]["rgc_s"]).transpose(1, 0, 2) for c in range(NCORES)], 0)[None].astype(np.float32)
    rgh_s = np.concatenate([unfm(R[c]["rgh_s"]) for c in range(NCORES)], 0)[None].astype(np.float32)
    ffn_s = np.stack([np.concatenate([unfm(R[c]["ffn_s"][l]).transpose(1, 0, 2) for c in range(NCORES)], 0) for l in range(2)]).astype(np.float32)
    return (y_prompt, y_sample, win_k_p, win_v_p, rgc_p, rgh_p, ffn_p, cv_s, wk_s, wv_s, rgc_s, rgh_s, ffn_s)
```

```python
import contextlib
import numpy as np
import concourse.bass as bass
import concourse.mybir as mybir
from concourse.bass_utils import run_bass_kernel_spmd

F32 = mybir.dt.float32
BF16 = mybir.dt.bfloat16
AF = mybir.ActivationFunctionType
ALU = mybir.AluOpType
AX = mybir.AxisListType

D = 1024
NCH = 8
WA = 512
NIN_E = 2560
DFF = 2816
F2 = 5632
NJ = 22
NS = 4
LCACHE = 2048
EPS = 1e-6
PATTERNS = ((128, 1), (512, 4), (2048, 16))
GELU = AF.Gelu_apprx_tanh

PAR_SPEC = [("b_ada", 96), ("ln_g", 4), ("ln_b", 4), ("rg_cw", 32), ("rg_cb", 8), ("rg_ba", 8),
            ("rg_bx", 8), ("rg_lam", 8), ("f_cw", 2 * 3 * 44), ("f_cb", 2 * 44), ("fin_g", 8)]
PAR_OFF = {}
_o = 0
for _n, _w in PAR_SPEC:
    PAR_OFF[_n] = _o
    _o += _w
NPAR = _o


class Prog:
    NDS = 8

    def __init__(self, nc, es):
        self.nc = nc
        self.eng = {"pe": nc.tensor, "act": nc.scalar, "dve": nc.vector, "pool": nc.gpsimd, "sp": nc.sync}
        self.sem = {e: es.enter_context(nc.semaphore("s_" + e)) for e in self.eng}
        self.cnt = {e: 0 for e in self.eng}
        self.dsem = {q: [es.enter_context(nc.semaphore("d_%s%d" % (q, i))) for i in range(self.NDS)]
                     for q in ("sp", "pool", "actq")}
        self.qeng = {"sp": "sp", "pool": "pool", "actq": "act"}
        self.dcnt = {q: 0 for q in self.dsem}
        self.seen = {e: {} for e in self.eng}
        self.ops = []

    def op(self, eng, fn, r=(), w=()):
        self.ops.append([eng, fn, tuple(r), tuple(w), False])

    def dma(self, q, fn, r=(), w=()):
        self.ops.append([q, fn, tuple(r), tuple(w), True])

    def _wait(self, e, sem, val):
        key = sem.name if hasattr(sem, "name") else id(sem)
        if self.seen[e].get(key, 0) >= val:
            return
        self.seen[e][key] = val
        self.eng[e].wait_ge(sem, val)

    def flush(self):
        import os
        self.nflush = getattr(self, "nflush", 0) + 1
        stop = int(os.environ.get("K_STOP", "99"))
        only = os.environ.get("K_ONLY")
        if self.nflush > stop or (only and str(self.nflush) not in only.split(",")):
            self.ops = []
            return
        kops = os.environ.get("K_OPS")
        if kops and self.nflush == stop:
            self.ops = self.ops[:int(kops)]
            print("last kept op:", self.ops[-1][0], self.ops[-1][2], self.ops[-1][3], "of", len(self.ops))
        ops = self.ops
        n = len(ops)
        lastw, readers = {}, {}
        deps = [None] * n
        needed = [False] * n
        for i, (e, fn, r, w, isd) in enumerate(ops):
            d = set()
            for k in r:
                if k in lastw:
                    d.add(lastw[k])
            for k in w:
                if k in lastw:
                    d.add(lastw[k])
                for j in readers.get(k, ()):
                    d.add(j)
            d.discard(i)
            dd = []
            for j in d:
                if ops[j][0] == "pe" and e == "pe" and not ops[j][4] and not isd:
                    continue
                dd.append(j)
                needed[j] = True
            deps[i] = sorted(dd)
            for k in r:
                readers.setdefault(k, []).append(i)
            for k in w:
                lastw[k] = i
                readers[k] = []
        lastop = {}
        for i, (e, fn, r, w, isd) in enumerate(ops):
            if not isd:
                lastop[e] = i
        for i in lastop.values():
            needed[i] = True
        sig = [None] * n
        for i, (e, fn, r, w, isd) in enumerate(ops):
            q = e
            if isd:
                e = self.qeng[q]
            for j in deps[i]:
                s, v = sig[j]
                self._wait(e, s, v)
            if isd:
                m = self.dcnt[q]
                self.dcnt[q] += 1
                s = self.dsem[q][m % self.NDS]
                rnd = m // self.NDS
                if rnd > 0:
                    self._wait(e, s, 16 * rnd)
                ins = fn()
                ins.then_inc(s, 16)
                sig[i] = (s, 16 * (rnd + 1))
            else:
                ins = fn()
                if needed[i]:
                    self.cnt[e] += 1
                    ins.then_inc(self.sem[e], 1)
                    sig[i] = (self.sem[e], self.cnt[e])
        self.ops = []
        self.barrier()

    def barrier(self, final=False):
        for e in self.eng:
            for e2 in self.eng:
                if e2 != e and self.cnt[e2] > 0:
                    self._wait(e, self.sem[e2], self.cnt[e2])
            for q in self.dsem:
                if q == "actq" and not final:
                    continue
                m = self.dcnt[q]
                for k in range(min(m, self.NDS)):
                    cntk = (m - k + self.NDS - 1) // self.NDS
                    self._wait(e, self.dsem[q][k], 16 * cntk)


def build_program(T):
    nc = bass.Bass("TRN2", target_bir_lowering=False)
    TK = min(LCACHE, T)
    NT5 = T // 512
    NT2 = T // 256

    def din(name, shape, dt=F32):
        return nc.dram_tensor(name, list(shape), dt, kind="ExternalInput").ap()

    def dout(name, shape):
        return nc.dram_tensor(name, list(shape), F32, kind="ExternalOutput").ap()

    def dscr(name, shape, dt=F32):
        import os
        kind = "ExternalOutput" if (os.environ.get("K_DBG") and name in ("x1T", "x2T", "x3T", "O_s", "aoT_s")) else "Internal"
        return nc.dram_tensor(name, list(shape), dt, kind=kind).ap()

    I = dict(
        xT=din("xT", [128, 8, T]), xsT=din("xsT", [128, 8, NS]), cT=din("cT", [128, 8, 1 + NS]),
        wada=din("wada", [2, 128, 8, 6144]), params=din("params", [128, NPAR]),
        w_in_e=din("w_in_e", [128, 8, NIN_E]), w_out_e=din("w_out_e", [128, 8, D]),
        w_out_eh=din("w_out_eh", [64, 8, D]), w_sguT=din("w_sguT", [128, 4, 128]),
        b_sgu=din("b_sgu", [1, 512]), sg0=din("sg0", [1, 8]),
        w_in_o=din("w_in_o", [128, 8, 2048]), rg_wa=din("rg_wa", [128, 8, 128]),
        rg_wx=din("rg_wx", [128, 8, 128]), w_out_o=din("w_out_o", [128, 8, D]),
        w_up=din("w_up", [2, 128, 8, F2]), w_dn=din("w_dn", [2, 128, NJ, D]),
        ck=din("ck", [NS, LCACHE, 512]), cv=din("cv", [NS, LCACHE, 512]),
        st_rgc=din("st_rgc", [128, 8, 3, NS]), st_rgh=din("st_rgh", [128, 8, NS]),
        st_ffn=din("st_ffn", [2, 128, 44, 2, NS]),
        rowmask=din("rowmask", [128, NS]), dmask=din("dmask", [128, 512]),
    )
    O = dict(
        yT=dout("yT", [128, 8, T]), ysT=dout("ysT", [128, 8, NS]),
        kT_o=dout("kT_o", [128, 4, TK]), vT_o=dout("vT_o", [128, 4, TK]),
        rgc_o=dout("rgc_o", [128, 8, 3]), rgh_o=dout("rgh_o", [128, 8]),
        ffn_o=dout("ffn_o", [2, 128, 44, 2]), cv_o=dout("cv_o", [128, 4, NS]),
        wk_s=dout("wk_s", [NS, LCACHE, 512]), wv_s=dout("wv_s", [NS, LCACHE, 512]),
        rgc_s=dout("rgc_s", [128, 8, 3, NS]), rgh_s=dout("rgh_s", [128, 8, NS]),
        ffn_s=dout("ffn_s", [2, 128, 44, 2, NS]),
    )
    X1 = dscr("x1T", [128, 8, T]); X2 = dscr("x2T", [128, 8, T]); X3 = dscr("x3T", [128, 8, T])
    QS = dscr("qT_s", [128, 4, T], BF16); KS = dscr("kT_s", [128, 4, T], BF16); VS = dscr("vT_s", [128, 4, T], BF16)
    AOS = dscr("aoT_s", [128, 4, T], BF16)
    OS = dscr("O_s", [3, T, 8 * 66])
    QKVS = dscr("qkv_s", [NS, 1536])
    XS1 = None

    top = contextlib.ExitStack()
    with top:
        P = Prog(nc, top)

        def SB(es, name, shape, dt=F32):
            return es.enter_context(nc.sbuf_tensor("sb_" + name, list(shape), dt))

        def PS(es, name, shape, dt=F32):
            return es.enter_context(nc.psum_tensor("ps_" + name, list(shape), dt))

        V, A, G, PE = nc.vector, nc.scalar, nc.gpsimd, nc.tensor

        par = SB(top, "par", [128, NPAR])
        ones_b = SB(top, "ones_b", [128, 128], BF16)
        ones_f = SB(top, "ones_f", [128, 128])
        ident_f = SB(top, "ident_f", [128, 128])
        ident_b = SB(top, "ident_b", [128, 128], BF16)
        modP = SB(top, "modP", [128, 2, 48])
        modS = SB(top, "modS", [128, 2, 48, NS])
        cst = SB(top, "cst", [128, 8])
        cst2 = SB(top, "cst2", [128, 8])
        xs_cur = SB(top, "xs_cur", [128, 8, NS])
        epsb = SB(top, "epsb", [128, 1])
        oneb = SB(top, "oneb", [128, 1])

        def pcol(name, i=0, n=1):
            o = PAR_OFF[name] + i
            return par[:, o:o + n]

        P.dma("sp", lambda: nc.sync.dma_start(out=par[:], in_=I["params"][:, :]), w=["par"])
        P.op("pool", lambda: G.memset(ones_f[:], 1.0), w=["ones_f"])
        P.op("pool", lambda: G.memset(ones_b[:], 1.0), w=["ones_b"])
        P.op("pool", lambda: G.memset(epsb[:], EPS), w=["epsb"])
        P.op("pool", lambda: G.memset(oneb[:], 1.0), w=["oneb"])
        P.op("pool", lambda: G.memset(ident_f[:], 1.0), w=["ident_f"])
        P.op("pool", lambda: G.affine_select(out=ident_f[:], in_=ident_f[:], pattern=[[-1, 128]],
                                             compare_op=ALU.is_equal, fill=0.0, base=0, channel_multiplier=1),
             r=["ident_f"], w=["ident_f"])
        P.op("dve", lambda: V.tensor_copy(out=ident_b[:], in_=ident_f[:]), r=["ident_f"], w=["ident_b"])
        P.dma("sp", lambda: nc.sync.dma_start(out=xs_cur[:], in_=I["xsT"][:, :, :]), w=["xs"])

        with contextlib.ExitStack() as ph:
            cT = SB(ph, "cT_sb", [128, 8, 1 + NS])
            cb = SB(ph, "cb_sb", [128, 8, 1 + NS], BF16)
            wada = SB(ph, "wada_sb", [128, 8, 6144], BF16)
            mps = PS(ph, "mod_ps", [128, 48, 1 + NS])
            P.dma("sp", lambda: nc.sync.dma_start(out=cT[:], in_=I["cT"][:, :, :]), w=["cT"])
            P.op("act", lambda: A.activation(out=cb[:], in_=cT[:], func=AF.Silu), r=["cT"], w=["cb"])
            for l in range(2):
                for kc in range(8):
                    P.dma("pool", (lambda l=l, kc=kc: G.dma_start(out=wada[:, kc, :], in_=I["wada"][l, :, kc, :])),
                          w=[("wada", kc)])
                for n in range(48):
                    for kc in range(8):
                        P.op("pe", (lambda n=n, kc=kc: PE.matmul(mps[:, n, :], lhsT=wada[:, kc, n * 128:(n + 1) * 128],
                                                                  rhs=cb[:, kc, :], start=(kc == 0), stop=(kc == 7))),
                             r=[("wada", kc), "cb"], w=["mps"])
                bada = par[:, PAR_OFF["b_ada"] + l * 48: PAR_OFF["b_ada"] + (l + 1) * 48]
                P.op("dve", (lambda l=l, bada=bada: V.tensor_tensor(out=modP[:, l, :], in0=mps[:, :, 0], in1=bada, op=ALU.add)),
                     r=["mps", "par"], w=["modP"])
                P.op("dve", (lambda l=l, bada=bada: V.tensor_tensor(
                    out=modS[:, l, :, :], in0=mps[:, :, 1:1 + NS],
                    in1=bada.unsqueeze(2).to_broadcast([128, 48, NS]), op=ALU.add)),
                     r=["mps", "par"], w=["modS"])
            for l in range(2):
                for c0 in (8, 32):
                    P.op("dve", (lambda l=l, c0=c0: V.tensor_scalar_add(out=modP[:, l, c0:c0 + 8], in0=modP[:, l, c0:c0 + 8], scalar1=1.0)),
                         r=["modP"], w=["modP"])
                    P.op("dve", (lambda l=l, c0=c0: V.tensor_scalar_add(out=modS[:, l, c0:c0 + 8, :], in0=modS[:, l, c0:c0 + 8, :], scalar1=1.0)),
                         r=["modS"], w=["modS"])
            lam = pcol("rg_lam", 0, 8)
            P.op("act", lambda: A.activation(out=cst[:], in_=lam, func=AF.Exp, scale=-1.0), r=["par"], w=["cst"])
            P.op("act", lambda: A.activation(out=cst[:], in_=cst[:], func=AF.Ln, bias=oneb[:, 0:1], scale=1.0), r=["cst", "oneb"], w=["cst"])
            P.op("dve", lambda: V.tensor_scalar_mul(out=cst2[:], in0=cst[:], scalar1=-16.0), r=["cst"], w=["cst2"])
            P.op("dve", lambda: V.tensor_scalar_mul(out=cst[:], in0=cst[:], scalar1=-8.0), r=["cst", "cst2"], w=["cst"])
            P.flush()

        SH1, SC1, G1, SH2, SC2, G2 = 0, 8, 16, 24, 32, 40

        def norm_prompt(xkey, tag, xt, TT, l, sh0, sc0, sqb, nps, rt, hT, xn, hkey=None):
            hkey = hkey or (tag, "hT")
            P.op("act", lambda: A.activation(out=sqb[:, :, :TT], in_=xt[:, :, :TT], func=AF.Square), r=[xkey], w=[(tag, "sqb")])
            for c in range(8):
                P.op("pe", (lambda c=c: PE.matmul(nps[:, :TT], lhsT=ones_b[:], rhs=sqb[:, c, :TT], start=(c == 0), stop=(c == 7))),
                     r=[(tag, "sqb"), "ones_b"], w=[(tag, "nps")])
            P.op("act", lambda: A.activation(out=rt[:, :TT], in_=nps[:, :TT], func=AF.Sqrt, scale=1.0 / D, bias=epsb[:, 0:1]),
                 r=[(tag, "nps"), "epsb"], w=[(tag, "rt")])
            P.op("dve", lambda: V.reciprocal(out=rt[:, :TT], in_=rt[:, :TT]), r=[(tag, "rt")], w=[(tag, "rt")])
            P.op("dve", lambda: V.tensor_tensor(out=xn[:, :, :TT], in0=xt[:, :, :TT],
                                                in1=rt[:, :TT].unsqueeze(1).to_broadcast([128, 8, TT]), op=ALU.mult),
                 r=[xkey, (tag, "rt")], w=[(tag, "xn")])
            for c in range(8):
                P.op("act", (lambda c=c: A.activation(out=hT[:, c, :TT], in_=xn[:, c, :TT], func=AF.Identity,
                                                      scale=modP[:, l, sc0 + c:sc0 + c + 1], bias=modP[:, l, sh0 + c:sh0 + c + 1])),
                     r=[(tag, "xn"), "modP"], w=[hkey])

        def norm_sample(tag, l, sh0, sc0, sq, nps, rt, hT, xn):
            P.op("dve", lambda: V.tensor_tensor(out=sq[:], in0=xs_cur[:], in1=xs_cur[:], op=ALU.mult), r=["xs"], w=[(tag, "sq")])
            for c in range(8):
                P.op("pe", (lambda c=c: PE.matmul(nps[:, :NS], lhsT=ones_f[:], rhs=sq[:, c, :], start=(c == 0), stop=(c == 7))),
                     r=[(tag, "sq"), "ones_f"], w=[(tag, "nps")])
            P.op("act", lambda: A.activation(out=rt[:, :NS], in_=nps[:, :NS], func=AF.Sqrt, scale=1.0 / D, bias=epsb[:, 0:1]),
                 r=[(tag, "nps"), "epsb"], w=[(tag, "rt")])
            P.op("dve", lambda: V.reciprocal(out=rt[:, :NS], in_=rt[:, :NS]), r=[(tag, "rt")], w=[(tag, "rt")])
            P.op("dve", lambda: V.tensor_tensor(out=xn[:], in0=xs_cur[:], in1=rt[:, :NS].unsqueeze(1).to_broadcast([128, 8, NS]), op=ALU.mult),
                 r=["xs", (tag, "rt")], w=[(tag, "xn")])
            P.op("dve", lambda: V.tensor_tensor(out=xn[:], in0=xn[:], in1=modS[:, l, sc0:sc0 + 8, :], op=ALU.mult),
                 r=[(tag, "xn"), "modS"], w=[(tag, "xn")])
            P.op("dve", lambda: V.tensor_tensor(out=hT[:], in0=xn[:], in1=modS[:, l, sh0:sh0 + 8, :], op=ALU.add),
                 r=[(tag, "xn"), "modS"], w=[(tag, "hT")])

        def load_w(q_tensor, dst, src_ap_fn, nk, key):
            for kc in range(nk):
                P.dma("pool", (lambda kc=kc: G.dma_start(out=dst[:, kc, :], in_=src_ap_fn(kc))), w=[(key, kc)])

        with contextlib.ExitStack() as ph:
            w_in = SB(ph, "w_in", [128, 8, NIN_E], BF16)
            load_w(None, w_in, lambda kc: I["w_in_e"][:, kc, :], 8, "w_in")
            for s in range(NS):
                P.dma("actq", (lambda s=s: nc.scalar.dma_start(out=O["wk_s"][s, 0:LCACHE - 1, :], in_=I["ck"][s, 1:LCACHE, :])), w=[("wk", s)])
                P.dma("actq", (lambda s=s: nc.scalar.dma_start(out=O["wv_s"][s, 0:LCACHE - 1, :], in_=I["cv"][s, 1:LCACHE, :])), w=[("wv", s)])
            wsT = SB(ph, "wsT", [128, 4, 128])
            wsTb = SB(ph, "wsTb", [128, 4, 128], BF16)
            bsb = SB(ph, "bsb", [128, 512])
            lng_bc = SB(ph, "lng_bc", [128, 512]); lnb_bc = SB(ph, "lnb_bc", [128, 512])
            sg0 = SB(ph, "sg0", [128, 8])
            P.dma("sp", lambda: nc.sync.dma_start(out=wsT[:], in_=I["w_sguT"][:, :, :]), w=["wsT"])
            P.dma("sp", lambda: nc.sync.dma_start(out=bsb[:], in_=I["b_sgu"][0:1, :].partition_broadcast(128)), w=["bsb"])
            P.dma("sp", lambda: nc.sync.dma_start(out=sg0[:], in_=I["sg0"][0:1, :].partition_broadcast(128)), w=["sg0"])
            P.op("pool", lambda: G.affine_select(out=wsT[:], in_=wsT[:], pattern=[[0, 4], [1, 128]], compare_op=ALU.is_ge,
                                                 fill=0.0, base=0, channel_multiplier=-1), r=["wsT"], w=["wsT"])
            P.op("dve", lambda: V.tensor_copy(out=wsTb[:], in_=wsT[:]), r=["wsT"], w=["wsTb"])
            dg = SB(ph, "dg", [128, 8, 128])
            for i2, nm in enumerate(("ln_g", "ln_b")):
                for c in range(4):
                    P.op("dve", (lambda i2=i2, c=c, nm=nm: V.tensor_scalar_mul(out=dg[:, i2 * 4 + c, :], in0=ident_f[:], scalar1=pcol(nm, c))),
                         r=["ident_f", "par"], w=[("dg", i2 * 4 + c)])

            xt_ = [SB(ph, "xt%d" % i, [128, 8, 512]) for i in range(2)]
            sqb = SB(ph, "sqb", [128, 8, 512], BF16)
            xn = SB(ph, "xn", [128, 8, 512])
            rt = SB(ph, "rt", [128, 512])
            hT = SB(ph, "hT", [128, 8, 512], BF16)
            uT = SB(ph, "uT", [128, 4, 512])
            gv_ = [SB(ph, "gv%d" % i, [128, 512]) for i in range(2)]
            st6_ = [SB(ph, "st6_%d" % i, [128, 6]) for i in range(2)]; mv_ = [SB(ph, "mv%d" % i, [128, 2]) for i in range(2)]; rs_ = [SB(ph, "rs%d" % i, [128, 1]) for i in range(2)]
            va_ = [SB(ph, "va%d" % i, [128, 512]) for i in range(2)]; vab_ = [SB(ph, "vab%d" % i, [128, 512], BF16) for i in range(4)]
            mixs_ = [SB(ph, "mixs%d" % i, [128, 512]) for i in range(2)]
            aoT = [SB(ph, "aoT%d" % i, [128, 4, 512], BF16) for i in range(1)]
            qTb = [SB(ph, "qTb%d" % i, [128, 4, 512], BF16) for i in range(1)]
            kTf = [SB(ph, "kTf%d" % i, [128, 4, 512]) for i in range(1)]
            vTf = [SB(ph, "vTf%d" % i, [128, 4, 512]) for i in range(1)]
            kTb = [SB(ph, "kTb%d" % i, [128, 4, 512], BF16) for i in range(1)]
            vTb = [SB(ph, "vTb%d" % i, [128, 4, 512], BF16) for i in range(1)]
            nps = PS(ph, "nps1", [128, 512])
            zps = [PS(ph, "zps%d" % i, [128, 512]) for i in range(3)]
            vps = [PS(ph, "vps%d" % i, [128, 512]) for i in range(2)]
            mps2 = [PS(ph, "mixps%d" % i, [128, 512]) for i in range(2)]
            zcount = [0]
            for i2 in range(2):
                for c in range(4):
                    P.op("pe", (lambda i2=i2, c=c: PE.matmul(zps[i2][:, c * 128:(c + 1) * 128], lhsT=ones_f[:],
                                                              rhs=dg[:, i2 * 4 + c, :], start=True, stop=True)),
                         r=[("dg", i2 * 4 + c), "ones_f"], w=[("zps", i2)])
            P.op("dve", lambda: V.tensor_copy(out=lng_bc[:], in_=zps[0][:]), r=[("zps", 0)], w=["lng_bc"])
            P.op("dve", lambda: V.tensor_copy(out=lnb_bc[:], in_=zps[1][:]), r=[("zps", 1)], w=["lnb_bc"])

            def loadx(i):
                b = i % 2
                P.dma("sp", (lambda: nc.sync.dma_start(out=xt_[b][:], in_=I["xT"][:, :, i * 512:(i + 1) * 512])), w=[("xt1", b)])

            loadx(0)
            for i in range(NT5):
                b = i % 2
                if i + 1 < NT5:
                    loadx(i + 1)
                tag = "p1"
                norm_prompt(("xt1", b), tag, xt_[b], 512, 0, SH1, SC1, sqb, nps, rt, hT, xn)
                t0 = i * 512
                xb_ = b
                b = 0

                def zchunk(n):
                    z = zcount[0] % 3
                    zcount[0] += 1
                    for kc in range(8):
                        P.op("pe", (lambda kc=kc, n=n, z=z: PE.matmul(zps[z][:], lhsT=w_in[:, kc, n * 128:(n + 1) * 128], rhs=hT[:, kc, :],
                                                                       start=(kc == 0), stop=(kc == 7))),
                             r=[("w_in", kc), (tag, "hT")], w=[("zps", z)])
                    return z
                for n in range(4):
                    z = zchunk(n)
                    P.op("act", (lambda n=n, z=z: A.activation(out=uT[:, n, :], in_=zps[z][:], func=GELU)), r=[("zps", z)], w=[("uT", n)])
                for blk in range(4):
                    vb = blk % 2
                    gv, st6, mv, rs, va, vab = gv_[vb], st6_[vb], mv_[vb], rs_[vb], va_[vb], vab_[blk]
                    for kc in range(8):
                        P.op("pe", (lambda kc=kc, blk=blk, vb=vb: PE.matmul(vps[vb][:], lhsT=hT[:, kc, blk * 128:(blk + 1) * 128], rhs=w_in[:, kc, 512:1024],
                                                                             start=(kc == 0), stop=(kc == 7))),
                             r=[("w_in", kc), (tag, "hT")], w=[("vps", vb)])
                    P.op("act", (lambda gv=gv, vb=vb: A.activation(out=gv[:], in_=vps[vb][:], func=GELU)), r=[("vps", vb)], w=[("gv", vb)])
                    P.op("dve", (lambda gv=gv, st6=st6: V.bn_stats(out=st6[:], in_=gv[:])), r=[("gv", vb)], w=[("st6", vb)])
                    P.op("dve", (lambda mv=mv, st6=st6: V.bn_aggr(out=mv[:], in_=st6[:])), r=[("st6", vb)], w=[("mv", vb)])
                    P.op("act", (lambda mv=mv, rs=rs: A.activation(out=rs[:], in_=mv[:, 1:2], func=AF.Sqrt, scale=1.0, bias=epsb[:, 0:1])), r=[("mv", vb), "epsb"], w=[("rs", vb)])
                    P.op("dve", (lambda rs=rs: V.reciprocal(out=rs[:], in_=rs[:])), r=[("rs", vb)], w=[("rs", vb)])
                    P.op("dve", (lambda va=va, gv=gv, mv=mv, rs=rs: V.tensor_scalar(out=va[:], in0=gv[:], scalar1=mv[:, 0:1], scalar2=rs[:, 0:1], op0=ALU.subtract, op1=ALU.mult)),
                         r=[("gv", vb), ("mv", vb), ("rs", vb)], w=[("va", vb)])
                    P.op("pool", (lambda va=va: G.tensor_tensor(out=va[:], in0=va[:], in1=lng_bc[:], op=ALU.mult)), r=[("va", vb), "lng_bc"], w=[("va", vb)])
                    P.op("pool", (lambda va=va, vab=vab: G.tensor_tensor(out=vab[:], in0=va[:], in1=lnb_bc[:], op=ALU.add)), r=[("va", vb), "lnb_bc"], w=[("vab", blk)])

                def sgu_mix():
                    for blk in range(4):
                        vb = blk % 2
                        vab, mixs = vab_[blk], mixs_[vb]
                        for g in range(4):
                            P.op("pe", (lambda g=g, vab=vab, vb=vb: PE.matmul(mps2[vb][:, g * 128:(g + 1) * 128], lhsT=vab[:, g * 128:(g + 1) * 128], rhs=wsTb[:, g, :],
                                                                               start=True, stop=True)), r=[("vab", blk), "wsTb"], w=[("mixps", vb)])
                        P.op("dve", (lambda mixs=mixs, vb=vb: V.tensor_tensor(out=mixs[:], in0=mps2[vb][:], in1=bsb[:], op=ALU.add)), r=[("mixps", vb), "bsb"], w=[("mixs", vb)])
                        P.op("pool", (lambda blk=blk, b=b, mixs=mixs: G.tensor_tensor(out=aoT[b][:, :, blk * 128:(blk + 1) * 128],
                                                                                      in0=mixs[:].rearrange("p (g t) -> p g t", g=4),
                                                                                      in1=uT[:, :, blk * 128:(blk + 1) * 128], op=ALU.mult)),
                             r=[("mixs", vb)] + [("uT", n) for n in range(4)], w=[("aoT", b, blk)])
                    P.dma("pool", (lambda b=b, t0=t0: G.dma_start(out=AOS[:, :, t0:t0 + 512], in_=aoT[b][:])),
                          r=[("aoT", b, k) for k in range(4)], w=[("AOS", i)])

                for n in range(4):
                    z = zchunk(8 + n)
                    P.op("act", (lambda n=n, z=z, b=b: A.activation(out=qTb[b][:, n, :], in_=zps[z][:], func=AF.Copy, scale=0.125)),
                         r=[("zps", z)], w=[("qTb", b, n)])
                P.dma("pool", (lambda b=b, t0=t0: G.dma_start(out=QS[:, :, t0:t0 + 512], in_=qTb[b][:])),
                      r=[("qTb", b, k) for k in range(4)], w=[("QS", i)])
                for (off, tf, tb, SCR, OUT, nm) in ((12, kTf, kTb, KS, O["kT_o"], "k"), (16, vTf, vTb, VS, O["vT_o"], "v")):
                    for n in range(4):
                        z = zchunk(off + n)
                        P.op("act", (lambda n=n, z=z, b=b, tf=tf: A.activation(out=tf[b][:, n, :], in_=zps[z][:], func=AF.Copy)),
                             r=[("zps", z)], w=[(nm + "f", b, n)])
                        P.op("dve", (lambda n=n, b=b, tf=tf, tb=tb: V.tensor_copy(out=tb[b][:, n, :], in_=tf[b][:, n, :])),
                             r=[(nm + "f", b, n)], w=[(nm + "b", b, n)])
                    P.dma("pool", (lambda b=b, t0=t0, tb=tb, SCR=SCR: G.dma_start(out=SCR[:, :, t0:t0 + 512], in_=tb[b][:])),
                          r=[(nm + "b", b, k) for k in range(4)], w=[(nm + "S", i)])
                    if t0 >= T - TK:
                        o0 = t0 - (T - TK)
                        P.dma("pool", (lambda b=b, o0=o0, tf=tf, OUT=OUT: G.dma_start(out=OUT[:, :, o0:o0 + 512], in_=tf[b][:])),
                              r=[(nm + "f", b, k) for k in range(4)], w=[(nm + "O", i)])
                sgu_mix()

            sq_s = SB(ph, "sq_s", [128, 8, NS]); xn_s = SB(ph, "xn_s", [128, 8, NS]); hTs = SB(ph, "hTs", [128, 8, NS], BF16)
            guv = SB(ph, "guv", [128, 8, NS]); gsq = SB(ph, "gsq", [128, 4, NS])
            mean_s = SB(ph, "mean_s", [128, NS]); var_s = SB(ph, "var_s", [128, NS]); msq_s = SB(ph, "msq_s", [128, NS])
            vas = SB(ph, "vas", [128, 4, NS]); aos = SB(ph, "aos", [128, 4, NS]); aosb = SB(ph, "aosb", [128, 4, NS], BF16)
            qkv = SB(ph, "qkv", [NS, 1536])
            norm_sample("s1", 0, SH1, SC1, sq_s, nps, rt, hTs, xn_s)
            for n in range(8):
                for kc in range(8):
                    P.op("pe", (lambda n=n, kc=kc: PE.matmul(zps[0][:, n * NS:(n + 1) * NS], lhsT=w_in[:, kc, n * 128:(n + 1) * 128], rhs=hTs[:, kc, :],
                                                              start=(kc == 0), stop=(kc == 7))), r=[("w_in", kc), ("s1", "hT")], w=[("zps", 0)])
            P.op("act", lambda: A.activation(out=guv[:], in_=zps[0][:, 0:8 * NS].rearrange("p (c s) -> p c s", c=8), func=GELU),
                 r=[("zps", 0)], w=["guv"])
            P.op("dve", lambda: V.tensor_tensor(out=gsq[:], in0=guv[:, 4:8, :], in1=guv[:, 4:8, :], op=ALU.mult), r=["guv"], w=["gsq"])
            for c in range(4):
                P.op("pe", (lambda c=c: PE.matmul(zps[1][:, 0:NS], lhsT=ones_f[:], rhs=guv[:, 4 + c, :], start=(c == 0), stop=(c == 3))),
                     r=["guv", "ones_f"], w=[("zps", 1)])
            for c in range(4):
                P.op("pe", (lambda c=c: PE.matmul(zps[1][:, 8:8 + NS], lhsT=ones_f[:], rhs=gsq[:, c, :], start=(c == 0), stop=(c == 3))),
                     r=["gsq", "ones_f"], w=[("zps", 1)])
            P.op("dve", lambda: V.tensor_scalar_mul(out=mean_s[:], in0=zps[1][:, 0:NS], scalar1=1.0 / WA), r=[("zps", 1)], w=["mean_s"])
            P.op("dve", lambda: V.tensor_scalar_mul(out=var_s[:], in0=zps[1][:, 8:8 + NS], scalar1=1.0 / WA), r=[("zps", 1)], w=["var_s"])
            P.op("dve", lambda: V.tensor_tensor(out=msq_s[:], in0=mean_s[:], in1=mean_s[:], op=ALU.mult), r=["mean_s"], w=["msq_s"])
            P.op("dve", lambda: V.tensor_tensor(out=var_s[:], in0=var_s[:], in1=msq_s[:], op=ALU.subtract), r=["var_s", "msq_s"], w=["var_s"])
            P.op("act", lambda: A.activation(out=var_s[:], in_=var_s[:], func=AF.Sqrt, scale=1.0, bias=epsb[:, 0:1]), r=["var_s", "epsb"], w=["var_s"])
            P.op("dve", lambda: V.reciprocal(out=var_s[:], in_=var_s[:]), r=["var_s"], w=["var_s"])
            P.op("dve", lambda: V.tensor_tensor(out=vas[:], in0=guv[:, 4:8, :], in1=mean_s[:].unsqueeze(1).to_broadcast([128, 4, NS]), op=ALU.subtract),
                 r=["guv", "mean_s"], w=["vas"])
            P.op("dve", lambda: V.tensor_tensor(out=vas[:], in0=vas[:], in1=var_s[:].unsqueeze(1).to_broadcast([128, 4, NS]), op=ALU.mult),
                 r=["vas", "var_s"], w=["vas"])
            P.op("dve", lambda: V.tensor_tensor(out=vas[:], in0=vas[:], in1=pcol("ln_g", 0, 4).unsqueeze(2).to_broadcast([128, 4, NS]), op=ALU.mult),
                 r=["vas", "par"], w=["vas"])
            P.op("dve", lambda: V.tensor_tensor(out=vas[:], in0=vas[:], in1=pcol("ln_b", 0, 4).unsqueeze(2).to_broadcast([128, 4, NS]), op=ALU.add),
                 r=["vas", "par"], w=["vas"])
            P.dma("sp", lambda: nc.sync.dma_start(out=O["cv_o"][:, :, :], in_=vas[:]), r=["vas"], w=["cv_o"])
            P.op("dve", lambda: V.tensor_tensor(out=aos[:], in0=vas[:], in1=sg0[:, 0:4].unsqueeze(2).to_broadcast([128, 4, NS]), op=ALU.mult),
                 r=["vas", "sg0"], w=["aos"])
            P.op("dve", lambda: V.tensor_tensor(out=aos[:], in0=aos[:], in1=sg0[:, 4:8].unsqueeze(2).to_broadcast([128, 4, NS]), op=ALU.add),
                 r=["aos", "sg0"], w=["aos"])
            P.op("dve", lambda: V.tensor_tensor(out=aosb[:], in0=aos[:], in1=guv[:, 0:4, :], op=ALU.mult), r=["aos", "guv"], w=["aosb"])
            AOSS = dscr("aoT_ss", [128, 4, NS], BF16)
            P.dma("sp", lambda: nc.sync.dma_start(out=AOSS[:, :, :], in_=aosb[:]), r=["aosb"], w=["AOSS"])
            for blk3 in range(3):
                for kc in range(8):
                    P.op("pe", (lambda blk3=blk3, kc=kc: PE.matmul(zps[2][0:NS, :], lhsT=hTs[:, kc, :], rhs=w_in[:, kc, 1024 + blk3 * 512: 1024 + (blk3 + 1) * 512],
                                                                    start=(kc == 0), stop=(kc == 7))), r=[("w_in", kc), ("s1", "hT")], w=[("zps", 2)])
                P.op("act", (lambda blk3=blk3: A.activation(out=qkv[:, blk3 * 512:(blk3 + 1) * 512], in_=zps[2][0:NS, :], func=AF.Copy,
                                                            scale=(0.125 if blk3 == 0 else 1.0))), r=[("zps", 2)], w=["qkv"])
            P.dma("sp", lambda: nc.sync.dma_start(out=QKVS[:, :], in_=qkv[:]), r=["qkv"], w=["QKVS"])
            P.flush()

        with contextlib.ExitStack() as ph:
            qT = SB(ph, "qT", [128, 4, T], BF16); kT = SB(ph, "kT", [128, 4, T], BF16); vT = SB(ph, "vT", [128, 4, T], BF16)
            NBT = T // 128
            Vp = SB(ph, "Vp", [128, NBT, 8 * 65], BF16)
            mask2 = SB(ph, "mask2", [128, 2, 128], BF16)
            E_ = [SB(ph, "E%d" % i, [128, 4, 256], BF16) for i in range(2)]
            PTs = [SB(ph, "PTs%d" % i, [128, 4, 2, 128], BF16) for i in range(2)]
            Osb = [SB(ph, "Osb%d" % i, [128, 8, 66]) for i in range(4)]
            negm = [SB(ph, "negm%d" % i, [128, 4]) for i in range(2)]
            Sps = [PS(ph, "Sps%d" % i, [128, 1024]) for i in range(2)]
            PTp = [PS(ph, "PTp%d" % i, [128, 1024], BF16) for i in range(2)]
            Ops = [PS(ph, "Ops%d" % i, [128, 512]) for i in range(2)]
            P.dma("sp", lambda: nc.sync.dma_start(out=qT[:], in_=QS[:, :, :]), w=["qT"])
            P.dma("sp", lambda: nc.sync.dma_start(out=kT[:], in_=KS[:, :, :]), w=["kT"])
            P.dma("sp", lambda: nc.sync.dma_start(out=vT[:], in_=VS[:, :, :]), w=["vT"])
            P.op("pool", lambda: G.memset(Vp[:], 1.0), w=[("Vp", k) for k in range(NBT)])
            P.op("pool", lambda: G.memset(mask2[:], 1.0), w=["mask2"])
            P.op("pool", lambda: G.affine_select(out=mask2[:, 0, :], in_=mask2[:, 0, :], pattern=[[-1, 128]], compare_op=ALU.is_ge, fill=0.0,
                                                 base=0, channel_multiplier=1), r=["mask2"], w=["mask2"])
            P.op("pool", lambda: G.affine_select(out=mask2[:, 1, :], in_=mask2[:, 1, :], pattern=[[1, 128]], compare_op=ALU.is_ge, fill=0.0,
                                                 base=0, channel_multiplier=-1), r=["mask2"], w=["mask2"])
            u = 0
            ucnt = [0]
            for pi, (wdw, d) in enumerate(PATTERNS):
                M = T // d
                nb = M // 128
                for r_ in range(d):
                    for b_ in range(nb):
                        bi = r_ * nb + b_
                        tok = slice(r_ + d * 128 * b_, r_ + d * 128 * b_ + d * 127 + 1, d)
                        pb = u % 2
                        for c in range(4):
                            P.op("pe", (lambda c=c, tok=tok, pb=pb: PE.transpose(PTp[pb][:, c * 128:(c + 1) * 128], vT[:, c, tok], ident_b[:])),
                                 r=["vT", "ident_b"], w=[("PTp", pb)])
                        P.op("act", (lambda bi=bi, pb=pb: A.activation(out=Vp[:, bi, :].rearrange("p (h e) -> p h e", h=8)[:, :, 0:64],
                                                                      in_=PTp[pb][:, 0:512].rearrange("p (h e) -> p h e", h=8), func=AF.Copy)),
                             r=[("PTp", pb)], w=[("Vp", bi)])
                        u += 1
                units = []
                for r_ in range(d):
                    for b_ in range(nb):
                        bi = r_ * nb + b_
                        nkb = 2 if b_ > 0 else 1
                        nk = 128 * nkb
                        qtok = slice(r_ + d * 128 * b_, r_ + d * 128 * b_ + d * 127 + 1, d)
                        k0 = r_ + d * 128 * (b_ - (nkb - 1))
                        ktok = slice(k0, k0 + d * (nk - 1) + 1, d)
                        for hh in range(2):
                            units.append(dict(bi=bi, nkb=nkb, nk=nk, qtok=qtok, ktok=ktok, hh=hh, ob=bi % 4))
                for ui, U in enumerate(units):
                    U["pb"] = (ucnt[0] + ui) % 2
                ucnt[0] += len(units)

                def colS(h4):
                    return (h4 % 2) * 512 + (h4 // 2) * 256

                def stageA(U):
                    pb, nk, hh, ob, qtok, ktok = U["pb"], U["nk"], U["hh"], U["ob"], U["qtok"], U["ktok"]
                    for h4 in range(4):
                        h = hh * 4 + h4
                        c, po = h // 2, (h % 2) * 64
                        P.op("pe", (lambda h4=h4, c=c, po=po: PE.matmul(Sps[pb][:, colS(h4): colS(h4) + nk], lhsT=qT[po:po + 64, c, qtok], rhs=kT[po:po + 64, c, ktok],
                                                                         start=True, stop=True)), r=["qT", "kT"], w=[("Sps", pb)])
                    for bk in range(2):
                        Sv = Sps[pb][:, bk * 512:(bk + 1) * 512].rearrange("p (h k) -> p h k", h=2)[:, :, 0:nk]
                        P.op("dve", (lambda Sv=Sv, bk=bk: V.tensor_reduce(out=Osb[ob][:, hh * 4 + bk:hh * 4 + bk + 3:2, 65], in_=Sv, axis=AX.X, op=ALU.max)),
                             r=[("Sps", pb)], w=[("Osb", ob, hh, "m")])
                    P.op("dve", (lambda: V.tensor_scalar_mul(out=negm[pb][:], in0=Osb[ob][:, hh * 4:(hh + 1) * 4, 65], scalar1=-1.0)),
                         r=[("Osb", ob, hh, "m")], w=[("negm", pb)])
                    for h4 in range(4):
                        P.op("act", (lambda h4=h4: A.activation(out=E_[pb][:, h4, 0:nk], in_=Sps[pb][:, colS(h4):colS(h4) + nk], func=AF.Exp,
                                                                 bias=negm[pb][:, h4:h4 + 1], scale=1.0)),
                             r=[("Sps", pb), ("negm", pb)], w=[("E", pb)])

                def stageB(U):
                    pb, nkb = U["pb"], U["nkb"]
                    for h4 in range(4):
                        for kb in range(nkb):
                            P.op("pe", (lambda h4=h4, kb=kb: PE.transpose(PTp[pb][:, (h4 * 2 + kb) * 128:(h4 * 2 + kb + 1) * 128],
                                                                          E_[pb][:, h4, kb * 128:(kb + 1) * 128], ident_b[:])),
                                 r=[("E", pb), "ident_b"], w=[("PTp", pb)])
                    PTv = PTp[pb][:].rearrange("p (h k q) -> p h k q", h=4, k=2)
                    if nkb == 2:
                        P.op("dve", (lambda: V.tensor_tensor(out=PTs[pb][:], in0=PTv, in1=mask2[:].unsqueeze(1).to_broadcast([128, 4, 2, 128]), op=ALU.mult)),
                             r=[("PTp", pb), "mask2"], w=[("PTs", pb)])
                    else:
                        P.op("dve", (lambda: V.tensor_tensor(out=PTs[pb][:, :, 0, :], in0=PTv[:, :, 0, :],
                                                             in1=mask2[:, 1, :].unsqueeze(1).to_broadcast([128, 4, 128]), op=ALU.mult)),
                             r=[("PTp", pb), "mask2"], w=[("PTs", pb)])

                def stageC(U, pi=pi):
                    pb, nkb, hh, ob, bi, qtok = U["pb"], U["nkb"], U["hh"], U["ob"], U["bi"], U["qtok"]
                    for h4 in range(4):
                        h = hh * 4 + h4
                        for kb in range(nkb):
                            kbi = bi - (nkb - 1) + kb
                            P.op("pe", (lambda h4=h4, h=h, kb=kb, kbi=kbi: PE.matmul(Ops[pb][:, h4 * 128:h4 * 128 + 65], lhsT=PTs[pb][:, h4, kb, :],
                                                                                     rhs=Vp[:, kbi, h * 65:(h + 1) * 65], start=(kb == 0), stop=(kb == nkb - 1))),
                                 r=[("PTs", pb), ("Vp", kbi)], w=[("Ops", pb)])
                    P.op("act", (lambda: A.activation(out=Osb[ob][:, hh * 4:(hh + 1) * 4, 0:65],
                                                      in_=Ops[pb][:].rearrange("p (h e) -> p h e", h=4)[:, :, 0:65], func=AF.Copy)),
                         r=[("Ops", pb)], w=[("Osb", ob, hh, "o")])
                    if hh == 1:
                        orow = OS[pi, qtok, :]
                        P.dma("pool", (lambda: G.dma_start(out=orow, in_=Osb[ob][:].rearrange("p h e -> p (h e)"))),
                              r=[("Osb", ob, 0, "o"), ("Osb", ob, 1, "o"), ("Osb", ob, 0, "m"), ("Osb", ob, 1, "m")], w=[("OS", pi, bi)])

                nu = len(units)
                for st in range(nu + 2):
                    if st < nu:
                        stageA(units[st])
                    if 0 <= st - 1 < nu:
                        stageB(units[st - 1])
                    if 0 <= st - 2 < nu:
                        stageC(units[st - 2])
            P.flush()

        with contextlib.ExitStack() as ph:
            Kt = [SB(ph, "Kt%d" % i, [128, 512]) for i in range(2)]
            Vt = [SB(ph, "Vt%d" % i, [128, 512]) for i in range(16)]
            qbc = [SB(ph, "qbc%d" % i, [128, 512]) for i in range(NS)]
            prod = SB(ph, "prod", [128, 512])
            ST = SB(ph, "ST", [128, 4, 128])
            Es = SB(ph, "Es", [128, 4, 128]); PTs2 = SB(ph, "PTs2", [128, 4, 128])
            mx = SB(ph, "mx_s", [128, 1]); den = SB(ph, "den_s", [128, 1])
            rowm = SB(ph, "rowm", [128, NS]); dmk = SB(ph, "dmk", [128, 512])
            acc = SB(ph, "acc_s", [128, 512]); bo = SB(ph, "bo_s", [128, 64])
            boT = SB(ph, "boT_s", [64, 128], BF16)
            w_oe = SB(ph, "w_oe_s", [128, 4, D], BF16); w_oh = SB(ph, "w_oh_s", [64, 8, D], BF16)
            aosb2 = SB(ph, "aosb2", [128, 4, NS], BF16)
            mo_s = SB(ph, "mo_s", [128, 8, NS])
            Sp = PS(ph, "Sp_s", [128, 512]); PTp2 = PS(ph, "PTp_s", [128, 512])
            Opp = [PS(ph, "Opp%d" % i, [128, 512]) for i in range(NS)]
            bTp = PS(ph, "bTp", [64, 128]); mop = PS(ph, "mop_s", [128, 8 * NS])
            P.dma("sp", lambda: nc.sync.dma_start(out=rowm[:], in_=I["rowmask"][:, :]), w=["rowm"])
            P.dma("sp", lambda: nc.sync.dma_start(out=dmk[:], in_=I["dmask"][:, :]), w=["dmk"])
            for kc in range(4):
                P.dma("pool", (lambda kc=kc: G.dma_start(out=w_oe[:, kc, :], in_=I["w_out_e"][:, kc, :])), w=[("w_oe", kc)])
            for h in range(8):
                P.dma("pool", (lambda h=h: G.dma_start(out=w_oh[:, h, :], in_=I["w_out_eh"][:, h, :])), w=[("w_oh", h)])
            P.dma("sp", lambda: nc.sync.dma_start(out=aosb2[:], in_=AOSS[:, :, :]), w=["aosb2"])
            P.op("pool", lambda: G.memset(ST[:], 0.0), w=["ST"])
            for s in range(NS):
                P.dma("sp", (lambda s=s: nc.sync.dma_start(out=O["wk_s"][s, LCACHE - 1:LCACHE, :], in_=QKVS[s:s + 1, 512:1024])), w=[("wk2", s)])
                P.dma("sp", (lambda s=s: nc.sync.dma_start(out=O["wv_s"][s, LCACHE - 1:LCACHE, :], in_=QKVS[s:s + 1, 1024:1536])), w=[("wv2", s)])
            for s in range(NS):
                P.dma("sp", (lambda s=s: nc.sync.dma_start(out=qbc[s][:], in_=QKVS[s:s + 1, 0:512].partition_broadcast(128))), w=[("qbc", s)])
            it = 0
            for s in range(NS):
                for p in range(4):
                    kb_ = it % 2
                    vi = s * 4 + p
                    if p < 3:
                        dd = PATTERNS[p][1]
                        r0 = LCACHE - 128 * dd
                        ksrc = I["ck"][s, r0:r0 + 127 * dd + 1:dd, :]
                        vsrc = I["cv"][s, r0:r0 + 127 * dd + 1:dd, :]
                    else:
                        ksrc = QKVS[s:s + 1, 512:1024].partition_broadcast(128)
                        vsrc = QKVS[s:s + 1, 1024:1536].partition_broadcast(128)
                    P.dma("sp", (lambda kb_=kb_, ksrc=ksrc: nc.sync.dma_start(out=Kt[kb_][:], in_=ksrc)), w=[("Kt", kb_)])
                    P.dma("sp", (lambda vi=vi, vsrc=vsrc: nc.sync.dma_start(out=Vt[vi][:], in_=vsrc)), w=[("Vt", vi)])
                    P.op("dve", (lambda kb_=kb_, s=s: V.tensor_tensor(out=prod[:], in0=Kt[kb_][:], in1=qbc[s][:], op=ALU.mult)),
                         r=[("Kt", kb_), ("qbc", s)], w=["prod"])
                    P.op("dve", (lambda s=s, p=p: V.tensor_reduce(out=ST[:, p, 32 * s:32 * s + 8], in_=prod[:].rearrange("p (h e) -> p h e", h=8),
                                                                  axis=AX.X, op=ALU.add)), r=["prod"], w=["ST"])
                    it += 1
            for p in range(4):
                P.op("pe", (lambda p=p: PE.transpose(Sp[:, p * 128:(p + 1) * 128], ST[:, p, :], ident_f[:])), r=["ST", "ident_f"], w=["Sp"])
            P.op("dve", lambda: V.tensor_reduce(out=mx[:], in_=Sp[:], axis=AX.X, op=ALU.max, negate=True), r=["Sp"], w=["mx"])
            P.op("act", lambda: A.activation(out=Es[:].rearrange("p a b -> p (a b)"), in_=Sp[:], func=AF.Exp, bias=mx[:, 0:1], scale=1.0), r=["Sp", "mx"], w=["Es"])
            P.op("dve", lambda: V.tensor_scalar_mul(out=Es[:, 3, :], in0=Es[:, 3, :], scalar1=3.0 / 128.0), r=["Es"], w=["Es"])
            P.op("dve", lambda: V.tensor_reduce(out=den[:], in_=Es[:].rearrange("p a b -> p (a b)"), axis=AX.X, op=ALU.add), r=["Es"], w=["den"])
            P.op("dve", lambda: V.reciprocal(out=den[:], in_=den[:]), r=["den"], w=["den"])
            for p in range(4):
                P.op("pe", (lambda p=p: PE.transpose(PTp2[:, p * 128:(p + 1) * 128], Es[:, p, :], ident_f[:])), r=["Es", "ident_f"], w=["PTp2"])
            P.op("dve", lambda: V.tensor_copy(out=PTs2[:].rearrange("p a b -> p (a b)"), in_=PTp2[:]), r=["PTp2"], w=["PTs2"])
            for s in range(NS):
                for p in range(4):
                    P.op("pe", (lambda s=s, p=p: PE.matmul(Opp[s][:], lhsT=PTs2[:, p, :], rhs=Vt[s * 4 + p][:], start=(p == 0), stop=(p == 3))),
                         r=["PTs2", ("Vt", s * 4 + p)], w=[("Opp", s)])
            P.op("dve", lambda: V.tensor_scalar_mul(out=acc[:], in0=Opp[0][:], scalar1=rowm[:, 0:1]), r=[("Opp", 0), "rowm"], w=["acc"])
            for s in range(1, NS):
                P.op("dve", (lambda s=s: V.scalar_tensor_tensor(out=acc[:], in0=Opp[s][:], scalar=rowm[:, s:s + 1], in1=acc[:], op0=ALU.mult, op1=ALU.add)),
                     r=[("Opp", s), "rowm", "acc"], w=["acc"])
            P.op("dve", lambda: V.tensor_tensor(out=acc[:], in0=acc[:], in1=dmk[:], op=ALU.mult), r=["acc", "dmk"], w=["acc"])
            P.op("dve", lambda: V.tensor_reduce(out=bo[:], in_=acc[:].rearrange("p (h e) -> p e h", h=8), axis=AX.X, op=ALU.add), r=["acc"], w=["bo"])
            P.op("dve", lambda: V.tensor_scalar_mul(out=bo[:], in0=bo[:], scalar1=den[:, 0:1]), r=["bo", "den"], w=["bo"])
            P.op("pe", lambda: PE.transpose(bTp[:], bo[:], ident_f[:]), r=["bo", "ident_f"], w=["bTp"])
            P.op("dve", lambda: V.tensor_copy(out=boT[:], in_=bTp[:]), r=["bTp"], w=["boT"])
            for n in range(8):
                for kc in range(4):
                    P.op("pe", (lambda n=n, kc=kc: PE.matmul(mop[:, n * NS:(n + 1) * NS], lhsT=w_oe[:, kc, n * 128:(n + 1) * 128], rhs=aosb2[:, kc, :],
                                                              start=(kc == 0), stop=False)), r=[("w_oe", kc), "aosb2"], w=["mop"])
                for h in range(8):
                    P.op("pe", (lambda n=n, h=h: PE.matmul(mop[:, n * NS:(n + 1) * NS], lhsT=w_oh[:, h, n * 128:(n + 1) * 128], rhs=boT[:, h:128:32],
                                                            start=False, stop=(h == 7))), r=[("w_oh", h), "boT"], w=["mop"])
            P.op("dve", lambda: V.tensor_tensor(out=mo_s[:], in0=mop[:].rearrange("p (c s) -> p c s", c=8), in1=modS[:, 0, G1:G1 + 8, :], op=ALU.mult),
                 r=["mop", "modS"], w=["mo_s"])
            P.op("dve", lambda: V.tensor_tensor(out=xs_cur[:], in0=xs_cur[:], in1=mo_s[:], op=ALU.add), r=["mo_s", "xs"], w=["xs"])
            P.flush()

        with contextlib.ExitStack() as ph:
            w_o = SB(ph, "w_o", [128, 8, D], BF16)
            load_w(None, w_o, lambda kc: I["w_out_e"][:, kc, :], 8, "w_o")
            xt_ = [SB(ph, "xt3_%d" % i, [128, 8, 512]) for i in range(2)]
            ao_ = [SB(ph, "ao3_%d" % i, [128, 4, 512], BF16) for i in range(2)]
            Om = [SB(ph, "Om%d" % i, [128, 4, 3, 8 * 66]) for i in range(2)]
            Mx = SB(ph, "Mx", [128, 8]); dm = SB(ph, "dm", [128, 3, 8]); ee = SB(ph, "ee", [128, 3, 8])
            wt = SB(ph, "wt", [128, 3, 8, 65]); ac = SB(ph, "ac", [128, 8, 65]); rd = SB(ph, "rd", [128, 8])
            bob = SB(ph, "bob", [128, 8, 64], BF16)
            boT3 = SB(ph, "boT3", [128, 4, 512], BF16)
            x1t = [SB(ph, "x1t%d" % i, [128, 8, 512]) for i in range(2)]
            tp = [PS(ph, "tp3_%d" % i, [128, 1024], BF16) for i in range(2)]
            ops_ = [PS(ph, "op3_%d" % i, [128, 512]) for i in range(3)]

            def load3(i):
                b = i % 2
                t0 = i * 512
                P.dma("sp", (lambda: nc.sync.dma_start(out=xt_[b][:], in_=I["xT"][:, :, t0:t0 + 512])), w=[("xt3", b)])
                P.dma("sp", (lambda: nc.sync.dma_start(out=ao_[b][:], in_=AOS[:, :, t0:t0 + 512])), w=[("ao3", b)])
                for pi in range(3):
                    P.dma("sp", (lambda pi=pi: nc.sync.dma_start(out=Om[b][:, :, pi, :], in_=OS[pi, t0:t0 + 512, :].rearrange("(k i) f -> i k f", i=128))),
                          w=[("Om", b, pi)])
            load3(0)
            oc = 0
            for i in range(NT5):
                b = i % 2
                t0 = i * 512
                if i + 1 < NT5:
                    load3(i + 1)
                for blk in range(4):
                    Ov = Om[b][:, blk, :, :].rearrange("p a (h e) -> p a h e", h=8)
                    omk = [("Om", b, pi) for pi in range(3)]
                    P.op("dve", (lambda Ov=Ov: V.tensor_tensor(out=Mx[:], in0=Ov[:, 0, :, 65], in1=Ov[:, 1, :, 65], op=ALU.max)), r=omk, w=["Mx"])
                    P.op("dve", (lambda Ov=Ov: V.tensor_tensor(out=Mx[:], in0=Mx[:], in1=Ov[:, 2, :, 65], op=ALU.max)), r=omk + ["Mx"], w=["Mx"])
                    P.op("dve", (lambda Ov=Ov: V.tensor_tensor(out=dm[:], in0=Ov[:, :, :, 65], in1=Mx[:].unsqueeze(1).to_broadcast([128, 3, 8]), op=ALU.subtract)),
                         r=omk + ["Mx"], w=["dm"])
                    P.op("act", lambda: A.activation(out=ee[:], in_=dm[:], func=AF.Exp), r=["dm"], w=["ee"])
                    P.op("pool", (lambda Ov=Ov: G.tensor_tensor(out=wt[:], in0=Ov[:, :, :, 0:65], in1=ee[:].unsqueeze(3).to_broadcast([128, 3, 8, 65]), op=ALU.mult)),
                         r=omk + ["ee"], w=["wt"])
                    P.op("dve", lambda: V.tensor_tensor(out=ac[:], in0=wt[:, 0, :, :], in1=wt[:, 1, :, :], op=ALU.add), r=["wt"], w=["ac"])
                    P.op("dve", lambda: V.tensor_tensor(out=ac[:], in0=ac[:], in1=wt[:, 2, :, :], op=ALU.add), r=["wt", "ac"], w=["ac"])
                    P.op("dve", lambda: V.reciprocal(out=rd[:], in_=ac[:, :, 64]), r=["ac"], w=["rd"])
                    P.op("dve", lambda: V.tensor_tensor(out=bob[:], in0=ac[:, :, 0:64], in1=rd[:].unsqueeze(2).to_broadcast([128, 8, 64]), op=ALU.mult),
                         r=["ac", "rd"], w=["bob"])
                    tb_ = (i * 4 + blk) % 2
                    for c in range(4):
                        P.op("pe", (lambda c=c, tb_=tb_: PE.transpose(tp[tb_][:, c * 128:(c + 1) * 128], bob[:].rearrange("p h e -> p (h e)")[:, c * 128:(c + 1) * 128], ident_b[:])),
                             r=["bob", "ident_b"], w=[("tp3", tb_)])
                    P.op("act", (lambda blk=blk, tb_=tb_: A.activation(out=boT3[:, :, blk * 128:(blk + 1) * 128],
                                                                      in_=tp[tb_][:, 0:512].rearrange("p (c t) -> p c t", c=4), func=AF.Copy)),
                         r=[("tp3", tb_)], w=[("boT3", blk)])
                for n in range(8):
                    z = oc % 3
                    oc += 1
                    for kc in range(8):
                        rhs = ao_[b][:, kc, :] if kc < 4 else boT3[:, kc - 4, :]
                        rk = [("ao3", b)] if kc < 4 else [("boT3", k) for k in range(4)]
                        P.op("pe", (lambda n=n, kc=kc, z=z, rhs=rhs: PE.matmul(ops_[z][:], lhsT=w_o[:, kc, n * 128:(n + 1) * 128], rhs=rhs, start=(kc == 0), stop=(kc == 7))),
                             r=[("w_o", kc)] + rk, w=[("op3", z)])
                    P.op("dve", (lambda n=n, z=z, b=b: V.scalar_tensor_tensor(out=x1t[b][:, n, :], in0=ops_[z][:], scalar=modP[:, 0, G1 + n:G1 + n + 1],
                                                                               in1=xt_[b][:, n, :], op0=ALU.mult, op1=ALU.add)),
                         r=[("op3", z), ("xt3", b), "modP"], w=[("x1t", b, n)])
                P.dma("pool", (lambda b=b, t0=t0: G.dma_start(out=X1[:, :, t0:t0 + 512], in_=x1t[b][:])), r=[("x1t", b, n) for n in range(8)], w=[("X1", i)])
            P.flush()

        def ffn_phase(l, XIN, XOUT, final):
            with contextlib.ExitStack() as ph:
                TT = 256
                w_up = SB(ph, "w_up%d" % l, [128, 8, F2], BF16)
                w_dn = SB(ph, "w_dn%d" % l, [128, NJ, D], BF16)
                JG = [(0, 6), (6, 12), (12, 18), (18, 22)]
                jgrp = {}
                for g, (j0, j1) in enumerate(JG):
                    for j in range(j0, j1):
                        jgrp[j] = g
                    for kc in range(8):
                        P.dma("pool", (lambda kc=kc, j0=j0, j1=j1: G.dma_start(out=w_up[:, kc, j0 * 128:j1 * 128], in_=I["w_up"][l, :, kc, j0 * 128:j1 * 128])),
                              w=[("w_up", kc, g, 0)])
                        P.dma("pool", (lambda kc=kc, j0=j0, j1=j1: G.dma_start(out=w_up[:, kc, (NJ + j0) * 128:(NJ + j1) * 128],
                                                                                in_=I["w_up"][l, :, kc, (NJ + j0) * 128:(NJ + j1) * 128])),
                              w=[("w_up", kc, g, 1)])
                    for j in range(j0, j1):
                        P.dma("pool", (lambda j=j: G.dma_start(out=w_dn[:, j, :], in_=I["w_dn"][l, :, j, :])), w=[("w_dn", j)])
                xt_ = [SB(ph, "xtf%d_%d" % (l, i), [128, 8, TT]) for i in range(2)]
                sqb = SB(ph, "sqbf%d" % l, [128, 8, TT], BF16); xn = SB(ph, "xnf%d" % l, [128, 8, TT]); rt = SB(ph, "rtf%d" % l, [128, TT])
                hT_ = [SB(ph, "hTf%d_%d" % (l, i), [128, 8, TT], BF16) for i in range(2)]
                up = [SB(ph, "up%d_%d" % (l, i), [128, 2, 2 + TT]) for i in range(2)]
                uc = [SB(ph, "uc%d_%d" % (l, i), [128, 2, TT]) for i in range(2)]
                sa = [SB(ph, "sa%d_%d" % (l, i), [128, TT]) for i in range(2)]
                actj = [SB(ph, "actj%d_%d" % (l, i), [128, TT], BF16) for i in range(2)]
                hist = SB(ph, "hist%d" % l, [128, 44, 2])
                _xo1 = SB(ph, "xo%d_0" % l, [128, 8, TT])
                xo = [_xo1, _xo1]
                nps = PS(ph, "npsf%d" % l, [128, 512])
                ups = [PS(ph, "ups%d_%d" % (l, i), [128, 512]) for i in range(3)]
                accp = PS(ph, "accp%d" % l, [128, 8 * TT])
                yo = None
                if final:
                    rty = SB(ph, "rty", [128, TT])
                fcw = lambda k, j: par[:, PAR_OFF["f_cw"] + (l * 3 + k) * 44 + j: PAR_OFF["f_cw"] + (l * 3 + k) * 44 + j + 1]
                fcb = lambda j: par[:, PAR_OFF["f_cb"] + l * 44 + j: PAR_OFF["f_cb"] + l * 44 + j + 1]
                P.op("pool", lambda: G.memset(hist[:], 0.0), w=[("hist", j) for j in range(44)])

                def loadf(i):
                    b = i % 2
                    P.dma("sp", (lambda: nc.sync.dma_start(out=xt_[b][:], in_=XIN[:, :, i * TT:(i + 1) * TT])), w=[("xtf", b)])
                loadf(0)
                jc_ = [0]
                tag = "ff"
                norm_prompt(("xtf", 0), tag, xt_[0], TT, l, SH2, SC2, sqb, nps, rt, hT_[0], xn, hkey=("hTf", 0))
                for i in range(NT2):
                    b = i % 2
                    if i + 1 < NT2:
                        loadf(i + 1)
                    hT = hT_[b]
                    hk = ("hTf", b)
                    zub = {}

                    def emit_up(j):
                        z = jc_[0] % 3
                        ub = jc_[0] % 2
                        jc_[0] += 1
                        zub[j] = (z, ub)
                        for half, jj in ((0, j), (1, j + NJ)):
                            for kc in range(8):
                                P.op("pe", (lambda half=half, jj=jj, kc=kc, z=z, hT=hT: PE.matmul(ups[z][:, half * TT:(half + 1) * TT], lhsT=w_up[:, kc, jj * 128:(jj + 1) * 128],
                                                                                            rhs=hT[:, kc, :], start=(kc == 0), stop=(kc == 7))),
                                     r=[("w_up", kc, jgrp[j], half), hk], w=[("ups", z)])
                    def elemA(j):
                        z, ub = zub[j]
                        P.op("act", (lambda z=z, ub=ub: A.activation(out=up[ub][:, :, 2:2 + TT], in_=ups[z][:, 0:2 * TT].rearrange("p (h t) -> p h t", h=2), func=AF.Copy)),
                             r=[("ups", z)], w=[("up", ub, 0), ("up", ub, 1)])
                        for half, jj in ((0, j), (1, j + NJ)):
                            P.op("pool", (lambda half=half, jj=jj, ub=ub: G.tensor_copy(out=up[ub][:, half, 0:2], in_=hist[:, jj, :])),
                                 r=[("hist", jj)], w=[("up", ub, half, "h")])
                            P.op("act", (lambda half=half, jj=jj, z=z, ub=ub: A.activation(out=uc[ub][:, half, :], in_=ups[z][:, half * TT:(half + 1) * TT], func=AF.Identity,
                                                                                            scale=fcw(2, jj), bias=fcb(jj))),
                                 r=[("ups", z), "par"], w=[("uc", ub, half)])
                            P.op("dve", (lambda half=half, jj=jj, ub=ub: V.scalar_tensor_tensor(out=uc[ub][:, half, :], in0=up[ub][:, half, 1:1 + TT], scalar=fcw(1, jj),
                                                                                                 in1=uc[ub][:, half, :], op0=ALU.mult, op1=ALU.add)),
                                 r=[("up", ub, half), ("up", ub, half, "h"), ("uc", ub, half), "par"], w=[("uc", ub, half)])
                            P.op("dve", (lambda half=half, jj=jj, ub=ub: V.scalar_tensor_tensor(out=uc[ub][:, half, :], in0=up[ub][:, half, 0:TT], scalar=fcw(0, jj),
                                                                                                 in1=uc[ub][:, half, :], op0=ALU.mult, op1=ALU.add)),
                                 r=[("up", ub, half), ("up", ub, half, "h"), ("uc", ub, half), "par"], w=[("uc", ub, half)])
                            P.op("pool", (lambda half=half, jj=jj, ub=ub: G.tensor_copy(out=hist[:, jj, :], in_=up[ub][:, half, TT:TT + 2])),
                                 r=[("up", ub, half)], w=[("hist", jj)])

                    def elemB(j):
                        z, ub = zub[j]
                        P.op("act", (lambda ub=ub: A.activation(out=sa[ub][:], in_=uc[ub][:, 0, :], func=AF.Silu)), r=[("uc", ub, 0)], w=[("sa", ub)])
                        P.op("pool", (lambda ub=ub: G.tensor_tensor(out=actj[ub][:], in0=sa[ub][:], in1=uc[ub][:, 1, :], op=ALU.mult)),
                             r=[("sa", ub), ("uc", ub, 1)], w=[("actj", ub)])
                        for n in range(8):
                            P.op("pe", (lambda n=n, j=j, ub=ub: PE.matmul(accp[:, n * TT:(n + 1) * TT], lhsT=w_dn[:, j, n * 128:(n + 1) * 128], rhs=actj[ub][:],
                                                                          start=(j == 0 and n % 2 == 0), stop=(j == NJ - 1), skip_group_check=True)),
                                 r=[("w_dn", j), ("actj", ub)], w=["accp"])

                    emit_up(0)
                    emit_up(1)
                    for j in range(NJ):
                        if j + 2 < NJ:
                            emit_up(j + 2)
                        elemA(j)
                        if j >= 1:
                            elemB(j - 1)
                        if j == 8 and i + 1 < NT2:
                            nb_ = (i + 1) % 2
                            norm_prompt(("xtf", nb_), tag, xt_[nb_], TT, l, SH2, SC2, sqb, nps, rt, hT_[nb_], xn, hkey=("hTf", nb_))
                    elemB(NJ - 1)
                    for n in range(8):
                        P.op("dve", (lambda n=n, b=b: V.scalar_tensor_tensor(out=xo[b][:, n, :], in0=accp[:, n * TT:(n + 1) * TT], scalar=modP[:, l, G2 + n:G2 + n + 1],
                                                                              in1=xt_[b][:, n, :], op0=ALU.mult, op1=ALU.add)),
                             r=["accp", ("xtf", b), "modP"], w=[("xo", 0, n)])
                    if not final:
                        P.dma("pool", (lambda b=b, i=i: G.dma_start(out=XOUT[:, :, i * TT:(i + 1) * TT], in_=xo[b][:])), r=[("xo", 0, n) for n in range(8)], w=[("XOUT", i)])
                    else:
                        xok = [("xo", 0, n) for n in range(8)]
                        P.op("act", (lambda b=b: A.activation(out=sqb[:], in_=xo[b][:], func=AF.Square)), r=xok, w=[(tag, "sqb")])
                        for c in range(8):
                            P.op("pe", (lambda c=c: PE.matmul(nps[:, :TT], lhsT=ones_b[:], rhs=sqb[:, c, :], start=(c == 0), stop=(c == 7))), r=[(tag, "sqb"), "ones_b"], w=[(tag, "nps")])
                        P.op("act", lambda: A.activation(out=rty[:], in_=nps[:, :TT], func=AF.Sqrt, scale=1.0 / D, bias=epsb[:, 0:1]), r=[(tag, "nps"), "epsb"], w=["rty"])
                        P.op("dve", lambda: V.reciprocal(out=rty[:], in_=rty[:]), r=["rty"], w=["rty"])
                        for c in range(8):
                            P.op("dve", (lambda c=c, b=b: V.scalar_tensor_tensor(out=xo[b][:, c, :], in0=xo[b][:, c, :], scalar=pcol("fin_g", c), in1=rty[:],
                                                                                  op0=ALU.mult, op1=ALU.mult)), r=[("xo", 0, c), "rty", "par"], w=[("xo", 0, c)])
                        P.dma("pool", (lambda b=b, i=i: G.dma_start(out=O["yT"][:, :, i * TT:(i + 1) * TT], in_=xo[b][:])), r=[("xo", 0, c) for c in range(8)], w=[("YO", i)])
                P.dma("sp", lambda: nc.sync.dma_start(out=O["ffn_o"][l, :, :, :], in_=hist[:]), r=[("hist", j) for j in range(44)], w=[("ffn_o", l)])

                sq_s = SB(ph, "sqs%d" % l, [128, 8, NS]); xn_s = SB(ph, "xns%d" % l, [128, 8, NS]); hTs = SB(ph, "hTs%d" % l, [128, 8, NS], BF16)
                stf = SB(ph, "stf%d" % l, [128, 44, 2, NS])
                ups_s = SB(ph, "ups_s%d" % l, [128, 44, NS]); ucs = SB(ph, "ucs%d" % l, [128, 44, NS]); tmp = SB(ph, "tmps%d" % l, [128, 44, NS])
                fo = SB(ph, "fo%d" % l, [128, 44, 2, NS])
                acts = SB(ph, "acts%d" % l, [128, NJ, NS], BF16); sas = SB(ph, "sas%d" % l, [128, NJ, NS])
                mo_s = SB(ph, "mofs%d" % l, [128, 8, NS])
                P.dma("sp", lambda: nc.sync.dma_start(out=stf[:], in_=I["st_ffn"][l, :, :, :, :]), w=["stf"])
                norm_sample(("fs", l), l, SH2, SC2, sq_s, nps, rt, hTs, xn_s)
                for jj in range(44):
                    for kc in range(8):
                        P.op("pe", (lambda jj=jj, kc=kc: PE.matmul(ups[0][:, jj * NS:(jj + 1) * NS], lhsT=w_up[:, kc, jj * 128:(jj + 1) * 128], rhs=hTs[:, kc, :],
                                                                    start=(kc == 0), stop=(kc == 7))), r=[("w_up", kc, jgrp[jj % NJ], jj // NJ), (("fs", l), "hT")], w=[("ups", 0)])
                upv = ups[0][:, 0:44 * NS].rearrange("p (j s) -> p j s", j=44)
                P.op("act", lambda: A.activation(out=ups_s[:], in_=upv, func=AF.Copy), r=[("ups", 0)], w=["ups_s"])
                bc = lambda k: par[:, PAR_OFF["f_cw"] + (l * 3 + k) * 44: PAR_OFF["f_cw"] + (l * 3 + k + 1) * 44].unsqueeze(2).to_broadcast([128, 44, NS])
                bcb = par[:, PAR_OFF["f_cb"] + l * 44: PAR_OFF["f_cb"] + (l + 1) * 44].unsqueeze(2).to_broadcast([128, 44, NS])
                P.op("dve", lambda: V.tensor_tensor(out=ucs[:], in0=ups_s[:], in1=bc(2), op=ALU.mult), r=["ups_s", "par"], w=["ucs"])
                P.op("dve", lambda: V.tensor_tensor(out=ucs[:], in0=ucs[:], in1=bcb, op=ALU.add), r=["ucs", "par"], w=["ucs"])
                for k in range(2):
                    P.op("dve", (lambda k=k: V.tensor_tensor(out=tmp[:], in0=stf[:, :, k, :], in1=bc(k), op=ALU.mult)), r=["stf", "par", "ucs"], w=["tmp"])
                    P.op("dve", lambda: V.tensor_tensor(out=ucs[:], in0=ucs[:], in1=tmp[:], op=ALU.add), r=["ucs", "tmp"], w=["ucs"])
                P.op("act", lambda: A.activation(out=sas[:], in_=ucs[:, 0:NJ, :], func=AF.Silu), r=["ucs"], w=["sas"])
                P.op("dve", lambda: V.tensor_tensor(out=acts[:], in0=sas[:], in1=ucs[:, NJ:44, :], op=ALU.mult), r=["sas", "ucs"], w=["acts"])
                for n in range(8):
                    for j in range(NJ):
                        P.op("pe", (lambda n=n, j=j: PE.matmul(ups[1][:, n * NS:(n + 1) * NS], lhsT=w_dn[:, j, n * 128:(n + 1) * 128], rhs=acts[:, j, :],
                                                                start=(j == 0), stop=(j == NJ - 1))), r=[("w_dn", j), "acts"], w=[("ups", 1)])
                P.op("dve", lambda: V.tensor_tensor(out=mo_s[:], in0=ups[1][:, 0:8 * NS].rearrange("p (c s) -> p c s", c=8), in1=modS[:, l, G2:G2 + 8, :], op=ALU.mult),
                     r=[("ups", 1), "modS"], w=["mofs"])
                P.op("dve", lambda: V.tensor_tensor(out=xs_cur[:], in0=xs_cur[:], in1=mo_s[:], op=ALU.add), r=["mofs", "xs"], w=["xs"])
                P.op("pool", lambda: G.tensor_copy(out=fo[:, :, 0, :], in_=stf[:, :, 1, :]), r=["stf"], w=["fo0"])
                P.op("pool", lambda: G.tensor_copy(out=fo[:, :, 1, :], in_=ups_s[:]), r=["ups_s"], w=["fo1"])
                P.dma("sp", lambda: nc.sync.dma_start(out=O["ffn_s"][l, :, :, :, :], in_=fo[:]), r=["fo0", "fo1"], w=[("ffn_s", l)])
                if final:
                    ys = SB(ph, "ys", [128, 8, NS])
                    P.op("dve", lambda: V.tensor_tensor(out=sq_s[:], in0=xs_cur[:], in1=xs_cur[:], op=ALU.mult), r=["xs"], w=["sqs_f"])
                    for c in range(8):
                        P.op("pe", (lambda c=c: PE.matmul(nps[:, :NS], lhsT=ones_f[:], rhs=sq_s[:, c, :], start=(c == 0), stop=(c == 7))), r=["sqs_f", "ones_f"], w=["nps_f"])
                    P.op("act", lambda: A.activation(out=rt[:, :NS], in_=nps[:, :NS], func=AF.Sqrt, scale=1.0 / D, bias=epsb[:, 0:1]), r=["nps_f", "epsb"], w=["rt_f"])
                    P.op("dve", lambda: V.reciprocal(out=rt[:, :NS], in_=rt[:, :NS]), r=["rt_f"], w=["rt_f"])
                    P.op("dve", lambda: V.tensor_tensor(out=ys[:], in0=xs_cur[:], in1=rt[:, :NS].unsqueeze(1).to_broadcast([128, 8, NS]), op=ALU.mult), r=["xs", "rt_f"], w=["ys"])
                    P.op("dve", lambda: V.tensor_tensor(out=ys[:], in0=ys[:], in1=pcol("fin_g", 0, 8).unsqueeze(2).to_broadcast([128, 8, NS]), op=ALU.mult), r=["ys", "par"], w=["ys"])
                    P.dma("sp", lambda: nc.sync.dma_start(out=O["ysT"][:, :, :], in_=ys[:]), r=["ys"], w=["ysT"])
                P.flush()

        ffn_phase(0, X1, X2, False)

        with contextlib.ExitStack() as ph:
            TT = 256
            w_in = SB(ph, "w_ino", [128, 8, 2048], BF16); w_o = SB(ph, "w_oo", [128, 8, D], BF16)
            wa = SB(ph, "wa_sb", [128, 8, 128], BF16); wx = SB(ph, "wx_sb", [128, 8, 128], BF16)
            load_w(None, w_in, lambda kc: I["w_in_o"][:, kc, :], 8, "w_ino")
            load_w(None, w_o, lambda kc: I["w_out_o"][:, kc, :], 8, "w_oo")
            P.dma("pool", lambda: G.dma_start(out=wa[:], in_=I["rg_wa"][:, :, :]), w=["wa"])
            P.dma("pool", lambda: G.dma_start(out=wx[:], in_=I["rg_wx"][:, :, :]), w=["wx"])
            xt_ = [SB(ph, "xtr%d" % i, [128, 8, TT]) for i in range(2)]
            sqb = SB(ph, "sqbr", [128, 8, TT], BF16); xn = SB(ph, "xnr", [128, 8, TT]); rt = SB(ph, "rtr", [128, TT]); hT = SB(ph, "hTr", [128, 8, TT], BF16)
            xr = SB(ph, "xr", [128, 8, 3 + TT])
            gg = [SB(ph, "gg%d" % i, [128, 8, TT]) for i in range(2)]
            xc = [SB(ph, "xc%d" % i, [128, 8, TT]) for i in range(2)]
            xcb = [SB(ph, "xcb%d" % i, [128, 8, TT], BF16) for i in range(2)]
            rr = SB(ph, "rr", [128, 8, TT]); ii = SB(ph, "ii", [128, 8, TT]); aa = SB(ph, "aa", [128, 8, TT]); ss = SB(ph, "ss", [128, 8, TT])
            hs = SB(ph, "hs", [128, 8, TT])
            yb = SB(ph, "yb", [128, 8, TT], BF16)
            hst = SB(ph, "hst", [128, 8])
            xo = [SB(ph, "xor%d" % i, [128, 8, TT]) for i in range(2)]
            nps = PS(ph, "npsr", [128, 512])
            zps = [PS(ph, "zpr%d" % i, [128, 512]) for i in range(3)]
            gps = [PS(ph, "gpr%d" % i, [128, 1024]) for i in range(2)]
            rcw = lambda k, c: par[:, PAR_OFF["rg_cw"] + k * 8 + c: PAR_OFF["rg_cw"] + k * 8 + c + 1]
            P.op("pool", lambda: G.memset(hst[:], 0.0), w=[("hst", c) for c in range(8)])
            P.op("pool", lambda: G.memset(xr[:, :, 0:3], 0.0), w=[("xrh", c) for c in range(8)])

            def loadr(i):
                b = i % 2
                P.dma("sp", (lambda: nc.sync.dma_start(out=xt_[b][:], in_=X2[:, :, i * TT:(i + 1) * TT])), w=[("xtr", b)])
            zc_ = [0]
            tag = "rg"

            def stage1a(i):
                b = i % 2
                norm_prompt(("xtr", b), tag, xt_[b], TT, 1, SH1, SC1, sqb, nps, rt, hT, xn)

            def stage1b(i):
                b = i % 2
                for n in list(range(8, 16)) + list(range(8)):
                    z = zc_[0] % 3
                    zc_[0] += 1
                    for kc in range(8):
                        P.op("pe", (lambda n=n, kc=kc, z=z: PE.matmul(zps[z][:, 0:TT], lhsT=w_in[:, kc, n * 128:(n + 1) * 128], rhs=hT[:, kc, :], start=(kc == 0), stop=(kc == 7))),
                             r=[("w_ino", kc), (tag, "hT")], w=[("zpr", z)])
                    if n < 8:
                        P.op("act", (lambda n=n, z=z, b=b: A.activation(out=gg[b][:, n, :], in_=zps[z][:, 0:TT], func=GELU)), r=[("zpr", z)], w=[("gg", b, n)])
                    else:
                        c = n - 8
                        P.op("act", (lambda c=c, z=z: A.activation(out=xr[:, c, 3:3 + TT], in_=zps[z][:, 0:TT], func=AF.Copy)), r=[("zpr", z)], w=[("xr", c)])
                        P.op("act", (lambda c=c, z=z, b=b: A.activation(out=xc[b][:, c, :], in_=zps[z][:, 0:TT], func=AF.Identity, scale=rcw(3, c), bias=pcol("rg_cb", c))),
                             r=[("zpr", z), "par"], w=[("xc", b, c)])
                        for k in range(3):
                            P.op("dve", (lambda c=c, k=k, b=b: V.scalar_tensor_tensor(out=xc[b][:, c, :], in0=xr[:, c, k:k + TT], scalar=rcw(k, c), in1=xc[b][:, c, :],
                                                                                        op0=ALU.mult, op1=ALU.add)),
                                 r=[("xr", c), ("xrh", c), ("xc", b, c), "par"], w=[("xc", b, c)])
                        P.op("pool", (lambda c=c: G.tensor_copy(out=xr[:, c, 0:3], in_=xr[:, c, TT:TT + 3])), r=[("xr", c)], w=[("xrh", c)])
                        P.op("pool", (lambda c=c, b=b: G.tensor_copy(out=xcb[b][:, c, :], in_=xc[b][:, c, :])), r=[("xc", b, c)], w=[("xcb", b, c)])

            def stage2a(i):
                b = i % 2
                for c in range(8):
                    q2 = c % 2
                    P.op("pe", (lambda c=c, q2=q2, b=b: PE.matmul(gps[q2][:, 0:TT], lhsT=wa[:, c, :], rhs=xcb[b][:, c, :], start=True, stop=True)), r=["wa", ("xcb", b, c)], w=[("gpr", q2)])
                    P.op("pe", (lambda c=c, q2=q2, b=b: PE.matmul(gps[q2][:, 512:512 + TT], lhsT=wx[:, c, :], rhs=xcb[b][:, c, :], start=True, stop=True)), r=["wx", ("xcb", b, c)], w=[("gpr", q2)])
                    P.op("act", (lambda c=c, q2=q2: A.activation(out=rr[:, c, :], in_=gps[q2][:, 0:TT], func=AF.Sigmoid, bias=pcol("rg_ba", c), scale=1.0)),
                         r=[("gpr", q2), "par"], w=[("rr", c)])
                    P.op("act", (lambda c=c, q2=q2: A.activation(out=ii[:, c, :], in_=gps[q2][:, 512:512 + TT], func=AF.Sigmoid, bias=pcol("rg_bx", c), scale=1.0)),
                         r=[("gpr", q2), "par"], w=[("ii", c)])
                    P.op("pool", (lambda c=c, b=b: G.tensor_tensor(out=ii[:, c, :], in0=ii[:, c, :], in1=xc[b][:, c, :], op=ALU.mult)), r=[("ii", c), ("xc", b, c)], w=[("ii", c)])
                for c in range(8):
                    P.op("act", (lambda c=c: A.activation(out=aa[:, c, :], in_=rr[:, c, :], func=AF.Exp, scale=cst[:, c:c + 1])), r=[("rr", c), "cst"], w=[("aa", c)])
                    P.op("act", (lambda c=c: A.activation(out=ss[:, c, :], in_=rr[:, c, :], func=AF.Exp, scale=cst2[:, c:c + 1])), r=[("rr", c), "cst2"], w=[("ss", c)])
                for c in range(8):
                    P.op("act", (lambda c=c: A.activation(out=ss[:, c, :], in_=ss[:, c, :], func=AF.Sqrt, scale=-1.0, bias=oneb[:, 0:1])), r=[("ss", c), "oneb"], w=[("ss", c)])
                    P.op("dve", (lambda c=c: V.tensor_tensor(out=ii[:, c, :], in0=ii[:, c, :], in1=ss[:, c, :], op=ALU.mult)), r=[("ii", c), ("ss", c)], w=[("ii", c)])
                    P.op("dve", (lambda c=c: V.tensor_tensor_scan(out=hs[:, c, :], data0=aa[:, c, :], data1=ii[:, c, :], initial=hst[:, c:c + 1], op0=ALU.mult, op1=ALU.add)),
                         r=[("aa", c), ("ii", c), ("hst", c)], w=[("hs", c)])
                    P.op("dve", (lambda c=c: V.tensor_copy(out=hst[:, c:c + 1], in_=hs[:, c, TT - 1:TT])), r=[("hs", c)], w=[("hst", c)])
                    P.op("pool", (lambda c=c, b=b: G.tensor_tensor(out=yb[:, c, :], in0=hs[:, c, :], in1=gg[b][:, c, :], op=ALU.mult)),
                         r=[("hs", c), ("gg", b, c)], w=[("yb", c)])

            def stage2b(i):
                b = i % 2
                for n in range(8):
                    z = zc_[0] % 3
                    zc_[0] += 1
                    for kc in range(8):
                        P.op("pe", (lambda n=n, kc=kc, z=z: PE.matmul(zps[z][:, 0:TT], lhsT=w_o[:, kc, n * 128:(n + 1) * 128], rhs=yb[:, kc, :], start=(kc == 0), stop=(kc == 7))),
                             r=[("w_oo", kc), ("yb", kc)], w=[("zpr", z)])
                    P.op("dve", (lambda n=n, z=z, b=b: V.scalar_tensor_tensor(out=xo[b][:, n, :], in0=zps[z][:, 0:TT], scalar=modP[:, 1, G1 + n:G1 + n + 1], in1=xt_[b][:, n, :],
                                                                               op0=ALU.mult, op1=ALU.add)), r=[("zpr", z), ("xtr", b), "modP"], w=[("xor", b, n)])
                P.dma("pool", (lambda b=b, i=i: G.dma_start(out=X3[:, :, i * TT:(i + 1) * TT], in_=xo[b][:])), r=[("xor", b, n) for n in range(8)], w=[("X3", i)])

            loadr(0)
            stage1a(0)
            stage1b(0)
            for i in range(NT2):
                if i + 1 < NT2:
                    loadr(i + 1)
                stage2a(i)
                if i + 1 < NT2:
                    stage1a(i + 1)
                    stage1b(i + 1)
                stage2b(i)
            P.dma("sp", lambda: nc.sync.dma_start(out=O["rgc_o"][:, :, :], in_=xr[:, :, 0:3]), r=[("xrh", c) for c in range(8)], w=["rgc_o"])
            P.dma("sp", lambda: nc.sync.dma_start(out=O["rgh_o"][:, :], in_=hst[:]), r=[("hst", c) for c in range(8)], w=["rgh_o"])

            sq_s = SB(ph, "sqsr", [128, 8, NS]); xn_s = SB(ph, "xnsr", [128, 8, NS]); hTs = SB(ph, "hTsr", [128, 8, NS], BF16)
            stc = SB(ph, "stc", [128, 8, 3, NS]); sth = SB(ph, "sth", [128, 8, NS])
            zs = SB(ph, "zs", [128, 16, NS]); ggs = SB(ph, "ggs", [128, 8, NS])
            xcs = SB(ph, "xcs", [128, 8, NS]); tmp = SB(ph, "tmpr", [128, 8, NS]); xcsb = SB(ph, "xcsb", [128, 8, NS], BF16)
            rs_ = SB(ph, "rs_", [128, 8, NS]); is_ = SB(ph, "is_", [128, 8, NS]); as_ = SB(ph, "as_", [128, 8, NS]); s2 = SB(ph, "s2_", [128, 8, NS])
            hn = SB(ph, "hn", [128, 8, NS]); ybs = SB(ph, "ybs", [128, 8, NS], BF16); mo_s = SB(ph, "mors", [128, 8, NS])
            co = SB(ph, "co", [128, 8, 3, NS])
            P.dma("sp", lambda: nc.sync.dma_start(out=stc[:], in_=I["st_rgc"][:, :, :, :]), w=["stc"])
            P.dma("sp", lambda: nc.sync.dma_start(out=sth[:], in_=I["st_rgh"][:, :, :]), w=["sth"])
            norm_sample("rs", 1, SH1, SC1, sq_s, nps, rt, hTs, xn_s)
            for n in range(16):
                for kc in range(8):
                    P.op("pe", (lambda n=n, kc=kc: PE.matmul(zps[0][:, n * NS:(n + 1) * NS], lhsT=w_in[:, kc, n * 128:(n + 1) * 128], rhs=hTs[:, kc, :], start=(kc == 0), stop=(kc == 7))),
                         r=[("w_ino", kc), ("rs", "hT")], w=[("zpr", 0)])
            P.op("act", lambda: A.activation(out=zs[:], in_=zps[0][:, 0:16 * NS].rearrange("p (c s) -> p c s", c=16), func=AF.Copy), r=[("zpr", 0)], w=["zs"])
            P.op("act", lambda: A.activation(out=ggs[:], in_=zs[:, 0:8, :], func=GELU), r=["zs"], w=["ggs"])
            bw = lambda k: par[:, PAR_OFF["rg_cw"] + k * 8: PAR_OFF["rg_cw"] + (k + 1) * 8].unsqueeze(2).to_broadcast([128, 8, NS])
            b8 = lambda nm: pcol(nm, 0, 8).unsqueeze(2).to_broadcast([128, 8, NS])
            P.op("dve", lambda: V.tensor_tensor(out=xcs[:], in0=zs[:, 8:16, :], in1=bw(3), op=ALU.mult), r=["zs", "par"], w=["xcs"])
            P.op("dve", lambda: V.tensor_tensor(out=xcs[:], in0=xcs[:], in1=b8("rg_cb"), op=ALU.add), r=["xcs", "par"], w=["xcs"])
            for k in range(3):
                P.op("dve", (lambda k=k: V.tensor_tensor(out=tmp[:], in0=stc[:, :, k, :], in1=bw(k), op=ALU.mult)), r=["stc", "par", "xcs"], w=["tmpr"])
                P.op("dve", lambda: V.tensor_tensor(out=xcs[:], in0=xcs[:], in1=tmp[:], op=ALU.add), r=["xcs", "tmpr"], w=["xcs"])
            P.op("dve", lambda: V.tensor_copy(out=xcsb[:], in_=xcs[:]), r=["xcs"], w=["xcsb"])
            for c in range(8):
                P.op("pe", (lambda c=c: PE.matmul(zps[1][:, c * NS:(c + 1) * NS], lhsT=wa[:, c, :], rhs=xcsb[:, c, :], start=True, stop=True)), r=["wa", "xcsb"], w=[("zpr", 1)])
                P.op("pe", (lambda c=c: PE.matmul(zps[1][:, 64 + c * NS: 64 + (c + 1) * NS], lhsT=wx[:, c, :], rhs=xcsb[:, c, :], start=True, stop=True)), r=["wx", "xcsb"], w=[("zpr", 1)])
            P.op("dve", lambda: V.tensor_tensor(out=rs_[:], in0=zps[1][:, 0:8 * NS].rearrange("p (c s) -> p c s", c=8), in1=b8("rg_ba"), op=ALU.add), r=[("zpr", 1), "par"], w=["rs_"])
            P.op("dve", lambda: V.tensor_tensor(out=is_[:], in0=zps[1][:, 64:64 + 8 * NS].rearrange("p (c s) -> p c s", c=8), in1=b8("rg_bx"), op=ALU.add), r=[("zpr", 1), "par"], w=["is_"])
            P.op("act", lambda: A.activation(out=rs_[:], in_=rs_[:], func=AF.Sigmoid), r=["rs_"], w=["rs_"])
            P.op("act", lambda: A.activation(out=is_[:], in_=is_[:], func=AF.Sigmoid), r=["is_"], w=["is_"])
            P.op("dve", lambda: V.tensor_tensor(out=as_[:], in0=rs_[:], in1=cst[:].unsqueeze(2).to_broadcast([128, 8, NS]), op=ALU.mult), r=["rs_", "cst"], w=["as_"])
            P.op("act", lambda: A.activation(out=s2[:], in_=as_[:], func=AF.Exp, scale=2.0), r=["as_"], w=["s2"])
            P.op("act", lambda: A.activation(out=as_[:], in_=as_[:], func=AF.Exp), r=["as_", "s2"], w=["as_"])
            P.op("act", lambda: A.activation(out=s2[:], in_=s2[:], func=AF.Sqrt, scale=-1.0, bias=oneb[:, 0:1]), r=["s2", "oneb"], w=["s2"])
            P.op("dve", lambda: V.tensor_tensor(out=is_[:], in0=is_[:], in1=xcs[:], op=ALU.mult), r=["is_", "xcs"], w=["is_"])
            P.op("dve", lambda: V.tensor_tensor(out=is_[:], in0=is_[:], in1=s2[:], op=ALU.mult), r=["is_", "s2"], w=["is_"])
            P.op("dve", lambda: V.tensor_tensor(out=hn[:], in0=as_[:], in1=sth[:], op=ALU.mult), r=["as_", "sth"], w=["hn"])
            P.op("dve", lambda: V.tensor_tensor(out=hn[:], in0=hn[:], in1=is_[:], op=ALU.add), r=["hn", "is_"], w=["hn"])
            P.dma("sp", lambda: nc.sync.dma_start(out=O["rgh_s"][:, :, :], in_=hn[:]), r=["hn"], w=["rgh_s"])
            P.op("pool", lambda: G.tensor_copy(out=co[:, :, 0:2, :], in_=stc[:, :, 1:3, :]), r=["stc"], w=["co0"])
            P.op("pool", lambda: G.tensor_copy(out=co[:, :, 2, :], in_=zs[:, 8:16, :]), r=["zs"], w=["co1"])
            P.dma("sp", lambda: nc.sync.dma_start(out=O["rgc_s"][:, :, :, :], in_=co[:]), r=["co0", "co1"], w=["rgc_s"])
            P.op("dve", lambda: V.tensor_tensor(out=ybs[:], in0=hn[:], in1=ggs[:], op=ALU.mult), r=["hn", "ggs"], w=["ybs"])
            for n in range(8):
                for kc in range(8):
                    P.op("pe", (lambda n=n, kc=kc: PE.matmul(zps[2][:, n * NS:(n + 1) * NS], lhsT=w_o[:, kc, n * 128:(n + 1) * 128], rhs=ybs[:, kc, :], start=(kc == 0), stop=(kc == 7))),
                         r=[("w_oo", kc), "ybs"], w=[("zpr", 2)])
            P.op("dve", lambda: V.tensor_tensor(out=mo_s[:], in0=zps[2][:, 0:8 * NS].rearrange("p (c s) -> p c s", c=8), in1=modS[:, 1, G1:G1 + 8, :], op=ALU.mult),
                 r=[("zpr", 2), "modS"], w=["mors"])
            P.op("dve", lambda: V.tensor_tensor(out=xs_cur[:], in0=xs_cur[:], in1=mo_s[:], op=ALU.add), r=["mors", "xs"], w=["xs"])
            P.flush()

        ffn_phase(1, X3, None, True)
        P.barrier(final=True)
    return nc


_CACHE = {}


def _fm(v, n):
    return np.ascontiguousarray(np.asarray(v, np.float32).reshape(n, 128).T)


def _wk(w):
    K, N = w.shape
    return np.ascontiguousarray(np.asarray(w, np.float32).reshape(K // 128, 128, N).transpose(1, 0, 2))


def kernel(**inp):
    inp = {k: np.asarray(v) for k, v in inp.items()}
    x_prompt = inp["x_prompt"]
    B, T, _ = x_prompt.shape
    TK = min(LCACHE, T)
    if T not in _CACHE:
        _CACHE[T] = build_program(T)
    nc = _CACHE[T]
    NCORES = 8
    par = np.zeros((128, NPAR), np.float32)

    def put(name, arr):
        par[:, PAR_OFF[name]:PAR_OFF[name] + arr.shape[1]] = arr
    put("b_ada", np.concatenate([_fm(inp["b_ada"][0], 48), _fm(inp["b_ada"][1], 48)], 1))
    put("ln_g", _fm(inp["ln_v_g"][0], 4)); put("ln_b", _fm(inp["ln_v_b"][0], 4))
    put("rg_cw", np.concatenate([_fm(inp["rg_conv_w"][0, k], 8) for k in range(4)], 1))
    put("rg_cb", _fm(inp["rg_conv_b"][0], 8)); put("rg_ba", _fm(inp["rg_b_a"][0], 8)); put("rg_bx", _fm(inp["rg_b_x"][0], 8))
    put("rg_lam", _fm(inp["rg_lambda"][0], 8))
    put("f_cw", np.concatenate([_fm(inp["ffn_conv_w"][l, k], 44) for l in range(2) for k in range(3)], 1))
    put("f_cb", np.concatenate([_fm(inp["ffn_conv_b"][l], 44) for l in range(2)], 1))
    put("fin_g", _fm(inp["final_g"], 8))
    w_oe = inp["w_out_even"][0]
    shared = dict(
        wada=np.stack([_wk(inp["w_ada"][l]) for l in range(2)]), params=par,
        w_in_e=_wk(inp["w_in_even"][0]), w_out_e=_wk(w_oe),
        w_out_eh=np.ascontiguousarray(w_oe[512:].reshape(8, 64, D).transpose(1, 0, 2)),
        w_sguT=np.ascontiguousarray(inp["w_sgu"][0].transpose(2, 0, 1)),
        b_sgu=np.ascontiguousarray(inp["b_sgu"][0].reshape(1, 512)),
        sg0=np.ascontiguousarray(np.concatenate([inp["w_sgu"][0][:, 0, 0], inp["b_sgu"][0][:, 0]]).reshape(1, 8)),
        w_in_o=_wk(inp["w_in_odd"][0]),
        rg_wa=np.ascontiguousarray(inp["rg_w_a"][0].transpose(1, 0, 2)), rg_wx=np.ascontiguousarray(inp["rg_w_x"][0].transpose(1, 0, 2)),
        w_out_o=_wk(inp["w_out_odd"][0]),
        w_up=np.stack([_wk(inp["ffn_w_up"][l]) for l in range(2)]), w_dn=np.stack([_wk(inp["ffn_w_down"][l]) for l in range(2)]),
    )
    rowmask = np.zeros((128, NS), np.float32)
    dmask = np.zeros((128, 8, 64), np.float32)
    for s in range(NS):
        rowmask[32 * s:32 * s + 8, s] = 1.0
        for h in range(8):
            dmask[32 * s + h, h, :] = 1.0
    shared["rowmask"] = rowmask
    shared["dmask"] = dmask.reshape(128, 512)
    in_maps = []
    SEQ_CORE = [0, 1, 4, 5][:B] if B <= 4 else list(range(B))
    zero_x = np.zeros((128, 8, T), np.float32)
    for c in range(NCORES):
        ss = slice(NS * c, NS * (c + 1))
        m = dict(shared)
        if c in SEQ_CORE:
            sq = SEQ_CORE.index(c)
            m["xT"] = np.ascontiguousarray(x_prompt[sq].T.reshape(8, 128, T).transpose(1, 0, 2))
            cp = inp["c_prompt"][sq:sq + 1]
        else:
            m["xT"] = zero_x
            cp = np.zeros((1, D), np.float32)
        m["xsT"] = np.ascontiguousarray(inp["x_sample"][ss, 0, :].T.reshape(8, 128, NS).transpose(1, 0, 2))
        cc = np.concatenate([cp, inp["c_sample"][ss]], 0)
        m["cT"] = np.ascontiguousarray(cc.T.reshape(8, 128, 1 + NS).transpose(1, 0, 2))
        m["ck"] = np.ascontiguousarray(inp["cache_win_k"][0, ss].reshape(NS, LCACHE, 512))
        m["cv"] = np.ascontiguousarray(inp["cache_win_v"][0, ss].reshape(NS, LCACHE, 512))
        m["st_rgc"] = np.ascontiguousarray(inp["state_rglru_conv"][0, ss].transpose(2, 1, 0).reshape(8, 128, 3, NS).transpose(1, 0, 2, 3))
        m["st_rgh"] = np.ascontiguousarray(inp["state_rglru_h"][0, ss].T.reshape(8, 128, NS).transpose(1, 0, 2))
        m["st_ffn"] = np.ascontiguousarray(inp["state_ffn_conv"][:, ss].transpose(0, 3, 2, 1).reshape(2, 44, 128, 2, NS).transpose(0, 2, 1, 3, 4))
        in_maps.append(m)
    res = run_bass_kernel_spmd(nc, in_maps, core_ids=list(range(NCORES)))
    R = res.results
    global _LAST
    _LAST = R

    def unfm(a):
        a = np.asarray(a)
        n = a.shape[1]
        rest = a.shape[2:]
        return np.moveaxis(a.transpose(1, 0, *range(2, a.ndim)).reshape(n * 128, *rest), 0, -1)

    y_prompt = np.stack([unfm(R[SEQ_CORE[b]]["yT"]) for b in range(B)]).astype(np.float32)
    y_sample = np.concatenate([unfm(R[c]["ysT"]) for c in range(NCORES)], 0)[:, None, :].astype(np.float32)
    win_k_p = np.stack([unfm(R[SEQ_CORE[b]]["kT_o"]).reshape(TK, 8, 64) for b in range(B)])[None].astype(np.float32)
    win_v_p = np.stack([unfm(R[SEQ_CORE[b]]["vT_o"]).reshape(TK, 8, 64) for b in range(B)])[None].astype(np.float32)
    rgc_p = np.stack([unfm(R[SEQ_CORE[b]]["rgc_o"]) for b in range(B)])[None].astype(np.float32)
    rgh_p = np.stack([unfm(R[SEQ_CORE[b]]["rgh_o"][:, :, None])[0] for b in range(B)])[None].astype(np.float32)
    ffn_p = np.stack([np.stack([unfm(R[SEQ_CORE[b]]["ffn_o"][l]) for b in range(B)]) for l in range(2)]).astype(np.float32)
    cv_s = np.concatenate([unfm(R[c]["cv_o"]) for c in range(NCORES)], 0)[None, :, None, :].astype(np.float32)
    wk_s = np.concatenate([R[c]["wk_s"].reshape(NS, LCACHE, 8, 64) for c in range(NCORES)], 0)[None].astype(np.float32)
    wv_s = np.concatenate([R[c]["wv_s"].reshape(NS, LCACHE, 8, 64) for c in range(NCORES)], 0)[None].astype(np.float32)
    rgc_s = np.concatenate([unfm(R[c]["rgc_s"]).transpose(1, 0, 2) for c in range(NCORES)], 0)[None].astype(np.float32)
    rgh_s = np.concatenate([unfm(R[c]["rgh_s"]) for c in range(NCORES)], 0)[None].astype(np.float32)
    ffn_s = np.stack([np.concatenate([unfm(R[c]["ffn_s"][l]).transpose(1, 0, 2) for c in range(NCORES)], 0) for l in range(2)]).astype(np.float32)
    return (y_prompt, y_sample, win_k_p, win_v_p, rgc_p, rgh_p, ffn_p, cv_s, wk_s, wv_s, rgc_s, rgh_s, ffn_s)
```

```python
import contextlib
import numpy as np
import concourse.bass as bass
import concourse.mybir as mybir
from concourse.bass_utils import run_bass_kernel_spmd

F32 = mybir.dt.float32
BF16 = mybir.dt.bfloat16
AF = mybir.ActivationFunctionType
ALU = mybir.AluOpType
AX = mybir.AxisListType

D = 1024
NCH = 8
WA = 512
NIN_E = 2560
DFF = 2816
F2 = 5632
NJ = 22
NS = 4
LCACHE = 2048
EPS = 1e-6
PATTERNS = ((128, 1), (512, 4), (2048, 16))
GELU = AF.Gelu_apprx_tanh

PAR_SPEC = [("b_ada", 96), ("ln_g", 4), ("ln_b", 4), ("rg_cw", 32), ("rg_cb", 8), ("rg_ba", 8),
            ("rg_bx", 8), ("rg_lam", 8), ("f_cw", 2 * 3 * 44), ("f_cb", 2 * 44), ("fin_g", 8)]
PAR_OFF = {}
_o = 0
for _n, _w in PAR_SPEC:
    PAR_OFF[_n] = _o
    _o += _w
NPAR = _o


class Prog:
    NDS = 8

    def __init__(self, nc, es):
        self.nc = nc
        self.eng = {"pe": nc.tensor, "act": nc.scalar, "dve": nc.vector, "pool": nc.gpsimd, "sp": nc.sync}
        self.sem = {e: es.enter_context(nc.semaphore("s_" + e)) for e in self.eng}
        self.cnt = {e: 0 for e in self.eng}
        self.dsem = {q: [es.enter_context(nc.semaphore("d_%s%d" % (q, i))) for i in range(self.NDS)]
                     for q in ("sp", "pool", "actq")}
        self.qeng = {"sp": "sp", "pool": "pool", "actq": "act"}
        self.dcnt = {q: 0 for q in self.dsem}
        self.seen = {e: {} for e in self.eng}
        self.ops = []

    def op(self, eng, fn, r=(), w=()):
        self.ops.append([eng, fn, tuple(r), tuple(w), False])

    def dma(self, q, fn, r=(), w=()):
        self.ops.append([q, fn, tuple(r), tuple(w), True])

    def _wait(self, e, sem, val):
        key = sem.name if hasattr(sem, "name") else id(sem)
        if self.seen[e].get(key, 0) >= val:
            return
        self.seen[e][key] = val
        self.eng[e].wait_ge(sem, val)

    def flush(self):
        import os
        self.nflush = getattr(self, "nflush", 0) + 1
        stop = int(os.environ.get("K_STOP", "99"))
        only = os.environ.get("K_ONLY")
        if self.nflush > stop or (only and str(self.nflush) not in only.split(",")):
            self.ops = []
            return
        kops = os.environ.get("K_OPS")
        if kops and self.nflush == stop:
            self.ops = self.ops[:int(kops)]
            print("last kept op:", self.ops[-1][0], self.ops[-1][2], self.ops[-1][3], "of", len(self.ops))
        ops = self.ops
        n = len(ops)
        lastw, readers = {}, {}
        deps = [None] * n
        needed = [False] * n
        for i, (e, fn, r, w, isd) in enumerate(ops):
            d = set()
            for k in r:
                if k in lastw:
                    d.add(lastw[k])
            for k in w:
                if k in lastw:
                    d.add(lastw[k])
                for j in readers.get(k, ()):
                    d.add(j)
            d.discard(i)
            dd = []
            for j in d:
                if ops[j][0] == "pe" and e == "pe" and not ops[j][4] and not isd:
                    continue
                dd.append(j)
                needed[j] = True
            deps[i] = sorted(dd)
            for k in r:
                readers.setdefault(k, []).append(i)
            for k in w:
                lastw[k] = i
                readers[k] = []
        lastop = {}
        for i, (e, fn, r, w, isd) in enumerate(ops):
            if not isd:
                lastop[e] = i
        for i in lastop.values():
            needed[i] = True
        sig = [None] * n
        for i, (e, fn, r, w, isd) in enumerate(ops):
            q = e
            if isd:
                e = self.qeng[q]
            for j in deps[i]:
                s, v = sig[j]
                self._wait(e, s, v)
            if isd:
                m = self.dcnt[q]
                self.dcnt[q] += 1
                s = self.dsem[q][m % self.NDS]
                rnd = m // self.NDS
                if rnd > 0:
                    self._wait(e, s, 16 * rnd)
                ins = fn()
                ins.then_inc(s, 16)
                sig[i] = (s, 16 * (rnd + 1))
            else:
                ins = fn()
                if needed[i]:
                    self.cnt[e] += 1
                    ins.then_inc(self.sem[e], 1)
                    sig[i] = (self.sem[e], self.cnt[e])
        self.ops = []
        self.barrier()

    def barrier(self, final=False):
        for e in self.eng:
            for e2 in self.eng:
                if e2 != e and self.cnt[e2] > 0:
                    self._wait(e, self.sem[e2], self.cnt[e2])
            for q in self.dsem:
                if q == "actq" and not final:
                    continue
                m = self.dcnt[q]
                for k in range(min(m, self.NDS)):
                    cntk = (m - k + self.NDS - 1) // self.NDS
                    self._wait(e, self.dsem[q][k], 16 * cntk)


def build_program(T):
    nc = bass.Bass("TRN2", target_bir_lowering=False)
    TK = min(LCACHE, T)
    NT5 = T // 512
    NT2 = T // 256

    def din(name, shape, dt=F32):
        return nc.dram_tensor(name, list(shape), dt, kind="ExternalInput").ap()

    def dout(name, shape):
        return nc.dram_tensor(name, list(shape), F32, kind="ExternalOutput").ap()

    def dscr(name, shape, dt=F32):
        import os
        kind = "ExternalOutput" if (os.environ.get("K_DBG") and name in ("x1T", "x2T", "x3T", "O_s", "aoT_s")) else "Internal"
        return nc.dram_tensor(name, list(shape), dt, kind=kind).ap()

    I = dict(
        xT=din("xT", [128, 8, T]), xsT=din("xsT", [128, 8, NS]), cT=din("cT", [128, 8, 1 + NS]),
        wada=din("wada", [2, 128, 8, 6144]), params=din("params", [128, NPAR]),
        w_in_e=din("w_in_e", [128, 8, NIN_E]), w_out_e=din("w_out_e", [128, 8, D]),
        w_out_eh=din("w_out_eh", [64, 8, D]), w_sguT=din("w_sguT", [128, 4, 128]),
        b_sgu=din("b_sgu", [1, 512]), sg0=din("sg0", [1, 8]),
        w_in_o=din("w_in_o", [128, 8, 2048]), rg_wa=din("rg_wa", [128, 8, 128]),
        rg_wx=din("rg_wx", [128, 8, 128]), w_out_o=din("w_out_o", [128, 8, D]),
        w_up=din("w_up", [2, 128, 8, F2]), w_dn=din("w_dn", [2, 128, NJ, D]),
        ck=din("ck", [NS, LCACHE, 512]), cv=din("cv", [NS, LCACHE, 512]),
        st_rgc=din("st_rgc", [128, 8, 3, NS]), st_rgh=din("st_rgh", [128, 8, NS]),
        st_ffn=din("st_ffn", [2, 128, 44, 2, NS]),
        rowmask=din("rowmask", [128, NS]), dmask=din("dmask", [128, 512]),
    )
    O = dict(
        yT=dout("yT", [128, 8, T]), ysT=dout("ysT", [128, 8, NS]),
        kT_o=dout("kT_o", [128, 4, TK]), vT_o=dout("vT_o", [128, 4, TK]),
        rgc_o=dout("rgc_o", [128, 8, 3]), rgh_o=dout("rgh_o", [128, 8]),
        ffn_o=dout("ffn_o", [2, 128, 44, 2]), cv_o=dout("cv_o", [128, 4, NS]),
        wk_s=dout("wk_s", [NS, LCACHE, 512]), wv_s=dout("wv_s", [NS, LCACHE, 512]),
        rgc_s=dout("rgc_s", [128, 8, 3, NS]), rgh_s=dout("rgh_s", [128, 8, NS]),
        ffn_s=dout("ffn_s", [2, 128, 44, 2, NS]),
    )
    X1 = dscr("x1T", [128, 8, T]); X2 = dscr("x2T", [128, 8, T]); X3 = dscr("x3T", [128, 8, T])
    QS = dscr("qT_s", [128, 4, T], BF16); KS = dscr("kT_s", [128, 4, T], BF16); VS = dscr("vT_s", [128, 4, T], BF16)
    AOS = dscr("aoT_s", [128, 4, T], BF16)
    OS = dscr("O_s", [3, T, 8 * 66])
    QKVS = dscr("qkv_s", [NS, 1536])
    XS1 = None

    top = contextlib.ExitStack()
    with top:
        P = Prog(nc, top)

        def SB(es, name, shape, dt=F32):
            return es.enter_context(nc.sbuf_tensor("sb_" + name, list(shape), dt))

        def PS(es, name, shape, dt=F32):
            return es.enter_context(nc.psum_tensor("ps_" + name, list(shape), dt))

        V, A, G, PE = nc.vector, nc.scalar, nc.gpsimd, nc.tensor

        par = SB(top, "par", [128, NPAR])
        ones_b = SB(top, "ones_b", [128, 128], BF16)
        ones_f = SB(top, "ones_f", [128, 128])
        ident_f = SB(top, "ident_f", [128, 128])
        ident_b = SB(top, "ident_b", [128, 128], BF16)
        modP = SB(top, "modP", [128, 2, 48])
        modS = SB(top, "modS", [128, 2, 48, NS])
        cst = SB(top, "cst", [128, 8])
        cst2 = SB(top, "cst2", [128, 8])
        xs_cur = SB(top, "xs_cur", [128, 8, NS])
        epsb = SB(top, "epsb", [128, 1])
        oneb = SB(top, "oneb", [128, 1])

        def pcol(name, i=0, n=1):
            o = PAR_OFF[name] + i
            return par[:, o:o + n]

        P.dma("sp", lambda: nc.sync.dma_start(out=par[:], in_=I["params"][:, :]), w=["par"])
        P.op("pool", lambda: G.memset(ones_f[:], 1.0), w=["ones_f"])
        P.op("pool", lambda: G.memset(ones_b[:], 1.0), w=["ones_b"])
        P.op("pool", lambda: G.memset(epsb[:], EPS), w=["epsb"])
        P.op("pool", lambda: G.memset(oneb[:], 1.0), w=["oneb"])
        P.op("pool", lambda: G.memset(ident_f[:], 1.0), w=["ident_f"])
        P.op("pool", lambda: G.affine_select(out=ident_f[:], in_=ident_f[:], pattern=[[-1, 128]],
                                             compare_op=ALU.is_equal, fill=0.0, base=0, channel_multiplier=1),
             r=["ident_f"], w=["ident_f"])
        P.op("dve", lambda: V.tensor_copy(out=ident_b[:], in_=ident_f[:]), r=["ident_f"], w=["ident_b"])
        P.dma("sp", lambda: nc.sync.dma_start(out=xs_cur[:], in_=I["xsT"][:, :, :]), w=["xs"])

        with contextlib.ExitStack() as ph:
            cT = SB(ph, "cT_sb", [128, 8, 1 + NS])
            cb = SB(ph, "cb_sb", [128, 8, 1 + NS], BF16)
            wada = SB(ph, "wada_sb", [128, 8, 6144], BF16)
            mps = PS(ph, "mod_ps", [128, 48, 1 + NS])
            P.dma("sp", lambda: nc.sync.dma_start(out=cT[:], in_=I["cT"][:, :, :]), w=["cT"])
            P.op("act", lambda: A.activation(out=cb[:], in_=cT[:], func=AF.Silu), r=["cT"], w=["cb"])
            for l in range(2):
                for kc in range(8):
                    P.dma("pool", (lambda l=l, kc=kc: G.dma_start(out=wada[:, kc, :], in_=I["wada"][l, :, kc, :])),
                          w=[("wada", kc)])
                for n in range(48):
                    for kc in range(8):
                        P.op("pe", (lambda n=n, kc=kc: PE.matmul(mps[:, n, :], lhsT=wada[:, kc, n * 128:(n + 1) * 128],
                                                                  rhs=cb[:, kc, :], start=(kc == 0), stop=(kc == 7))),
                             r=[("wada", kc), "cb"], w=["mps"])
                bada = par[:, PAR_OFF["b_ada"] + l * 48: PAR_OFF["b_ada"] + (l + 1) * 48]
                P.op("dve", (lambda l=l, bada=bada: V.tensor_tensor(out=modP[:, l, :], in0=mps[:, :, 0], in1=bada, op=ALU.add)),
                     r=["mps", "par"], w=["modP"])
                P.op("dve", (lambda l=l, bada=bada: V.tensor_tensor(
                    out=modS[:, l, :, :], in0=mps[:, :, 1:1 + NS],
                    in1=bada.unsqueeze(2).to_broadcast([128, 48, NS]), op=ALU.add)),
                     r=["mps", "par"], w=["modS"])
            for l in range(2):
                for c0 in (8, 32):
                    P.op("dve", (lambda l=l, c0=c0: V.tensor_scalar_add(out=modP[:, l, c0:c0 + 8], in0=modP[:, l, c0:c0 + 8], scalar1=1.0)),
                         r=["modP"], w=["modP"])
                    P.op("dve", (lambda l=l, c0=c0: V.tensor_scalar_add(out=modS[:, l, c0:c0 + 8, :], in0=modS[:, l, c0:c0 + 8, :], scalar1=1.0)),
                         r=["modS"], w=["modS"])
            lam = pcol("rg_lam", 0, 8)
            P.op("act", lambda: A.activation(out=cst[:], in_=lam, func=AF.Exp, scale=-1.0), r=["par"], w=["cst"])
            P.op("act", lambda: A.activation(out=cst[:], in_=cst[:], func=AF.Ln, bias=oneb[:, 0:1], scale=1.0), r=["cst", "oneb"], w=["cst"])
            P.op("dve", lambda: V.tensor_scalar_mul(out=cst2[:], in0=cst[:], scalar1=-16.0), r=["cst"], w=["cst2"])
            P.op("dve", lambda: V.tensor_scalar_mul(out=cst[:], in0=cst[:], scalar1=-8.0), r=["cst", "cst2"], w=["cst"])
            P.flush()

        SH1, SC1, G1, SH2, SC2, G2 = 0, 8, 16, 24, 32, 40

        def norm_prompt(xkey, tag, xt, TT, l, sh0, sc0, sqb, nps, rt, hT, xn, hkey=None):
            hkey = hkey or (tag, "hT")
            P.op("act", lambda: A.activation(out=sqb[:, :, :TT], in_=xt[:, :, :TT], func=AF.Square), r=[xkey], w=[(tag, "sqb")])
            for c in range(8):
                P.op("pe", (lambda c=c: PE.matmul(nps[:, :TT], lhsT=ones_b[:], rhs=sqb[:, c, :TT], start=(c == 0), stop=(c == 7))),
                     r=[(tag, "sqb"), "ones_b"], w=[(tag, "nps")])
            P.op("act", lambda: A.activation(out=rt[:, :TT], in_=nps[:, :TT], func=AF.Sqrt, scale=1.0 / D, bias=epsb[:, 0:1]),
                 r=[(tag, "nps"), "epsb"], w=[(tag, "rt")])
            P.op("dve", lambda: V.reciprocal(out=rt[:, :TT], in_=rt[:, :TT]), r=[(tag, "rt")], w=[(tag, "rt")])
            P.op("dve", lambda: V.tensor_tensor(out=xn[:, :, :TT], in0=xt[:, :, :TT],
                                                in1=rt[:, :TT].unsqueeze(1).to_broadcast([128, 8, TT]), op=ALU.mult),
                 r=[xkey, (tag, "rt")], w=[(tag, "xn")])
            for c in range(8):
                P.op("act", (lambda c=c: A.activation(out=hT[:, c, :TT], in_=xn[:, c, :TT], func=AF.Identity,
                                                      scale=modP[:, l, sc0 + c:sc0 + c + 1], bias=modP[:, l, sh0 + c:sh0 + c + 1])),
                     r=[(tag, "xn"), "modP"], w=[hkey])

        def norm_sample(tag, l, sh0, sc0, sq, nps, rt, hT, xn):
            P.op("dve", lambda: V.tensor_tensor(out=sq[:], in0=xs_cur[:], in1=xs_cur[:], op=ALU.mult), r=["xs"], w=[(tag, "sq")])
            for c in range(8):
                P.op("pe", (lambda c=c: PE.matmul(nps[:, :NS], lhsT=ones_f[:], rhs=sq[:, c, :], start=(c == 0), stop=(c == 7))),
                     r=[(tag, "sq"), "ones_f"], w=[(tag, "nps")])
            P.op("act", lambda: A.activation(out=rt[:, :NS], in_=nps[:, :NS], func=AF.Sqrt, scale=1.0 / D, bias=epsb[:, 0:1]),
                 r=[(tag, "nps"), "epsb"], w=[(tag, "rt")])
            P.op("dve", lambda: V.reciprocal(out=rt[:, :NS], in_=rt[:, :NS]), r=[(tag, "rt")], w=[(tag, "rt")])
            P.op("dve", lambda: V.tensor_tensor(out=xn[:], in0=xs_cur[:], in1=rt[:, :NS].unsqueeze(1).to_broadcast([128, 8, NS]), op=ALU.mult),
                 r=["xs", (tag, "rt")], w=[(tag, "xn")])
            P.op("dve", lambda: V.tensor_tensor(out=xn[:], in0=xn[:], in1=modS[:, l, sc0:sc0 + 8, :], op=ALU.mult),
                 r=[(tag, "xn"), "modS"], w=[(tag, "xn")])
            P.op("dve", lambda: V.tensor_tensor(out=hT[:], in0=xn[:], in1=modS[:, l, sh0:sh0 + 8, :], op=ALU.add),
                 r=[(tag, "xn"), "modS"], w=[(tag, "hT")])

        def load_w(q_tensor, dst, src_ap_fn, nk, key):
            for kc in range(nk):
                P.dma("pool", (lambda kc=kc: G.dma_start(out=dst[:, kc, :], in_=src_ap_fn(kc))), w=[(key, kc)])

        with contextlib.ExitStack() as ph:
            w_in = SB(ph, "w_in", [128, 8, NIN_E], BF16)
            load_w(None, w_in, lambda kc: I["w_in_e"][:, kc, :], 8, "w_in")
            for s in range(NS):
                P.dma("actq", (lambda s=s: nc.scalar.dma_start(out=O["wk_s"][s, 0:LCACHE - 1, :], in_=I["ck"][s, 1:LCACHE, :])), w=[("wk", s)])
                P.dma("actq", (lambda s=s: nc.scalar.dma_start(out=O["wv_s"][s, 0:LCACHE - 1, :], in_=I["cv"][s, 1:LCACHE, :])), w=[("wv", s)])
            wsT = SB(ph, "wsT", [128, 4, 128])
            wsTb = SB(ph, "wsTb", [128, 4, 128], BF16)
            bsb = SB(ph, "bsb", [128, 512])
            lng_bc = SB(ph, "lng_bc", [128, 512]); lnb_bc = SB(ph, "lnb_bc", [128, 512])
            sg0 = SB(ph, "sg0", [128, 8])
            P.dma("sp", lambda: nc.sync.dma_start(out=wsT[:], in_=I["w_sguT"][:, :, :]), w=["wsT"])
            P.dma("sp", lambda: nc.sync.dma_start(out=bsb[:], in_=I["b_sgu"][0:1, :].partition_broadcast(128)), w=["bsb"])
            P.dma("sp", lambda: nc.sync.dma_start(out=sg0[:], in_=I["sg0"][0:1, :].partition_broadcast(128)), w=["sg0"])
            P.op("pool", lambda: G.affine_select(out=wsT[:], in_=wsT[:], pattern=[[0, 4], [1, 128]], compare_op=ALU.is_ge,
                                                 fill=0.0, base=0, channel_multiplier=-1), r=["wsT"], w=["wsT"])
            P.op("dve", lambda: V.tensor_copy(out=wsTb[:], in_=wsT[:]), r=["wsT"], w=["wsTb"])
            dg = SB(ph, "dg", [128, 8, 128])
            for i2, nm in enumerate(("ln_g", "ln_b")):
                for c in range(4):
                    P.op("dve", (lambda i2=i2, c=c, nm=nm: V.tensor_scalar_mul(out=dg[:, i2 * 4 + c, :], in0=ident_f[:], scalar1=pcol(nm, c))),
                         r=["ident_f", "par"], w=[("dg", i2 * 4 + c)])

            xt_ = [SB(ph, "xt%d" % i, [128, 8, 512]) for i in range(2)]
            sqb = SB(ph, "sqb", [128, 8, 512], BF16)
            xn = SB(ph, "xn", [128, 8, 512])
            rt = SB(ph, "rt", [128, 512])
            hT = SB(ph, "hT", [128, 8, 512], BF16)
            uT = SB(ph, "uT", [128, 4, 512])
            gv_ = [SB(ph, "gv%d" % i, [128, 512]) for i in range(2)]
            st6_ = [SB(ph, "st6_%d" % i, [128, 6]) for i in range(2)]; mv_ = [SB(ph, "mv%d" % i, [128, 2]) for i in range(2)]; rs_ = [SB(ph, "rs%d" % i, [128, 1]) for i in range(2)]
            va_ = [SB(ph, "va%d" % i, [128, 512]) for i in range(2)]; vab_ = [SB(ph, "vab%d" % i, [128, 512], BF16) for i in range(4)]
            mixs_ = [SB(ph, "mixs%d" % i, [128, 512]) for i in range(2)]
            aoT = [SB(ph, "aoT%d" % i, [128, 4, 512], BF16) for i in range(1)]
            qTb = [SB(ph, "qTb%d" % i, [128, 4, 512], BF16) for i in range(1)]
            kTf = [SB(ph, "kTf%d" % i, [128, 4, 512]) for i in range(1)]
            vTf = [SB(ph, "vTf%d" % i, [128, 4, 512]) for i in range(1)]
            kTb = [SB(ph, "kTb%d" % i, [128, 4, 512], BF16) for i in range(1)]
            vTb = [SB(ph, "vTb%d" % i, [128, 4, 512], BF16) for i in range(1)]
            nps = PS(ph, "nps1", [128, 512])
            zps = [PS(ph, "zps%d" % i, [128, 512]) for i in range(3)]
            vps = [PS(ph, "vps%d" % i, [128, 512]) for i in range(2)]
            mps2 = [PS(ph, "mixps%d" % i, [128, 512]) for i in range(2)]
            zcount = [0]
            for i2 in range(2):
                for c in range(4):
                    P.op("pe", (lambda i2=i2, c=c: PE.matmul(zps[i2][:, c * 128:(c + 1) * 128], lhsT=ones_f[:],
                                                              rhs=dg[:, i2 * 4 + c, :], start=True, stop=True)),
                         r=[("dg", i2 * 4 + c), "ones_f"], w=[("zps", i2)])
            P.op("dve", lambda: V.tensor_copy(out=lng_bc[:], in_=zps[0][:]), r=[("zps", 0)], w=["lng_bc"])
            P.op("dve", lambda: V.tensor_copy(out=lnb_bc[:], in_=zps[1][:]), r=[("zps", 1)], w=["lnb_bc"])

            def loadx(i):
                b = i % 2
                P.dma("sp", (lambda: nc.sync.dma_start(out=xt_[b][:], in_=I["xT"][:, :, i * 512:(i + 1) * 512])), w=[("xt1", b)])

            loadx(0)
            for i in range(NT5):
                b = i % 2
                if i + 1 < NT5:
                    loadx(i + 1)
                tag = "p1"
                norm_prompt(("xt1", b), tag, xt_[b], 512, 0, SH1, SC1, sqb, nps, rt, hT, xn)
                t0 = i * 512
                xb_ = b
                b = 0

                def zchunk(n):
                    z = zcount[0] % 3
                    zcount[0] += 1
                    for kc in range(8):
                        P.op("pe", (lambda kc=kc, n=n, z=z: PE.matmul(zps[z][:], lhsT=w_in[:, kc, n * 128:(n + 1) * 128], rhs=hT[:, kc, :],
                                                                       start=(kc == 0), stop=(kc == 7))),
                             r=[("w_in", kc), (tag, "hT")], w=[("zps", z)])
                    return z
                for n in range(4):
                    z = zchunk(n)
                    P.op("act", (lambda n=n, z=z: A.activation(out=uT[:, n, :], in_=zps[z][:], func=GELU)), r=[("zps", z)], w=[("uT", n)])
                for blk in range(4):
                    vb = blk % 2
                    gv, st6, mv, rs, va, vab = gv_[vb], st6_[vb], mv_[vb], rs_[vb], va_[vb], vab_[blk]
                    for kc in range(8):
                        P.op("pe", (lambda kc=kc, blk=blk, vb=vb: PE.matmul(vps[vb][:], lhsT=hT[:, kc, blk * 128:(blk + 1) * 128], rhs=w_in[:, kc, 512:1024],
                                                                             start=(kc == 0), stop=(kc == 7))),
                             r=[("w_in", kc), (tag, "hT")], w=[("vps", vb)])
                    P.op("act", (lambda gv=gv, vb=vb: A.activation(out=gv[:], in_=vps[vb][:], func=GELU)), r=[("vps", vb)], w=[("gv", vb)])
                    P.op("dve", (lambda gv=gv, st6=st6: V.bn_stats(out=st6[:], in_=gv[:])), r=[("gv", vb)], w=[("st6", vb)])
                    P.op("dve", (lambda mv=mv, st6=st6: V.bn_aggr(out=mv[:], in_=st6[:])), r=[("st6", vb)], w=[("mv", vb)])
                    P.op("act", (lambda mv=mv, rs=rs: A.activation(out=rs[:], in_=mv[:, 1:2], func=AF.Sqrt, scale=1.0, bias=epsb[:, 0:1])), r=[("mv", vb), "epsb"], w=[("rs", vb)])
                    P.op("dve", (lambda rs=rs: V.reciprocal(out=rs[:], in_=rs[:])), r=[("rs", vb)], w=[("rs", vb)])
                    P.op("dve", (lambda va=va, gv=gv, mv=mv, rs=rs: V.tensor_scalar(out=va[:], in0=gv[:], scalar1=mv[:, 0:1], scalar2=rs[:, 0:1], op0=ALU.subtract, op1=ALU.mult)),
                         r=[("gv", vb), ("mv", vb), ("rs", vb)], w=[("va", vb)])
                    P.op("pool", (lambda va=va: G.tensor_tensor(out=va[:], in0=va[:], in1=lng_bc[:], op=ALU.mult)), r=[("va", vb), "lng_bc"], w=[("va", vb)])
                    P.op("pool", (lambda va=va, vab=vab: G.tensor_tensor(out=vab[:], in0=va[:], in1=lnb_bc[:], op=ALU.add)), r=[("va", vb), "lnb_bc"], w=[("vab", blk)])

                def sgu_mix():
                    for blk in range(4):
                        vb = blk % 2
                        vab, mixs = vab_[blk], mixs_[vb]
                        for g in range(4):
                            P.op("pe", (lambda g=g, vab=vab, vb=vb: PE.matmul(mps2[vb][:, g * 128:(g + 1) * 128], lhsT=vab[:, g * 128:(g + 1) * 128], rhs=wsTb[:, g, :],
                                                                               start=True, stop=True)), r=[("vab", blk), "wsTb"], w=[("mixps", vb)])
                        P.op("dve", (lambda mixs=mixs, vb=vb: V.tensor_tensor(out=mixs[:], in0=mps2[vb][:], in1=bsb[:], op=ALU.add)), r=[("mixps", vb), "bsb"], w=[("mixs", vb)])
                        P.op("pool", (lambda blk=blk, b=b, mixs=mixs: G.tensor_tensor(out=aoT[b][:, :, blk * 128:(blk + 1) * 128],
                                                                                      in0=mixs[:].rearrange("p (g t) -> p g t", g=4),
                                                                                      in1=uT[:, :, blk * 128:(blk + 1) * 128], op=ALU.mult)),
                             r=[("mixs", vb)] + [("uT", n) for n in range(4)], w=[("aoT", b, blk)])
                    P.dma("pool", (lambda b=b, t0=t0: G.dma_start(out=AOS[:, :, t0:t0 + 512], in_=aoT[b][:])),
                          r=[("aoT", b, k) for k in range(4)], w=[("AOS", i)])

                for n in range(4):
                    z = zchunk(8 + n)
                    P.op("act", (lambda n=n, z=z, b=b: A.activation(out=qTb[b][:, n, :], in_=zps[z][:], func=AF.Copy, scale=0.125)),
                         r=[("zps", z)], w=[("qTb", b, n)])
                P.dma("pool", (lambda b=b, t0=t0: G.dma_start(out=QS[:, :, t0:t0 + 512], in_=qTb[b][:])),
                      r=[("qTb", b, k) for k in range(4)], w=[("QS", i)])
                for (off, tf, tb, SCR, OUT, nm) in ((12, kTf, kTb, KS, O["kT_o"], "k"), (16, vTf, vTb, VS, O["vT_o"], "v")):
                    for n in range(4):
                        z = zchunk(off + n)
                        P.op("act", (lambda n=n, z=z, b=b, tf=tf: A.activation(out=tf[b][:, n, :], in_=zps[z][:], func=AF.Copy)),
                             r=[("zps", z)], w=[(nm + "f", b, n)])
                        P.op("dve", (lambda n=n, b=b, tf=tf, tb=tb: V.tensor_copy(out=tb[b][:, n, :], in_=tf[b][:, n, :])),
                             r=[(nm + "f", b, n)], w=[(nm + "b", b, n)])
                    P.dma("pool", (lambda b=b, t0=t0, tb=tb, SCR=SCR: G.dma_start(out=SCR[:, :, t0:t0 + 512], in_=tb[b][:])),
                          r=[(nm + "b", b, k) for k in range(4)], w=[(nm + "S", i)])
                    if t0 >= T - TK:
                        o0 = t0 - (T - TK)
                        P.dma("pool", (lambda b=b, o0=o0, tf=tf, OUT=OUT: G.dma_start(out=OUT[:, :, o0:o0 + 512], in_=tf[b][:])),
                              r=[(nm + "f", b, k) for k in range(4)], w=[(nm + "O", i)])
                sgu_mix()

            sq_s = SB(ph, "sq_s", [128, 8, NS]); xn_s = SB(ph, "xn_s", [128, 8, NS]); hTs = SB(ph, "hTs", [128, 8, NS], BF16)
            guv = SB(ph, "guv", [128, 8, NS]); gsq = SB(ph, "gsq", [128, 4, NS])
            mean_s = SB(ph, "mean_s", [128, NS]); var_s = SB(ph, "var_s", [128, NS]); msq_s = SB(ph, "msq_s", [128, NS])
            vas = SB(ph, "vas", [128, 4, NS]); aos = SB(ph, "aos", [128, 4, NS]); aosb = SB(ph, "aosb", [128, 4, NS], BF16)
            qkv = SB(ph, "qkv", [NS, 1536])
            norm_sample("s1", 0, SH1, SC1, sq_s, nps, rt, hTs, xn_s)
            for n in range(8):
                for kc in range(8):
                    P.op("pe", (lambda n=n, kc=kc: PE.matmul(zps[0][:, n * NS:(n + 1) * NS], lhsT=w_in[:, kc, n * 128:(n + 1) * 128], rhs=hTs[:, kc, :],
                                                              start=(kc == 0), stop=(kc == 7))), r=[("w_in", kc), ("s1", "hT")], w=[("zps", 0)])
            P.op("act", lambda: A.activation(out=guv[:], in_=zps[0][:, 0:8 * NS].rearrange("p (c s) -> p c s", c=8), func=GELU),
                 r=[("zps", 0)], w=["guv"])
            P.op("dve", lambda: V.tensor_tensor(out=gsq[:], in0=guv[:, 4:8, :], in1=guv[:, 4:8, :], op=ALU.mult), r=["guv"], w=["gsq"])
            for c in range(4):
                P.op("pe", (lambda c=c: PE.matmul(zps[1][:, 0:NS], lhsT=ones_f[:], rhs=guv[:, 4 + c, :], start=(c == 0), stop=(c == 3))),
                     r=["guv", "ones_f"], w=[("zps", 1)])
            for c in range(4):
                P.op("pe", (lambda c=c: PE.matmul(zps[1][:, 8:8 + NS], lhsT=ones_f[:], rhs=gsq[:, c, :], start=(c == 0), stop=(c == 3))),
                     r=["gsq", "ones_f"], w=[("zps", 1)])
            P.op("dve", lambda: V.tensor_scalar_mul(out=mean_s[:], in0=zps[1][:, 0:NS], scalar1=1.0 / WA), r=[("zps", 1)], w=["mean_s"])
            P.op("dve", lambda: V.tensor_scalar_mul(out=var_s[:], in0=zps[1][:, 8:8 + NS], scalar1=1.0 / WA), r=[("zps", 1)], w=["var_s"])
            P.op("dve", lambda: V.tensor_tensor(out=msq_s[:], in0=mean_s[:], in1=mean_s[:], op=ALU.mult), r=["mean_s"], w=["msq_s"])
            P.op("dve", lambda: V.tensor_tensor(out=var_s[:], in0=var_s[:], in1=msq_s[:], op=ALU.subtract), r=["var_s", "msq_s"], w=["var_s"])
            P.op("act", lambda: A.activation(out=var_s[:], in_=var_s[:], func=AF.Sqrt, scale=1.0, bias=epsb[:, 0:1]), r=["var_s", "epsb"], w=["var_s"])
            P.op("dve", lambda: V.reciprocal(out=var_s[:], in_=var_s[:]), r=["var_s"], w=["var_s"])
            P.op("dve", lambda: V.tensor_tensor(out=vas[:], in0=guv[:, 4:8, :], in1=mean_s[:].unsqueeze(1).to_broadcast([128, 4, NS]), op=ALU.subtract),
                 r=["guv", "mean_s"], w=["vas"])
            P.op("dve", lambda: V.tensor_tensor(out=vas[:], in0=vas[:], in1=var_s[:].unsqueeze(1).to_broadcast([128, 4, NS]), op=ALU.mult),
                 r=["vas", "var_s"], w=["vas"])
            P.op("dve", lambda: V.tensor_tensor(out=vas[:], in0=vas[:], in1=pcol("ln_g", 0, 4).unsqueeze(2).to_broadcast([128, 4, NS]), op=ALU.mult),
                 r=["vas", "par"], w=["vas"])
            P.op("dve", lambda: V.tensor_tensor(out=vas[:], in0=vas[:], in1=pcol("ln_b", 0, 4).unsqueeze(2).to_broadcast([128, 4, NS]), op=ALU.add),
                 r=["vas", "par"], w=["vas"])
            P.dma("sp", lambda: nc.sync.dma_start(out=O["cv_o"][:, :, :], in_=vas[:]), r=["vas"], w=["cv_o"])
            P.op("dve", lambda: V.tensor_tensor(out=aos[:], in0=vas[:], in1=sg0[:, 0:4].unsqueeze(2).to_broadcast([128, 4, NS]), op=ALU.mult),
                 r=["vas", "sg0"], w=["aos"])
            P.op("dve", lambda: V.tensor_tensor(out=aos[:], in0=aos[:], in1=sg0[:, 4:8].unsqueeze(2).to_broadcast([128, 4, NS]), op=ALU.add),
                 r=["aos", "sg0"], w=["aos"])
            P.op("dve", lambda: V.tensor_tensor(out=aosb[:], in0=aos[:], in1=guv[:, 0:4, :], op=ALU.mult), r=["aos", "guv"], w=["aosb"])
            AOSS = dscr("aoT_ss", [128, 4, NS], BF16)
            P.dma("sp", lambda: nc.sync.dma_start(out=AOSS[:, :, :], in_=aosb[:]), r=["aosb"], w=["AOSS"])
            for blk3 in range(3):
                for kc in range(8):
                    P.op("pe", (lambda blk3=blk3, kc=kc: PE.matmul(zps[2][0:NS, :], lhsT=hTs[:, kc, :], rhs=w_in[:, kc, 1024 + blk3 * 512: 1024 + (blk3 + 1) * 512],
                                                                    start=(kc == 0), stop=(kc == 7))), r=[("w_in", kc), ("s1", "hT")], w=[("zps", 2)])
                P.op("act", (lambda blk3=blk3: A.activation(out=qkv[:, blk3 * 512:(blk3 + 1) * 512], in_=zps[2][0:NS, :], func=AF.Copy,
                                                            scale=(0.125 if blk3 == 0 else 1.0))), r=[("zps", 2)], w=["qkv"])
            P.dma("sp", lambda: nc.sync.dma_start(out=QKVS[:, :], in_=qkv[:]), r=["qkv"], w=["QKVS"])
            P.flush()

        with contextlib.ExitStack() as ph:
            qT = SB(ph, "qT", [128, 4, T], BF16); kT = SB(ph, "kT", [128, 4, T], BF16); vT = SB(ph, "vT", [128, 4, T], BF16)
            NBT = T // 128
            Vp = SB(ph, "Vp", [128, NBT, 8 * 65], BF16)
            mask2 = SB(ph, "mask2", [128, 2, 128], BF16)
            E_ = [SB(ph, "E%d" % i, [128, 4, 256], BF16) for i in range(2)]
            PTs = [SB(ph, "PTs%d" % i, [128, 4, 2, 128], BF16) for i in range(2)]
            Osb = [SB(ph, "Osb%d" % i, [128, 8, 66]) for i in range(4)]
            negm = [SB(ph, "negm%d" % i, [128, 4]) for i in range(2)]
            Sps = [PS(ph, "Sps%d" % i, [128, 1024]) for i in range(2)]
            PTp = [PS(ph, "PTp%d" % i, [128, 1024], BF16) for i in range(2)]
            Ops = [PS(ph, "Ops%d" % i, [128, 512]) for i in range(2)]
            P.dma("sp", lambda: nc.sync.dma_start(out=qT[:], in_=QS[:, :, :]), w=["qT"])
            P.dma("sp", lambda: nc.sync.dma_start(out=kT[:], in_=KS[:, :, :]), w=["kT"])
            P.dma("sp", lambda: nc.sync.dma_start(out=vT[:], in_=VS[:, :, :]), w=["vT"])
            P.op("pool", lambda: G.memset(Vp[:], 1.0), w=[("Vp", k) for k in range(NBT)])
            P.op("pool", lambda: G.memset(mask2[:], 1.0), w=["mask2"])
            P.op("pool", lambda: G.affine_select(out=mask2[:, 0, :], in_=mask2[:, 0, :], pattern=[[-1, 128]], compare_op=ALU.is_ge, fill=0.0,
                                                 base=0, channel_multiplier=1), r=["mask2"], w=["mask2"])
            P.op("pool", lambda: G.affine_select(out=mask2[:, 1, :], in_=mask2[:, 1, :], pattern=[[1, 128]], compare_op=ALU.is_ge, fill=0.0,
                                                 base=0, channel_multiplier=-1), r=["mask2"], w=["mask2"])
            u = 0
            ucnt = [0]
            for pi, (wdw, d) in enumerate(PATTERNS):
                M = T // d
                nb = M // 128
                for r_ in range(d):
                    for b_ in range(nb):
                        bi = r_ * nb + b_
                        tok = slice(r_ + d * 128 * b_, r_ + d * 128 * b_ + d * 127 + 1, d)
                        pb = u % 2
                        for c in range(4):
                            P.op("pe", (lambda c=c, tok=tok, pb=pb: PE.transpose(PTp[pb][:, c * 128:(c + 1) * 128], vT[:, c, tok], ident_b[:])),
                                 r=["vT", "ident_b"], w=[("PTp", pb)])
                        P.op("act", (lambda bi=bi, pb=pb: A.activation(out=Vp[:, bi, :].rearrange("p (h e) -> p h e", h=8)[:, :, 0:64],
                                                                      in_=PTp[pb][:, 0:512].rearrange("p (h e) -> p h e", h=8), func=AF.Copy)),
                             r=[("PTp", pb)], w=[("Vp", bi)])
                        u += 1
                units = []
                for r_ in range(d):
                    for b_ in range(nb):
                        bi = r_ * nb + b_
                        nkb = 2 if b_ > 0 else 1
                        nk = 128 * nkb
                        qtok = slice(r_ + d * 128 * b_, r_ + d * 128 * b_ + d * 127 + 1, d)
                        k0 = r_ + d * 128 * (b_ - (nkb - 1))
                        ktok = slice(k0, k0 + d * (nk - 1) + 1, d)
                        for hh in range(2):
                            units.append(dict(bi=bi, nkb=nkb, nk=nk, qtok=qtok, ktok=ktok, hh=hh, ob=bi % 4))
                for ui, U in enumerate(units):
                    U["pb"] = (ucnt[0] + ui) % 2
                ucnt[0] += len(units)

                def colS(h4):
                    return (h4 % 2) * 512 + (h4 // 2) * 256

                def stageA(U):
                    pb, nk, hh, ob, qtok, ktok = U["pb"], U["nk"], U["hh"], U["ob"], U["qtok"], U["ktok"]
                    for h4 in range(4):
                        h = hh * 4 + h4
                        c, po = h // 2, (h % 2) * 64
                        P.op("pe", (lambda h4=h4, c=c, po=po: PE.matmul(Sps[pb][:, colS(h4): colS(h4) + nk], lhsT=qT[po:po + 64, c, qtok], rhs=kT[po:po + 64, c, ktok],
                                                                         start=True, stop=True)), r=["qT", "kT"], w=[("Sps", pb)])
                    for bk in range(2):
                        Sv = Sps[pb][:, bk * 512:(bk + 1) * 512].rearrange("p (h k) -> p h k", h=2)[:, :, 0:nk]
                        P.op("dve", (lambda Sv=Sv, bk=bk: V.tensor_reduce(out=Osb[ob][:, hh * 4 + bk:hh * 4 + bk + 3:2, 65], in_=Sv, axis=AX.X, op=ALU.max)),
                             r=[("Sps", pb)], w=[("Osb", ob, hh, "m")])
                    P.op("dve", (lambda: V.tensor_scalar_mul(out=negm[pb][:], in0=Osb[ob][:, hh * 4:(hh + 1) * 4, 65], scalar1=-1.0)),
                         r=[("Osb", ob, hh, "m")], w=[("negm", pb)])
                    for h4 in range(4):
                        P.op("act", (lambda h4=h4: A.activation(out=E_[pb][:, h4, 0:nk], in_=Sps[pb][:, colS(h4):colS(h4) + nk], func=AF.Exp,
                                                                 bias=negm[pb][:, h4:h4 + 1], scale=1.0)),
                             r=[("Sps", pb), ("negm", pb)], w=[("E", pb)])

                def stageB(U):
                    pb, nkb = U["pb"], U["nkb"]
                    for h4 in range(4):
                        for kb in range(nkb):
                            P.op("pe", (lambda h4=h4, kb=kb: PE.transpose(PTp[pb][:, (h4 * 2 + kb) * 128:(h4 * 2 + kb + 1) * 128],
                                                                          E_[pb][:, h4, kb * 128:(kb + 1) * 128], ident_b[:])),
                                 r=[("E", pb), "ident_b"], w=[("PTp", pb)])
                    PTv = PTp[pb][:].rearrange("p (h k q) -> p h k q", h=4, k=2)
                    if nkb == 2:
                        P.op("dve", (lambda: V.tensor_tensor(out=PTs[pb][:], in0=PTv, in1=mask2[:].unsqueeze(1).to_broadcast([128, 4, 2, 128]), op=ALU.mult)),
                             r=[("PTp", pb), "mask2"], w=[("PTs", pb)])
                    else:
                        P.op("dve", (lambda: V.tensor_tensor(out=PTs[pb][:, :, 0, :], in0=PTv[:, :, 0, :],
                                                             in1=mask2[:, 1, :].unsqueeze(1).to_broadcast([128, 4, 128]), op=ALU.mult)),
                             r=[("PTp", pb), "mask2"], w=[("PTs", pb)])

                def stageC(U, pi=pi):
                    pb, nkb, hh, ob, bi, qtok = U["pb"], U["nkb"], U["hh"], U["ob"], U["bi"], U["qtok"]
                    for h4 in range(4):
                        h = hh * 4 + h4
                        for kb in range(nkb):
                            kbi = bi - (nkb - 1) + kb
                            P.op("pe", (lambda h4=h4, h=h, kb=kb, kbi=kbi: PE.matmul(Ops[pb][:, h4 * 128:h4 * 128 + 65], lhsT=PTs[pb][:, h4, kb, :],
                                                                                     rhs=Vp[:, kbi, h * 65:(h + 1) * 65], start=(kb == 0), stop=(kb == nkb - 1))),
                                 r=[("PTs", pb), ("Vp", kbi)], w=[("Ops", pb)])
                    P.op("act", (lambda: A.activation(out=Osb[ob][:, hh * 4:(hh + 1) * 4, 0:65],
                                                      in_=Ops[pb][:].rearrange("p (h e) -> p h e", h=4)[:, :, 0:65], func=AF.Copy)),
                         r=[("Ops", pb)], w=[("Osb", ob, hh, "o")])
                    if hh == 1:
                        orow = OS[pi, qtok, :]
                        P.dma("pool", (lambda: G.dma_start(out=orow, in_=Osb[ob][:].rearrange("p h e -> p (h e)"))),
                              r=[("Osb", ob, 0, "o"), ("Osb", ob, 1, "o"), ("Osb", ob, 0, "m"), ("Osb", ob, 1, "m")], w=[("OS", pi, bi)])

                nu = len(units)
                for st in range(nu + 2):
                    if st < nu:
                        stageA(units[st])
                    if 0 <= st - 1 < nu:
                        stageB(units[st - 1])
                    if 0 <= st - 2 < nu:
                        stageC(units[st - 2])
            P.flush()

        with contextlib.ExitStack() as ph:
            Kt = [SB(ph, "Kt%d" % i, [128, 512]) for i in range(2)]
            Vt = [SB(ph, "Vt%d" % i, [128, 512]) for i in range(16)]
            qbc = [SB(ph, "qbc%d" % i, [128, 512]) for i in range(NS)]
            prod = SB(ph, "prod", [128, 512])
            ST = SB(ph, "ST", [128, 4, 128])
            Es = SB(ph, "Es", [128, 4, 128]); PTs2 = SB(ph, "PTs2", [128, 4, 128])
            mx = SB(ph, "mx_s", [128, 1]); den = SB(ph, "den_s", [128, 1])
            rowm = SB(ph, "rowm", [128, NS]); dmk = SB(ph, "dmk", [128, 512])
            acc = SB(ph, "acc_s", [128, 512]); bo = SB(ph, "bo_s", [128, 64])
            boT = SB(ph, "boT_s", [64, 128], BF16)
            w_oe = SB(ph, "w_oe_s", [128, 4, D], BF16); w_oh = SB(ph, "w_oh_s", [64, 8, D], BF16)
            aosb2 = SB(ph, "aosb2", [128, 4, NS], BF16)
            mo_s = SB(ph, "mo_s", [128, 8, NS])
            Sp = PS(ph, "Sp_s", [128, 512]); PTp2 = PS(ph, "PTp_s", [128, 512])
            Opp = [PS(ph, "Opp%d" % i, [128, 512]) for i in range(NS)]
            bTp = PS(ph, "bTp", [64, 128]); mop = PS(ph, "mop_s", [128, 8 * NS])
            P.dma("sp", lambda: nc.sync.dma_start(out=rowm[:], in_=I["rowmask"][:, :]), w=["rowm"])
            P.dma("sp", lambda: nc.sync.dma_start(out=dmk[:], in_=I["dmask"][:, :]), w=["dmk"])
            for kc in range(4):
                P.dma("pool", (lambda kc=kc: G.dma_start(out=w_oe[:, kc, :], in_=I["w_out_e"][:, kc, :])), w=[("w_oe", kc)])
            for h in range(8):
                P.dma("pool", (lambda h=h: G.dma_start(out=w_oh[:, h, :], in_=I["w_out_eh"][:, h, :])), w=[("w_oh", h)])
            P.dma("sp", lambda: nc.sync.dma_start(out=aosb2[:], in_=AOSS[:, :, :]), w=["aosb2"])
            P.op("pool", lambda: G.memset(ST[:], 0.0), w=["ST"])
            for s in range(NS):
                P.dma("sp", (lambda s=s: nc.sync.dma_start(out=O["wk_s"][s, LCACHE - 1:LCACHE, :], in_=QKVS[s:s + 1, 512:1024])), w=[("wk2", s)])
                P.dma("sp", (lambda s=s: nc.sync.dma_start(out=O["wv_s"][s, LCACHE - 1:LCACHE, :], in_=QKVS[s:s + 1, 1024:1536])), w=[("wv2", s)])
            for s in range(NS):
                P.dma("sp", (lambda s=s: nc.sync.dma_start(out=qbc[s][:], in_=QKVS[s:s + 1, 0:512].partition_broadcast(128))), w=[("qbc", s)])
            it = 0
            for s in range(NS):
                for p in range(4):
                    kb_ = it % 2
                    vi = s * 4 + p
                    if p < 3:
                        dd = PATTERNS[p][1]
                        r0 = LCACHE - 128 * dd
                        ksrc = I["ck"][s, r0:r0 + 127 * dd + 1:dd, :]
                        vsrc = I["cv"][s, r0:r0 + 127 * dd + 1:dd, :]
                    else:
                        ksrc = QKVS[s:s + 1, 512:1024].partition_broadcast(128)
                        vsrc = QKVS[s:s + 1, 1024:1536].partition_broadcast(128)
                    P.dma("sp", (lambda kb_=kb_, ksrc=ksrc: nc.sync.dma_start(out=Kt[kb_][:], in_=ksrc)), w=[("Kt", kb_)])
                    P.dma("sp", (lambda vi=vi, vsrc=vsrc: nc.sync.dma_start(out=Vt[vi][:], in_=vsrc)), w=[("Vt", vi)])
                    P.op("dve", (lambda kb_=kb_, s=s: V.tensor_tensor(out=prod[:], in0=Kt[kb_][:], in1=qbc[s][:], op=ALU.mult)),
                         r=[("Kt", kb_), ("qbc", s)], w=["prod"])
                    P.op("dve", (lambda s=s, p=p: V.tensor_reduce(out=ST[:, p, 32 * s:32 * s + 8], in_=prod[:].rearrange("p (h e) -> p h e", h=8),
                                                                  axis=AX.X, op=ALU.add)), r=["prod"], w=["ST"])
                    it += 1
            for p in range(4):
                P.op("pe", (lambda p=p: PE.transpose(Sp[:, p * 128:(p + 1) * 128], ST[:, p, :], ident_f[:])), r=["ST", "ident_f"], w=["Sp"])
            P.op("dve", lambda: V.tensor_reduce(out=mx[:], in_=Sp[:], axis=AX.X, op=ALU.max, negate=True), r=["Sp"], w=["mx"])
            P.op("act", lambda: A.activation(out=Es[:].rearrange("p a b -> p (a b)"), in_=Sp[:], func=AF.Exp, bias=mx[:, 0:1], scale=1.0), r=["Sp", "mx"], w=["Es"])
            P.op("dve", lambda: V.tensor_scalar_mul(out=Es[:, 3, :], in0=Es[:, 3, :], scalar1=3.0 / 128.0), r=["Es"], w=["Es"])
            P.op("dve", lambda: V.tensor_reduce(out=den[:], in_=Es[:].rearrange("p a b -> p (a b)"), axis=AX.X, op=ALU.add), r=["Es"], w=["den"])
            P.op("dve", lambda: V.reciprocal(out=den[:], in_=den[:]), r=["den"], w=["den"])
            for p in range(4):
                P.op("pe", (lambda p=p: PE.transpose(PTp2[:, p * 128:(p + 1) * 128], Es[:, p, :], ident_f[:])), r=["Es", "ident_f"], w=["PTp2"])
            P.op("dve", lambda: V.tensor_copy(out=PTs2[:].rearrange("p a b -> p (a b)"), in_=PTp2[:]), r=["PTp2"], w=["PTs2"])
            for s in range(NS):
                for p in range(4):
                    P.op("pe", (lambda s=s, p=p: PE.matmul(Opp[s][:], lhsT=PTs2[:, p, :], rhs=Vt[s * 4 + p][:], start=(p == 0), stop=(p == 3))),
                         r=["PTs2", ("Vt", s * 4 + p)], w=[("Opp", s)])
            P.op("dve", lambda: V.tensor_scalar_mul(out=acc[:], in0=Opp[0][:], scalar1=rowm[:, 0:1]), r=[("Opp", 0), "rowm"], w=["acc"])
            for s in range(1, NS):
                P.op("dve", (lambda s=s: V.scalar_tensor_tensor(out=acc[:], in0=Opp[s][:], scalar=rowm[:, s:s + 1], in1=acc[:], op0=ALU.mult, op1=ALU.add)),
                     r=[("Opp", s), "rowm", "acc"], w=["acc"])
            P.op("dve", lambda: V.tensor_tensor(out=acc[:], in0=acc[:], in1=dmk[:], op=ALU.mult), r=["acc", "dmk"], w=["acc"])
            P.op("dve", lambda: V.tensor_reduce(out=bo[:], in_=acc[:].rearrange("p (h e) -> p e h", h=8), axis=AX.X, op=ALU.add), r=["acc"], w=["bo"])
            P.op("dve", lambda: V.tensor_scalar_mul(out=bo[:], in0=bo[:], scalar1=den[:, 0:1]), r=["bo", "den"], w=["bo"])
            P.op("pe", lambda: PE.transpose(bTp[:], bo[:], ident_f[:]), r=["bo", "ident_f"], w=["bTp"])
            P.op("dve", lambda: V.tensor_copy(out=boT[:], in_=bTp[:]), r=["bTp"], w=["boT"])
            for n in range(8):
                for kc in range(4):
                    P.op("pe", (lambda n=n, kc=kc: PE.matmul(mop[:, n * NS:(n + 1) * NS], lhsT=w_oe[:, kc, n * 128:(n + 1) * 128], rhs=aosb2[:, kc, :],
                                                              start=(kc == 0), stop=False)), r=[("w_oe", kc), "aosb2"], w=["mop"])
                for h in range(8):
                    P.op("pe", (lambda n=n, h=h: PE.matmul(mop[:, n * NS:(n + 1) * NS], lhsT=w_oh[:, h, n * 128:(n + 1) * 128], rhs=boT[:, h:128:32],
                                                            start=False, stop=(h == 7))), r=[("w_oh", h), "boT"], w=["mop"])
            P.op("dve", lambda: V.tensor_tensor(out=mo_s[:], in0=mop[:].rearrange("p (c s) -> p c s", c=8), in1=modS[:, 0, G1:G1 + 8, :], op=ALU.mult),
                 r=["mop", "modS"], w=["mo_s"])
            P.op("dve", lambda: V.tensor_tensor(out=xs_cur[:], in0=xs_cur[:], in1=mo_s[:], op=ALU.add), r=["mo_s", "xs"], w=["xs"])
            P.flush()

        with contextlib.ExitStack() as ph:
            w_o = SB(ph, "w_o", [128, 8, D], BF16)
            load_w(None, w_o, lambda kc: I["w_out_e"][:, kc, :], 8, "w_o")
            xt_ = [SB(ph, "xt3_%d" % i, [128, 8, 512]) for i in range(2)]
            ao_ = [SB(ph, "ao3_%d" % i, [128, 4, 512], BF16) for i in range(2)]
            Om = [SB(ph, "Om%d" % i, [128, 4, 3, 8 * 66]) for i in range(2)]
            Mx = SB(ph, "Mx", [128, 8]); dm = SB(ph, "dm", [128, 3, 8]); ee = SB(ph, "ee", [128, 3, 8])
            wt = SB(ph, "wt", [128, 3, 8, 65]); ac = SB(ph, "ac", [128, 8, 65]); rd = SB(ph, "rd", [128, 8])
            bob = SB(ph, "bob", [128, 8, 64], BF16)
            boT3 = SB(ph, "boT3", [128, 4, 512], BF16)
            x1t = [SB(ph, "x1t%d" % i, [128, 8, 512]) for i in range(2)]
            tp = [PS(ph, "tp3_%d" % i, [128, 1024], BF16) for i in range(2)]
            ops_ = [PS(ph, "op3_%d" % i, [128, 512]) for i in range(3)]

            def load3(i):
                b = i % 2
                t0 = i * 512
                P.dma("sp", (lambda: nc.sync.dma_start(out=xt_[b][:], in_=I["xT"][:, :, t0:t0 + 512])), w=[("xt3", b)])
                P.dma("sp", (lambda: nc.sync.dma_start(out=ao_[b][:], in_=AOS[:, :, t0:t0 + 512])), w=[("ao3", b)])
                for pi in range(3):
                    P.dma("sp", (lambda pi=pi: nc.sync.dma_start(out=Om[b][:, :, pi, :], in_=OS[pi, t0:t0 + 512, :].rearrange("(k i) f -> i k f", i=128))),
                          w=[("Om", b, pi)])
            load3(0)
            oc = 0
            for i in range(NT5):
                b = i % 2
                t0 = i * 512
                if i + 1 < NT5:
                    load3(i + 1)
                for blk in range(4):
                    Ov = Om[b][:, blk, :, :].rearrange("p a (h e) -> p a h e", h=8)
                    omk = [("Om", b, pi) for pi in range(3)]
                    P.op("dve", (lambda Ov=Ov: V.tensor_tensor(out=Mx[:], in0=Ov[:, 0, :, 65], in1=Ov[:, 1, :, 65], op=ALU.max)), r=omk, w=["Mx"])
                    P.op("dve", (lambda Ov=Ov: V.tensor_tensor(out=Mx[:], in0=Mx[:], in1=Ov[:, 2, :, 65], op=ALU.max)), r=omk + ["Mx"], w=["Mx"])
                    P.op("dve", (lambda Ov=Ov: V.tensor_tensor(out=dm[:], in0=Ov[:, :, :, 65], in1=Mx[:].unsqueeze(1).to_broadcast([128, 3, 8]), op=ALU.subtract)),
                         r=omk + ["Mx"], w=["dm"])
                    P.op("act", lambda: A.activation(out=ee[:], in_=dm[:], func=AF.Exp), r=["dm"], w=["ee"])
                    P.op("pool", (lambda Ov=Ov: G.tensor_tensor(out=wt[:], in0=Ov[:, :, :, 0:65], in1=ee[:].unsqueeze(3).to_broadcast([128, 3, 8, 65]), op=ALU.mult)),
                         r=omk + ["ee"], w=["wt"])
                    P.op("dve", lambda: V.tensor_tensor(out=ac[:], in0=wt[:, 0, :, :], in1=wt[:, 1, :, :], op=ALU.add), r=["wt"], w=["ac"])
                    P.op("dve", lambda: V.tensor_tensor(out=ac[:], in0=ac[:], in1=wt[:, 2, :, :], op=ALU.add), r=["wt", "ac"], w=["ac"])
                    P.op("dve", lambda: V.reciprocal(out=rd[:], in_=ac[:, :, 64]), r=["ac"], w=["rd"])
                    P.op("dve", lambda: V.tensor_tensor(out=bob[:], in0=ac[:, :, 0:64], in1=rd[:].unsqueeze(2).to_broadcast([128, 8, 64]), op=ALU.mult),
                         r=["ac", "rd"], w=["bob"])
                    tb_ = (i * 4 + blk) % 2
                    for c in range(4):
                        P.op("pe", (lambda c=c, tb_=tb_: PE.transpose(tp[tb_][:, c * 128:(c + 1) * 128], bob[:].rearrange("p h e -> p (h e)")[:, c * 128:(c + 1) * 128], ident_b[:])),
                             r=["bob", "ident_b"], w=[("tp3", tb_)])
                    P.op("act", (lambda blk=blk, tb_=tb_: A.activation(out=boT3[:, :, blk * 128:(blk + 1) * 128],
                                                                      in_=tp[tb_][:, 0:512].rearrange("p (c t) -> p c t", c=4), func=AF.Copy)),
                         r=[("tp3", tb_)], w=[("boT3", blk)])
                for n in range(8):
                    z = oc % 3
                    oc += 1
                    for kc in range(8):
                        rhs = ao_[b][:, kc, :] if kc < 4 else boT3[:, kc - 4, :]
                        rk = [("ao3", b)] if kc < 4 else [("boT3", k) for k in range(4)]
                        P.op("pe", (lambda n=n, kc=kc, z=z, rhs=rhs: PE.matmul(ops_[z][:], lhsT=w_o[:, kc, n * 128:(n + 1) * 128], rhs=rhs, start=(kc == 0), stop=(kc == 7))),
                             r=[("w_o", kc)] + rk, w=[("op3", z)])
                    P.op("dve", (lambda n=n, z=z, b=b: V.scalar_tensor_tensor(out=x1t[b][:, n, :], in0=ops_[z][:], scalar=modP[:, 0, G1 + n:G1 + n + 1],
                                                                               in1=xt_[b][:, n, :], op0=ALU.mult, op1=ALU.add)),
                         r=[("op3", z), ("xt3", b), "modP"], w=[("x1t", b, n)])
                P.dma("sp", (lambda b=b, t0=t0: nc.sync.dma_start(out=X1[:, :, t0:t0 + 512], in_=x1t[b][:])), r=[("x1t", b, n) for n in range(8)], w=[("X1", i)])
            P.flush()

        def ffn_phase(l, XIN, XOUT, final):
            with contextlib.ExitStack() as ph:
                TT = 256
                w_up = SB(ph, "w_up%d" % l, [128, 8, F2], BF16)
                w_dn = SB(ph, "w_dn%d" % l, [128, NJ, D], BF16)
                JG = [(0, 6), (6, 12), (12, 18), (18, 22)]
                jgrp = {}
                for g, (j0, j1) in enumerate(JG):
                    for j in range(j0, j1):
                        jgrp[j] = g
                    for kc in range(8):
                        P.dma("pool", (lambda kc=kc, j0=j0, j1=j1: G.dma_start(out=w_up[:, kc, j0 * 128:j1 * 128], in_=I["w_up"][l, :, kc, j0 * 128:j1 * 128])),
                              w=[("w_up", kc, g, 0)])
                        P.dma("pool", (lambda kc=kc, j0=j0, j1=j1: G.dma_start(out=w_up[:, kc, (NJ + j0) * 128:(NJ + j1) * 128],
                                                                                in_=I["w_up"][l, :, kc, (NJ + j0) * 128:(NJ + j1) * 128])),
                              w=[("w_up", kc, g, 1)])
                    for j in range(j0, j1):
                        P.dma("pool", (lambda j=j: G.dma_start(out=w_dn[:, j, :], in_=I["w_dn"][l, :, j, :])), w=[("w_dn", j)])
                xt_ = [SB(ph, "xtf%d_%d" % (l, i), [128, 8, TT]) for i in range(2)]
                sqb = SB(ph, "sqbf%d" % l, [128, 8, TT], BF16); xn = SB(ph, "xnf%d" % l, [128, 8, TT]); rt = SB(ph, "rtf%d" % l, [128, TT])
                hT_ = [SB(ph, "hTf%d_%d" % (l, i), [128, 8, TT], BF16) for i in range(2)]
                up = [SB(ph, "up%d_%d" % (l, i), [128, 2, 2 + TT]) for i in range(2)]
                uc = [SB(ph, "uc%d_%d" % (l, i), [128, 2, TT]) for i in range(2)]
                sa = [SB(ph, "sa%d_%d" % (l, i), [128, TT]) for i in range(2)]
                actj = [SB(ph, "actj%d_%d" % (l, i), [128, TT], BF16) for i in range(2)]
                hist = SB(ph, "hist%d" % l, [128, 44, 2])
                _xo1 = SB(ph, "xo%d_0" % l, [128, 8, TT])
                xo = [_xo1, _xo1]
                nps = PS(ph, "npsf%d" % l, [128, 512])
                ups = [PS(ph, "ups%d_%d" % (l, i), [128, 512]) for i in range(3)]
                accp = PS(ph, "accp%d" % l, [128, 8 * TT])
                yo = None
                if final:
                    rty = SB(ph, "rty", [128, TT])
                fcw = lambda k, j: par[:, PAR_OFF["f_cw"] + (l * 3 + k) * 44 + j: PAR_OFF["f_cw"] + (l * 3 + k) * 44 + j + 1]
                fcb = lambda j: par[:, PAR_OFF["f_cb"] + l * 44 + j: PAR_OFF["f_cb"] + l * 44 + j + 1]
                P.op("pool", lambda: G.memset(hist[:], 0.0), w=[("hist", j) for j in range(44)])

                def loadf(i):
                    b = i % 2
                    P.dma("sp", (lambda: nc.sync.dma_start(out=xt_[b][:], in_=XIN[:, :, i * TT:(i + 1) * TT])), w=[("xtf", b)])
                loadf(0)
                jc_ = [0]
                tag = "ff"
                norm_prompt(("xtf", 0), tag, xt_[0], TT, l, SH2, SC2, sqb, nps, rt, hT_[0], xn, hkey=("hTf", 0))
                for i in range(NT2):
                    b = i % 2
                    if i + 1 < NT2:
                        loadf(i + 1)
                    hT = hT_[b]
                    hk = ("hTf", b)
                    zub = {}

                    def emit_up(j):
                        z = jc_[0] % 3
                        ub = jc_[0] % 2
                        jc_[0] += 1
                        zub[j] = (z, ub)
                        for half, jj in ((0, j), (1, j + NJ)):
                            for kc in range(8):
                                P.op("pe", (lambda half=half, jj=jj, kc=kc, z=z, hT=hT: PE.matmul(ups[z][:, half * TT:(half + 1) * TT], lhsT=w_up[:, kc, jj * 128:(jj + 1) * 128],
                                                                                            rhs=hT[:, kc, :], start=(kc == 0), stop=(kc == 7))),
                                     r=[("w_up", kc, jgrp[j], half), hk], w=[("ups", z)])
                    def elemA(j):
                        z, ub = zub[j]
                        P.op("act", (lambda z=z, ub=ub: A.activation(out=up[ub][:, :, 2:2 + TT], in_=ups[z][:, 0:2 * TT].rearrange("p (h t) -> p h t", h=2), func=AF.Copy)),
                             r=[("ups", z)], w=[("up", ub, 0), ("up", ub, 1)])
                        for half, jj in ((0, j), (1, j + NJ)):
                            P.op("pool", (lambda half=half, jj=jj, ub=ub: G.tensor_copy(out=up[ub][:, half, 0:2], in_=hist[:, jj, :])),
                                 r=[("hist", jj)], w=[("up", ub, half, "h")])
                            P.op("act", (lambda half=half, jj=jj, z=z, ub=ub: A.activation(out=uc[ub][:, half, :], in_=ups[z][:, half * TT:(half + 1) * TT], func=AF.Identity,
                                                                                            scale=fcw(2, jj), bias=fcb(jj))),
                                 r=[("ups", z), "par"], w=[("uc", ub, half)])
                            P.op("dve", (lambda half=half, jj=jj, ub=ub: V.scalar_tensor_tensor(out=uc[ub][:, half, :], in0=up[ub][:, half, 1:1 + TT], scalar=fcw(1, jj),
                                                                                                 in1=uc[ub][:, half, :], op0=ALU.mult, op1=ALU.add)),
                                 r=[("up", ub, half), ("up", ub, half, "h"), ("uc", ub, half), "par"], w=[("uc", ub, half)])
                            P.op("dve", (lambda half=half, jj=jj, ub=ub: V.scalar_tensor_tensor(out=uc[ub][:, half, :], in0=up[ub][:, half, 0:TT], scalar=fcw(0, jj),
                                                                                                 in1=uc[ub][:, half, :], op0=ALU.mult, op1=ALU.add)),
                                 r=[("up", ub, half), ("up", ub, half, "h"), ("uc", ub, half), "par"], w=[("uc", ub, half)])
                            P.op("pool", (lambda half=half, jj=jj, ub=ub: G.tensor_copy(out=hist[:, jj, :], in_=up[ub][:, half, TT:TT + 2])),
                                 r=[("up", ub, half)], w=[("hist", jj)])

                    def elemB(j):
                        z, ub = zub[j]
                        P.op("act", (lambda ub=ub: A.activation(out=sa[ub][:], in_=uc[ub][:, 0, :], func=AF.Silu)), r=[("uc", ub, 0)], w=[("sa", ub)])
                        P.op("pool", (lambda ub=ub: G.tensor_tensor(out=actj[ub][:], in0=sa[ub][:], in1=uc[ub][:, 1, :], op=ALU.mult)),
                             r=[("sa", ub), ("uc", ub, 1)], w=[("actj", ub)])
                        for n in range(8):
                            P.op("pe", (lambda n=n, j=j, ub=ub: PE.matmul(accp[:, n * TT:(n + 1) * TT], lhsT=w_dn[:, j, n * 128:(n + 1) * 128], rhs=actj[ub][:],
                                                                          start=(j == 0 and n % 2 == 0), stop=(j == NJ - 1), skip_group_check=True)),
                                 r=[("w_dn", j), ("actj", ub)], w=["accp"])

                    emit_up(0)
                    emit_up(1)
                    for j in range(NJ):
                        if j + 2 < NJ:
                            emit_up(j + 2)
                        elemA(j)
                        if j >= 1:
                            elemB(j - 1)
                        if j == 8 and i + 1 < NT2:
                            nb_ = (i + 1) % 2
                            norm_prompt(("xtf", nb_), tag, xt_[nb_], TT, l, SH2, SC2, sqb, nps, rt, hT_[nb_], xn, hkey=("hTf", nb_))
                    elemB(NJ - 1)
                    for n in range(8):
                        P.op("dve", (lambda n=n, b=b: V.scalar_tensor_tensor(out=xo[b][:, n, :], in0=accp[:, n * TT:(n + 1) * TT], scalar=modP[:, l, G2 + n:G2 + n + 1],
                                                                              in1=xt_[b][:, n, :], op0=ALU.mult, op1=ALU.add)),
                             r=["accp", ("xtf", b), "modP"], w=[("xo", 0, n)])
                    if not final:
                        P.dma("sp", (lambda b=b, i=i: nc.sync.dma_start(out=XOUT[:, :, i * TT:(i + 1) * TT], in_=xo[b][:])), r=[("xo", 0, n) for n in range(8)], w=[("XOUT", i)])
                    else:
                        xok = [("xo", 0, n) for n in range(8)]
                        P.op("act", (lambda b=b: A.activation(out=sqb[:], in_=xo[b][:], func=AF.Square)), r=xok, w=[(tag, "sqb")])
                        for c in range(8):
                            P.op("pe", (lambda c=c: PE.matmul(nps[:, :TT], lhsT=ones_b[:], rhs=sqb[:, c, :], start=(c == 0), stop=(c == 7))), r=[(tag, "sqb"), "ones_b"], w=[(tag, "nps")])
                        P.op("act", lambda: A.activation(out=rty[:], in_=nps[:, :TT], func=AF.Sqrt, scale=1.0 / D, bias=epsb[:, 0:1]), r=[(tag, "nps"), "epsb"], w=["rty"])
                        P.op("dve", lambda: V.reciprocal(out=rty[:], in_=rty[:]), r=["rty"], w=["rty"])
                        for c in range(8):
                            P.op("dve", (lambda c=c, b=b: V.scalar_tensor_tensor(out=xo[b][:, c, :], in0=xo[b][:, c, :], scalar=pcol("fin_g", c), in1=rty[:],
                                                                                  op0=ALU.mult, op1=ALU.mult)), r=[("xo", 0, c), "rty", "par"], w=[("xo", 0, c)])
                        P.dma("sp", (lambda b=b, i=i: nc.sync.dma_start(out=O["yT"][:, :, i * TT:(i + 1) * TT], in_=xo[b][:])), r=[("xo", 0, c) for c in range(8)], w=[("YO", i)])
                P.dma("sp", lambda: nc.sync.dma_start(out=O["ffn_o"][l, :, :, :], in_=hist[:]), r=[("hist", j) for j in range(44)], w=[("ffn_o", l)])

                sq_s = SB(ph, "sqs%d" % l, [128, 8, NS]); xn_s = SB(ph, "xns%d" % l, [128, 8, NS]); hTs = SB(ph, "hTs%d" % l, [128, 8, NS], BF16)
                stf = SB(ph, "stf%d" % l, [128, 44, 2, NS])
                ups_s = SB(ph, "ups_s%d" % l, [128, 44, NS]); ucs = SB(ph, "ucs%d" % l, [128, 44, NS]); tmp = SB(ph, "tmps%d" % l, [128, 44, NS])
                fo = SB(ph, "fo%d" % l, [128, 44, 2, NS])
                acts = SB(ph, "acts%d" % l, [128, NJ, NS], BF16); sas = SB(ph, "sas%d" % l, [128, NJ, NS])
                mo_s = SB(ph, "mofs%d" % l, [128, 8, NS])
                P.dma("sp", lambda: nc.sync.dma_start(out=stf[:], in_=I["st_ffn"][l, :, :, :, :]), w=["stf"])
                norm_sample(("fs", l), l, SH2, SC2, sq_s, nps, rt, hTs, xn_s)
                for jj in range(44):
                    for kc in range(8):
                        P.op("pe", (lambda jj=jj, kc=kc: PE.matmul(ups[0][:, jj * NS:(jj + 1) * NS], lhsT=w_up[:, kc, jj * 128:(jj + 1) * 128], rhs=hTs[:, kc, :],
                                                                    start=(kc == 0), stop=(kc == 7))), r=[("w_up", kc, jgrp[jj % NJ], jj // NJ), (("fs", l), "hT")], w=[("ups", 0)])
                upv = ups[0][:, 0:44 * NS].rearrange("p (j s) -> p j s", j=44)
                P.op("act", lambda: A.activation(out=ups_s[:], in_=upv, func=AF.Copy), r=[("ups", 0)], w=["ups_s"])
                bc = lambda k: par[:, PAR_OFF["f_cw"] + (l * 3 + k) * 44: PAR_OFF["f_cw"] + (l * 3 + k + 1) * 44].unsqueeze(2).to_broadcast([128, 44, NS])
                bcb = par[:, PAR_OFF["f_cb"] + l * 44: PAR_OFF["f_cb"] + (l + 1) * 44].unsqueeze(2).to_broadcast([128, 44, NS])
                P.op("dve", lambda: V.tensor_tensor(out=ucs[:], in0=ups_s[:], in1=bc(2), op=ALU.mult), r=["ups_s", "par"], w=["ucs"])
                P.op("dve", lambda: V.tensor_tensor(out=ucs[:], in0=ucs[:], in1=bcb, op=ALU.add), r=["ucs", "par"], w=["ucs"])
                for k in range(2):
                    P.op("dve", (lambda k=k: V.tensor_tensor(out=tmp[:], in0=stf[:, :, k, :], in1=bc(k), op=ALU.mult)), r=["stf", "par", "ucs"], w=["tmp"])
                    P.op("dve", lambda: V.tensor_tensor(out=ucs[:], in0=ucs[:], in1=tmp[:], op=ALU.add), r=["ucs", "tmp"], w=["ucs"])
                P.op("act", lambda: A.activation(out=sas[:], in_=ucs[:, 0:NJ, :], func=AF.Silu), r=["ucs"], w=["sas"])
                P.op("dve", lambda: V.tensor_tensor(out=acts[:], in0=sas[:], in1=ucs[:, NJ:44, :], op=ALU.mult), r=["sas", "ucs"], w=["acts"])
                for n in range(8):
                    for j in range(NJ):
                        P.op("pe", (lambda n=n, j=j: PE.matmul(ups[1][:, n * NS:(n + 1) * NS], lhsT=w_dn[:, j, n * 128:(n + 1) * 128], rhs=acts[:, j, :],
                                                                start=(j == 0), stop=(j == NJ - 1))), r=[("w_dn", j), "acts"], w=[("ups", 1)])
                P.op("dve", lambda: V.tensor_tensor(out=mo_s[:], in0=ups[1][:, 0:8 * NS].rearrange("p (c s) -> p c s", c=8), in1=modS[:, l, G2:G2 + 8, :], op=ALU.mult),
                     r=[("ups", 1), "modS"], w=["mofs"])
                P.op("dve", lambda: V.tensor_tensor(out=xs_cur[:], in0=xs_cur[:], in1=mo_s[:], op=ALU.add), r=["mofs", "xs"], w=["xs"])
                P.op("pool", lambda: G.tensor_copy(out=fo[:, :, 0, :], in_=stf[:, :, 1, :]), r=["stf"], w=["fo0"])
                P.op("pool", lambda: G.tensor_copy(out=fo[:, :, 1, :], in_=ups_s[:]), r=["ups_s"], w=["fo1"])
                P.dma("sp", lambda: nc.sync.dma_start(out=O["ffn_s"][l, :, :, :, :], in_=fo[:]), r=["fo0", "fo1"], w=[("ffn_s", l)])
                if final:
                    ys = SB(ph, "ys", [128, 8, NS])
                    P.op("dve", lambda: V.tensor_tensor(out=sq_s[:], in0=xs_cur[:], in1=xs_cur[:], op=ALU.mult), r=["xs"], w=["sqs_f"])
                    for c in range(8):
                        P.op("pe", (lambda c=c: PE.matmul(nps[:, :NS], lhsT=ones_f[:], rhs=sq_s[:, c, :], start=(c == 0), stop=(c == 7))), r=["sqs_f", "ones_f"], w=["nps_f"])
                    P.op("act", lambda: A.activation(out=rt[:, :NS], in_=nps[:, :NS], func=AF.Sqrt, scale=1.0 / D, bias=epsb[:, 0:1]), r=["nps_f", "epsb"], w=["rt_f"])
                    P.op("dve", lambda: V.reciprocal(out=rt[:, :NS], in_=rt[:, :NS]), r=["rt_f"], w=["rt_f"])
                    P.op("dve", lambda: V.tensor_tensor(out=ys[:], in0=xs_cur[:], in1=rt[:, :NS].unsqueeze(1).to_broadcast([128, 8, NS]), op=ALU.mult), r=["xs", "rt_f"], w=["ys"])
                    P.op("dve", lambda: V.tensor_tensor(out=ys[:], in0=ys[:], in1=pcol("fin_g", 0, 8).unsqueeze(2).to_broadcast([128, 8, NS]), op=ALU.mult), r=["ys", "par"], w=["ys"])
                    P.dma("sp", lambda: nc.sync.dma_start(out=O["ysT"][:, :, :], in_=ys[:]), r=["ys"], w=["ysT"])
                P.flush()

        ffn_phase(0, X1, X2, False)

        with contextlib.ExitStack() as ph:
            TT = 256
            w_in = SB(ph, "w_ino", [128, 8, 2048], BF16); w_o = SB(ph, "w_oo", [128, 8, D], BF16)
            wa = SB(ph, "wa_sb", [128, 8, 128], BF16); wx = SB(ph, "wx_sb", [128, 8, 128], BF16)
            load_w(None, w_in, lambda kc: I["w_in_o"][:, kc, :], 8, "w_ino")
            load_w(None, w_o, lambda kc: I["w_out_o"][:, kc, :], 8, "w_oo")
            P.dma("pool", lambda: G.dma_start(out=wa[:], in_=I["rg_wa"][:, :, :]), w=["wa"])
            P.dma("pool", lambda: G.dma_start(out=wx[:], in_=I["rg_wx"][:, :, :]), w=["wx"])
            xt_ = [SB(ph, "xtr%d" % i, [128, 8, TT]) for i in range(2)]
            sqb = SB(ph, "sqbr", [128, 8, TT], BF16); xn = SB(ph, "xnr", [128, 8, TT]); rt = SB(ph, "rtr", [128, TT]); hT = SB(ph, "hTr", [128, 8, TT], BF16)
            xr = SB(ph, "xr", [128, 8, 3 + TT])
            gg = [SB(ph, "gg%d" % i, [128, 8, TT]) for i in range(2)]
            xc = [SB(ph, "xc%d" % i, [128, 8, TT]) for i in range(2)]
            xcb = [SB(ph, "xcb%d" % i, [128, 8, TT], BF16) for i in range(2)]
            rr = SB(ph, "rr", [128, 8, TT]); ii = SB(ph, "ii", [128, 8, TT]); aa = SB(ph, "aa", [128, 8, TT]); ss = SB(ph, "ss", [128, 8, TT])
            hs = SB(ph, "hs", [128, 8, TT])
            yb = SB(ph, "yb", [128, 8, TT], BF16)
            hst = SB(ph, "hst", [128, 8])
            xo = [SB(ph, "xor%d" % i, [128, 8, TT]) for i in range(2)]
            nps = PS(ph, "npsr", [128, 512])
            zps = [PS(ph, "zpr%d" % i, [128, 512]) for i in range(3)]
            gps = [PS(ph, "gpr%d" % i, [128, 1024]) for i in range(2)]
            rcw = lambda k, c: par[:, PAR_OFF["rg_cw"] + k * 8 + c: PAR_OFF["rg_cw"] + k * 8 + c + 1]
            P.op("pool", lambda: G.memset(hst[:], 0.0), w=[("hst", c) for c in range(8)])
            P.op("pool", lambda: G.memset(xr[:, :, 0:3], 0.0), w=[("xrh", c) for c in range(8)])

            def loadr(i):
                b = i % 2
                P.dma("sp", (lambda: nc.sync.dma_start(out=xt_[b][:], in_=X2[:, :, i * TT:(i + 1) * TT])), w=[("xtr", b)])
            zc_ = [0]
            tag = "rg"

            def stage1a(i):
                b = i % 2
                norm_prompt(("xtr", b), tag, xt_[b], TT, 1, SH1, SC1, sqb, nps, rt, hT, xn)

            def stage1b(i):
                b = i % 2
                for n in list(range(8, 16)) + list(range(8)):
                    z = zc_[0] % 3
                    zc_[0] += 1
                    for kc in range(8):
                        P.op("pe", (lambda n=n, kc=kc, z=z: PE.matmul(zps[z][:, 0:TT], lhsT=w_in[:, kc, n * 128:(n + 1) * 128], rhs=hT[:, kc, :], start=(kc == 0), stop=(kc == 7))),
                             r=[("w_ino", kc), (tag, "hT")], w=[("zpr", z)])
                    if n < 8:
                        P.op("act", (lambda n=n, z=z, b=b: A.activation(out=gg[b][:, n, :], in_=zps[z][:, 0:TT], func=GELU)), r=[("zpr", z)], w=[("gg", b, n)])
                    else:
                        c = n - 8
                        P.op("act", (lambda c=c, z=z: A.activation(out=xr[:, c, 3:3 + TT], in_=zps[z][:, 0:TT], func=AF.Copy)), r=[("zpr", z)], w=[("xr", c)])
                        P.op("act", (lambda c=c, z=z, b=b: A.activation(out=xc[b][:, c, :], in_=zps[z][:, 0:TT], func=AF.Identity, scale=rcw(3, c), bias=pcol("rg_cb", c))),
                             r=[("zpr", z), "par"], w=[("xc", b, c)])
                        for k in range(3):
                            P.op("dve", (lambda c=c, k=k, b=b: V.scalar_tensor_tensor(out=xc[b][:, c, :], in0=xr[:, c, k:k + TT], scalar=rcw(k, c), in1=xc[b][:, c, :],
                                                                                        op0=ALU.mult, op1=ALU.add)),
                                 r=[("xr", c), ("xrh", c), ("xc", b, c), "par"], w=[("xc", b, c)])
                        P.op("pool", (lambda c=c: G.tensor_copy(out=xr[:, c, 0:3], in_=xr[:, c, TT:TT + 3])), r=[("xr", c)], w=[("xrh", c)])
                        P.op("pool", (lambda c=c, b=b: G.tensor_copy(out=xcb[b][:, c, :], in_=xc[b][:, c, :])), r=[("xc", b, c)], w=[("xcb", b, c)])

            def stage2a(i):
                b = i % 2
                for c in range(8):
                    q2 = c % 2
                    P.op("pe", (lambda c=c, q2=q2, b=b: PE.matmul(gps[q2][:, 0:TT], lhsT=wa[:, c, :], rhs=xcb[b][:, c, :], start=True, stop=True)), r=["wa", ("xcb", b, c)], w=[("gpr", q2)])
                    P.op("pe", (lambda c=c, q2=q2, b=b: PE.matmul(gps[q2][:, 512:512 + TT], lhsT=wx[:, c, :], rhs=xcb[b][:, c, :], start=True, stop=True)), r=["wx", ("xcb", b, c)], w=[("gpr", q2)])
                    P.op("act", (lambda c=c, q2=q2: A.activation(out=rr[:, c, :], in_=gps[q2][:, 0:TT], func=AF.Sigmoid, bias=pcol("rg_ba", c), scale=1.0)),
                         r=[("gpr", q2), "par"], w=[("rr", c)])
                    P.op("act", (lambda c=c, q2=q2: A.activation(out=ii[:, c, :], in_=gps[q2][:, 512:512 + TT], func=AF.Sigmoid, bias=pcol("rg_bx", c), scale=1.0)),
                         r=[("gpr", q2), "par"], w=[("ii", c)])
                    P.op("pool", (lambda c=c, b=b: G.tensor_tensor(out=ii[:, c, :], in0=ii[:, c, :], in1=xc[b][:, c, :], op=ALU.mult)), r=[("ii", c), ("xc", b, c)], w=[("ii", c)])
                for c in range(8):
                    P.op("act", (lambda c=c: A.activation(out=aa[:, c, :], in_=rr[:, c, :], func=AF.Exp, scale=cst[:, c:c + 1])), r=[("rr", c), "cst"], w=[("aa", c)])
                    P.op("act", (lambda c=c: A.activation(out=ss[:, c, :], in_=rr[:, c, :], func=AF.Exp, scale=cst2[:, c:c + 1])), r=[("rr", c), "cst2"], w=[("ss", c)])
                for c in range(8):
                    P.op("act", (lambda c=c: A.activation(out=ss[:, c, :], in_=ss[:, c, :], func=AF.Sqrt, scale=-1.0, bias=oneb[:, 0:1])), r=[("ss", c), "oneb"], w=[("ss", c)])
                    P.op("dve", (lambda c=c: V.tensor_tensor(out=ii[:, c, :], in0=ii[:, c, :], in1=ss[:, c, :], op=ALU.mult)), r=[("ii", c), ("ss", c)], w=[("ii", c)])
                    P.op("dve", (lambda c=c: V.tensor_tensor_scan(out=hs[:, c, :], data0=aa[:, c, :], data1=ii[:, c, :], initial=hst[:, c:c + 1], op0=ALU.mult, op1=ALU.add)),
                         r=[("aa", c), ("ii", c), ("hst", c)], w=[("hs", c)])
                    P.op("dve", (lambda c=c: V.tensor_copy(out=hst[:, c:c + 1], in_=hs[:, c, TT - 1:TT])), r=[("hs", c)], w=[("hst", c)])
                    P.op("pool", (lambda c=c, b=b: G.tensor_tensor(out=yb[:, c, :], in0=hs[:, c, :], in1=gg[b][:, c, :], op=ALU.mult)),
                         r=[("hs", c), ("gg", b, c)], w=[("yb", c)])

            def stage2b(i):
                b = i % 2
                for n in range(8):
                    z = zc_[0] % 3
                    zc_[0] += 1
                    for kc in range(8):
                        P.op("pe", (lambda n=n, kc=kc, z=z: PE.matmul(zps[z][:, 0:TT], lhsT=w_o[:, kc, n * 128:(n + 1) * 128], rhs=yb[:, kc, :], start=(kc == 0), stop=(kc == 7))),
                             r=[("w_oo", kc), ("yb", kc)], w=[("zpr", z)])
                    P.op("dve", (lambda n=n, z=z, b=b: V.scalar_tensor_tensor(out=xo[b][:, n, :], in0=zps[z][:, 0:TT], scalar=modP[:, 1, G1 + n:G1 + n + 1], in1=xt_[b][:, n, :],
                                                                               op0=ALU.mult, op1=ALU.add)), r=[("zpr", z), ("xtr", b), "modP"], w=[("xor", b, n)])
                P.dma("sp", (lambda b=b, i=i: nc.sync.dma_start(out=X3[:, :, i * TT:(i + 1) * TT], in_=xo[b][:])), r=[("xor", b, n) for n in range(8)], w=[("X3", i)])

            loadr(0)
            stage1a(0)
            stage1b(0)
            for i in range(NT2):
                if i + 1 < NT2:
                    loadr(i + 1)
                stage2a(i)
                if i + 1 < NT2:
                    stage1a(i + 1)
                    stage1b(i + 1)
                stage2b(i)
            P.dma("sp", lambda: nc.sync.dma_start(out=O["rgc_o"][:, :, :], in_=xr[:, :, 0:3]), r=[("xrh", c) for c in range(8)], w=["rgc_o"])
            P.dma("sp", lambda: nc.sync.dma_start(out=O["rgh_o"][:, :], in_=hst[:]), r=[("hst", c) for c in range(8)], w=["rgh_o"])

            sq_s = SB(ph, "sqsr", [128, 8, NS]); xn_s = SB(ph, "xnsr", [128, 8, NS]); hTs = SB(ph, "hTsr", [128, 8, NS], BF16)
            stc = SB(ph, "stc", [128, 8, 3, NS]); sth = SB(ph, "sth", [128, 8, NS])
            zs = SB(ph, "zs", [128, 16, NS]); ggs = SB(ph, "ggs", [128, 8, NS])
            xcs = SB(ph, "xcs", [128, 8, NS]); tmp = SB(ph, "tmpr", [128, 8, NS]); xcsb = SB(ph, "xcsb", [128, 8, NS], BF16)
            rs_ = SB(ph, "rs_", [128, 8, NS]); is_ = SB(ph, "is_", [128, 8, NS]); as_ = SB(ph, "as_", [128, 8, NS]); s2 = SB(ph, "s2_", [128, 8, NS])
            hn = SB(ph, "hn", [128, 8, NS]); ybs = SB(ph, "ybs", [128, 8, NS], BF16); mo_s = SB(ph, "mors", [128, 8, NS])
            co = SB(ph, "co", [128, 8, 3, NS])
            P.dma("sp", lambda: nc.sync.dma_start(out=stc[:], in_=I["st_rgc"][:, :, :, :]), w=["stc"])
            P.dma("sp", lambda: nc.sync.dma_start(out=sth[:], in_=I["st_rgh"][:, :, :]), w=["sth"])
            norm_sample("rs", 1, SH1, SC1, sq_s, nps, rt, hTs, xn_s)
            for n in range(16):
                for kc in range(8):
                    P.op("pe", (lambda n=n, kc=kc: PE.matmul(zps[0][:, n * NS:(n + 1) * NS], lhsT=w_in[:, kc, n * 128:(n + 1) * 128], rhs=hTs[:, kc, :], start=(kc == 0), stop=(kc == 7))),
                         r=[("w_ino", kc), ("rs", "hT")], w=[("zpr", 0)])
            P.op("act", lambda: A.activation(out=zs[:], in_=zps[0][:, 0:16 * NS].rearrange("p (c s) -> p c s", c=16), func=AF.Copy), r=[("zpr", 0)], w=["zs"])
            P.op("act", lambda: A.activation(out=ggs[:], in_=zs[:, 0:8, :], func=GELU), r=["zs"], w=["ggs"])
            bw = lambda k: par[:, PAR_OFF["rg_cw"] + k * 8: PAR_OFF["rg_cw"] + (k + 1) * 8].unsqueeze(2).to_broadcast([128, 8, NS])
            b8 = lambda nm: pcol(nm, 0, 8).unsqueeze(2).to_broadcast([128, 8, NS])
            P.op("dve", lambda: V.tensor_tensor(out=xcs[:], in0=zs[:, 8:16, :], in1=bw(3), op=ALU.mult), r=["zs", "par"], w=["xcs"])
            P.op("dve", lambda: V.tensor_tensor(out=xcs[:], in0=xcs[:], in1=b8("rg_cb"), op=ALU.add), r=["xcs", "par"], w=["xcs"])
            for k in range(3):
                P.op("dve", (lambda k=k: V.tensor_tensor(out=tmp[:], in0=stc[:, :, k, :], in1=bw(k), op=ALU.mult)), r=["stc", "par", "xcs"], w=["tmpr"])
                P.op("dve", lambda: V.tensor_tensor(out=xcs[:], in0=xcs[:], in1=tmp[:], op=ALU.add), r=["xcs", "tmpr"], w=["xcs"])
            P.op("dve", lambda: V.tensor_copy(out=xcsb[:], in_=xcs[:]), r=["xcs"], w=["xcsb"])
            for c in range(8):
                P.op("pe", (lambda c=c: PE.matmul(zps[1][:, c * NS:(c + 1) * NS], lhsT=wa[:, c, :], rhs=xcsb[:, c, :], start=True, stop=True)), r=["wa", "xcsb"], w=[("zpr", 1)])
                P.op("pe", (lambda c=c: PE.matmul(zps[1][:, 64 + c * NS: 64 + (c + 1) * NS], lhsT=wx[:, c, :], rhs=xcsb[:, c, :], start=True, stop=True)), r=["wx", "xcsb"], w=[("zpr", 1)])
            P.op("dve", lambda: V.tensor_tensor(out=rs_[:], in0=zps[1][:, 0:8 * NS].rearrange("p (c s) -> p c s", c=8), in1=b8("rg_ba"), op=ALU.add), r=[("zpr", 1), "par"], w=["rs_"])
            P.op("dve", lambda: V.tensor_tensor(out=is_[:], in0=zps[1][:, 64:64 + 8 * NS].rearrange("p (c s) -> p c s", c=8), in1=b8("rg_bx"), op=ALU.add), r=[("zpr", 1), "par"], w=["is_"])
            P.op("act", lambda: A.activation(out=rs_[:], in_=rs_[:], func=AF.Sigmoid), r=["rs_"], w=["rs_"])
            P.op("act", lambda: A.activation(out=is_[:], in_=is_[:], func=AF.Sigmoid), r=["is_"], w=["is_"])
            P.op("dve", lambda: V.tensor_tensor(out=as_[:], in0=rs_[:], in1=cst[:].unsqueeze(2).to_broadcast([128, 8, NS]), op=ALU.mult), r=["rs_", "cst"], w=["as_"])
            P.op("act", lambda: A.activation(out=s2[:], in_=as_[:], func=AF.Exp, scale=2.0), r=["as_"], w=["s2"])
            P.op("act", lambda: A.activation(out=as_[:], in_=as_[:], func=AF.Exp), r=["as_", "s2"], w=["as_"])
            P.op("act", lambda: A.activation(out=s2[:], in_=s2[:], func=AF.Sqrt, scale=-1.0, bias=oneb[:, 0:1]), r=["s2", "oneb"], w=["s2"])
            P.op("dve", lambda: V.tensor_tensor(out=is_[:], in0=is_[:], in1=xcs[:], op=ALU.mult), r=["is_", "xcs"], w=["is_"])
            P.op("dve", lambda: V.tensor_tensor(out=is_[:], in0=is_[:], in1=s2[:], op=ALU.mult), r=["is_", "s2"], w=["is_"])
            P.op("dve", lambda: V.tensor_tensor(out=hn[:], in0=as_[:], in1=sth[:], op=ALU.mult), r=["as_", "sth"], w=["hn"])
            P.op("dve", lambda: V.tensor_tensor(out=hn[:], in0=hn[:], in1=is_[:], op=ALU.add), r=["hn", "is_"], w=["hn"])
            P.dma("sp", lambda: nc.sync.dma_start(out=O["rgh_s"][:, :, :], in_=hn[:]), r=["hn"], w=["rgh_s"])
            P.op("pool", lambda: G.tensor_copy(out=co[:, :, 0:2, :], in_=stc[:, :, 1:3, :]), r=["stc"], w=["co0"])
            P.op("pool", lambda: G.tensor_copy(out=co[:, :, 2, :], in_=zs[:, 8:16, :]), r=["zs"], w=["co1"])
            P.dma("sp", lambda: nc.sync.dma_start(out=O["rgc_s"][:, :, :, :], in_=co[:]), r=["co0", "co1"], w=["rgc_s"])
            P.op("dve", lambda: V.tensor_tensor(out=ybs[:], in0=hn[:], in1=ggs[:], op=ALU.mult), r=["hn", "ggs"], w=["ybs"])
            for n in range(8):
                for kc in range(8):
                    P.op("pe", (lambda n=n, kc=kc: PE.matmul(zps[2][:, n * NS:(n + 1) * NS], lhsT=w_o[:, kc, n * 128:(n + 1) * 128], rhs=ybs[:, kc, :], start=(kc == 0), stop=(kc == 7))),
                         r=[("w_oo", kc), "ybs"], w=[("zpr", 2)])
            P.op("dve", lambda: V.tensor_tensor(out=mo_s[:], in0=zps[2][:, 0:8 * NS].rearrange("p (c s) -> p c s", c=8), in1=modS[:, 1, G1:G1 + 8, :], op=ALU.mult),
                 r=[("zpr", 2), "modS"], w=["mors"])
            P.op("dve", lambda: V.tensor_tensor(out=xs_cur[:], in0=xs_cur[:], in1=mo_s[:], op=ALU.add), r=["mors", "xs"], w=["xs"])
            P.flush()

        ffn_phase(1, X3, None, True)
        P.barrier(final=True)
    return nc


_CACHE = {}


def _fm(v, n):
    return np.ascontiguousarray(np.asarray(v, np.float32).reshape(n, 128).T)


def _wk(w):
    K, N = w.shape
    return np.ascontiguousarray(np.asarray(w, np.float32).reshape(K // 128, 128, N).transpose(1, 0, 2))


def kernel(**inp):
    inp = {k: np.asarray(v) for k, v in inp.items()}
    x_prompt = inp["x_prompt"]
    B, T, _ = x_prompt.shape
    TK = min(LCACHE, T)
    if T not in _CACHE:
        _CACHE[T] = build_program(T)
    nc = _CACHE[T]
    NCORES = 8
    par = np.zeros((128, NPAR), np.float32)

    def put(name, arr):
        par[:, PAR_OFF[name]:PAR_OFF[name] + arr.shape[1]] = arr
    put("b_ada", np.concatenate([_fm(inp["b_ada"][0], 48), _fm(inp["b_ada"][1], 48)], 1))
    put("ln_g", _fm(inp["ln_v_g"][0], 4)); put("ln_b", _fm(inp["ln_v_b"][0], 4))
    put("rg_cw", np.concatenate([_fm(inp["rg_conv_w"][0, k], 8) for k in range(4)], 1))
    put("rg_cb", _fm(inp["rg_conv_b"][0], 8)); put("rg_ba", _fm(inp["rg_b_a"][0], 8)); put("rg_bx", _fm(inp["rg_b_x"][0], 8))
    put("rg_lam", _fm(inp["rg_lambda"][0], 8))
    put("f_cw", np.concatenate([_fm(inp["ffn_conv_w"][l, k], 44) for l in range(2) for k in range(3)], 1))
    put("f_cb", np.concatenate([_fm(inp["ffn_conv_b"][l], 44) for l in range(2)], 1))
    put("fin_g", _fm(inp["final_g"], 8))
    w_oe = inp["w_out_even"][0]
    shared = dict(
        wada=np.stack([_wk(inp["w_ada"][l]) for l in range(2)]), params=par,
        w_in_e=_wk(inp["w_in_even"][0]), w_out_e=_wk(w_oe),
        w_out_eh=np.ascontiguousarray(w_oe[512:].reshape(8, 64, D).transpose(1, 0, 2)),
        w_sguT=np.ascontiguousarray(inp["w_sgu"][0].transpose(2, 0, 1)),
        b_sgu=np.ascontiguousarray(inp["b_sgu"][0].reshape(1, 512)),
        sg0=np.ascontiguousarray(np.concatenate([inp["w_sgu"][0][:, 0, 0], inp["b_sgu"][0][:, 0]]).reshape(1, 8)),
        w_in_o=_wk(inp["w_in_odd"][0]),
        rg_wa=np.ascontiguousarray(inp["rg_w_a"][0].transpose(1, 0, 2)), rg_wx=np.ascontiguousarray(inp["rg_w_x"][0].transpose(1, 0, 2)),
        w_out_o=_wk(inp["w_out_odd"][0]),
        w_up=np.stack([_wk(inp["ffn_w_up"][l]) for l in range(2)]), w_dn=np.stack([_wk(inp["ffn_w_down"][l]) for l in range(2)]),
    )
    rowmask = np.zeros((128, NS), np.float32)
    dmask = np.zeros((128, 8, 64), np.float32)
    for s in range(NS):
        rowmask[32 * s:32 * s + 8, s] = 1.0
        for h in range(8):
            dmask[32 * s + h, h, :] = 1.0
    shared["rowmask"] = rowmask
    shared["dmask"] = dmask.reshape(128, 512)
    in_maps = []
    SEQ_CORE = [0, 1, 4, 5][:B] if B <= 4 else list(range(B))
    zero_x = np.zeros((128, 8, T), np.float32)
    for c in range(NCORES):
        ss = slice(NS * c, NS * (c + 1))
        m = dict(shared)
        if c in SEQ_CORE:
            sq = SEQ_CORE.index(c)
            m["xT"] = np.ascontiguousarray(x_prompt[sq].T.reshape(8, 128, T).transpose(1, 0, 2))
            cp = inp["c_prompt"][sq:sq + 1]
        else:
            m["xT"] = zero_x
            cp = np.zeros((1, D), np.float32)
        m["xsT"] = np.ascontiguousarray(inp["x_sample"][ss, 0, :].T.reshape(8, 128, NS).transpose(1, 0, 2))
        cc = np.concatenate([cp, inp["c_sample"][ss]], 0)
        m["cT"] = np.ascontiguousarray(cc.T.reshape(8, 128, 1 + NS).transpose(1, 0, 2))
        m["ck"] = np.ascontiguousarray(inp["cache_win_k"][0, ss].reshape(NS, LCACHE, 512))
        m["cv"] = np.ascontiguousarray(inp["cache_win_v"][0, ss].reshape(NS, LCACHE, 512))
        m["st_rgc"] = np.ascontiguousarray(inp["state_rglru_conv"][0, ss].transpose(2, 1, 0).reshape(8, 128, 3, NS).transpose(1, 0, 2, 3))
        m["st_rgh"] = np.ascontiguousarray(inp["state_rglru_h"][0, ss].T.reshape(8, 128, NS).transpose(1, 0, 2))
        m["st_ffn"] = np.ascontiguousarray(inp["state_ffn_conv"][:, ss].transpose(0, 3, 2, 1).reshape(2, 44, 128, 2, NS).transpose(0, 2, 1, 3, 4))
        in_maps.append(m)
    res = run_bass_kernel_spmd(nc, in_maps, core_ids=list(range(NCORES)))
    R = res.results
    global _LAST
    _LAST = R

    def unfm(a):
        a = np.asarray(a)
        n = a.shape[1]
        rest = a.shape[2:]
        return np.moveaxis(a.transpose(1, 0, *range(2, a.ndim)).reshape(n * 128, *rest), 0, -1)

    y_prompt = np.stack([unfm(R[SEQ_CORE[b]]["yT"]) for b in range(B)]).astype(np.float32)
    y_sample = np.concatenate([unfm(R[c]["ysT"]) for c in range(NCORES)], 0)[:, None, :].astype(np.float32)
    win_k_p = np.stack([unfm(R[SEQ_CORE[b]]["kT_o"]).reshape(TK, 8, 64) for b in range(B)])[None].astype(np.float32)
    win_v_p = np.stack([unfm(R[SEQ_CORE[b]]["vT_o"]).reshape(TK, 8, 64) for b in range(B)])[None].astype(np.float32)
    rgc_p = np.stack([unfm(R[SEQ_CORE[b]]["rgc_o"]) for b in range(B)])[None].astype(np.float32)
    rgh_p = np.stack([unfm(R[SEQ_CORE[b]]["rgh_o"][:, :, None])[0] for b in range(B)])[None].astype(np.float32)
    ffn_p = np.stack([np.stack([unfm(R[SEQ_CORE[b]]["ffn_o"][l]) for b in range(B)]) for l in range(2)]).astype(np.float32)
    cv_s = np.concatenate([unfm(R[c]["cv_o"]) for c in range(NCORES)], 0)[None, :, None, :].astype(np.float32)
    wk_s = np.concatenate([R[c]["wk_s"].reshape(NS, LCACHE, 8, 64) for c in range(NCORES)], 0)[None].astype(np.float32)
    wv_s = np.concatenate([R[c]["wv_s"].reshape(NS, LCACHE, 8, 64) for c in range(NCORES)], 0)[None].astype(np.float32)
    rgc_s = np.concatenate([unfm(R[c]["rgc_s"]).transpose(1, 0, 2) for c in range(NCORES)], 0)[None].astype(np.float32)
    rgh_s = np.concatenate([unfm(R[c]["rgh_s"]) for c in range(NCORES)], 0)[None].astype(np.float32)
    ffn_s = np.stack([np.concatenate([unfm(R[c]["ffn_s"][l]).transpose(1, 0, 2) for c in range(NCORES)], 0) for l in range(2)]).astype(np.float32)
    return (y_prompt, y_sample, win_k_p, win_v_p, rgc_p, rgh_p, ffn_p, cv_s, wk_s, wv_s, rgc_s, rgh_s, ffn_s)
```
